# Optimizing a Trainium2 kernel written in Bass

```python
import math
import jax, jax.numpy as jnp
from jax import lax
import numpy as np

D_MODEL = 2048
BATCH = 4
SEQ = 2048
DEPTH = 4

GRID_W = 64
CTX_LEN = 256
N_GROUPS = 4
GROUP_W = D_MODEL // N_GROUPS
MIX_W = N_GROUPS * GROUP_W
CONV_WIDTH = 3

RWKV_HEAD = 64
RWKV_HEADS = GROUP_W // RWKV_HEAD
RWKV_DECAY_LORA = 96
RWKV_ICLR_LORA = 96
RWKV_GATE_LORA = 256
RWKV_IN = 3 * GROUP_W + RWKV_DECAY_LORA + RWKV_ICLR_LORA + RWKV_GATE_LORA
RWKV_SPLITS = [GROUP_W, 2 * GROUP_W, 3 * GROUP_W, 3 * GROUP_W + RWKV_DECAY_LORA,
               3 * GROUP_W + RWKV_DECAY_LORA + RWKV_ICLR_LORA]
RWKV_DECAY_SCALE = math.exp(-0.5)
RWKV_GN_EPS = 64e-5

MLA_HEADS = 4
MLA_NOPE = 128
MLA_ROPE = 64
MLA_V = GROUP_W // MLA_HEADS
MLA_Q_RANK = 384
MLA_KV_RANK = 256
MLA_IN = MLA_Q_RANK + MLA_KV_RANK + MLA_ROPE
MLA_SCALE = (MLA_NOPE + MLA_ROPE) ** -0.5
Q_BLOCK = 128

CONV_GROUPS = 8
CONV_IN = 3 * GROUP_W

RET_HEADS = 4
RET_HEAD = GROUP_W // RET_HEADS
RET_CHUNK = 128
RET_IN = 4 * GROUP_W
GN_EPS = 1e-5

IN_W = RWKV_IN + MLA_IN + CONV_IN + RET_IN
GROUP_SPLITS = [RWKV_IN, RWKV_IN + MLA_IN, RWKV_IN + MLA_IN + CONV_IN]

D_FF = 5632
N_MOD = 6
ROPE_BASE = 10000.0
NORM_EPS = 1e-6

kernel_name = 'hybrid_parallel_groups_dit_block'

f32 = jnp.float32


def rms_norm(x, g):
    xf = x.astype(f32)
    y = xf * lax.rsqrt(jnp.mean(xf * xf, axis=-1, keepdims=True) + NORM_EPS)
    return (y * g.astype(f32)).astype(x.dtype)


def head_norm(y, eps):
    yf = y.astype(f32)
    mu = jnp.mean(yf, axis=-1, keepdims=True)
    var = jnp.mean(jnp.square(yf - mu), axis=-1, keepdims=True)
    return (yf - mu) * lax.rsqrt(var + eps)


def dwconv3(u, w):
    ch = u.shape[-1]
    return lax.conv_general_dilated(u, w[:, None, :].astype(u.dtype), window_strides=(1,),
                                    padding=((1, 1),), dimension_numbers=('NWC', 'WIO', 'NWC'),
                                    feature_group_count=ch)


def token_shift(u, mu_prev, mu_next):
    prev = jnp.pad(u, ((0, 0), (1, 0), (0, 0)))[:, :-1]
    nxt = jnp.pad(u, ((0, 0), (0, 1), (0, 0)))[:, 1:]
    return u + mu_prev * (prev - u) + mu_next * (nxt - u)


def axial_rope_angles(rows, rot_dim):
    half = rot_dim // 2
    inv = ROPE_BASE ** (-jnp.arange(0, half, 2, dtype=f32) / half)
    row = jnp.repeat(jnp.arange(rows, dtype=f32), GRID_W)
    col = jnp.tile(jnp.arange(GRID_W, dtype=f32), rows)
    return row[:, None] * inv[None, :], col[:, None] * inv[None, :]


def rope_rotate(x, ang):
    x1, x2 = jnp.split(x, 2, axis=-1)
    cos = jnp.cos(ang)[None, :, None, :].astype(x.dtype)
    sin = jnp.sin(ang)[None, :, None, :].astype(x.dtype)
    return jnp.concatenate([x1 * cos - x2 * sin, x2 * cos + x1 * sin], axis=-1)


def axial_rope(x, angs):
    d = x.shape[-1] // 2
    return jnp.concatenate([rope_rotate(x[..., :d], angs[0]), rope_rotate(x[..., d:], angs[1])], axis=-1)


def rwkv_heads(t):
    return t.reshape(t.shape[:-1] + (RWKV_HEADS, RWKV_HEAD))


def rwkv_features(p, shift, w0, w_up, a0, a_up, g_up, vecs):
    p = token_shift(p.astype(f32), shift[0], shift[1])
    r, k, v, wd, ad, gd = jnp.split(p, RWKV_SPLITS, axis=-1)
    k_k, k_a, r_k = vecs[0], vecs[1], vecs[2]
    kk = rwkv_heads(k * k_k)
    kk = kk * lax.rsqrt(jnp.sum(kk * kk, axis=-1, keepdims=True) + 1e-12)
    per_dir = []
    for d in range(2):
        w = jnp.exp(-RWKV_DECAY_SCALE * jax.nn.sigmoid(w0[d] + jnp.tanh(wd) @ w_up[d]))
        a = jax.nn.sigmoid(a0[d] + ad @ a_up[d])
        k_d = k * (1.0 + (a - 1.0) * k_a)
        per_dir.append((rwkv_heads(w), rwkv_heads(k_d), -kk, kk * rwkv_heads(a)))
    gate = jax.nn.sigmoid(gd) @ g_up
    bonus = jnp.sum(rwkv_heads(r * k * r_k), axis=-1, keepdims=True) * rwkv_heads(v)
    return rwkv_heads(r), rwkv_heads(v), per_dir, gate, bonus


def rwkv7_scan(r, v, w, k, a, b, s0, reverse):
    def step(s, inp):
        r_t, v_t, w_t, k_t, a_t, b_t = inp
        sa = jnp.einsum('bhvk,bhk->bhv', s, a_t)
        s = s * w_t[:, :, None, :] + sa[..., None] * b_t[:, :, None, :] + v_t[..., None] * k_t[:, :, None, :]
        return s, jnp.einsum('bhvk,bhk->bhv', s, r_t)
    xs = tuple(t.transpose(1, 0, 2, 3) for t in (r, v, w, k, a, b))
    s_fin, ys = lax.scan(step, s0, xs, reverse=reverse)
    return s_fin, ys.transpose(1, 0, 2, 3)


def rwkv_mixer(p_lat, p_ctx, shift, w0, w_up, a0, a_up, g_up, vecs):
    r_l, v_l, dirs_l, g_l, bonus_l = rwkv_features(p_lat, shift, w0, w_up, a0, a_up, g_up, vecs)
    r_c, v_c, dirs_c, g_c, bonus_c = rwkv_features(p_ctx, shift, w0, w_up, a0, a_up, g_up, vecs)
    s0 = jnp.zeros((p_lat.shape[0], RWKV_HEADS, RWKV_HEAD, RWKV_HEAD), f32)
    ys_l, ys_c = [], []
    for d, rev in enumerate((False, True)):
        s_ctx, y_c = rwkv7_scan(r_c, v_c, *dirs_c[d], s0, rev)
        _, y_l = rwkv7_scan(r_l, v_l, *dirs_l[d], s_ctx, rev)
        ys_c.append(y_c)
        ys_l.append(y_l)
    ln_g = rwkv_heads(vecs[3].astype(f32))
    ln_b = rwkv_heads(vecs[4].astype(f32))

    def finish(y, bonus, gate):
        y = head_norm(y, RWKV_GN_EPS) * ln_g + ln_b + bonus
        return y.reshape(y.shape[:-2] + (GROUP_W,)) * gate

    y_lat = finish(ys_l[0] + ys_l[1], bonus_l, g_l)
    y_ctx = finish(ys_c[0] + ys_c[1], bonus_c, g_c)
    return y_lat.astype(p_lat.dtype), y_ctx.astype(p_ctx.dtype)


def mla_project(p, q_norm, kv_norm, w_uq, w_ukv, angs):
    bsz, seq_len = p.shape[0], p.shape[1]
    c_q, c_kv, k_r = jnp.split(p, [MLA_Q_RANK, MLA_Q_RANK + MLA_KV_RANK], axis=-1)
    q = (rms_norm(c_q, q_norm) @ w_uq).reshape(bsz, seq_len, MLA_HEADS, MLA_NOPE + MLA_ROPE)
    kv = (rms_norm(c_kv, kv_norm) @ w_ukv).reshape(bsz, seq_len, MLA_HEADS, MLA_NOPE + MLA_V)
    q_nope, q_rope = q[..., :MLA_NOPE], q[..., MLA_NOPE:]
    k_nope, v = kv[..., :MLA_NOPE], kv[..., MLA_NOPE:]
    k_r = k_r[:, :, None, :]
    if angs is not None:
        q_rope = axial_rope(q_rope, angs)
        k_r = axial_rope(k_r, angs)
    k = jnp.concatenate([k_nope, jnp.broadcast_to(k_r, k_nope.shape[:-1] + (MLA_ROPE,))], axis=-1)
    q = jnp.concatenate([q_nope, q_rope], axis=-1)
    return q, k, v


def softmax_attend(q, k, v):
    s = jnp.einsum('bqhd,bkhd->bhqk', q, k).astype(f32) * MLA_SCALE
    pr = jax.nn.softmax(s, axis=-1).astype(v.dtype)
    return jnp.einsum('bhqk,bkhd->bqhd', pr, v)


def blocked_attend(q, k, v):
    bsz, seq_len, n_h, d_k = q.shape
    qb = q.reshape(bsz, seq_len // Q_BLOCK, Q_BLOCK, n_h, d_k).swapaxes(0, 1)
    out = lax.map(lambda qq: softmax_attend(qq, k, v), qb)
    return out.swapaxes(0, 1).reshape(bsz, seq_len, n_h, v.shape[-1])


def mla_mixer(p_lat, p_ctx, q_norm, kv_norm, w_uq, w_ukv, angs):
    q_l, k_l, v_l = mla_project(p_lat, q_norm, kv_norm, w_uq, w_ukv, angs)
    q_c, k_c, v_c = mla_project(p_ctx, q_norm, kv_norm, w_uq, w_ukv, None)
    y_c = softmax_attend(q_c, k_c, v_c)
    k_all = jnp.concatenate([k_l, k_c], axis=1)
    v_all = jnp.concatenate([v_l, v_c], axis=1)
    y_l = blocked_attend(q_l, k_all, v_all)
    return y_l.reshape(y_l.shape[:2] + (GROUP_W,)), y_c.reshape(y_c.shape[:2] + (GROUP_W,))


def conv_mixer(p, conv_w):
    b_gate, c_gate, u = jnp.split(p, 3, axis=-1)
    return b_gate * dwconv3(c_gate * u, conv_w)


def ret_project(p, angs):
    q, k, v, g = jnp.split(p.astype(f32), 4, axis=-1)
    hs = lambda t: t.reshape(t.shape[:-1] + (RET_HEADS, RET_HEAD))
    q, k, v = hs(q), hs(k), hs(v)
    if angs is not None:
        q = axial_rope(q, angs)
        k = axial_rope(k, angs)
    return q, k * RET_HEAD ** -0.5, v, g


def retention_chunked(q, k, v, log_gamma, s0, inclusive):
    bsz, seq_len, n_h, _ = q.shape
    d_v = v.shape[-1]
    n_chunks = seq_len // RET_CHUNK
    pos = jnp.arange(RET_CHUNK, dtype=f32)
    diff = pos[:, None] - pos[None, :]
    mask = (diff >= 0) if inclusive else (diff > 0)
    decay_in = jnp.where(mask[None], jnp.exp(jnp.where(mask, diff, 0.0)[None] * log_gamma[:, None, None]), 0.0)
    xi = jnp.exp((pos + 1.0)[:, None] * log_gamma[None, :])
    zeta = jnp.exp((RET_CHUNK - 1.0 - pos)[:, None] * log_gamma[None, :])
    g_chunk = jnp.exp(RET_CHUNK * log_gamma)

    def to_chunks(t):
        return t.reshape(bsz, n_chunks, RET_CHUNK, n_h, t.shape[-1]).swapaxes(0, 1)

    def step(s, blk):
        qb, kb, vb = blk
        att = jnp.einsum('bihd,bjhd->bhij', qb, kb) * decay_in
        inner = jnp.einsum('bhij,bjhe->bihe', att, vb)
        cross = jnp.einsum('bihd,bhde->bihe', qb, s) * xi[None, :, :, None]
        s_new = s * g_chunk[None, :, None, None] + jnp.einsum('bjhd,bjhe->bhde', kb * zeta[None, :, :, None], vb)
        return s_new, inner + cross

    s_fin, out = lax.scan(step, s0, (to_chunks(q), to_chunks(k), to_chunks(v)))
    return s_fin, out.swapaxes(0, 1).reshape(bsz, seq_len, n_h, d_v)


def retention_mixer(p_lat, p_ctx, ret_decay, gn_g, angs):
    log_gamma = -jnp.exp(ret_decay.astype(f32))
    q_l, k_l, v_l, g_l = ret_project(p_lat, angs)
    q_c, k_c, v_c, g_c = ret_project(p_ctx, None)
    s0 = jnp.zeros((p_lat.shape[0], RET_HEADS, RET_HEAD, RET_HEAD), f32)
    flip = lambda t: jnp.flip(t, axis=1)
    s_cf, o_cf = retention_chunked(q_c, k_c, v_c, log_gamma[0], s0, True)
    _, o_lf = retention_chunked(q_l, k_l, v_l, log_gamma[0], s_cf, True)
    s_cb, o_cb = retention_chunked(flip(q_c), flip(k_c), flip(v_c), log_gamma[1], s0, False)
    _, o_lb = retention_chunked(flip(q_l), flip(k_l), flip(v_l), log_gamma[1], s_cb, False)
    gn = gn_g.astype(f32).reshape(RET_HEADS, RET_HEAD)

    def finish(o, g):
        o = head_norm(o, GN_EPS) * gn
        return jax.nn.silu(g) * o.reshape(o.shape[:-2] + (GROUP_W,))

    y_lat = finish(o_lf + flip(o_lb), g_l)
    y_ctx = finish(o_cf + flip(o_cb), g_c)
    return y_lat.astype(p_lat.dtype), y_ctx.astype(p_ctx.dtype)


def conv_ffn(h, w_up, w_conv, w_down):
    u = dwconv3(h @ w_up, w_conv)
    gate, val = jnp.split(u, 2, axis=-1)
    return (jax.nn.silu(gate) * val) @ w_down


def layer_forward(x, xc, c, c_ctx, mod_w, mod_b, norm_g, w_in, rwkv_shift, rwkv_w0, rwkv_w_up,
                  rwkv_a0, rwkv_a_up, rwkv_g_up, rwkv_vecs, mla_q_norm, mla_kv_norm, mla_w_uq,
                  mla_w_ukv, conv_w, ret_decay, ret_gn_g, w_out, mlp_w_up, mlp_conv, mlp_w_down,
                  angs_mla, angs_ret, update_ctx):
    bsz = x.shape[0]
    m = (jax.nn.silu(c) @ mod_w + mod_b).reshape(bsz, N_MOD, 1, D_MODEL)
    mc = (jax.nn.silu(c_ctx) @ mod_w + mod_b).reshape(N_MOD, D_MODEL)

    h = rms_norm(x, norm_g[0]) * (1.0 + m[:, 1]) + m[:, 0]
    hc = rms_norm(xc, norm_g[0]) * (1.0 + mc[1]) + mc[0]
    pa, pb, pcv, pd = jnp.split(h @ w_in, GROUP_SPLITS, axis=-1)
    pa_c, pb_c, pcv_c, pd_c = jnp.split(hc @ w_in, GROUP_SPLITS, axis=-1)
    ya, ya_c = rwkv_mixer(pa, pa_c, rwkv_shift, rwkv_w0, rwkv_w_up, rwkv_a0, rwkv_a_up, rwkv_g_up, rwkv_vecs)
    yb, yb_c = mla_mixer(pb, pb_c, mla_q_norm, mla_kv_norm, mla_w_uq, mla_w_ukv, angs_mla)
    yd, yd_c = retention_mixer(pd, pd_c, ret_decay, ret_gn_g, angs_ret)
    y = jnp.concatenate([ya, yb, conv_mixer(pcv, conv_w), yd], axis=-1) @ w_out
    x = x + m[:, 2] * rms_norm(y, norm_g[1])
    if update_ctx:
        yc = jnp.concatenate([ya_c, yb_c, conv_mixer(pcv_c, conv_w), yd_c], axis=-1) @ w_out
        xc = xc + mc[2] * rms_norm(yc, norm_g[1])

    h = rms_norm(x, norm_g[2]) * (1.0 + m[:, 4]) + m[:, 3]
    x = x + m[:, 5] * rms_norm(conv_ffn(h, mlp_w_up, mlp_conv, mlp_w_down), norm_g[3])
    if update_ctx:
        hc = rms_norm(xc, norm_g[2]) * (1.0 + mc[4]) + mc[3]
        xc = xc + mc[5] * rms_norm(conv_ffn(hc, mlp_w_up, mlp_conv, mlp_w_down), norm_g[3])
    return x, xc


def setup_inputs(seed: int = 0) -> dict:
    key = jax.random.key(seed)
    ks = jax.random.split(key, 26)

    def nrm(k, shape, scale):
        return jax.random.normal(k, shape, f32) * scale

    nl = DEPTH
    h_idx = jnp.arange(RET_HEADS, dtype=f32)
    ret_base = jnp.log(-jnp.log1p(-jnp.exp2(-5.0 - h_idx)))
    vec_off = jnp.array([0.85, 1.0, 0.0, 1.0, 0.0], f32)[None, :, None]
    vec_scale = jnp.array([0.05, 0.05, 0.1, 0.02, 0.02], f32)[None, :, None]
    return {
        'x': nrm(ks[0], (BATCH, SEQ, D_MODEL), 1.0),
        'c': nrm(ks[1], (BATCH, D_MODEL), 1.0),
        'ctx': nrm(ks[2], (BATCH, CTX_LEN, D_MODEL), 1.0),
        'c_ctx': nrm(ks[3], (D_MODEL,), 1.0),
        'mod_w': nrm(ks[4], (nl, D_MODEL, N_MOD * D_MODEL), 0.5 * D_MODEL ** -0.5),
        'mod_b': nrm(ks[5], (nl, N_MOD * D_MODEL), 0.02),
        'norm_g': 1.0 + nrm(ks[6], (nl, 4, D_MODEL), 0.02),
        'w_in': nrm(ks[7], (nl, D_MODEL, IN_W), D_MODEL ** -0.5),
        'rwkv_shift': 0.25 + nrm(ks[8], (nl, 2, RWKV_IN), 0.05),
        'rwkv_w0': -3.0 + nrm(ks[9], (nl, 2, GROUP_W), 1.0),
        'rwkv_w_up': nrm(ks[10], (nl, 2, RWKV_DECAY_LORA, GROUP_W), 0.1 * RWKV_DECAY_LORA ** -0.5),
        'rwkv_a0': nrm(ks[11], (nl, 2, GROUP_W), 0.5),
        'rwkv_a_up': nrm(ks[12], (nl, 2, RWKV_ICLR_LORA, GROUP_W), 0.1 * RWKV_ICLR_LORA ** -0.5),
        'rwkv_g_up': nrm(ks[13], (nl, RWKV_GATE_LORA, GROUP_W), RWKV_GATE_LORA ** -0.5),
        'rwkv_vecs': vec_off + vec_scale * jax.random.normal(ks[14], (nl, 5, GROUP_W), f32),
        'mla_q_norm': 1.0 + nrm(ks[15], (nl, MLA_Q_RANK), 0.02),
        'mla_kv_norm': 1.0 + nrm(ks[16], (nl, MLA_KV_RANK), 0.02),
        'mla_w_uq': nrm(ks[17], (nl, MLA_Q_RANK, MLA_HEADS * (MLA_NOPE + MLA_ROPE)), MLA_Q_RANK ** -0.5),
        'mla_w_ukv': nrm(ks[18], (nl, MLA_KV_RANK, MLA_HEADS * (MLA_NOPE + MLA_V)), MLA_KV_RANK ** -0.5),
        'conv_w': nrm(ks[19], (nl, CONV_WIDTH, GROUP_W), CONV_WIDTH ** -0.5),
        'ret_decay': ret_base[None, None, :] + nrm(ks[20], (nl, 2, RET_HEADS), 0.1),
        'ret_gn_g': 1.0 + nrm(ks[21], (nl, GROUP_W), 0.02),
        'w_out': nrm(ks[22], (nl, MIX_W, D_MODEL), MIX_W ** -0.5),
        'mlp_w_up': nrm(ks[23], (nl, D_MODEL, 2 * D_FF), D_MODEL ** -0.5),
        'mlp_conv': nrm(ks[24], (nl, CONV_WIDTH, 2 * D_FF), CONV_WIDTH ** -0.5),
        'mlp_w_down': nrm(ks[25], (nl, D_FF, D_MODEL), D_FF ** -0.5),
    }


def reference(x, c, ctx, c_ctx, mod_w, mod_b, norm_g, w_in, rwkv_shift, rwkv_w0, rwkv_w_up,
              rwkv_a0, rwkv_a_up, rwkv_g_up, rwkv_vecs, mla_q_norm, mla_kv_norm, mla_w_uq,
              mla_w_ukv, conv_w, ret_decay, ret_gn_g, w_out, mlp_w_up, mlp_conv, mlp_w_down):
    rows = x.shape[1] // GRID_W
    angs_mla = axial_rope_angles(rows, MLA_ROPE)
    angs_ret = axial_rope_angles(rows, RET_HEAD)
    xc = ctx
    for l in range(DEPTH):
        x, xc = layer_forward(
            x, xc, c, c_ctx, mod_w[l], mod_b[l], norm_g[l], w_in[l], rwkv_shift[l], rwkv_w0[l],
            rwkv_w_up[l], rwkv_a0[l], rwkv_a_up[l], rwkv_g_up[l], rwkv_vecs[l], mla_q_norm[l],
            mla_kv_norm[l], mla_w_uq[l], mla_w_ukv[l], conv_w[l], ret_decay[l], ret_gn_g[l],
            w_out[l], mlp_w_up[l], mlp_conv[l], mlp_w_down[l], angs_mla, angs_ret,
            update_ctx=(l < DEPTH - 1))
    return x
```

```python
import math
from contextlib import ExitStack
import numpy as np
import concourse.bass as bass
import concourse.mybir as mybir
from concourse.bass_utils import run_bass_kernel_spmd

F32 = mybir.dt.float32
BF16 = mybir.dt.bfloat16
AF = mybir.ActivationFunctionType
ALU = mybir.AluOpType
AX = mybir.AxisListType

N_DMA_SEMS = 12
N_BG_SEMS = 12
NCORES = 8

D = 2048
DEPTH = 4
BATCH = 4
SEQ = 2048
CTX = 256
TALL = SEQ + CTX
IN_W = 6272
D_FF = 5632
NMOD = 6
NORM_EPS = 1e-6


class Buf:
    __slots__ = ("name", "w", "r", "excl")

    def __init__(self, name="", excl=False):
        self.name = name
        self.w = {}
        self.r = {}
        self.excl = excl


class KB:
    def __init__(self, nc, stack):
        self.nc = nc
        self.engs = ["pe", "act", "dve", "pool", "sp"]
        self.ops = {e: [] for e in self.engs}
        self.cnt = {e: 0 for e in self.engs}
        self.waited = {e: {} for e in self.engs}
        self.sems = {}
        for e in self.engs:
            self.sems[e] = stack.enter_context(nc.semaphore("s_" + e))
        for i in range(N_DMA_SEMS):
            self.sems["d%d" % i] = stack.enter_context(nc.semaphore("s_d%d" % i))
        for i in range(N_BG_SEMS):
            self.sems["c%d" % i] = stack.enter_context(nc.semaphore("s_c%d" % i))
        self.dcnt = [0] * N_DMA_SEMS
        self.dnext = 0
        self.ccnt = [0] * N_BG_SEMS
        self.cnext = 0
        self.out_toks = []

    def _deps(self, reads, writes):
        deps = {}

        def add(dd):
            for sk, v in dd.items():
                if deps.get(sk, 0) < v:
                    deps[sk] = v
        for b in reads:
            add(b.w)
            if b.excl:
                add(b.r)
        for b in writes:
            add(b.w)
            add(b.r)
        return deps

    def _emit_waits(self, eng, deps, skip_self=False):
        for sk, v in deps.items():
            if skip_self and sk == eng:
                continue
            if self.waited[eng].get(sk, 0) >= v:
                continue
            self.waited[eng][sk] = v
            self.ops[eng].append(("w", self.sems[sk], v))

    @staticmethod
    def _mark(tok, reads, writes):
        sk, v = tok
        for b in reads:
            if b.r.get(sk, 0) < v:
                b.r[sk] = v
        for b in writes:
            if b.w.get(sk, 0) < v:
                b.w[sk] = v

    def op(self, eng, fn, reads=(), writes=()):
        deps = self._deps(reads, writes)
        self._emit_waits(eng, deps, skip_self=(eng == "pe"))
        self.cnt[eng] += 1
        tok = (eng, self.cnt[eng])
        self.ops[eng].append(("i", fn, self.sems[eng], 1))
        self._mark(tok, reads, writes)
        return tok

    def dma(self, eng, fn, reads=(), writes=(), is_out=False, bg=False):
        deps = self._deps(reads, writes)
        if bg:
            i = self.cnext
            self.cnext = (self.cnext + 1) % N_BG_SEMS
            sk = "c%d" % i
            cnts = self.ccnt
        else:
            i = self.dnext
            self.dnext = (self.dnext + 1) % N_DMA_SEMS
            sk = "d%d" % i
            cnts = self.dcnt
        if cnts[i] > 0:
            v = 16 * cnts[i]
            if deps.get(sk, 0) < v:
                deps[sk] = v
        self._emit_waits(eng, deps)
        cnts[i] += 1
        tok = (sk, 16 * cnts[i])
        self.ops[eng].append(("i", fn, self.sems[sk], 16))
        self._mark(tok, reads, writes)
        if is_out:
            self.out_toks.append(tok)
        return tok

    def barrier(self):
        deps = {e: self.cnt[e] for e in self.engs if self.cnt[e] > 0}
        for i in range(N_DMA_SEMS):
            if self.dcnt[i] > 0:
                deps["d%d" % i] = 16 * self.dcnt[i]
        for i in range(N_BG_SEMS):
            if self.ccnt[i] > 0:
                deps["c%d" % i] = 16 * self.ccnt[i]
        for e in self.engs:
            self._emit_waits(e, dict(deps))

    def finish(self, block):
        deps = {}
        for sk, v in self.out_toks:
            if deps.get(sk, 0) < v:
                deps[sk] = v
        self._emit_waits("sp", deps)
        m = {"pe": block.tensor, "act": block.scalar, "dve": block.vector,
             "pool": block.gpsimd, "sp": block.sync}

        def mk(e):
            lst = self.ops[e]

            def body(engine):
                for it in lst:
                    if it[0] == "w":
                        engine.wait_ge(it[1], it[2])
                    else:
                        it[1](engine).then_inc(it[2], it[3])
            return body

        for e in self.engs:
            if self.ops[e]:
                m[e](mk(e))


class Ctx:
    def __init__(self, name="k"):
        self.nc = bass.Bass("TRN2", target_bir_lowering=False)
        self.st = ExitStack()
        self.kb = KB(self.nc, self.st)
        self.n = 0

    def dram_in(self, name, shape, dt=F32):
        return self.nc.dram_tensor(name, list(shape), dt, kind="ExternalInput").ap()

    def dram_out(self, name, shape, dt=F32):
        return self.nc.dram_tensor(name, list(shape), dt, kind="ExternalOutput").ap()

    def sb(self, shape, dt=F32, name=None):
        self.n += 1
        return self.st.enter_context(self.nc.sbuf_tensor(name or ("t%d" % self.n), list(shape), dt))

    def ps(self, shape, dt=F32, name=None):
        self.n += 1
        return self.st.enter_context(self.nc.psum_tensor(name or ("p%d" % self.n), list(shape), dt))

    def done(self):
        block = self.st.enter_context(self.nc.Block())
        self.kb.finish(block)
        self.st.close()
        return self.nc


def token_tiles(T, mx=512):
    out = []
    s = 0
    while s < T:
        n = min(mx, T - s)
        out.append((s, n))
        s += n
    return out


T1 = 1152
TP = TALL + 4
RW_BLK = 128
RW_SCALE = math.exp(-0.5)
NCH = TALL // 128
RET_EPS = 1e-5
MLA_SCALE = 192.0 ** -0.5
RWKV_GN_EPS = 64e-5
NFF = D_FF // 128

def emit_rstd(C, x_sb, Bx, nk, T, ones_sb, Bones, rstd_sb, Brstd, sq_tiles, Bsq, ps_tiles, Bps, eps, ranges=None):
    kb = C.kb
    k = 0
    for (t0, tn) in token_tiles(T):
        pj = (t0 // 512) % len(ps_tiles)
        p = ps_tiles[pj]
        for kc in range(nk):
            j = k % len(sq_tiles); k += 1
            sq = sq_tiles[j]
            kb.op("act", lambda e, sq=sq, kc=kc, t0=t0, tn=tn: e.activation(out=sq[:, 0:tn], in_=x_sb[:, kc, t0:t0 + tn],
                                                                              func=AF.Square),
                  reads=[Bx], writes=[Bsq[j]])
            kb.op("pe", lambda e, p=p, sq=sq, kc=kc, tn=tn: e.matmul(p[:, 0:tn], lhsT=ones_sb[:], rhs=sq[:, 0:tn],
                                                                      start=(kc == 0), stop=(kc == nk - 1)),
                  reads=[Bones, Bsq[j]], writes=[Bps[pj]])
        kb.op("act", lambda e, p=p, t0=t0, tn=tn: e.activation(out=rstd_sb[:, t0:t0 + tn], in_=p[:, 0:tn],
                                                                func=AF.Sqrt, bias=eps_ap(C, eps), scale=1.0),
              reads=[Bps[pj]], writes=[Brstd])
        kb.op("dve", lambda e, t0=t0, tn=tn: e.reciprocal(out=rstd_sb[:, t0:t0 + tn], in_=rstd_sb[:, t0:t0 + tn]),
              reads=[Brstd], writes=[Brstd])


_EPS = {}


def eps_ap(C, eps):
    return _EPS[(id(C), eps)][:, 0:1]


def make_eps(C, eps):
    t = C.sb([128, 1])
    b = Buf()
    C.kb.op("dve", lambda e: e.memset(t[:], eps), writes=[b])
    C.kb.barrier()
    _EPS[(id(C), eps)] = t
    return t


def vec_layout(v):
    n = v.shape[0]
    return np.ascontiguousarray(v.reshape(n, 16, 128).transpose(2, 0, 1))


def _rw_consts():
    ident = np.eye(128, dtype=np.float32)
    cst = np.concatenate([ident, np.ones((128, 128), np.float32)], axis=1)
    s = np.arange(64)[:, None]; t = np.arange(64)[None, :]
    us = (s < t).astype(np.float32); ui = (s <= t).astype(np.float32)
    unit = np.concatenate([us, ui, us, ui], axis=1)
    maskA = np.concatenate([unit, unit], axis=1)
    ls = (t < s).astype(np.float32)
    maskL = np.tile(ls, (1, 8))
    idu = np.tile(np.eye(64, dtype=np.float32), (1, 16))
    mk = np.concatenate([maskA, maskL, idu, np.ones((64, 64), np.float32)], axis=1)
    return cst, np.ascontiguousarray(mk)


def _tile8(v):
    return np.ascontiguousarray(v.reshape(8, 64).T)


def _rope_swap_idx(d):
    q = d // 4
    idx = np.arange(d)
    out = np.empty(d, np.int64)
    for base in (0, d // 2):
        out[base:base + q] = idx[base + q:base + 2 * q]
        out[base + q:base + 2 * q] = idx[base:base + q]
    return out


def _rope_tables(d):
    half = d // 2
    inv = 10000.0 ** (-np.arange(0, half, 2, dtype=np.float32) / half)
    rows = SEQ // 64
    row = np.repeat(np.arange(rows, dtype=np.float32), 64)
    col = np.tile(np.arange(64, dtype=np.float32), rows)
    ar = (row[:, None] * inv[None, :]).astype(np.float32)
    ac = (col[:, None] * inv[None, :]).astype(np.float32)
    cos = np.ones((d, TALL), np.float32); sin = np.zeros((d, TALL), np.float32)
    q = d // 4
    for base, ang in ((0, ar), (half, ac)):
        c = np.cos(ang).T.astype(np.float32); s = np.sin(ang).T.astype(np.float32)
        cos[base:base + q, CTX:] = c; cos[base + q:base + 2 * q, CTX:] = c
        sin[base:base + q, CTX:] = -s; sin[base + q:base + 2 * q, CTX:] = s
    return cos, sin


def emit_p1(C, io, classes):
    kb = C.kb
    C.push()
    xT = io["xT"]
    w = io["w"]
    pT = io["pT"]
    x_sb = C.sb([128, 16, T1]); Bx = Buf()
    h_sb = C.sb([128, 16, T1], BF16); Bh = Buf()
    v_sb = C.sb([128, 5, 16]); Bv = Buf()
    gp = C.sb([128, 2, 16]); Bgp = Buf()
    ones = C.sb([128, 128]); Bones = Buf()
    rstd = C.sb([128, T1]); Brstd = Buf()
    sq = [C.sb([128, 512]) for _ in range(2)]; Bsq = [Buf(), Buf()]
    tmp = [C.sb([128, T1]) for _ in range(2)]; Btmp = [Buf(), Buf()]
    wt = [C.sb([128, 16, 512], BF16) for _ in range(2)]; Bw = [Buf(), Buf()]
    ot = [C.sb([128, T1]) for _ in range(2)]; Bo = [Buf(), Buf()]
    pss = [C.ps([128, 512]) for _ in range(2)]; Bpss = [Buf(), Buf()]
    psm = [C.ps([128, 512]) for _ in range(4)]; Bpsm = [Buf() for _ in range(4)]

    xv = xT.rearrange("(kc p) t -> p kc t", p=128)
    for kc in range(16):
        q = "sp" if kc % 2 == 0 else "act"
        kb.dma(q, lambda e, kc=kc: e.dma_start(out=x_sb[:, kc, :], in_=xv[:, kc, :]), writes=[Bx])
    for vi, vsrc in enumerate(io["vec_srcs"]):
        kb.dma("sp", lambda e, vi=vi, vsrc=vsrc: e.dma_start(out=v_sb[:, vi, :], in_=vsrc), writes=[Bv])
    kb.op("dve", lambda e: e.memset(ones[:], 1.0 / D), writes=[Bones])
    for ci in range(2):
        kb.op("dve", lambda e, ci=ci: e.scalar_tensor_tensor(out=gp[:, ci, :], in0=v_sb[:, 2 + 2 * ci, :], scalar=1.0,
                                                             in1=v_sb[:, 0, :], op0=ALU.add, op1=ALU.mult),
              reads=[Bv], writes=[Bgp])
    emit_rstd(C, x_sb, Bx, 16, T1, ones, Bones, rstd, Brstd, sq, Bsq, pss, Bpss, NORM_EPS)
    for kc in range(16):
        j = kc % 2
        t = tmp[j]
        kb.op("dve", lambda e, t=t, kc=kc: e.tensor_tensor(out=t[:], in0=x_sb[:, kc, :], in1=rstd[:], op=ALU.mult),
              reads=[Bx, Brstd], writes=[Btmp[j]])
        for (ci, a, b) in classes:
            kb.op("act", lambda e, t=t, kc=kc, ci=ci, a=a, b=b: e.activation(
                out=h_sb[:, kc, a:b], in_=t[:, a:b], func=AF.Identity,
                bias=v_sb[:, 1 + 2 * ci, kc:kc + 1], scale=gp[:, ci, kc:kc + 1]),
                reads=[Btmp[j], Bgp, Bv], writes=[Bh])
    nblk = (IN_W + 511) // 512
    oi = 0
    pi = 0
    for nb in range(nblk):
        n0 = nb * 512
        nw = min(512, IN_W - n0)
        j = nb % 2
        wtile = wt[j]
        src = w[:, n0:n0 + nw].rearrange("(kc p) n -> p kc n", p=128)
        kb.dma(io["wdma"](), lambda e, wtile=wtile, src=src, nw=nw: e.dma_start(out=wtile[:, :, 0:nw], in_=src),
               writes=[Bw[j]])
        for nc_ in range(nw // 128):
            oj = oi % 2; oi += 1
            o = ot[oj]
            for (t0, tn) in token_tiles(T1):
                pj = pi % 4; pi += 1
                p = psm[pj]
                for kc in range(16):
                    kb.op("pe", lambda e, p=p, wtile=wtile, kc=kc, nc_=nc_, t0=t0, tn=tn: e.matmul(
                        p[:, 0:tn], lhsT=wtile[:, kc, nc_ * 128:(nc_ + 1) * 128], rhs=h_sb[:, kc, t0:t0 + tn],
                        start=(kc == 0), stop=(kc == 15)),
                        reads=[Bw[j], Bh], writes=[Bpsm[pj]])
                ev = "act" if pi % 2 == 0 else "dve"
                if ev == "act":
                    kb.op("act", lambda e, p=p, o=o, t0=t0, tn=tn: e.copy(out=o[:, t0:t0 + tn], in_=p[:, 0:tn]),
                          reads=[Bpsm[pj]], writes=[Bo[oj]])
                else:
                    kb.op("dve", lambda e, p=p, o=o, t0=t0, tn=tn: e.tensor_copy(out=o[:, t0:t0 + tn], in_=p[:, 0:tn]),
                          reads=[Bpsm[pj]], writes=[Bo[oj]])
            r0 = n0 + nc_ * 128
            kb.dma("sp", lambda e, o=o, r0=r0: e.dma_start(out=pT[r0:r0 + 128, :], in_=o[:]),
                   reads=[Bo[oj]], writes=[Buf()], is_out=True)
    C.pop()


def emit_p2r(C, io):
    kb = C.kb
    C.push()
    u_in = io["u"]
    u2_in = io["u2"]
    mu_in = io["mu"]
    mu2_in = io["mu2"]
    par_in = io["par"]
    wup_in = io["wup"]
    aup_in = io["aup"]
    gup_in = io["gup"]
    cst_in = io["cst"]
    mk_in = io["mk"]
    y_out = io["yT"]
    bon_out = io["bonT"]
    gate_out = io["gateT"]

    def T(shape, dt=F32):
        return C.sb(shape, dt), Buf()

    mu, Bmu = T([64, 24, 2]); c0, Bc0 = T([64, 24, 1])
    mu2, Bmu2 = T([128, 4, 2]); c02, Bc02 = T([128, 4, 1])
    par, Bpar = T([64, 5, 8]); omk, Bomk = T([64, 8])
    wup, Bwup = T([96, 512], mybir.dt.float32r); aup, Baup = T([96, 512], mybir.dt.float32r); gup, Bgup = T([128, 2, 512], mybir.dt.float32r); ONE64R, BONE64R = T([64, 64], mybir.dt.float32r)
    cst, Bcst = T([128, 64]); mk, Bmk = T([64, 2 * 256 + 8 * 64 + 16 * 64 + 64])
    ident = cst[0:64, 0:64]
    maskA = mk[:, 0:512].rearrange("p (a b) -> p a b", a=2)
    maskL = mk[:, 512:1024].rearrange("p (a b) -> p a b", a=8)
    identU = mk[:, 1024:2048].rearrange("p (a b) -> p a b", a=16)
    ONES = mk[:, 2048:2112]; BONES = Bmk
    kb.dma("sp", lambda e: e.dma_start(out=mu[:], in_=mu_in), writes=[Bmu])
    kb.dma("sp", lambda e: e.dma_start(out=mu2[:], in_=mu2_in), writes=[Bmu2])
    kb.dma("sp", lambda e: e.dma_start(out=par[:], in_=par_in), writes=[Bpar])
    C.push()
    wup32, Bwup32 = T([96, 512]); aup32, Baup32 = T([96, 512]); gup32, Bgup32 = T([128, 2, 512]); one32, Bone32 = T([64, 64])
    kb.dma("sp", lambda e: e.dma_start(out=wup32[:], in_=wup_in), writes=[Bwup32])
    kb.dma("sp", lambda e: e.dma_start(out=aup32[:], in_=aup_in), writes=[Baup32])
    kb.dma("sp", lambda e: e.dma_start(out=gup32[:], in_=gup_in.rearrange("(kc p) n -> p kc n", p=128)), writes=[Bgup32])
    kb.dma("sp", lambda e: e.dma_start(out=one32[:], in_=cst_in[0:64, 128:192]), writes=[Bone32])
    kb.op("act", lambda e: e.copy(out=wup[:], in_=wup32[:]), reads=[Bwup32], writes=[Bwup])
    kb.op("act", lambda e: e.copy(out=aup[:], in_=aup32[:]), reads=[Baup32], writes=[Baup])
    kb.op("act", lambda e: e.copy(out=gup[:], in_=gup32[:]), reads=[Bgup32], writes=[Bgup])
    kb.op("act", lambda e: e.copy(out=ONE64R[:], in_=one32[:]), reads=[Bone32], writes=[BONE64R])
    C.pop()
    kb.dma("sp", lambda e: e.dma_start(out=cst[:], in_=cst_in[:, 0:64]), writes=[Bcst])
    kb.dma("sp", lambda e: e.dma_start(out=mk[:], in_=mk_in), writes=[Bmk])
    for (m_, Bm_, c_, Bc_) in [(mu, Bmu, c0, Bc0), (mu2, Bmu2, c02, Bc02)]:
        kb.op("dve", lambda e, m_=m_, c_=c_: e.tensor_tensor(out=c_[:], in0=m_[:, :, 0:1], in1=m_[:, :, 1:2], op=ALU.add), reads=[Bm_], writes=[Bc_])
        kb.op("dve", lambda e, c_=c_: e.tensor_scalar(out=c_[:], in0=c_[:], scalar1=-1.0, scalar2=1.0, op0=ALU.mult, op1=ALU.add),
              reads=[Bc_], writes=[Bc_])
    kb.op("dve", lambda e: e.tensor_scalar(out=omk[:], in0=par[:, 3, :], scalar1=-1.0, scalar2=1.0, op0=ALU.mult, op1=ALU.add),
          reads=[Bpar], writes=[Bomk])

    NB = RW_BLK
    NH = 8
    U, BU = T([64, 24, NB + 2]); U2, BU2 = T([128, 4, NB + 2])
    P, BP = T([64, 24, NB])
    P2, BP2 = T([128, 4, NB]); TMPP2, BTMPP2 = T([128, 4, NB])
    TWD, BTWD = T([128, NB], mybir.dt.float32r); SG, BSG = T([128, 2, NB], mybir.dt.float32r)
    LW, BLW = T([64, NH, NB]); A, BA = T([64, NH, NB])
    KK, BKK = T([64, NH, NB]); SQ, BSQ = T([64, NH, NB], mybir.dt.float32r); RN, BRN = T([64, NH, NB])
    KD, BKD = T([64, NH, NB]); BS, BBS = T([64, NH, NB]); TA, BTA = RN, BRN
    LREL, BLREL = T([64, NH, NB]); EPOS, BEPOS = T([64, NH, NB]); ENEG, BENEG = T([64, NH, NB])
    EPREV, BEPREV = T([64, NH, NB]); EBAR, BEBAR = T([64, NH, NB]); PC, BPC = T([64, NH, 2])
    GATE, BGATE = EPOS, BEPOS; BON, BBON = ENEG, BENEG
    R32 = mybir.dt.float32r
    AR, BAR = T([64, NH, 2, 128], R32); BH, BBH = T([64, NH, 2, 64], R32); KH, BKH = T([64, NH, 2, 64], R32)
    BB, BBB = T([64, NH, 2, 64]); KBr, BKBr = T([64, NH, 2, 64])
    TM, BTM = T([64, 16, 5, 64], R32); GS, BGS = T([64, 16, 256], R32)
    MM = [T([64, 16, 64], R32) for _ in range(2)]
    WT = [T([64, 16, 128], R32) for _ in range(2)]
    XF, BXF = T([64, 16, 64], R32); UA, BUA = T([64, 16, 128]); APT, BAPT = T([64, 16, 64], R32)
    UU, BUU = T([64, 8, 64], R32); YO, BYO = T([64, NH, NB])
    TMPP = UA[:].rearrange("p a b -> p (a b)")[:, 0:12 * NB].rearrange("p (a b) -> p a b", a=12); BTMPP = BUA
    ST = [T([64, NH, 64], R32) for _ in range(2)]
    kb.op("dve", lambda e: e.tensor_scalar(out=ST[0][0][:], in0=identU[:, 0:8, :], scalar1=0.0, scalar2=None, op0=ALU.mult), reads=[Bmk], writes=[ST[0][1]])
    kb.op("dve", lambda e: e.tensor_scalar(out=ST[1][0][:], in0=identU[:, 0:8, :], scalar1=0.0, scalar2=None, op0=ALU.mult), reads=[Bmk], writes=[ST[1][1]])
    psl = [(C.ps([128, 512]), Buf(excl=True)) for _ in range(8)]
    pcnt = [0]

    def nps():
        r = psl[pcnt[0] % 8]
        pcnt[0] += 1
        return r

    def bc(ap, shape):
        return ap.broadcast_to(shape)

    def v3(ps, a, n=None):
        n = n or 512
        return ps[0:64, 0:n].rearrange("p (a b) -> p a b", a=a)

    yv = y_out.rearrange("(h p) t -> p h t", p=64)
    bv = bon_out.rearrange("(h p) t -> p h t", p=64)
    gv = gate_out.rearrange("(h p) t -> p h t", p=64)
    uv = u_in.rearrange("(i p) t -> p i t", p=64)
    u2v = u2_in.rearrange("(i p) t -> p i t", p=128)

    blocks = [(1 + i * NB, i * NB) for i in range(CTX // NB)] + [(259 + i * NB, CTX + i * NB) for i in range(SEQ // NB)]
    gc = 0
    for (pc0, t0) in blocks:
        if io.get("hook") is not None:
            io["hook"]()
        for i in range(24):
            q = "sp" if i % 2 == 0 else "act"
            kb.dma(q, lambda e, i=i, pc0=pc0: e.dma_start(out=U[:, i, :], in_=uv[:, i, pc0 - 1:pc0 + NB + 1]), writes=[BU])
        for i in range(4):
            kb.dma("sp", lambda e, i=i, pc0=pc0: e.dma_start(out=U2[:, i, :], in_=u2v[:, i, pc0 - 1:pc0 + NB + 1]), writes=[BU2])
        for (U_, BU_, P_, BP_, TP_, BTP_, m_, Bm_, c_, Bc_, S3) in [
                (U[:, 0:12, :], BU, P[:, 0:12, :], BP, TMPP, BTMPP, mu[:, 0:12, :], Bmu, c0[:, 0:12, :], Bc0, [64, 12, NB]),
                (U[:, 12:24, :], BU, P[:, 12:24, :], BP, TMPP, BTMPP, mu[:, 12:24, :], Bmu, c0[:, 12:24, :], Bc0, [64, 12, NB]),
                (U2, BU2, P2, BP2, TMPP2, BTMPP2, mu2, Bmu2, c02, Bc02, [128, 4, NB])]:
            kb.op("dve", lambda e, U_=U_, P_=P_, c_=c_, S3=S3: e.tensor_tensor(out=P_[:], in0=U_[:, :, 1:NB + 1], in1=bc(c_[:], S3), op=ALU.mult),
                  reads=[BU_, Bc_], writes=[BP_])
            kb.op("pool", lambda e, U_=U_, TP_=TP_, m_=m_, S3=S3: e.tensor_tensor(out=TP_[:], in0=U_[:, :, 0:NB], in1=bc(m_[:, :, 0:1], S3), op=ALU.mult),
                  reads=[BU_, Bm_], writes=[BTP_])
            kb.op("dve", lambda e, P_=P_, TP_=TP_: e.tensor_tensor(out=P_[:], in0=P_[:], in1=TP_[:], op=ALU.add), reads=[BP_, BTP_], writes=[BP_])
            kb.op("pool", lambda e, U_=U_, TP_=TP_, m_=m_, S3=S3: e.tensor_tensor(out=TP_[:], in0=U_[:, :, 2:NB + 2], in1=bc(m_[:, :, 1:2], S3), op=ALU.mult),
                  reads=[BU_, Bm_], writes=[BTP_])
            kb.op("dve", lambda e, P_=P_, TP_=TP_: e.tensor_tensor(out=P_[:], in0=P_[:], in1=TP_[:], op=ALU.add), reads=[BP_, BTP_], writes=[BP_])
        r_ = P[:, 0:8, :]; k_ = P[:, 8:16, :]; v_ = P[:, 16:24, :]
        S4 = [64, NH, NB]
        for (src_i, upw, Bup, pidx, dst, Bdst, func) in [(0, wup, Bwup, 0, LW, BLW, AF.Tanh), (1, aup, Baup, 1, A, BA, AF.Identity)]:
            kb.op("act", lambda e, src_i=src_i, func=func: e.activation(out=TWD[:], in_=P2[:, src_i, :], func=func),
                  reads=[BP2], writes=[BTWD])
            for hh in range(2):
                ps, Bps = nps()
                for hi in range(4):
                    h = hh * 4 + hi
                    kb.op("pe", lambda e, ps=ps, upw=upw, h=h, hi=hi: e.matmul(ps[0:64, hi * NB:(hi + 1) * NB], lhsT=upw[0:96, h * 64:(h + 1) * 64],
                                                                         rhs=TWD[0:96, :], start=True, stop=True),
                          reads=[Bup, BTWD], writes=[Bps])
                for hi in range(4):
                    h = hh * 4 + hi
                    kb.op("act", lambda e, ps=ps, h=h, hi=hi, dst=dst, pidx=pidx: e.activation(
                        out=dst[:, h, :], in_=ps[0:64, hi * NB:(hi + 1) * NB], func=AF.Sigmoid, bias=par[:, pidx, h:h + 1], scale=1.0),
                        reads=[Bps, Bpar], writes=[Bdst])
        kb.op("dve", lambda e: e.tensor_scalar(out=LW[:], in0=LW[:], scalar1=-RW_SCALE, scalar2=None, op0=ALU.mult),
              reads=[BLW], writes=[BLW])
        AUX = io.get("aux", True)
        if AUX:
            kb.op("act", lambda e: e.activation(out=SG[:], in_=P2[:, 2:4, :], func=AF.Sigmoid), reads=[BP2], writes=[BSG])
            for hh in range(2):
                ps, Bps = nps()
                for hi in range(4):
                    h = hh * 4 + hi
                    for kc in range(2):
                        kb.op("pe", lambda e, ps=ps, h=h, hi=hi, kc=kc: e.matmul(ps[0:64, hi * NB:(hi + 1) * NB], lhsT=gup[:, kc, h * 64:(h + 1) * 64],
                                                                           rhs=SG[:, kc, :], start=(kc == 0), stop=(kc == 1)),
                              reads=[Bgup, BSG], writes=[Bps])
                kb.op("act", lambda e, ps=ps, hh=hh: e.copy(out=GATE[:, hh * 4:hh * 4 + 4, :], in_=v3(ps, 4)), reads=[Bps], writes=[BGATE])
            kb.dma("sp", lambda e, t0=t0: e.dma_start(out=gv[:, :, t0:t0 + NB], in_=GATE[:]), reads=[BGATE], writes=[Buf()], is_out=True)

        def headsum(src, Bsrc, fn_evac):
            for hh in range(2):
                ps, Bps = nps()
                for hi in range(4):
                    h = hh * 4 + hi
                    kb.op("pe", lambda e, ps=ps, h=h, hi=hi: e.matmul(ps[0:64, hi * NB:(hi + 1) * NB], lhsT=ONE64R[:], rhs=src[:, h, :], start=True, stop=True),
                          reads=[BONE64R, Bsrc], writes=[Bps])
                fn_evac(ps, Bps, hh)
        kb.op("dve", lambda e: e.tensor_tensor(out=KK[:], in0=k_, in1=bc(par[:, 2, :].unsqueeze(2), S4), op=ALU.mult),
              reads=[BP, Bpar], writes=[BKK])
        kb.op("dve", lambda e: e.tensor_tensor(out=SQ[:], in0=KK[:], in1=KK[:], op=ALU.mult), reads=[BKK], writes=[BSQ])
        headsum(SQ, BSQ, lambda ps, Bps, hh: kb.op("act", lambda e: e.activation(out=RN[:, hh * 4:hh * 4 + 4, :], in_=v3(ps, 4), func=AF.Sqrt,
                                                                                    bias=eps_ap(C, 1e-12)[0:64, :], scale=1.0), reads=[Bps], writes=[BRN]))
        kb.op("dve", lambda e: e.reciprocal(out=RN[:], in_=RN[:]), reads=[BRN], writes=[BRN])
        kb.op("dve", lambda e: e.tensor_tensor(out=KK[:], in0=KK[:], in1=RN[:], op=ALU.mult), reads=[BKK, BRN], writes=[BKK])
        if AUX:
            kb.op("dve", lambda e: e.tensor_tensor(out=SQ[:], in0=r_, in1=k_, op=ALU.mult), reads=[BP], writes=[BSQ])
            kb.op("dve", lambda e: e.tensor_tensor(out=SQ[:], in0=SQ[:], in1=bc(par[:, 4, :].unsqueeze(2), S4), op=ALU.mult),
                  reads=[BSQ, Bpar], writes=[BSQ])
            headsum(SQ, BSQ, lambda ps, Bps, hh: kb.op("dve", lambda e: e.tensor_tensor(out=BON[:, hh * 4:hh * 4 + 4, :], in0=v3(ps, 4),
                                                                                         in1=v_[:, hh * 4:hh * 4 + 4, :], op=ALU.mult),
                                                       reads=[Bps, BP], writes=[BBON]))
            kb.dma("sp", lambda e, t0=t0: e.dma_start(out=bv[:, :, t0:t0 + NB], in_=BON[:]), reads=[BBON], writes=[Buf()], is_out=True)
        kb.op("dve", lambda e: e.tensor_tensor(out=TA[:], in0=A[:], in1=bc(par[:, 3, :].unsqueeze(2), S4), op=ALU.mult),
              reads=[BA, Bpar], writes=[BTA])
        kb.op("dve", lambda e: e.tensor_tensor(out=TA[:], in0=TA[:], in1=bc(omk[:].unsqueeze(2), S4), op=ALU.add),
              reads=[BTA, Bomk], writes=[BTA])
        kb.op("dve", lambda e: e.tensor_tensor(out=KD[:], in0=TA[:], in1=k_, op=ALU.mult), reads=[BTA, BP], writes=[BKD])
        kb.op("pool", lambda e: e.tensor_tensor(out=BS[:], in0=KK[:], in1=A[:], op=ALU.mult), reads=[BKK, BA], writes=[BBS])
        for h in range(NH):
            for ch in range(2):
                kb.op("dve", lambda e, h=h, ch=ch: e.tensor_tensor_scan(
                    out=LREL[:, h, ch * 64:(ch + 1) * 64], data0=ONES,
                    data1=LW[:, h, ch * 64:(ch + 1) * 64], initial=0.0, op0=ALU.mult, op1=ALU.add),
                    reads=[BLW, BONES], writes=[BLREL])
        kb.op("act", lambda e: e.activation(out=EPOS[:], in_=LREL[:], func=AF.Exp), reads=[BLREL], writes=[BEPOS])
        kb.op("act", lambda e: e.activation(out=ENEG[:], in_=LREL[:], func=AF.Exp, scale=-1.0), reads=[BLREL], writes=[BENEG])
        kb.op("dve", lambda e: e.tensor_tensor(out=EPREV[:], in0=LREL[:], in1=LW[:], op=ALU.subtract), reads=[BLREL, BLW], writes=[BEPREV])
        kb.op("act", lambda e: e.activation(out=EPREV[:], in_=EPREV[:], func=AF.Exp), reads=[BEPREV], writes=[BEPREV])
        L5 = LREL[:].rearrange("p c (h t) -> p c h t", h=2)
        kb.op("dve", lambda e, L5=L5: e.tensor_tensor(out=EBAR[:].rearrange("p c (h t) -> p c h t", h=2),
                                                       in0=bc(L5[:, :, :, 63:64], [64, NH, 2, 64]), in1=L5, op=ALU.subtract),
              reads=[BLREL], writes=[BEBAR])
        kb.op("act", lambda e: e.activation(out=EBAR[:], in_=EBAR[:], func=AF.Exp), reads=[BEBAR], writes=[BEBAR])
        kb.op("act", lambda e, L5=L5: e.activation(out=PC[:].unsqueeze(3), in_=L5[:, :, :, 63:64], func=AF.Exp), reads=[BLREL], writes=[BPC])

        def v5(t):
            return t[:].rearrange("p c (h t) -> p c h t", h=2)
        kb.op("dve", lambda e: e.scalar_tensor_tensor(out=AR[:].rearrange("p c h t -> p (c h) t")[:, :, 0:64], in0=KK[:].rearrange("p c (h t) -> p (c h) t", h=2),
                                                      scalar=-1.0, in1=EPREV[:].rearrange("p c (h t) -> p (c h) t", h=2), op0=ALU.mult, op1=ALU.mult),
              reads=[BKK, BEPREV], writes=[BAR])
        kb.op("dve", lambda e: e.tensor_tensor(out=AR[:, :, :, 64:128], in0=r_.rearrange("p c (h t) -> p c h t", h=2), in1=v5(EPOS), op=ALU.mult),
              reads=[BP, BEPOS], writes=[BAR])
        kb.op("dve", lambda e: e.tensor_tensor(out=BH[:], in0=v5(BS), in1=v5(ENEG), op=ALU.mult), reads=[BBS, BENEG], writes=[BBH])
        kb.op("dve", lambda e: e.tensor_tensor(out=KH[:], in0=v5(KD), in1=v5(ENEG), op=ALU.mult), reads=[BKD, BENEG], writes=[BKH])
        kb.op("dve", lambda e: e.tensor_tensor(out=BB[:], in0=v5(BS), in1=v5(EBAR), op=ALU.mult), reads=[BBS, BEBAR], writes=[BBB])
        kb.op("pool", lambda e: e.tensor_tensor(out=KBr[:], in0=v5(KD), in1=v5(EBAR), op=ALU.mult), reads=[BKD, BEBAR], writes=[BKBr])

        for ch in range(2):
            for hp in range(4):
                u0 = ch * 8 + hp * 2
                ps, Bps = nps()
                for e_ in range(2):
                    h = hp * 2 + e_
                    srcs = [(AR[:, h, ch, 0:64].bitcast(F32), BAR), (P[:, 16 + h, ch * 64:(ch + 1) * 64], BP), (BB[:, h, ch, :], BBB), (KBr[:, h, ch, :], BKBr)]
                    for ai, (s_ap, s_b) in enumerate(srcs):
                        col = (e_ * 4 + ai) * 64
                        kb.op("pe", lambda e, ps=ps, col=col, s_ap=s_ap: e.transpose(out=ps[0:64, col:col + 64], in_=s_ap, identity=ident),
                              reads=[s_b, Bcst], writes=[Bps])
                kb.op("act", lambda e, ps=ps, u0=u0: e.copy(out=TM[:, u0:u0 + 2, 1:5, :],
                                                             in_=ps[0:64, :].rearrange("p (e a k) -> p e a k", a=4, e=2)),
                      reads=[Bps], writes=[BTM])
        for ch in range(2):
            for hp in range(4):
                u0 = ch * 8 + hp * 2
                ps, Bps = nps()
                for e_ in range(2):
                    h = hp * 2 + e_
                    for hi, (lt, Blt) in enumerate([(BH, BBH), (KH, BKH)]):
                        kb.op("pe", lambda e, ps=ps, lt=lt, e_=e_, hi=hi, h=h, ch=ch: e.matmul(
                            ps[0:64, e_ * 256 + hi * 128:e_ * 256 + hi * 128 + 128], lhsT=lt[:, h, ch, :], rhs=AR[:, h, ch, :],
                            start=True, stop=True), reads=[Blt, BAR], writes=[Bps])
                kb.op("dve", lambda e, ps=ps, u0=u0: e.tensor_tensor(out=GS[:, u0:u0 + 2, :], in0=v3(ps, 2), in1=maskA, op=ALU.mult),
                      reads=[Bps, Bmk], writes=[BGS])
        for ch in range(2):
            ps, Bps = nps()
            for h in range(8):
                kb.op("pe", lambda e, ps=ps, h=h, ch=ch: e.matmul(ps[0:64, h * 64:(h + 1) * 64], lhsT=AR[:, h, ch, 0:64],
                                                                rhs=BH[:, h, ch, :], start=True, stop=True),
                      reads=[BAR, BBH], writes=[Bps])
            kb.op("dve", lambda e, ps=ps, ch=ch: e.tensor_tensor(out=MM[0][0][:, ch * 8:(ch + 1) * 8, :], in0=v3(ps, 8),
                                                                  in1=maskL, op=ALU.mult), reads=[Bps, Bmk], writes=[MM[0][1]])
        kb.op("dve", lambda e: e.tensor_tensor(out=WT[0][0][:, :, 64:128], in0=GS[:, :, 0:64], in1=identU, op=ALU.add),
              reads=[BGS, Bmk], writes=[WT[0][1]])
        for g in range(4):
            us = range(g * 4, g * 4 + 4)
            ps1, Bps1 = nps(); ps2, Bps2 = nps()
            for ui, u in enumerate(us):
                kb.op("pe", lambda e, ps1=ps1, ui=ui, u=u: e.matmul(ps1[0:64, ui * 64:(ui + 1) * 64], lhsT=MM[0][0][:, u, :], rhs=GS[:, u, 0:64],
                                                                   start=True, stop=True), reads=[MM[0][1], BGS], writes=[Bps1])
                kb.op("pe", lambda e, ps2=ps2, ui=ui, u=u: e.matmul(ps2[0:64, ui * 64:(ui + 1) * 64], lhsT=GS[:, u, 0:64], rhs=MM[0][0][:, u, :],
                                                                   start=True, stop=True), reads=[MM[0][1], BGS], writes=[Bps2])
            kb.op("act", lambda e, ps1=ps1, g=g: e.copy(out=WT[0][0][:, g * 4:g * 4 + 4, 0:64], in_=v3(ps1, 4, 256)),
                  reads=[Bps1], writes=[WT[0][1]])
            kb.op("act", lambda e, ps2=ps2, g=g: e.copy(out=MM[1][0][:, g * 4:g * 4 + 4, :], in_=v3(ps2, 4, 256)),
                  reads=[Bps2], writes=[MM[1][1]])
        cur, mcur = 0, 1
        for lvl in range(4):
            for g in range(4):
                us = range(g * 4, g * 4 + 4)
                ps1, Bps1 = nps(); ps2, Bps2 = nps()
                for ui, u in enumerate(us):
                    kb.op("pe", lambda e, ps1=ps1, ui=ui, u=u, cur=cur, mcur=mcur: e.matmul(
                        ps1[0:64, ui * 128:(ui + 1) * 128], lhsT=MM[mcur][0][:, u, :], rhs=WT[cur][0][:, u, :], start=True, stop=True),
                        reads=[MM[mcur][1], WT[cur][1]], writes=[Bps1])
                    kb.op("pe", lambda e, ps2=ps2, ui=ui, u=u, cur=cur, mcur=mcur: e.matmul(
                        ps2[0:64, ui * 64:(ui + 1) * 64], lhsT=WT[cur][0][:, u, 0:64], rhs=MM[mcur][0][:, u, :], start=True, stop=True),
                        reads=[MM[mcur][1], WT[cur][1]], writes=[Bps2])
                v1 = v3(ps1, 4)
                kb.op("act", lambda e, v1=v1, g=g, cur=cur: e.copy(out=WT[1 - cur][0][:, g * 4:g * 4 + 4, 0:64], in_=v1[:, :, 0:64]),
                      reads=[Bps1], writes=[WT[1 - cur][1]])
                kb.op("dve", lambda e, v1=v1, g=g, cur=cur: e.tensor_tensor(out=WT[1 - cur][0][:, g * 4:g * 4 + 4, 64:128], in0=v1[:, :, 64:128],
                                                                            in1=WT[cur][0][:, g * 4:g * 4 + 4, 64:128], op=ALU.add),
                      reads=[Bps1, WT[cur][1]], writes=[WT[1 - cur][1]])
                kb.op("act", lambda e, ps2=ps2, g=g, mcur=mcur: e.copy(out=MM[1 - mcur][0][:, g * 4:g * 4 + 4, :], in_=v3(ps2, 4, 256)),
                      reads=[Bps2], writes=[MM[1 - mcur][1]])
            cur, mcur = 1 - cur, 1 - mcur
        for g in range(2):
            ps1, Bps1 = nps()
            for ui in range(8):
                u = g * 8 + ui
                kb.op("pe", lambda e, ps1=ps1, ui=ui, u=u, cur=cur, mcur=mcur: e.matmul(
                    ps1[0:64, ui * 64:(ui + 1) * 64], lhsT=MM[mcur][0][:, u, :], rhs=WT[cur][0][:, u, 64:128], start=True, stop=True),
                    reads=[MM[mcur][1], WT[cur][1]], writes=[Bps1])
            kb.op("dve", lambda e, ps1=ps1, g=g, cur=cur: e.tensor_tensor(out=XF[:, g * 8:g * 8 + 8, :], in0=v3(ps1, 8),
                                                                         in1=WT[cur][0][:, g * 8:g * 8 + 8, 64:128], op=ALU.add),
                  reads=[Bps1, WT[cur][1]], writes=[BXF])
        for g in range(2):
            ps1, Bps1 = nps()
            for ui in range(8):
                u = g * 8 + ui
                kb.op("pe", lambda e, ps1=ps1, ui=ui, u=u: e.matmul(ps1[0:64, ui * 64:(ui + 1) * 64], lhsT=GS[:, u, 128:192], rhs=TM[:, u, 2, :],
                                                                   start=True, stop=True), reads=[BGS, BTM], writes=[Bps1])
            kb.op("act", lambda e, ps1=ps1, g=g: e.copy(out=TM[:, g * 8:g * 8 + 8, 0, :], in_=v3(ps1, 8)), reads=[Bps1], writes=[BTM])
        for g in range(4):
            ps1, Bps1 = nps()
            for ui in range(4):
                u = g * 4 + ui
                kb.op("pe", lambda e, ps1=ps1, ui=ui, u=u: e.matmul(ps1[0:64, ui * 128:(ui + 1) * 128], lhsT=XF[:, u, :],
                                                                   rhs=TM[:, u, 0:2, :].rearrange("p a k -> p (a k)"),
                                                                   start=True, stop=True), reads=[BXF, BTM], writes=[Bps1])
            kb.op("act", lambda e, ps1=ps1, g=g: e.copy(out=UA[:, g * 4:g * 4 + 4, :], in_=v3(ps1, 4)), reads=[Bps1], writes=[BUA])
        for g in range(2):
            ps1, Bps1 = nps()
            for ui in range(8):
                u = g * 8 + ui
                kb.op("pe", lambda e, ps1=ps1, ui=ui, u=u: e.matmul(ps1[0:64, ui * 64:(ui + 1) * 64], lhsT=TM[:, u, 1, :], rhs=XF[:, u, :],
                                                                   start=True, stop=True), reads=[BTM, BXF], writes=[Bps1])
            kb.op("act", lambda e, ps1=ps1, g=g: e.copy(out=APT[:, g * 8:g * 8 + 8, :], in_=v3(ps1, 8)), reads=[Bps1], writes=[BAPT])
        for ch in range(2):
            scur, snxt = ST[gc % 2], ST[(gc + 1) % 2]
            gc += 1
            psu, Bpsu = nps()
            for h in range(8):
                u = ch * 8 + h
                kb.op("pe", lambda e, psu=psu, h=h, u=u, scur=scur: e.matmul(
                    psu[0:64, h * 64:(h + 1) * 64], lhsT=APT[:, u, :], rhs=scur[0][:, h, :], start=True, stop=True),
                    reads=[BAPT, scur[1]], writes=[Bpsu])
            kb.op("dve", lambda e, psu=psu, ch=ch: e.tensor_tensor(out=UU[:], in0=v3(psu, 8), in1=UA[:, ch * 8:ch * 8 + 8, 0:64], op=ALU.add),
                  reads=[Bpsu, BUA], writes=[BUU])
            psy, Bpsy = nps(); pss_, Bpss = nps()
            for h in range(8):
                u = ch * 8 + h
                oy = psy[0:64, h * 64:(h + 1) * 64]
                kb.op("pe", lambda e, oy=oy, h=h, u=u: e.matmul(oy, lhsT=UU[:, h, :], rhs=GS[:, u, 64:128], start=True, stop=False),
                      reads=[BUU, BGS], writes=[Bpsy])
                kb.op("pe", lambda e, oy=oy, u=u: e.matmul(oy, lhsT=TM[:, u, 2, :], rhs=GS[:, u, 192:256], start=False, stop=False),
                      reads=[BTM, BGS], writes=[Bpsy])
                kb.op("pe", lambda e, oy=oy, h=h, ch=ch, scur=scur: e.matmul(oy, lhsT=scur[0][:, h, :], rhs=AR[:, h, ch, 64:128],
                                                                            start=False, stop=True),
                      reads=[scur[1], BAR], writes=[Bpsy])
                os_ = pss_[0:64, h * 64:(h + 1) * 64]
                kb.op("pe", lambda e, os_=os_, h=h, u=u: e.matmul(os_, lhsT=TM[:, u, 3, :], rhs=UU[:, h, :], start=True, stop=False),
                      reads=[BTM, BUU], writes=[Bpss])
                kb.op("pe", lambda e, os_=os_, u=u: e.matmul(os_, lhsT=TM[:, u, 4, :], rhs=TM[:, u, 2, :], start=False, stop=True),
                      reads=[BTM], writes=[Bpss])
            kb.op("act", lambda e, psy=psy, ch=ch: e.copy(out=YO[:, :, ch * 64:(ch + 1) * 64], in_=v3(psy, 8)), reads=[Bpsy], writes=[BYO])
            kb.op("dve", lambda e, pss_=pss_, ch=ch, scur=scur, snxt=snxt: e.tensor_tensor(
                out=snxt[0][:], in0=scur[0][:], in1=bc(PC[:, :, ch:ch + 1], [64, NH, 64]), op=ALU.mult),
                reads=[scur[1], BPC], writes=[snxt[1]])
            kb.op("dve", lambda e, pss_=pss_, snxt=snxt: e.tensor_tensor(out=snxt[0][:], in0=v3(pss_, 8), in1=snxt[0][:], op=ALU.add),
                  reads=[Bpss, snxt[1]], writes=[snxt[1]])
        kb.dma("sp", lambda e, t0=t0: e.dma_start(out=yv[:, :, t0:t0 + NB], in_=YO[:]), reads=[BYO], writes=[Buf()], is_out=True)
    C.pop()


def emit_p2d(C, io):
    kb = C.kb
    C.push()
    rope_in = io["rope"]
    rd_in = io["rd"]
    gn_in = io["gn"]
    tab_in = io["tab"]
    cst_in = io["cst"]
    y_out = io["yT"]

    def T(shape, dt=F32):
        return C.sb(shape, dt), Buf()
    rope, Brope = T([128, 2, TALL])
    rd, Brd = T([128, 8]); lg, Blg = T([128, 8]); gch, Bgch = T([128, 8])
    gn, Bgn = T([128, 4]); tab, Btab = T([128, 770]); cst, Bcst = T([128, 256])
    ident = cst[:, 0:128]; ONESM = cst[:, 128:256]
    DEC = [T([128, 128]) for _ in range(8)]
    XI = [T([128, 128]) for _ in range(8)]
    ZE = [T([128, 1]) for _ in range(8)]
    kb.dma("sp", lambda e: e.dma_start(out=rope[:], in_=rope_in.rearrange("a p t -> p a t")), writes=[Brope])
    kb.dma("sp", lambda e: e.dma_start(out=rd[:], in_=rd_in), writes=[Brd])
    kb.dma("sp", lambda e: e.dma_start(out=gn[:], in_=gn_in), writes=[Bgn])
    kb.dma("sp", lambda e: e.dma_start(out=tab[:], in_=tab_in), writes=[Btab])
    kb.dma("sp", lambda e: e.dma_start(out=cst[:], in_=cst_in), writes=[Bcst])
    kb.op("act", lambda e: e.activation(out=lg[:], in_=rd[:], func=AF.Exp), reads=[Brd], writes=[Blg])
    kb.op("dve", lambda e: e.tensor_scalar(out=lg[:], in0=lg[:], scalar1=-1.0, scalar2=None, op0=ALU.mult), reads=[Blg], writes=[Blg])
    kb.op("act", lambda e: e.activation(out=gch[:], in_=lg[:], func=AF.Exp, scale=128.0), reads=[Blg], writes=[Bgch])
    for d in range(2):
        for h in range(4):
            i = d * 4 + h
            kb.op("act", lambda e, d=d, i=i: e.activation(out=DEC[i][0][:], in_=tab[:, d * 256:d * 256 + 128], func=AF.Exp, scale=lg[:, i:i + 1]),
                  reads=[Btab, Blg], writes=[DEC[i][1]])
            kb.op("dve", lambda e, d=d, i=i: e.tensor_tensor(out=DEC[i][0][:], in0=DEC[i][0][:], in1=tab[:, d * 256 + 128:d * 256 + 256], op=ALU.mult),
                  reads=[Btab, DEC[i][1]], writes=[DEC[i][1]])
            kb.op("act", lambda e, d=d, i=i: e.activation(out=XI[i][0][:], in_=tab[:, 512 + d * 128:512 + d * 128 + 128], func=AF.Exp, scale=lg[:, i:i + 1]),
                  reads=[Btab, Blg], writes=[XI[i][1]])
            kb.op("act", lambda e, d=d, i=i: e.activation(out=ZE[i][0][:], in_=tab[:, 768 + d:769 + d], func=AF.Exp, scale=lg[:, i:i + 1]),
                  reads=[Btab, Blg], writes=[ZE[i][1]])
    X6 = [T([128, TALL], mybir.dt.float32r if i_ < 2 else F32) for i_ in range(6)]
    QX = [T([128, TALL], mybir.dt.float32r) for _ in range(2)]
    KT, BKT = T([128, NCH, 128]); VT, BVT = T([128, NCH, 128], mybir.dt.float32r)
    KZ = [T([128, 128], mybir.dt.float32r) for _ in range(2)]
    ATT = [T([128, 128], mybir.dt.float32r) for _ in range(2)]
    O, BO = T([128, TALL]); TMP, BTMP = T([128, TALL])
    S = [T([128, 128], mybir.dt.float32r) for _ in range(2)]
    psl = [(C.ps([128, 512]), Buf(excl=True)) for _ in range(8)]
    pcnt = [0]

    def nps():
        r = psl[pcnt[0] % 8]
        pcnt[0] += 1
        return r
    yv = y_out.rearrange("(h p) t -> p h t", p=128)
    SCALE = 128.0 ** -0.5
    for h in range(4):
        for a in range(6):
            for pi_, (pr0, pr1, srcap) in enumerate(io["qkv"](h, a)):
                q_ = "sp" if (a + pi_) % 2 == 0 else "act"
                kb.dma(q_, lambda e, a=a, pr0=pr0, pr1=pr1, srcap=srcap: e.dma_start(out=(X6[a][0][pr0:pr1, :].bitcast(F32) if a < 2 else X6[a][0][pr0:pr1, :]), in_=srcap), writes=[X6[a][1]])
        (q, Bq), (k, Bk), (v, Bv), (g, Bg), (q2, Bq2), (k2, Bk2) = X6
        for (x, Bx, x2, Bx2, eng2) in [(q, Bq, q2, Bq2, "pool"), (k, Bk, k2, Bk2, "pool")]:
            kb.op("dve", lambda e, x=x: e.tensor_tensor(out=TMP[:], in0=x[:].bitcast(F32), in1=rope[:, 0, :], op=ALU.mult), reads=[Bx, Brope], writes=[BTMP])
            kb.op(eng2, lambda e, x2=x2: e.tensor_tensor(out=x2[:], in0=x2[:], in1=rope[:, 1, :], op=ALU.mult), reads=[Bx2, Brope], writes=[Bx2])
            kb.op("dve", lambda e, x=x, x2=x2: e.tensor_tensor(out=x[:], in0=TMP[:], in1=x2[:], op=ALU.add), reads=[BTMP, Bx2], writes=[Bx])
        kb.op("act", lambda e: e.mul(out=k[:], in_=k[:].bitcast(F32), mul=SCALE), reads=[Bk], writes=[Bk])
        for (src, Bsrc, dst, Bdst) in [(k, Bk, KT, BKT), (v, Bv, VT, BVT)]:
            for c4 in range(0, NCH, 4):
                ps, Bps = nps()
                n = min(4, NCH - c4)
                for ci in range(n):
                    c = c4 + ci
                    kb.op("pe", lambda e, ps=ps, ci=ci, c=c, src=src: e.transpose(out=ps[:, ci * 128:(ci + 1) * 128], in_=(src[:, c * 128:(c + 1) * 128].bitcast(F32) if src is k else src[:, c * 128:(c + 1) * 128]), identity=ident),
                          reads=[Bsrc, Bcst], writes=[Bps])
                kb.op("act", lambda e, ps=ps, c4=c4, n=n, dst=dst: e.copy(out=dst[:, c4:c4 + n, :], in_=ps[:, 0:n * 128].rearrange("p (a b) -> p a b", a=n)),
                      reads=[Bps], writes=[Bdst])
        for d in range(2):
            i = d * 4 + h
            kb.op("dve", lambda e, d=d, i=i: e.tensor_tensor(
                out=QX[d][0][:].rearrange("p (c t) -> p c t", c=NCH), in0=q[:].bitcast(F32).rearrange("p (c t) -> p c t", c=NCH),
                in1=XI[i][0][:].unsqueeze(1).broadcast_to([128, NCH, 128]), op=ALU.mult), reads=[Bq, XI[i][1]], writes=[QX[d][1]])
        for d in range(2):
            i = d * 4 + h
            st, Bst = S[d]
            kb.op("dve", lambda e, st=st: e.tensor_scalar(out=st[:], in0=ident, scalar1=0.0, scalar2=None, op0=ALU.mult), reads=[Bcst], writes=[Bst])
            order = [0, 1] + list(range(2, NCH)) if d == 0 else [1, 0] + list(range(NCH - 1, 1, -1))
            for n_, c in enumerate(order):
                cs = slice(c * 128, (c + 1) * 128)
                at, Bat = ATT[n_ % 2]
                kz, Bkz = KZ[n_ % 2]
                ps, Bps = nps()
                kb.op("pe", lambda e, ps=ps, cs=cs: e.matmul(ps[:, 0:128], lhsT=k[:, cs], rhs=q[:, cs], start=True, stop=True),
                      reads=[Bk, Bq], writes=[Bps])
                kb.op("dve", lambda e, ps=ps, at=at, i=i: e.tensor_tensor(out=at[:], in0=ps[:, 0:128], in1=DEC[i][0][:], op=ALU.mult),
                      reads=[Bps, DEC[i][1]], writes=[Bat])
                kb.op("dve", lambda e, kz=kz, c=c, i=i: e.tensor_tensor(out=kz[:], in0=KT[:, c, :], in1=ZE[i][0][:].broadcast_to([128, 128]), op=ALU.mult),
                      reads=[BKT, ZE[i][1]], writes=[Bkz])
                po, Bpo = nps()
                kb.op("pe", lambda e, po=po, c=c, at=at: e.matmul(po[:, 0:128], lhsT=VT[:, c, :], rhs=at[:], start=True, stop=False),
                      reads=[BVT, Bat], writes=[Bpo])
                kb.op("pe", lambda e, po=po, cs=cs, st=st, d=d: e.matmul(po[:, 0:128], lhsT=st[:], rhs=QX[d][0][:, cs], start=False, stop=True),
                      reads=[Bst, QX[d][1]], writes=[Bpo])
                if d == 0:
                    kb.op("act", lambda e, po=po, cs=cs: e.copy(out=O[:, cs], in_=po[:, 0:128]), reads=[Bpo], writes=[BO])
                else:
                    kb.op("dve", lambda e, po=po, cs=cs: e.tensor_tensor(out=O[:, cs], in0=po[:, 0:128], in1=O[:, cs], op=ALU.add),
                          reads=[Bpo, BO], writes=[BO])
                pn, Bpn = nps()
                kb.op("pe", lambda e, pn=pn, kz=kz, c=c: e.matmul(pn[:, 0:128], lhsT=kz[:], rhs=VT[:, c, :], start=True, stop=True),
                      reads=[Bkz, BVT], writes=[Bpn])
                kb.op("dve", lambda e, pn=pn, st=st, i=i: e.scalar_tensor_tensor(out=st[:], in0=st[:].bitcast(F32), scalar=gch[:, i:i + 1], in1=pn[:, 0:128],
                                                                                 op0=ALU.mult, op1=ALU.add), reads=[Bst, Bgch, Bpn], writes=[Bst])
        for (t0, tn) in token_tiles(TALL):
            ts_ = slice(t0, t0 + tn)
            pm, Bpm = nps()
            kb.op("pe", lambda e, pm=pm, ts_=ts_, tn=tn: e.matmul(pm[:, 0:tn], lhsT=ONESM, rhs=O[:, ts_], start=True, stop=True), reads=[Bcst, BO], writes=[Bpm])
            kb.op("dve", lambda e, pm=pm, ts_=ts_, tn=tn: e.tensor_tensor(out=O[:, ts_], in0=O[:, ts_], in1=pm[:, 0:tn], op=ALU.subtract),
                  reads=[Bpm, BO], writes=[BO])
            kb.op("act", lambda e, ts_=ts_: e.activation(out=TMP[:, ts_], in_=O[:, ts_], func=AF.Square), reads=[BO], writes=[BTMP])
            pv, Bpv = nps()
            kb.op("pe", lambda e, pv=pv, ts_=ts_, tn=tn: e.matmul(pv[:, 0:tn], lhsT=ONESM, rhs=TMP[:, ts_], start=True, stop=True), reads=[Bcst, BTMP], writes=[Bpv])
            kb.op("act", lambda e, pv=pv, ts_=ts_, tn=tn: e.activation(out=TMP[:, ts_], in_=pv[:, 0:tn], func=AF.Sqrt, bias=eps_ap(C, RET_EPS), scale=1.0),
                  reads=[Bpv], writes=[BTMP])
            kb.op("dve", lambda e, ts_=ts_: e.reciprocal(out=TMP[:, ts_], in_=TMP[:, ts_]), reads=[BTMP], writes=[BTMP])
            kb.op("dve", lambda e, ts_=ts_, h=h: e.scalar_tensor_tensor(out=O[:, ts_], in0=O[:, ts_], scalar=gn[:, h:h + 1], in1=TMP[:, ts_],
                                                                       op0=ALU.mult, op1=ALU.mult), reads=[BO, Bgn, BTMP], writes=[BO])
            kb.op("act", lambda e, ts_=ts_: e.activation(out=TMP[:, ts_], in_=g[:, ts_], func=AF.Silu), reads=[Bg], writes=[BTMP])
            kb.op("dve", lambda e, ts_=ts_: e.tensor_tensor(out=O[:, ts_], in0=O[:, ts_], in1=TMP[:, ts_], op=ALU.mult), reads=[BO, BTMP], writes=[BO])
        kb.dma("sp", lambda e, h=h: e.dma_start(out=yv[:, h, :], in_=O[:]), reads=[BO], writes=[Buf()], is_out=True)
    C.pop()


def emit_p2b(C, io):
    kb = C.kb
    C.push()
    nrm_in = io["nrm"]
    wq_in = io["wq"]
    wkv_in = io["wkv"]
    rope_in = io["rope"]
    cst_in = io["cst"]
    y_out = io["y"]

    def T(shape, dt=F32):
        return C.sb(shape, dt), Buf()
    cxr, Bcx = T([128, 5, TALL], mybir.dt.float32r); cx = cxr[:].bitcast(F32); kr, Bkr = T([64, 2, TALL]); nrm, Bnrm = T([128, 5])
    wq, Bwq = T([128, 3, 4, 256], mybir.dt.float32r); wkv, Bwkv = T([128, 2, 4, 256], mybir.dt.float32r)
    C.push()
    wq32, Bwq32 = T([128, 3, 4, 256]); wkv32, Bwkv32 = T([128, 2, 4, 256])
    kb.dma("sp", lambda e: e.dma_start(out=wq32[:], in_=wq_in), writes=[Bwq32])
    kb.dma("sp", lambda e: e.dma_start(out=wkv32[:], in_=wkv_in), writes=[Bwkv32])
    kb.op("act", lambda e: e.copy(out=wq[:], in_=wq32[:]), reads=[Bwq32], writes=[Bwq])
    kb.op("act", lambda e: e.copy(out=wkv[:], in_=wkv32[:]), reads=[Bwkv32], writes=[Bwkv])
    C.pop()
    rope, Brope = T([64, 2, TALL])
    cst, Bcst = T([128, 384])
    ident = cst[:, 0:128]
    rstd, Brstd = T([128, TALL])
    R32 = mybir.dt.float32r
    krr, Bkrr = T([64, TALL], R32); qn, Bqn = T([128, TALL], R32); qrr, Bqrr = T([64, TALL], R32); kn, Bkn = T([128, TALL], R32)
    t1, Bt1 = T([64, TALL]); VT, BVT = T([128, NCH, 128], R32)
    Pm, BPm = T([128, TALL]); PT, BPT = T([128, NCH, 128], R32)
    krr32, Bkrr32 = T([64, TALL]); qrr32, Bqrr32 = T([64, TALL])
    sq = [T([128, 512]) for _ in range(2)]
    mx, Bmx = T([128, 1]); nb, Bnb = T([128, 1]); rs, Brs = T([128, 1]); ri, Bri = T([128, 1])
    OT = [T([128, 128]) for _ in range(2)]
    OT2 = [T([128, 128]) for _ in range(2)]
    big = C.ps([128, 2560]); Bbig = Buf(excl=True)
    psl = [(C.ps([128, 512]), Buf(excl=True)) for _ in range(3)]
    pcnt = [0]

    def nps():
        r = psl[pcnt[0] % 3]
        pcnt[0] += 1
        return r
    for a in range(5):
        kb.dma("sp" if a % 2 == 0 else "act", lambda e, a=a: e.dma_start(out=cx[:, a, :], in_=io["cx"][a]), writes=[Bcx])
    for (pr0, pr1, ai, srcap) in io["kr"]:
        kb.dma("sp", lambda e, pr0=pr0, pr1=pr1, ai=ai, srcap=srcap: e.dma_start(out=kr[pr0:pr1, ai, :], in_=srcap), writes=[Bkr])
    kb.dma("sp", lambda e: e.dma_start(out=rope[:], in_=rope_in.rearrange("a p t -> p a t")), writes=[Brope])
    kb.dma("sp", lambda e: e.dma_start(out=nrm[:], in_=nrm_in), writes=[Bnrm])
    kb.dma("sp", lambda e: e.dma_start(out=cst[:], in_=cst_in), writes=[Bcst])
    k = 0
    for (tiles, ones_ap) in [([0, 1, 2], cst[:, 128:256]), ([3, 4], cst[:, 256:384])]:
        for (t0, tn) in token_tiles(TALL):
            ps, Bps = nps()
            for ii, a in enumerate(tiles):
                s_, Bs_ = sq[k % 2]; k += 1
                kb.op("act", lambda e, s_=s_, a=a, t0=t0, tn=tn: e.activation(out=s_[:, 0:tn], in_=cx[:, a, t0:t0 + tn], func=AF.Square),
                      reads=[Bcx], writes=[Bs_])
                kb.op("pe", lambda e, ps=ps, s_=s_, tn=tn, ii=ii, ones_ap=ones_ap, n=len(tiles): e.matmul(
                    ps[:, 0:tn], lhsT=ones_ap, rhs=s_[:, 0:tn], start=(ii == 0), stop=(ii == n - 1)), reads=[Bcst, Bs_], writes=[Bps])
            kb.op("act", lambda e, ps=ps, t0=t0, tn=tn: e.activation(out=rstd[:, t0:t0 + tn], in_=ps[:, 0:tn], func=AF.Sqrt,
                                                                    bias=eps_ap(C, NORM_EPS), scale=1.0), reads=[Bps], writes=[Brstd])
        kb.op("dve", lambda e: e.reciprocal(out=rstd[:], in_=rstd[:]), reads=[Brstd], writes=[Brstd])
        for a in tiles:
            kb.op("dve", lambda e, a=a: e.scalar_tensor_tensor(out=cxr[:, a, :], in0=cx[:, a, :], scalar=nrm[:, a:a + 1], in1=rstd[:],
                                                               op0=ALU.mult, op1=ALU.mult), reads=[Bcx, Bnrm, Brstd], writes=[Bcx])
    kb.op("dve", lambda e: e.tensor_tensor(out=krr32[:], in0=kr[:, 0, :], in1=rope[:, 0, :], op=ALU.mult), reads=[Bkr, Brope], writes=[Bkrr32])
    kb.op("pool", lambda e: e.tensor_tensor(out=t1[:], in0=kr[:, 1, :], in1=rope[:, 1, :], op=ALU.mult), reads=[Bkr, Brope], writes=[Bt1])
    kb.op("dve", lambda e: e.tensor_tensor(out=krr[:], in0=krr32[:], in1=t1[:], op=ALU.add), reads=[Bkrr32, Bt1], writes=[Bkrr])
    oi = 0
    for hh in range(4):
        for (t0, tn) in token_tiles(TALL):
            ts_ = slice(t0, t0 + tn)
            ps, Bps = nps()
            for kc in range(3):
                kb.op("pe", lambda e, ps=ps, kc=kc, ts_=ts_, tn=tn, hh=hh: e.matmul(ps[:, 0:tn], lhsT=wq[:, kc, hh, 0:128], rhs=cxr[:, kc, ts_],
                                                                                   start=(kc == 0), stop=(kc == 2)), reads=[Bwq, Bcx], writes=[Bps])
            kb.op("act", lambda e, ps=ps, ts_=ts_, tn=tn: e.copy(out=qn[:, ts_], in_=ps[:, 0:tn]), reads=[Bps], writes=[Bqn])
            ps, Bps = nps()
            for kc in range(2):
                kb.op("pe", lambda e, ps=ps, kc=kc, ts_=ts_, tn=tn, hh=hh: e.matmul(ps[:, 0:tn], lhsT=wkv[:, kc, hh, 0:128], rhs=cxr[:, 3 + kc, ts_],
                                                                                   start=(kc == 0), stop=(kc == 1)), reads=[Bwkv, Bcx], writes=[Bps])
            kb.op("act", lambda e, ps=ps, ts_=ts_, tn=tn: e.copy(out=kn[:, ts_], in_=ps[:, 0:tn]), reads=[Bps], writes=[Bkn])
            for ri_, (c0, dst, Bdst) in enumerate([(128, qrr32, Bqrr32), (192, t1, Bt1)]):
                ps, Bps = nps()
                for kc in range(3):
                    kb.op("pe", lambda e, ps=ps, kc=kc, ts_=ts_, tn=tn, hh=hh, c0=c0: e.matmul(ps[0:64, 0:tn], lhsT=wq[:, kc, hh, c0:c0 + 64], rhs=cxr[:, kc, ts_],
                                                                                              start=(kc == 0), stop=(kc == 2)), reads=[Bwq, Bcx], writes=[Bps])
                kb.op("dve", lambda e, ps=ps, ts_=ts_, tn=tn, dst=dst, ri_=ri_: e.tensor_tensor(out=dst[:, ts_], in0=ps[0:64, 0:tn], in1=rope[:, ri_, ts_], op=ALU.mult),
                      reads=[Bps, Brope], writes=[Bdst])
        kb.op("dve", lambda e: e.tensor_tensor(out=qrr[:], in0=qrr32[:], in1=t1[:], op=ALU.add), reads=[Bqrr32, Bt1], writes=[Bqrr])
        for c4 in range(0, NCH, 4):
            ps, Bps = nps()
            n = min(4, NCH - c4)
            for ci in range(n):
                c = c4 + ci
                for kc in range(2):
                    kb.op("pe", lambda e, ps=ps, ci=ci, c=c, kc=kc, hh=hh: e.matmul(ps[:, ci * 128:(ci + 1) * 128], lhsT=cxr[:, 3 + kc, c * 128:(c + 1) * 128],
                                                                                   rhs=wkv[:, kc, hh, 128:256], start=(kc == 0), stop=(kc == 1)),
                          reads=[Bcx, Bwkv], writes=[Bps])
            kb.op("act", lambda e, ps=ps, c4=c4, n=n: e.copy(out=VT[:, c4:c4 + n, :], in_=ps[:, 0:n * 128].rearrange("p (a b) -> p a b", a=n)),
                  reads=[Bps], writes=[BVT])
        for qt in range(NCH):
            qs = slice(qt * 128, (qt + 1) * 128)
            nk = CTX if qt < 2 else TALL
            for (t0, tn) in token_tiles(nk):
                kb.op("pe", lambda e, qs=qs, t0=t0, tn=tn: e.matmul(big[:, t0:t0 + tn], lhsT=qn[:, qs], rhs=kn[:, t0:t0 + tn], start=True, stop=False),
                      reads=[Bqn, Bkn], writes=[Bbig])
                kb.op("pe", lambda e, qs=qs, t0=t0, tn=tn: e.matmul(big[:, t0:t0 + tn], lhsT=qrr[:, qs], rhs=krr[:, t0:t0 + tn], start=False, stop=True),
                      reads=[Bqrr, Bkrr], writes=[Bbig])
            kb.op("dve", lambda e, nk=nk: e.tensor_reduce(out=mx[:], in_=big[:, 0:nk], axis=AX.X, op=ALU.max), reads=[Bbig], writes=[Bmx])
            kb.op("dve", lambda e: e.tensor_scalar(out=nb[:], in0=mx[:], scalar1=-MLA_SCALE, scalar2=None, op0=ALU.mult), reads=[Bmx], writes=[Bnb])
            kb.op("act", lambda e, nk=nk: e.activation(out=Pm[:, 0:nk], in_=big[:, 0:nk], func=AF.Exp, bias=nb[:, 0:1], scale=MLA_SCALE, accum_out=rs[:, 0:1]),
                  reads=[Bbig, Bnb], writes=[BPm, Brs])
            kb.op("dve", lambda e: e.reciprocal(out=ri[:], in_=rs[:]), reads=[Brs], writes=[Bri])
            nkt = nk // 128
            for c4 in range(0, nkt, 4):
                ps, Bps = nps()
                n = min(4, nkt - c4)
                for ci in range(n):
                    c = c4 + ci
                    kb.op("pe", lambda e, ps=ps, ci=ci, c=c: e.transpose(out=ps[:, ci * 128:(ci + 1) * 128], in_=Pm[:, c * 128:(c + 1) * 128], identity=ident),
                          reads=[BPm, Bcst], writes=[Bps])
                ev = "act" if (c4 // 4) % 2 == 0 else "dve"
                if ev == "act":
                    kb.op("act", lambda e, ps=ps, c4=c4, n=n: e.copy(out=PT[:, c4:c4 + n, :], in_=ps[:, 0:n * 128].rearrange("p (a b) -> p a b", a=n)),
                          reads=[Bps], writes=[BPT])
                else:
                    kb.op("dve", lambda e, ps=ps, c4=c4, n=n: e.tensor_copy(out=PT[:, c4:c4 + n, :], in_=ps[:, 0:n * 128].rearrange("p (a b) -> p a b", a=n)),
                          reads=[Bps], writes=[BPT])
            po, Bpo = nps()
            for c in range(nkt):
                kb.op("pe", lambda e, po=po, c=c, nkt=nkt: e.matmul(po[:, 0:128], lhsT=PT[:, c, :], rhs=VT[:, c, :], start=(c == 0), stop=(c == nkt - 1)),
                      reads=[BPT, BVT], writes=[Bpo])
            ot, Bot = OT[oi % 2]; oi += 1
            kb.op("act", lambda e, po=po, ot=ot: e.activation(out=ot[:], in_=po[:, 0:128], func=AF.Copy, scale=ri[:, 0:1]), reads=[Bpo, Bri], writes=[Bot])
            pt2, Bpt2 = nps()
            kb.op("pe", lambda e, pt2=pt2, ot=ot: e.transpose(out=pt2[:, 0:128], in_=ot[:], identity=ident), reads=[Bot, Bcst], writes=[Bpt2])
            ot2, Bot2 = OT2[oi % 2]
            kb.op("dve", lambda e, pt2=pt2, ot2=ot2: e.tensor_copy(out=ot2[:], in_=pt2[:, 0:128]), reads=[Bpt2], writes=[Bot2])
            kb.dma("sp", lambda e, ot2=ot2, qs=qs, hh=hh: e.dma_start(out=y_out[hh * 128:(hh + 1) * 128, qs], in_=ot2[:]), reads=[Bot2], writes=[io["By"]])
    C.pop()


def emit_p2c(C, io):
    kb = C.kb
    C.push()
    cw_in = io["cw"]
    ln_in = io["ln"]
    cst_in = io["cst"]

    def T(shape, dt=F32):
        return C.sb(shape, dt), Buf()
    cw, Bcw = T([128, 4, 3]); ln, Bln = T([128, 4, 2]); cst, Bcst = T([128, 128])
    kb.dma("sp", lambda e: e.dma_start(out=cw[:], in_=cw_in), writes=[Bcw])
    kb.dma("sp", lambda e: e.dma_start(out=ln[:], in_=ln_in), writes=[Bln])
    kb.dma("sp", lambda e: e.dma_start(out=cst[:], in_=cst_in), writes=[Bcst])
    X3 = [T([128, TP]) for _ in range(3)]
    OC, BOC = T([128, TP])
    Y, BY = T([128, TALL]); Y2, BY2 = T([128, TALL]); TMP, BTMP = T([128, TALL])
    BN, BBN = T([128, TALL]); GT, BGT = T([128, TALL])
    psl = [(C.ps([128, 512]), Buf(excl=True)) for _ in range(4)]
    pcnt = [0]

    def nps():
        r = psl[pcnt[0] % 4]
        pcnt[0] += 1
        return r
    n = TP - 2
    for a in range(4):
        for i in range(3):
            srcap = io["cv"](i, a)
            kb.op("pool", lambda e, i=i: e.memset(X3[i][0][:, 0:1], 0.0), writes=[X3[i][1]])
            kb.op("pool", lambda e, i=i: e.memset(X3[i][0][:, 257:259], 0.0), writes=[X3[i][1]])
            kb.op("pool", lambda e, i=i: e.memset(X3[i][0][:, TP - 1:TP], 0.0), writes=[X3[i][1]])
            kb.dma("sp", lambda e, i=i, srcap=srcap: e.dma_start(out=X3[i][0][:, 1:257], in_=srcap[:, 0:CTX]), writes=[X3[i][1]])
            kb.dma("act", lambda e, i=i, srcap=srcap: e.dma_start(out=X3[i][0][:, 259:259 + SEQ], in_=srcap[:, CTX:TALL]), writes=[X3[i][1]])
        (bgt, Bbgt), (cg, Bcg), (u, Bu) = X3
        kb.op("dve", lambda e: e.tensor_tensor(out=cg[:], in0=cg[:], in1=u[:], op=ALU.mult), reads=[Bcg, Bu], writes=[Bcg])
        kb.op("pool", lambda e: e.memset(OC[:], 0.0), writes=[BOC])
        kb.op("act", lambda e, a=a: e.activation(out=OC[:, 1:1 + n], in_=cg[:, 1:1 + n], func=AF.Copy, scale=cw[:, a, 1:2]), reads=[Bcg, Bcw], writes=[BOC])
        kb.op("dve", lambda e, a=a: e.scalar_tensor_tensor(out=OC[:, 1:1 + n], in0=cg[:, 0:n], scalar=cw[:, a, 0:1], in1=OC[:, 1:1 + n], op0=ALU.mult, op1=ALU.add),
              reads=[Bcg, Bcw, BOC], writes=[BOC])
        kb.op("dve", lambda e, a=a: e.scalar_tensor_tensor(out=OC[:, 1:1 + n], in0=cg[:, 2:2 + n], scalar=cw[:, a, 2:3], in1=OC[:, 1:1 + n], op0=ALU.mult, op1=ALU.add),
              reads=[Bcg, Bcw, BOC], writes=[BOC])
        kb.op("dve", lambda e: e.tensor_tensor(out=OC[:], in0=OC[:], in1=bgt[:], op=ALU.mult), reads=[BOC, Bbgt], writes=[BOC])
        kb.dma("sp", lambda e, a=a: e.dma_start(out=io["yc"](a)[:, 0:CTX], in_=OC[:, 1:257]), reads=[BOC], writes=[io["By"]])
        kb.dma("act", lambda e, a=a: e.dma_start(out=io["yc"](a)[:, CTX:TALL], in_=OC[:, 259:259 + SEQ]), reads=[BOC], writes=[io["By"]])
        kb.dma("sp", lambda e, a=a: e.dma_start(out=Y[:], in_=io["yf"](a)), reads=[io["Byf"]], writes=[BY])
        kb.dma("act", lambda e, a=a: e.dma_start(out=Y2[:], in_=io["yb"](a)), reads=[io["Byb"]], writes=[BY2])
        kb.dma("sp", lambda e, a=a: e.dma_start(out=BN[:], in_=io["bon"](a)), reads=[io["Bbg"]], writes=[BBN])
        kb.dma("act", lambda e, a=a: e.dma_start(out=GT[:], in_=io["gate"](a)), reads=[io["Bbg"]], writes=[BGT])
        kb.op("dve", lambda e: e.tensor_tensor(out=Y[:], in0=Y[:], in1=Y2[:], op=ALU.add), reads=[BY, BY2], writes=[BY])
        for (t0, tn) in token_tiles(TALL):
            ts_ = slice(t0, t0 + tn)
            pm, Bpm = nps()
            kb.op("pe", lambda e, pm=pm, ts_=ts_, tn=tn: e.matmul(pm[:, 0:tn], lhsT=cst[:], rhs=Y[:, ts_], start=True, stop=True), reads=[Bcst, BY], writes=[Bpm])
            kb.op("dve", lambda e, pm=pm, ts_=ts_, tn=tn: e.tensor_tensor(out=Y[:, ts_], in0=Y[:, ts_], in1=pm[:, 0:tn], op=ALU.subtract),
                  reads=[Bpm, BY], writes=[BY])
            kb.op("act", lambda e, ts_=ts_: e.activation(out=TMP[:, ts_], in_=Y[:, ts_], func=AF.Square), reads=[BY], writes=[BTMP])
            pv, Bpv = nps()
            kb.op("pe", lambda e, pv=pv, ts_=ts_, tn=tn: e.matmul(pv[:, 0:tn], lhsT=cst[:], rhs=TMP[:, ts_], start=True, stop=True), reads=[Bcst, BTMP], writes=[Bpv])
            kb.op("act", lambda e, pv=pv, ts_=ts_, tn=tn: e.activation(out=TMP[:, ts_], in_=pv[:, 0:tn], func=AF.Sqrt, bias=eps_ap(C, RWKV_GN_EPS), scale=1.0),
                  reads=[Bpv], writes=[BTMP])
            kb.op("dve", lambda e, ts_=ts_: e.reciprocal(out=TMP[:, ts_], in_=TMP[:, ts_]), reads=[BTMP], writes=[BTMP])
            kb.op("dve", lambda e, ts_=ts_: e.tensor_tensor(out=Y[:, ts_], in0=Y[:, ts_], in1=TMP[:, ts_], op=ALU.mult), reads=[BY, BTMP], writes=[BY])
            kb.op("act", lambda e, ts_=ts_, a=a: e.activation(out=Y[:, ts_], in_=Y[:, ts_], func=AF.Identity, bias=ln[:, a, 1:2], scale=ln[:, a, 0:1]),
                  reads=[BY, Bln], writes=[BY])
            kb.op("dve", lambda e, ts_=ts_: e.tensor_tensor(out=Y[:, ts_], in0=Y[:, ts_], in1=BN[:, ts_], op=ALU.add), reads=[BY, BBN], writes=[BY])
            kb.op("dve", lambda e, ts_=ts_: e.tensor_tensor(out=Y[:, ts_], in0=Y[:, ts_], in1=GT[:, ts_], op=ALU.mult), reads=[BY, BGT], writes=[BY])
        kb.dma("sp", lambda e, a=a: e.dma_start(out=io["ya"](a), in_=Y[:]), reads=[BY], writes=[io["By"]])
    C.pop()


def emit_p3(C, io, segs):
    kb = C.kb
    C.push()
    yT = io["yT"]
    xT = io["xT"]
    msk = io["msk"]
    wo = io["wo"]
    wu = io["wu"]
    wc = io["wc"]
    wd = io["wd"]
    xo = io["xo"]
    SM = max(s[1] for s in segs); TM_ = SM + 2

    def T(shape, dt=F32):
        return C.sb(shape, dt), Buf()
    x_sb, Bx = T([128, 16, TM_]); y_sb, By = T([128, 16, TM_], BF16); o_sb, Bo = T([128, 16, TM_])
    a_sb, Ba = T([128, NFF, SM], BF16)
    v_sb, Bv = T([128, 2, 7, 16]); m_sb, Bm = T([128, 2 * len(segs)]); wc_sb, Bwc = T([128, 2 * NFF, 3])
    gm, Bgm = T([128, 2, 4, 16])
    ones, Bones = T([128, 128]); rstd, Brstd = T([128, TM_])
    sq = [T([128, 512]) for _ in range(2)]
    tmp = [T([128, TM_]) for _ in range(2)]
    wt = [T([128, 16, 256], BF16) for _ in range(4)]
    wdt = [T([128, NFF, 128], BF16) for _ in range(2)]
    ug = [T([128, 2, TM_]) for _ in range(2)]
    cvt = [T([128, 2, SM]) for _ in range(2)]
    pss = [(C.ps([128, 512]), Buf(excl=True)) for _ in range(2)]
    psm = [(C.ps([128, 512]), Buf(excl=True)) for _ in range(6)]
    pcnt = [0]

    def nps():
        r = psm[pcnt[0] % 6]
        pcnt[0] += 1
        return r
    for (ci_, vi, vsrc) in io["vec_srcs"]:
        kb.dma("sp", lambda e, ci_=ci_, vi=vi, vsrc=vsrc: e.dma_start(out=v_sb[:, ci_, vi, :], in_=vsrc), writes=[Bv])
    kb.dma("sp", lambda e: e.dma_start(out=m_sb[:], in_=msk), writes=[Bm])
    kb.dma("sp", lambda e: e.dma_start(out=wc_sb[:], in_=wc), writes=[Bwc])
    kb.op("dve", lambda e: e.memset(ones[:], 1.0 / D), writes=[Bones])
    for ci in range(2):
        kb.op("dve", lambda e, ci=ci: e.tensor_tensor(out=gm[:, ci, 0, :], in0=v_sb[:, ci, 0, :], in1=v_sb[:, ci, 3, :], op=ALU.mult), reads=[Bv], writes=[Bgm])
        kb.op("dve", lambda e, ci=ci: e.scalar_tensor_tensor(out=gm[:, ci, 1, :], in0=v_sb[:, ci, 5, :], scalar=1.0, in1=v_sb[:, ci, 1, :],
                                                             op0=ALU.add, op1=ALU.mult), reads=[Bv], writes=[Bgm])
        kb.op("dve", lambda e, ci=ci: e.tensor_tensor(out=gm[:, ci, 2, :], in0=v_sb[:, ci, 2, :], in1=v_sb[:, ci, 6, :], op=ALU.mult), reads=[Bv], writes=[Bgm])
    xv = xT.rearrange("(kc p) t -> p kc t", p=128)
    yv = yT.rearrange("(kc p) t -> p kc t", p=128)
    xov = xo.rearrange("(kc p) t -> p kc t", p=128)
    wi = [0]

    def half_tiles(n):
        h = (n + 1) // 2
        return [(0, h), (h, n - h)] if n > 512 else [(0, n)]
    for si, (lo, S, ci, hl, hr) in enumerate(segs):
        Tn = S + 2
        out0 = lo
        a0 = 0 if hl else 1
        a1 = Tn if hr else Tn - 1
        g0 = lo - 1 + a0
        if not hl:
            kb.op("dve", lambda e: e.memset(x_sb[:, :, 0:1], 0.0), writes=[Bx])
            kb.op("dve", lambda e: e.memset(y_sb[:, :, 0:1], 0.0), writes=[By])
        if not hr:
            kb.op("dve", lambda e, Tn=Tn: e.memset(x_sb[:, :, Tn - 1:Tn], 0.0), writes=[Bx])
            kb.op("dve", lambda e, Tn=Tn: e.memset(y_sb[:, :, Tn - 1:Tn], 0.0), writes=[By])
        for kc in range(16):
            kb.dma("sp" if kc % 2 == 0 else "act", lambda e, kc=kc, a0=a0, a1=a1, g0=g0: e.dma_start(out=x_sb[:, kc, a0:a1], in_=xv[:, kc, g0:g0 + a1 - a0]), reads=[io["Bx"]], writes=[Bx])
        for kc in range(16):
            kb.dma("pool", lambda e, kc=kc, a0=a0, a1=a1, g0=g0: e.dma_start(out=y_sb[:, kc, a0:a1], in_=yv[:, kc, g0:g0 + a1 - a0]), reads=[io["By"]], writes=[By])
        for nb in range(8):
            j = wi[0] % 4; wi[0] += 1
            w_, Bw_ = wt[j]
            kb.dma(io["wdma"](), lambda e, w_=w_, nb=nb: e.dma_start(out=w_[:], in_=wo[:, nb * 256:(nb + 1) * 256].rearrange("(kc p) n -> p kc n", p=128)), writes=[Bw_])
            for nc_ in range(2):
                dch = nb * 2 + nc_
                for (t0, tn) in half_tiles(Tn):
                    p, Bp = nps()
                    for kc in range(16):
                        kb.op("pe", lambda e, p=p, w_=w_, kc=kc, nc_=nc_, t0=t0, tn=tn: e.matmul(p[:, 0:tn], lhsT=w_[:, kc, nc_ * 128:(nc_ + 1) * 128],
                                                                                            rhs=y_sb[:, kc, t0:t0 + tn], start=(kc == 0), stop=(kc == 15)),
                              reads=[Bw_, By], writes=[Bp])
                    kb.op("act", lambda e, p=p, dch=dch, t0=t0, tn=tn: e.copy(out=o_sb[:, dch, t0:t0 + tn], in_=p[:, 0:tn]), reads=[Bp], writes=[Bo])
        emit_rstd(C, o_sb, Bo, 16, Tn, ones, Bones, rstd, Brstd, [s[0] for s in sq], [s[1] for s in sq], [p[0] for p in pss], [p[1] for p in pss], NORM_EPS)
        for kc in range(16):
            t_, Bt_ = tmp[kc % 2]
            kb.op("pool", lambda e, t_=t_, kc=kc, Tn=Tn: e.tensor_tensor(out=t_[:, 0:Tn], in0=o_sb[:, kc, 0:Tn], in1=rstd[:, 0:Tn], op=ALU.mult),
                  reads=[Bo, Brstd], writes=[Bt_])
            kb.op("dve", lambda e, t_=t_, kc=kc, Tn=Tn, ci=ci: e.scalar_tensor_tensor(out=x_sb[:, kc, 0:Tn], in0=t_[:, 0:Tn], scalar=gm[:, ci, 0, kc:kc + 1],
                                                                                 in1=x_sb[:, kc, 0:Tn], op0=ALU.mult, op1=ALU.add),
                  reads=[Bt_, Bgm, Bx], writes=[Bx])
        emit_rstd(C, x_sb, Bx, 16, Tn, ones, Bones, rstd, Brstd, [s[0] for s in sq], [s[1] for s in sq], [p[0] for p in pss], [p[1] for p in pss], NORM_EPS)
        for kc in range(16):
            t_, Bt_ = tmp[kc % 2]
            kb.op("dve", lambda e, t_=t_, kc=kc, Tn=Tn: e.tensor_tensor(out=t_[:, 0:Tn], in0=x_sb[:, kc, 0:Tn], in1=rstd[:, 0:Tn], op=ALU.mult),
                  reads=[Bx, Brstd], writes=[Bt_])
            kb.op("act", lambda e, t_=t_, kc=kc, Tn=Tn, ci=ci: e.activation(out=y_sb[:, kc, 0:Tn], in_=t_[:, 0:Tn], func=AF.Identity,
                                                                        bias=v_sb[:, ci, 4, kc:kc + 1], scale=gm[:, ci, 1, kc:kc + 1]),
                  reads=[Bt_, Bgm, Bv], writes=[By])
        for blk in range(NFF // 2):
            wts = []
            for half in range(2):
                j = wi[0] % 4; wi[0] += 1
                w_, Bw_ = wt[j]
                c0 = half * D_FF + blk * 256
                kb.dma(io["wdma"](), lambda e, w_=w_, c0=c0: e.dma_start(out=w_[:], in_=wu[:, c0:c0 + 256].rearrange("(kc p) n -> p kc n", p=128)), writes=[Bw_])
                wts.append((w_, Bw_))
            for cc in range(2):
                c = blk * 2 + cc
                u_, Bu_ = ug[c % 2]
                cv_, Bcv_ = cvt[c % 2]
                for half in range(2):
                    w_, Bw_ = wts[half]
                    for (t0, tn) in half_tiles(Tn):
                        p, Bp = nps()
                        for kc in range(16):
                            kb.op("pe", lambda e, p=p, w_=w_, kc=kc, cc=cc, t0=t0, tn=tn: e.matmul(p[:, 0:tn], lhsT=w_[:, kc, cc * 128:(cc + 1) * 128],
                                                                                              rhs=y_sb[:, kc, t0:t0 + tn], start=(kc == 0), stop=(kc == 15)),
                                  reads=[Bw_, By], writes=[Bp])
                        kb.op("act", lambda e, p=p, u_=u_, half=half, t0=t0, tn=tn: e.copy(out=u_[:, half, t0:t0 + tn], in_=p[:, 0:tn]), reads=[Bp], writes=[Bu_])
                kb.op("dve", lambda e, u_=u_, si=si: e.tensor_tensor(out=u_[:, :, 0:1], in0=u_[:, :, 0:1], in1=m_sb[:, 2 * si:2 * si + 1].unsqueeze(1).broadcast_to([128, 2, 1]),
                                                                      op=ALU.mult), reads=[Bu_, Bm], writes=[Bu_])
                kb.op("dve", lambda e, u_=u_, si=si, Tn=Tn: e.tensor_tensor(out=u_[:, :, Tn - 1:Tn], in0=u_[:, :, Tn - 1:Tn],
                                                                             in1=m_sb[:, 2 * si + 1:2 * si + 2].unsqueeze(1).broadcast_to([128, 2, 1]), op=ALU.mult),
                      reads=[Bu_, Bm], writes=[Bu_])
                for half in range(2):
                    wrow = half * NFF + c
                    kb.op("act", lambda e, u_=u_, cv_=cv_, half=half, wrow=wrow, S=S: e.activation(out=cv_[:, half, 0:S], in_=u_[:, half, 1:1 + S], func=AF.Copy,
                                                                                             scale=wc_sb[:, wrow, 1:2]), reads=[Bu_, Bwc], writes=[Bcv_])
                    eng = "dve"
                    kb.op(eng, lambda e, u_=u_, cv_=cv_, half=half, wrow=wrow, S=S: e.scalar_tensor_tensor(out=cv_[:, half, 0:S], in0=u_[:, half, 0:S], scalar=wc_sb[:, wrow, 0:1],
                                                                                                    in1=cv_[:, half, 0:S], op0=ALU.mult, op1=ALU.add),
                          reads=[Bu_, Bwc, Bcv_], writes=[Bcv_])
                    kb.op(eng, lambda e, u_=u_, cv_=cv_, half=half, wrow=wrow, S=S: e.scalar_tensor_tensor(out=cv_[:, half, 0:S], in0=u_[:, half, 2:2 + S], scalar=wc_sb[:, wrow, 2:3],
                                                                                                    in1=cv_[:, half, 0:S], op0=ALU.mult, op1=ALU.add),
                          reads=[Bu_, Bwc, Bcv_], writes=[Bcv_])
                kb.op("act", lambda e, cv_=cv_, S=S: e.activation(out=cv_[:, 0, 0:S], in_=cv_[:, 0, 0:S], func=AF.Silu), reads=[Bcv_], writes=[Bcv_])
                kb.op("pool", lambda e, cv_=cv_, c=c, S=S: e.tensor_tensor(out=a_sb[:, c, 0:S], in0=cv_[:, 0, 0:S], in1=cv_[:, 1, 0:S], op=ALU.mult),
                      reads=[Bcv_], writes=[Ba])
        for nb in range(D // 128):
            w_, Bw_ = wdt[nb % 2]
            kb.dma(io["wdma"](), lambda e, w_=w_, nb=nb: e.dma_start(out=w_[:], in_=wd[nb]), writes=[Bw_])
            for nc_ in range(1):
                dch = nb
                p, Bp = nps()
                for c in range(NFF):
                    kb.op("pe", lambda e, p=p, w_=w_, c=c, nc_=nc_, S=S: e.matmul(p[:, 0:S], lhsT=w_[:, c, nc_ * 128:(nc_ + 1) * 128], rhs=a_sb[:, c, 0:S],
                                                                             start=(c == 0), stop=(c == NFF - 1)), reads=[Bw_, Ba], writes=[Bp])
                kb.op("act", lambda e, p=p, dch=dch, S=S: e.copy(out=o_sb[:, dch, 0:S], in_=p[:, 0:S]), reads=[Bp], writes=[Bo])
        emit_rstd(C, o_sb, Bo, 16, S, ones, Bones, rstd, Brstd, [s[0] for s in sq], [s[1] for s in sq], [p[0] for p in pss], [p[1] for p in pss], NORM_EPS)
        for kc in range(16):
            t_, Bt_ = tmp[kc % 2]
            kb.op("pool", lambda e, t_=t_, kc=kc, S=S: e.tensor_tensor(out=t_[:, 0:S], in0=o_sb[:, kc, 0:S], in1=rstd[:, 0:S], op=ALU.mult),
                  reads=[Bo, Brstd], writes=[Bt_])
            kb.op("dve", lambda e, t_=t_, kc=kc, S=S, ci=ci: e.scalar_tensor_tensor(out=t_[:, 0:S], in0=t_[:, 0:S], scalar=gm[:, ci, 2, kc:kc + 1],
                                                                               in1=x_sb[:, kc, 1:1 + S], op0=ALU.mult, op1=ALU.add),
                  reads=[Bt_, Bgm, Bx], writes=[Bt_])
            kb.dma("sp", lambda e, t_=t_, kc=kc, S=S, out0=out0: e.dma_start(out=xov[:, kc, out0:out0 + S], in_=t_[:, 0:S]), reads=[Bt_], writes=[io["Bxo"]])
    C.pop()


def _ctx_push(self):
    self._stk.append(self.st)
    self.st = ExitStack()


def _ctx_pop(self):
    self.kb.barrier()
    self.st.close()
    self.st = self._stk.pop()


def _ctx_scratch(self, name, shape, dt=F32):
    return self.nc.dram_tensor(name, list(shape), dt, kind="Internal").ap()


Ctx.push = _ctx_push
Ctx.pop = _ctx_pop
Ctx.scratch = _ctx_scratch


def emit_p0f(C, io, nl):
    kb = C.kb
    C.push()

    def T(shape, dt=F32):
        return C.sb(shape, dt), Buf()
    c_sb, Bc = T([128, 16, 2]); s_sb, Bs = T([128, 16, 2]); mb_sb, Bmb = T([128, nl, 96]); mv, Bmv = T([128, 2, nl, 96])
    wt = [T([128, 16, 512]) for _ in range(2)]
    pst = [(C.ps([128, 512]), Buf(excl=True)) for _ in range(2)]
    kb.dma("sp", lambda e: e.dma_start(out=c_sb[:], in_=io["cT"]), writes=[Bc])
    kb.dma("sp", lambda e: e.dma_start(out=mb_sb[:], in_=io["mb"]), writes=[Bmb])
    kb.op("act", lambda e: e.activation(out=s_sb[:], in_=c_sb[:], func=AF.Silu), reads=[Bc], writes=[Bs])
    it = 0
    for l in range(nl):
        for nt in range(24):
            w, Bw = wt[it % 2]; p, Bp = pst[it % 2]
            src = io["mw"][l, :, nt * 512:(nt + 1) * 512].rearrange("(kc p) n -> p kc n", p=128)
            kb.dma("sp" if it % 2 == 0 else "act", lambda e, w=w, src=src: e.dma_start(out=w[:], in_=src), writes=[Bw])
            for cc in range(4):
                for kc in range(16):
                    kb.op("pe", lambda e, p=p, w=w, cc=cc, kc=kc: e.matmul(p[:, cc * 2:cc * 2 + 2], lhsT=w[:, kc, cc * 128:(cc + 1) * 128], rhs=s_sb[:, kc, :],
                                                                         start=(kc == 0), stop=(kc == 15)), reads=[Bw, Bs], writes=[Bp])
            g0 = nt * 4
            kb.op("dve", lambda e, p=p, l=l, g0=g0: e.tensor_tensor(out=mv[:, :, l, g0:g0 + 4], in0=p[:, 0:8].rearrange("p (c r) -> p r c", r=2),
                                                                   in1=mb_sb[:, l, g0:g0 + 4].unsqueeze(1).broadcast_to([128, 2, 4]), op=ALU.add),
                  reads=[Bp, Bmb], writes=[Bmv])
            it += 1
    kb.dma("sp", lambda e: e.dma_start(out=io["mscr"], in_=mv[:]), reads=[Bmv], writes=[Buf()])
    C.pop()


def emit_reverse(C, io, jobs):
    kb = C.kb
    C.push()

    def T(shape, dt=F32):
        return C.sb(shape, dt), Buf()
    cst, Bcst = T([128, 256])
    kb.dma("sp", lambda e: e.dma_start(out=cst[:, 0:128], in_=io["ident"]), writes=[Bcst])
    kb.dma("sp", lambda e: e.dma_start(out=cst[:, 128:256], in_=io["jmat"]), writes=[Bcst])
    ident = cst[:, 0:128]; J = cst[:, 128:256]
    X = [T([128, TALL]) for _ in range(2)]
    XR = [T([128, TALL]) for _ in range(2)]
    TK = [T([128, 512]) for _ in range(2)]
    psl = [(C.ps([128, 512]), Buf(excl=True)) for _ in range(4)]
    pc = [0]

    def nps():
        r = psl[pc[0] % 4]; pc[0] += 1
        return r
    for ji, (src, dst, n, off_c, off_l) in enumerate(jobs):
        x, Bx = X[ji % 2]; xr, Bxr = XR[ji % 2]
        kb.dma("sp" if ji % 2 == 0 else "act", lambda e, x=x, src=src, n=n: e.dma_start(out=x[0:n, :], in_=src), writes=[Bx])
        for b4 in range(0, NCH, 4):
            nb = min(4, NCH - b4)
            tk, Btk = TK[(b4 // 4) % 2]
            ps, Bps = nps()
            for bi in range(nb):
                b = b4 + bi
                kb.op("pe", lambda e, ps=ps, bi=bi, b=b, x=x, n=n: e.transpose(out=ps[:, bi * 128:bi * 128 + n], in_=x[0:n, b * 128:(b + 1) * 128], identity=ident[0:n, 0:n]),
                      reads=[Bx, Bcst], writes=[Bps])
            kb.op("act", lambda e, ps=ps, tk=tk, nb=nb, n=n: e.copy(out=tk[:, 0:nb * 128].rearrange("p (a b) -> p a b", a=nb)[:, :, 0:n],
                                                                   in_=ps[:, 0:nb * 128].rearrange("p (a b) -> p a b", a=nb)[:, :, 0:n]), reads=[Bps], writes=[Btk])
            ps2, Bps2 = nps()
            for bi in range(nb):
                kb.op("pe", lambda e, ps2=ps2, bi=bi, tk=tk, n=n: e.matmul(ps2[0:n, bi * 128:(bi + 1) * 128], lhsT=tk[:, bi * 128:bi * 128 + n], rhs=J, start=True, stop=True),
                      reads=[Btk, Bcst], writes=[Bps2])
            for bi in range(nb):
                b = b4 + bi
                mb_ = (1 - b) if b < 2 else (2 + (15 - (b - 2)))
                kb.op("dve", lambda e, ps2=ps2, bi=bi, mb_=mb_, xr=xr, n=n: e.tensor_copy(out=xr[0:n, mb_ * 128:(mb_ + 1) * 128], in_=ps2[0:n, bi * 128:(bi + 1) * 128]),
                      reads=[Bps2], writes=[Bxr])
        kb.dma("sp", lambda e, xr=xr, dst=dst, n=n, off_c=off_c: e.dma_start(out=dst[:, off_c:off_c + CTX], in_=xr[0:n, 0:CTX]), reads=[Bxr], writes=[Buf()])
        kb.dma("act", lambda e, xr=xr, dst=dst, n=n, off_l=off_l: e.dma_start(out=dst[:, off_l:off_l + SEQ], in_=xr[0:n, CTX:TALL]), reads=[Bxr], writes=[Buf()])
    C.pop()


def cast_jobs(jobs):
    out = []
    for (dst, src, rows, cols, rblk, cb) in jobs:
        for r0 in range(0, rows, rblk):
            n = min(rblk, rows - r0)
            out.append((dst, src, r0, n, cb))
    return out


def emit_cast_one(C, job):
    if job[0] == "dn":
        _, dst, src, nb = job
        C.kb.dma("pool", lambda e: e.dma_start(out=dst[nb], in_=src[:, nb * 128:(nb + 1) * 128].rearrange("(c p) n -> p c n", p=128)), writes=[Buf()], bg=True)
        return
    dst, src, r0, n, cb = job
    C.kb.dma("pool", lambda e: e.dma_start(out=dst[r0:r0 + n, :].rearrange("r (a b) -> r a b", b=cb),
                                           in_=src[r0:r0 + n, :].rearrange("r (a b) -> r a b", b=cb)), writes=[Buf()], bg=True)


def emit_cast(C, jobs):
    for j in cast_jobs(jobs):
        emit_cast_one(C, j)


_WQ = [0]


def _wdma():
    _WQ[0] += 1
    return "sp" if _WQ[0] % 2 == 0 else "act"


P3F_SEGS = [(0, 256, 0, False, False), (256, 410, 1, False, True), (666, 410, 1, True, True), (1076, 410, 1, True, True), (1486, 410, 1, True, True), (1896, 408, 1, True, False)]


def build_fused(nl=DEPTH, dbg=False):
    C = Ctx()
    C._stk = []
    kb = C.kb
    for eps in (NORM_EPS, 1e-12, RET_EPS, RWKV_GN_EPS):
        make_eps(C, eps)
    I = C.dram_in
    x0T = I("x0T", [D, TALL]); cT = I("cT", [128, 16, 2]); mw = I("mw", [nl, D, NMOD * D]); mb = I("mb", [128, nl, 96]); ng = I("ng", [128, nl, 4, 16])
    w_in = I("w_in", [nl, D, IN_W]); w_out = I("w_out", [nl, D, D]); w_up = I("w_up", [nl, D, 2 * D_FF]); w_cv = I("w_cv", [nl, 128, 2 * NFF, 3]); w_dn = I("w_dn", [nl, D_FF, D])
    r_mu = I("r_mu", [nl, 2, 64, 24, 2]); r_mu2 = I("r_mu2", [nl, 2, 128, 4, 2]); r_par = I("r_par", [nl, 2, 64, 5, 8])
    r_wup = I("r_wup", [nl, 2, 96, 512]); r_aup = I("r_aup", [nl, 2, 96, 512]); r_gup = I("r_gup", [nl, 256, 512])
    r_cst = I("r_cst", [128, 256]); r_mk = I("r_mk", [64, 2112]); r_ln = I("r_ln", [nl, 128, 4, 2]); c_w = I("c_w", [nl, 128, 4, 3]); c_bd = I("c_bd", [128, 128])
    a_nrm = I("a_nrm", [nl, 128, 5]); a_wq = I("a_wq", [nl, 128, 3, 4, 256]); a_wkv = I("a_wkv", [nl, 128, 2, 4, 256]); a_rope = I("a_rope", [2, 64, TALL]); a_cst = I("a_cst", [128, 384])
    d_rope = I("d_rope", [2, 128, TALL]); d_rd = I("d_rd", [nl, 128, 8]); d_gn = I("d_gn", [nl, 128, 4]); d_tab = I("d_tab", [128, 770]); d_cst = I("d_cst", [128, 256])
    p3_msk = I("p3_msk", [128, 12]); jmat = I("jmat", [128, 128]); identm = I("identm", [128, 128])
    xo = C.dram_out("xo", [D, SEQ])
    S = C.scratch
    xs = [S("xsA", [D, TALL]), S("xsB", [D, TALL])]
    pT = S("pT", [IN_W, TALL]); yT = S("yT", [D, TALL]); mscr = S("mscr", [128, 2, nl, 96])
    u_f = S("u_f", [1536, TP]); u2_f = S("u2_f", [512, TP]); u_r = S("u_r", [1536, TP]); u2_r = S("u2_r", [512, TP])
    y_f = S("y_f", [512, TALL]); y_r = S("y_r", [512, TALL]); y_b = S("y_b", [512, TALL])
    bon = S("bon", [512, TALL]); gate = S("gate", [512, TALL]); bon2 = S("bon2", [512, TALL]); gate2 = S("gate2", [512, TALL])
    wb_in = S("wb_in", [D, IN_W], BF16); wb_out = S("wb_out", [D, D], BF16); wb_up = S("wb_up", [D, 2 * D_FF], BF16); wb_dn = S("wb_dn", [16, 128, NFF, 128], BF16)
    emit_cast(C, [(wb_in, w_in[0], D, IN_W, 512, 896)])
    dbg_outs = {}
    if dbg:
        dbg_outs = {"d_pT": C.dram_out("d_pT", [IN_W, TALL]), "d_yT": C.dram_out("d_yT", [D, TALL]), "d_x1": C.dram_out("d_x1", [D, TALL]),
                    "d_m": C.dram_out("d_m", [128, 2, nl, 96])}
    C.push()
    Z = C.sb([128, TP]); Bz = Buf()
    kb.op("dve", lambda e: e.memset(Z[:], 0.0), writes=[Bz])
    qi = 0
    for (t_, nr) in [(u_f, 1536), (u2_f, 512), (u_r, 1536), (u2_r, 512)]:
        for r0 in range(0, nr, 128):
            kb.dma("sp" if qi % 2 == 0 else "act", lambda e, t_=t_, r0=r0: e.dma_start(out=t_[r0:r0 + 128, :], in_=Z[:]), reads=[Bz], writes=[Buf()])
            qi += 1
    C.pop()
    emit_p0f(C, {"cT": cT, "mw": mw, "mb": mb, "mscr": mscr}, nl)
    if dbg:
        kb.dma("sp", lambda e: e.dma_start(out=dbg_outs["d_m"], in_=mscr), writes=[Buf()], is_out=True)
    sw64 = [(0, 16, 16), (16, 32, 0), (32, 48, 48), (48, 64, 32)]
    sw128 = [(0, 32, 32), (32, 64, 0), (64, 96, 96), (96, 128, 64)]
    x_cur = x0T
    for l in range(nl):
        x_next = xs[l % 2]
        for (c0, classes) in [(0, [(0, 0, CTX), (1, CTX, T1)]), (T1, [(1, 0, T1)])]:
            emit_p1(C, {"xT": x_cur[:, c0:c0 + T1], "w": wb_in, "wdma": _wdma, "pT": pT[:, c0:c0 + T1],
                        "vec_srcs": [ng[:, l, 0, :], mscr[:, 1, l, 0:16], mscr[:, 1, l, 16:32], mscr[:, 0, l, 0:16], mscr[:, 0, l, 16:32]]}, classes)
        if dbg and l == 0:
            kb.dma("sp", lambda e: e.dma_start(out=dbg_outs["d_pT"], in_=pT), writes=[Buf()], is_out=True)
        for (r0, n, dst, d0) in [(0, 1536, u_f, 0), (1536, 96, u2_f, 0), (1632, 96, u2_f, 128), (1728, 256, u2_f, 256)]:
            kb.dma("sp", lambda e, r0=r0, n=n, dst=dst, d0=d0: e.dma_start(out=dst[d0:d0 + n, 1:1 + CTX], in_=pT[r0:r0 + n, 0:CTX]), writes=[Buf()])
            kb.dma("act", lambda e, r0=r0, n=n, dst=dst, d0=d0: e.dma_start(out=dst[d0:d0 + n, 259:259 + SEQ], in_=pT[r0:r0 + n, CTX:TALL]), writes=[Buf()])
        jobs = [(pT[128 * i:128 * i + 128, :], u_r[128 * i:128 * i + 128, :], 128, 1, 259) for i in range(12)]
        jobs += [(pT[1536:1632, :], u2_r[0:96, :], 96, 1, 259), (pT[1632:1728, :], u2_r[128:224, :], 96, 1, 259),
                 (pT[1728:1856, :], u2_r[256:384, :], 128, 1, 259), (pT[1856:1984, :], u2_r[384:512, :], 128, 1, 259)]
        emit_reverse(C, {"ident": identm, "jmat": jmat}, jobs)
        cj = [(wb_out, w_out[l], D, D, 1024, 1024), (wb_up, w_up[l], D, 2 * D_FF, 256, 1024), ]
        cjd = [("dn", wb_dn, w_dn[l], nb) for nb in range(16)]
        if l + 1 < nl:
            cj.append((wb_in, w_in[l + 1], D, IN_W, 512, 896))
        pending = cast_jobs(cj) + cjd

        def hook(pending=pending):
            if pending:
                emit_cast_one(C, pending.pop(0))
        for d, (u_, u2_, yo_, bo_, go_) in enumerate([(u_f, u2_f, y_f, bon, gate), (u_r, u2_r, y_r, bon2, gate2)]):
            emit_p2r(C, {"u": u_, "u2": u2_, "mu": r_mu[l, d], "mu2": r_mu2[l, d], "par": r_par[l, d], "wup": r_wup[l, d], "aup": r_aup[l, d],
                         "gup": r_gup[l], "cst": r_cst, "mk": r_mk, "yT": yo_, "bonT": bo_, "gateT": go_, "hook": hook, "aux": (d == 0)})
        while pending:
            hook()
        emit_reverse(C, {"ident": identm, "jmat": jmat}, [(y_r[128 * i:128 * i + 128, :], y_b[128 * i:128 * i + 128, :], 128, 0, CTX) for i in range(4)])
        dB = Buf()
        emit_p2c(C, {"cw": c_w[l], "ln": r_ln[l], "cst": c_bd, "cv": (lambda i, a: pT[2688 + 512 * i + 128 * a:2688 + 512 * i + 128 * a + 128, :]),
                     "yf": (lambda a: y_f[128 * a:128 * a + 128, :]), "yb": (lambda a: y_b[128 * a:128 * a + 128, :]),
                     "bon": (lambda a: bon[128 * a:128 * a + 128, :]), "gate": (lambda a: gate[128 * a:128 * a + 128, :]),
                     "yc": (lambda a: yT[1024 + 128 * a:1024 + 128 * a + 128, :]), "ya": (lambda a: yT[128 * a:128 * a + 128, :]),
                     "By": dB, "Byf": dB, "Byb": dB, "Bbg": dB})
        emit_p2b(C, {"nrm": a_nrm[l], "wq": a_wq[l], "wkv": a_wkv[l], "rope": a_rope, "cst": a_cst,
                     "cx": [pT[1984 + 128 * a:1984 + 128 * a + 128, :] for a in range(5)],
                     "kr": [(0, 64, 0, pT[2624:2688, :])] + [(a0, a1, 1, pT[2624 + s0:2624 + s0 + 16, :]) for (a0, a1, s0) in sw64],
                     "y": yT[512:1024, :], "By": dB})

        def qkv_src(h, a):
            base = 4224 + 128 * h
            if a < 4:
                return [(0, 128, pT[base + 512 * a:base + 512 * a + 128, :])]
            b2 = base + 512 * (a - 4)
            return [(a0, a1, pT[b2 + s0:b2 + s0 + 32, :]) for (a0, a1, s0) in sw128]
        emit_p2d(C, {"qkv": qkv_src, "rope": d_rope, "rd": d_rd[l], "gn": d_gn[l], "tab": d_tab, "cst": d_cst, "yT": yT[1536:2048, :]})
        if dbg and l == 0:
            kb.dma("sp", lambda e: e.dma_start(out=dbg_outs["d_yT"], in_=yT), writes=[Buf()], is_out=True)
        last = (l == DEPTH - 1) and (nl == DEPTH)
        segs = P3F_SEGS[1:] if last else P3F_SEGS
        msk_ap = p3_msk[:, 2:12] if last else p3_msk
        vs = []
        for ci_, r in ((0, 1), (1, 0)):
            for k in range(3):
                vs.append((ci_, k, ng[:, l, k + 1, :]))
            for k in range(4):
                vs.append((ci_, 3 + k, mscr[:, r, l, (2 + k) * 16:(3 + k) * 16]))
        emit_p3(C, {"yT": yT, "xT": x_cur, "vec_srcs": vs, "msk": msk_ap, "wo": wb_out, "wu": wb_up, "wc": w_cv[l], "wd": wb_dn, "wdma": _wdma, "xo": x_next,
                    "Bx": dB, "By": dB, "Bxo": dB}, segs)
        if dbg and l == 0:
            kb.dma("sp", lambda e, x_next=x_next: e.dma_start(out=dbg_outs["d_x1"], in_=x_next), writes=[Buf()], is_out=True)
        x_cur = x_next
    kb.barrier()
    kb.dma("sp", lambda e, x_cur=x_cur: e.dma_start(out=xo, in_=x_cur[:, CTX:TALL]), writes=[Buf()], is_out=True)
    return C.done()


_FUSED = {}


def _prep_inputs(b, nl, x, c, ctx, c_ctx, mod_w, mod_b, norm_g, w_in, rwkv_shift, rwkv_w0, rwkv_w_up, rwkv_a0, rwkv_a_up,
                 rwkv_g_up, rwkv_vecs, mla_q_norm, mla_kv_norm, mla_w_uq, mla_w_ukv, conv_w, ret_decay, ret_gn_g,
                 w_out, mlp_w_up, mlp_conv, mlp_w_down, shared):
    ca = np.ascontiguousarray
    im = dict(shared)
    im["x0T"] = ca(np.concatenate([ctx[b], x[b]], axis=0).T)
    cc = np.stack([c[b], c_ctx], axis=1)
    im["cT"] = ca(cc.reshape(16, 128, 2).transpose(1, 0, 2))
    return im


def _prep_shared(nl, mod_w, mod_b, norm_g, w_in, rwkv_shift, rwkv_w0, rwkv_w_up, rwkv_a0, rwkv_a_up,
                 rwkv_g_up, rwkv_vecs, mla_q_norm, mla_kv_norm, mla_w_uq, mla_w_ukv, conv_w, ret_decay, ret_gn_g,
                 w_out, mlp_w_up, mlp_conv, mlp_w_down):
    ca = np.ascontiguousarray
    sh = {}
    sh["mw"] = ca(mod_w[:nl]); sh["mb"] = ca(mod_b[:nl].reshape(nl, 96, 128).transpose(2, 0, 1))
    sh["ng"] = ca(norm_g[:nl].reshape(nl, 4, 16, 128).transpose(3, 0, 1, 2))
    sh["w_in"] = ca(w_in[:nl]); sh["w_out"] = ca(w_out[:nl]); sh["w_up"] = ca(mlp_w_up[:nl]); sh["w_dn"] = ca(mlp_w_down[:nl])
    sh["w_cv"] = ca(mlp_conv[:nl].transpose(0, 2, 1).reshape(nl, 2 * NFF, 128, 3).transpose(0, 2, 1, 3))
    r_mu = np.zeros((nl, 2, 64, 24, 2), np.float32); r_mu2 = np.zeros((nl, 2, 128, 4, 2), np.float32); r_par = np.zeros((nl, 2, 64, 5, 8), np.float32)
    for l in range(nl):
        for d in range(2):
            sh_ = rwkv_shift[l] if d == 0 else rwkv_shift[l][::-1]
            r_mu[l, d] = sh_[:, 0:1536].T.reshape(24, 64, 2).transpose(1, 0, 2)
            m2 = np.zeros((512, 2), np.float32)
            m2[0:96] = sh_[:, 1536:1632].T; m2[128:224] = sh_[:, 1632:1728].T; m2[256:512] = sh_[:, 1728:1984].T
            r_mu2[l, d] = m2.reshape(4, 128, 2).transpose(1, 0, 2)
            r_par[l, d] = np.stack([_tile8(rwkv_w0[l, d]), _tile8(rwkv_a0[l, d]), _tile8(rwkv_vecs[l, 0]), _tile8(rwkv_vecs[l, 1]), _tile8(rwkv_vecs[l, 2])], axis=1)
    sh["r_mu"] = r_mu; sh["r_mu2"] = r_mu2; sh["r_par"] = r_par
    sh["r_wup"] = ca(rwkv_w_up[:nl]); sh["r_aup"] = ca(rwkv_a_up[:nl]); sh["r_gup"] = ca(rwkv_g_up[:nl])
    cst, mk = _rw_consts()
    sh["r_cst"] = cst; sh["r_mk"] = mk
    sh["r_ln"] = ca(np.stack([rwkv_vecs[:nl, 3].reshape(nl, 4, 128), rwkv_vecs[:nl, 4].reshape(nl, 4, 128)], axis=-1).transpose(0, 2, 1, 3))
    sh["c_w"] = ca(conv_w[:nl].reshape(nl, 3, 4, 128).transpose(0, 3, 2, 1))
    sh["c_bd"] = (np.kron(np.eye(2, dtype=np.float32), np.ones((64, 64), np.float32)) / 64.0).astype(np.float32)
    sh["a_nrm"] = ca(np.concatenate([mla_q_norm[:nl].reshape(nl, 3, 128), mla_kv_norm[:nl].reshape(nl, 2, 128)], axis=1).transpose(0, 2, 1))
    sw = _rope_swap_idx(64)
    wq = np.zeros((nl, 384, 4, 256), np.float32); wkv = np.zeros((nl, 256, 4, 256), np.float32)
    for l in range(nl):
        for h in range(4):
            wh = mla_w_uq[l][:, 192 * h:192 * h + 192]
            wq[l, :, h, 0:192] = wh; wq[l, :, h, 192:256] = wh[:, 128:192][:, sw]
            wkv[l, :, h, :] = mla_w_ukv[l][:, 256 * h:256 * h + 256]
    sh["a_wq"] = ca(wq.reshape(nl, 3, 128, 4, 256).transpose(0, 2, 1, 3, 4)); sh["a_wkv"] = ca(wkv.reshape(nl, 2, 128, 4, 256).transpose(0, 2, 1, 3, 4))
    cos, sin = _rope_tables(64); sh["a_rope"] = ca(np.stack([cos, sin]))
    sh["a_cst"] = np.concatenate([np.eye(128, dtype=np.float32), np.full((128, 128), 1.0 / 384, np.float32), np.full((128, 128), 1.0 / 256, np.float32)], axis=1)
    cos, sin = _rope_tables(128); sh["d_rope"] = ca(np.stack([cos, sin]))
    sh["d_rd"] = ca(np.tile(ret_decay[:nl].reshape(nl, 1, 8), (1, 128, 1)))
    sh["d_gn"] = ca(ret_gn_g[:nl].reshape(nl, 4, 128).transpose(0, 2, 1))
    j = np.arange(128)[:, None]; i = np.arange(128)[None, :]
    tab = np.zeros((128, 770), np.float32)
    tab[:, 0:128] = np.maximum(i - j, 0); tab[:, 128:256] = (i >= j)
    tab[:, 256:384] = np.maximum(j - i, 0); tab[:, 384:512] = (j > i)
    tab[:, 512:640] = (i + 1); tab[:, 640:768] = (128 - i)
    tab[:, 768] = 127 - np.arange(128); tab[:, 769] = np.arange(128)
    sh["d_tab"] = tab
    sh["d_cst"] = np.concatenate([np.eye(128, dtype=np.float32), np.full((128, 128), 1.0 / 128, np.float32)], axis=1)
    mk_ = []
    for (_, _, _, hl, hr) in P3F_SEGS:
        mk_ += [float(hl), float(hr)]
    sh["p3_msk"] = ca(np.tile(np.array(mk_, np.float32)[None], (128, 1)))
    sh["jmat"] = ca(np.eye(128, dtype=np.float32)[::-1]); sh["identm"] = np.eye(128, dtype=np.float32)
    return sh


def run_fused(inputs, nl=DEPTH, dbg=False):
    key = (nl, dbg)
    if key not in _FUSED:
        _FUSED[key] = build_fused(nl, dbg)
    f = lambda a: np.ascontiguousarray(np.asarray(a, dtype=np.float32))
    inp = {k: f(v) for k, v in inputs.items()}
    names = ["mod_w", "mod_b", "norm_g", "w_in", "rwkv_shift", "rwkv_w0", "rwkv_w_up", "rwkv_a0", "rwkv_a_up", "rwkv_g_up", "rwkv_vecs", "mla_q_norm",
             "mla_kv_norm", "mla_w_uq", "mla_w_ukv", "conv_w", "ret_decay", "ret_gn_g", "w_out", "mlp_w_up", "mlp_conv", "mlp_w_down"]
    shared = _prep_shared(nl, *[inp[k] for k in names])
    in_maps = []
    for core in range(NCORES):
        b = core % BATCH
        in_maps.append(_prep_inputs(b, nl, inp["x"], inp["c"], inp["ctx"], inp["c_ctx"], *[None] * 22, shared))
    res = run_bass_kernel_spmd(_FUSED[key], in_maps, core_ids=list(range(NCORES)))
    return res


def kernel(**inputs):
    res = run_fused(inputs)
    out = np.stack([res.results[b]["xo"].T for b in range(BATCH)], axis=0)
    return np.ascontiguousarray(out).astype(np.float32)
```

```python
import math
from contextlib import ExitStack
import numpy as np
import concourse.bass as bass
import concourse.mybir as mybir
from concourse.bass_utils import run_bass_kernel_spmd

F32 = mybir.dt.float32
BF16 = mybir.dt.bfloat16
AF = mybir.ActivationFunctionType
ALU = mybir.AluOpType
AX = mybir.AxisListType

N_DMA_SEMS = 12
N_BG_SEMS = 12
NCORES = 8

D = 2048
DEPTH = 4
BATCH = 4
SEQ = 2048
CTX = 256
TALL = SEQ + CTX
IN_W = 6272
D_FF = 5632
NMOD = 6
NORM_EPS = 1e-6


class Buf:
    __slots__ = ("name", "w", "r", "excl")

    def __init__(self, name="", excl=False):
        self.name = name
        self.w = {}
        self.r = {}
        self.excl = excl


class KB:
    def __init__(self, nc, stack):
        self.nc = nc
        self.engs = ["pe", "act", "dve", "pool", "sp"]
        self.ops = {e: [] for e in self.engs}
        self.cnt = {e: 0 for e in self.engs}
        self.waited = {e: {} for e in self.engs}
        self.sems = {}
        for e in self.engs:
            self.sems[e] = stack.enter_context(nc.semaphore("s_" + e))
        for i in range(N_DMA_SEMS):
            self.sems["d%d" % i] = stack.enter_context(nc.semaphore("s_d%d" % i))
        for i in range(N_BG_SEMS):
            self.sems["c%d" % i] = stack.enter_context(nc.semaphore("s_c%d" % i))
        self.dcnt = [0] * N_DMA_SEMS
        self.dnext = 0
        self.ccnt = [0] * N_BG_SEMS
        self.cnext = 0
        self.out_toks = []

    def _deps(self, reads, writes):
        deps = {}

        def add(dd):
            for sk, v in dd.items():
                if deps.get(sk, 0) < v:
                    deps[sk] = v
        for b in reads:
            add(b.w)
            if b.excl:
                add(b.r)
        for b in writes:
            add(b.w)
            add(b.r)
        return deps

    def _emit_waits(self, eng, deps, skip_self=False):
        for sk, v in deps.items():
            if skip_self and sk == eng:
                continue
            if self.waited[eng].get(sk, 0) >= v:
                continue
            self.waited[eng][sk] = v
            self.ops[eng].append(("w", self.sems[sk], v))

    @staticmethod
    def _mark(tok, reads, writes):
        sk, v = tok
        for b in reads:
            if b.r.get(sk, 0) < v:
                b.r[sk] = v
        for b in writes:
            if b.w.get(sk, 0) < v:
                b.w[sk] = v

    def op(self, eng, fn, reads=(), writes=()):
        deps = self._deps(reads, writes)
        self._emit_waits(eng, deps, skip_self=(eng == "pe"))
        self.cnt[eng] += 1
        tok = (eng, self.cnt[eng])
        self.ops[eng].append(("i", fn, self.sems[eng], 1))
        self._mark(tok, reads, writes)
        return tok

    def dma(self, eng, fn, reads=(), writes=(), is_out=False, bg=False):
        deps = self._deps(reads, writes)
        if bg:
            i = self.cnext
            self.cnext = (self.cnext + 1) % N_BG_SEMS
            sk = "c%d" % i
            cnts = self.ccnt
        else:
            i = self.dnext
            self.dnext = (self.dnext + 1) % N_DMA_SEMS
            sk = "d%d" % i
            cnts = self.dcnt
        if cnts[i] > 0:
            v = 16 * cnts[i]
            if deps.get(sk, 0) < v:
                deps[sk] = v
        self._emit_waits(eng, deps)
        cnts[i] += 1
        tok = (sk, 16 * cnts[i])
        self.ops[eng].append(("i", fn, self.sems[sk], 16))
        self._mark(tok, reads, writes)
        if is_out:
            self.out_toks.append(tok)
        return tok

    def barrier(self):
        deps = {e: self.cnt[e] for e in self.engs if self.cnt[e] > 0}
        for i in range(N_DMA_SEMS):
            if self.dcnt[i] > 0:
                deps["d%d" % i] = 16 * self.dcnt[i]
        for i in range(N_BG_SEMS):
            if self.ccnt[i] > 0:
                deps["c%d" % i] = 16 * self.ccnt[i]
        for e in self.engs:
            self._emit_waits(e, dict(deps))

    def finish(self, block):
        deps = {}
        for sk, v in self.out_toks:
            if deps.get(sk, 0) < v:
                deps[sk] = v
        self._emit_waits("sp", deps)
        m = {"pe": block.tensor, "act": block.scalar, "dve": block.vector,
             "pool": block.gpsimd, "sp": block.sync}

        def mk(e):
            lst = self.ops[e]

            def body(engine):
                for it in lst:
                    if it[0] == "w":
                        engine.wait_ge(it[1], it[2])
                    else:
                        it[1](engine).then_inc(it[2], it[3])
            return body

        for e in self.engs:
            if self.ops[e]:
                m[e](mk(e))


class Ctx:
    def __init__(self, name="k"):
        self.nc = bass.Bass("TRN2", target_bir_lowering=False)
        self.st = ExitStack()
        self.kb = KB(self.nc, self.st)
        self.n = 0

    def dram_in(self, name, shape, dt=F32):
        return self.nc.dram_tensor(name, list(shape), dt, kind="ExternalInput").ap()

    def dram_out(self, name, shape, dt=F32):
        return self.nc.dram_tensor(name, list(shape), dt, kind="ExternalOutput").ap()

    def sb(self, shape, dt=F32, name=None):
        self.n += 1
        return self.st.enter_context(self.nc.sbuf_tensor(name or ("t%d" % self.n), list(shape), dt))

    def ps(self, shape, dt=F32, name=None):
        self.n += 1
        return self.st.enter_context(self.nc.psum_tensor(name or ("p%d" % self.n), list(shape), dt))

    def done(self):
        block = self.st.enter_context(self.nc.Block())
        self.kb.finish(block)
        self.st.close()
        return self.nc


def token_tiles(T, mx=512):
    out = []
    s = 0
    while s < T:
        n = min(mx, T - s)
        out.append((s, n))
        s += n
    return out


T1 = 1152
TP = TALL + 4
RW_BLK = 128
RW_SCALE = math.exp(-0.5)
NCH = TALL // 128
RET_EPS = 1e-5
MLA_SCALE = 192.0 ** -0.5
RWKV_GN_EPS = 64e-5
NFF = D_FF // 128

def emit_rstd(C, x_sb, Bx, nk, T, ones_sb, Bones, rstd_sb, Brstd, sq_tiles, Bsq, ps_tiles, Bps, eps, ranges=None):
    kb = C.kb
    k = 0
    for (t0, tn) in token_tiles(T):
        pj = (t0 // 512) % len(ps_tiles)
        p = ps_tiles[pj]
        for kc in range(nk):
            j = k % len(sq_tiles); k += 1
            sq = sq_tiles[j]
            kb.op("act", lambda e, sq=sq, kc=kc, t0=t0, tn=tn: e.activation(out=sq[:, 0:tn], in_=x_sb[:, kc, t0:t0 + tn],
                                                                              func=AF.Square),
                  reads=[Bx], writes=[Bsq[j]])
            kb.op("pe", lambda e, p=p, sq=sq, kc=kc, tn=tn: e.matmul(p[:, 0:tn], lhsT=ones_sb[:], rhs=sq[:, 0:tn],
                                                                      start=(kc == 0), stop=(kc == nk - 1)),
                  reads=[Bones, Bsq[j]], writes=[Bps[pj]])
        kb.op("act", lambda e, p=p, t0=t0, tn=tn: e.activation(out=rstd_sb[:, t0:t0 + tn], in_=p[:, 0:tn],
                                                                func=AF.Sqrt, bias=eps_ap(C, eps), scale=1.0),
              reads=[Bps[pj]], writes=[Brstd])
        kb.op("dve", lambda e, t0=t0, tn=tn: e.reciprocal(out=rstd_sb[:, t0:t0 + tn], in_=rstd_sb[:, t0:t0 + tn]),
              reads=[Brstd], writes=[Brstd])


_EPS = {}


def eps_ap(C, eps):
    return _EPS[(id(C), eps)][:, 0:1]


def make_eps(C, eps):
    t = C.sb([128, 1])
    b = Buf()
    C.kb.op("dve", lambda e: e.memset(t[:], eps), writes=[b])
    C.kb.barrier()
    _EPS[(id(C), eps)] = t
    return t


def vec_layout(v):
    n = v.shape[0]
    return np.ascontiguousarray(v.reshape(n, 16, 128).transpose(2, 0, 1))


def _rw_consts():
    ident = np.eye(128, dtype=np.float32)
    cst = np.concatenate([ident, np.ones((128, 128), np.float32)], axis=1)
    s = np.arange(64)[:, None]; t = np.arange(64)[None, :]
    us = (s < t).astype(np.float32); ui = (s <= t).astype(np.float32)
    unit = np.concatenate([us, ui, us, ui], axis=1)
    maskA = np.concatenate([unit, unit], axis=1)
    ls = (t < s).astype(np.float32)
    maskL = np.tile(ls, (1, 8))
    idu = np.tile(np.eye(64, dtype=np.float32), (1, 16))
    mk = np.concatenate([maskA, maskL, idu, np.ones((64, 64), np.float32)], axis=1)
    return cst, np.ascontiguousarray(mk)


def _tile8(v):
    return np.ascontiguousarray(v.reshape(8, 64).T)


def _rope_swap_idx(d):
    q = d // 4
    idx = np.arange(d)
    out = np.empty(d, np.int64)
    for base in (0, d // 2):
        out[base:base + q] = idx[base + q:base + 2 * q]
        out[base + q:base + 2 * q] = idx[base:base + q]
    return out


def _rope_tables(d):
    half = d // 2
    inv = 10000.0 ** (-np.arange(0, half, 2, dtype=np.float32) / half)
    rows = SEQ // 64
    row = np.repeat(np.arange(rows, dtype=np.float32), 64)
    col = np.tile(np.arange(64, dtype=np.float32), rows)
    ar = (row[:, None] * inv[None, :]).astype(np.float32)
    ac = (col[:, None] * inv[None, :]).astype(np.float32)
    cos = np.ones((d, TALL), np.float32); sin = np.zeros((d, TALL), np.float32)
    q = d // 4
    for base, ang in ((0, ar), (half, ac)):
        c = np.cos(ang).T.astype(np.float32); s = np.sin(ang).T.astype(np.float32)
        cos[base:base + q, CTX:] = c; cos[base + q:base + 2 * q, CTX:] = c
        sin[base:base + q, CTX:] = -s; sin[base + q:base + 2 * q, CTX:] = s
    return cos, sin


def emit_p1(C, io, classes):
    kb = C.kb
    C.push()
    xT = io["xT"]
    w = io["w"]
    pT = io["pT"]
    x_sb = C.sb([128, 16, T1]); Bx = Buf()
    h_sb = C.sb([128, 16, T1], BF16); Bh = Buf()
    v_sb = C.sb([128, 5, 16]); Bv = Buf()
    gp = C.sb([128, 2, 16]); Bgp = Buf()
    ones = C.sb([128, 128]); Bones = Buf()
    rstd = C.sb([128, T1]); Brstd = Buf()
    sq = [C.sb([128, 512]) for _ in range(2)]; Bsq = [Buf(), Buf()]
    tmp = [C.sb([128, T1]) for _ in range(2)]; Btmp = [Buf(), Buf()]
    wt = [C.sb([128, 16, 512], BF16) for _ in range(2)]; Bw = [Buf(), Buf()]
    ot = [C.sb([128, T1]) for _ in range(2)]; Bo = [Buf(), Buf()]
    pss = [C.ps([128, 512]) for _ in range(2)]; Bpss = [Buf(), Buf()]
    psm = [C.ps([128, 512]) for _ in range(4)]; Bpsm = [Buf() for _ in range(4)]

    xv = xT.rearrange("(kc p) t -> p kc t", p=128)
    for kc in range(16):
        q = "sp" if kc % 2 == 0 else "act"
        kb.dma(q, lambda e, kc=kc: e.dma_start(out=x_sb[:, kc, :], in_=xv[:, kc, :]), writes=[Bx])
    for vi, vsrc in enumerate(io["vec_srcs"]):
        kb.dma("sp", lambda e, vi=vi, vsrc=vsrc: e.dma_start(out=v_sb[:, vi, :], in_=vsrc), writes=[Bv])
    kb.op("dve", lambda e: e.memset(ones[:], 1.0 / D), writes=[Bones])
    for ci in range(2):
        kb.op("dve", lambda e, ci=ci: e.scalar_tensor_tensor(out=gp[:, ci, :], in0=v_sb[:, 2 + 2 * ci, :], scalar=1.0,
                                                             in1=v_sb[:, 0, :], op0=ALU.add, op1=ALU.mult),
              reads=[Bv], writes=[Bgp])
    emit_rstd(C, x_sb, Bx, 16, T1, ones, Bones, rstd, Brstd, sq, Bsq, pss, Bpss, NORM_EPS)
    for kc in range(16):
        j = kc % 2
        t = tmp[j]
        kb.op("dve", lambda e, t=t, kc=kc: e.tensor_tensor(out=t[:], in0=x_sb[:, kc, :], in1=rstd[:], op=ALU.mult),
              reads=[Bx, Brstd], writes=[Btmp[j]])
        for (ci, a, b) in classes:
            kb.op("act", lambda e, t=t, kc=kc, ci=ci, a=a, b=b: e.activation(
                out=h_sb[:, kc, a:b], in_=t[:, a:b], func=AF.Identity,
                bias=v_sb[:, 1 + 2 * ci, kc:kc + 1], scale=gp[:, ci, kc:kc + 1]),
                reads=[Btmp[j], Bgp, Bv], writes=[Bh])
    nblk = (IN_W + 511) // 512
    oi = 0
    pi = 0
    for nb in range(nblk):
        n0 = nb * 512
        nw = min(512, IN_W - n0)
        j = nb % 2
        wtile = wt[j]
        src = w[:, n0:n0 + nw].rearrange("(kc p) n -> p kc n", p=128)
        kb.dma(io["wdma"](), lambda e, wtile=wtile, src=src, nw=nw: e.dma_start(out=wtile[:, :, 0:nw], in_=src),
               writes=[Bw[j]])
        for nc_ in range(nw // 128):
            oj = oi % 2; oi += 1
            o = ot[oj]
            for (t0, tn) in token_tiles(T1):
                pj = pi % 4; pi += 1
                p = psm[pj]
                for kc in range(16):
                    kb.op("pe", lambda e, p=p, wtile=wtile, kc=kc, nc_=nc_, t0=t0, tn=tn: e.matmul(
                        p[:, 0:tn], lhsT=wtile[:, kc, nc_ * 128:(nc_ + 1) * 128], rhs=h_sb[:, kc, t0:t0 + tn],
                        start=(kc == 0), stop=(kc == 15)),
                        reads=[Bw[j], Bh], writes=[Bpsm[pj]])
                ev = "act" if pi % 2 == 0 else "dve"
                if ev == "act":
                    kb.op("act", lambda e, p=p, o=o, t0=t0, tn=tn: e.copy(out=o[:, t0:t0 + tn], in_=p[:, 0:tn]),
                          reads=[Bpsm[pj]], writes=[Bo[oj]])
                else:
                    kb.op("dve", lambda e, p=p, o=o, t0=t0, tn=tn: e.tensor_copy(out=o[:, t0:t0 + tn], in_=p[:, 0:tn]),
                          reads=[Bpsm[pj]], writes=[Bo[oj]])
            r0 = n0 + nc_ * 128
            kb.dma("sp", lambda e, o=o, r0=r0: e.dma_start(out=pT[r0:r0 + 128, :], in_=o[:]),
                   reads=[Bo[oj]], writes=[Buf()], is_out=True)
    C.pop()


def emit_p2r(C, io):
    kb = C.kb
    C.push()
    u_in = io["u"]
    u2_in = io["u2"]
    mu_in = io["mu"]
    mu2_in = io["mu2"]
    par_in = io["par"]
    wup_in = io["wup"]
    aup_in = io["aup"]
    gup_in = io["gup"]
    cst_in = io["cst"]
    mk_in = io["mk"]
    y_out = io["yT"]
    bon_out = io["bonT"]
    gate_out = io["gateT"]

    def T(shape, dt=F32):
        return C.sb(shape, dt), Buf()

    mu, Bmu = T([64, 24, 2]); c0, Bc0 = T([64, 24, 1])
    mu2, Bmu2 = T([128, 4, 2]); c02, Bc02 = T([128, 4, 1])
    par, Bpar = T([64, 5, 8]); omk, Bomk = T([64, 8])
    wup, Bwup = T([96, 512]); aup, Baup = T([96, 512]); gup, Bgup = T([128, 2, 512])
    cst, Bcst = T([128, 256]); mk, Bmk = T([64, 2 * 256 + 8 * 64 + 16 * 64 + 64])
    ident = cst[0:64, 0:64]; ONE64 = cst[0:64, 128:192]
    maskA = mk[:, 0:512].rearrange("p (a b) -> p a b", a=2)
    maskL = mk[:, 512:1024].rearrange("p (a b) -> p a b", a=8)
    identU = mk[:, 1024:2048].rearrange("p (a b) -> p a b", a=16)
    ONES = mk[:, 2048:2112]; BONES = Bmk
    kb.dma("sp", lambda e: e.dma_start(out=mu[:], in_=mu_in), writes=[Bmu])
    kb.dma("sp", lambda e: e.dma_start(out=mu2[:], in_=mu2_in), writes=[Bmu2])
    kb.dma("sp", lambda e: e.dma_start(out=par[:], in_=par_in), writes=[Bpar])
    kb.dma("sp", lambda e: e.dma_start(out=wup[:], in_=wup_in), writes=[Bwup])
    kb.dma("sp", lambda e: e.dma_start(out=aup[:], in_=aup_in), writes=[Baup])
    kb.dma("sp", lambda e: e.dma_start(out=gup[:], in_=gup_in.rearrange("(kc p) n -> p kc n", p=128)), writes=[Bgup])
    kb.dma("sp", lambda e: e.dma_start(out=cst[:], in_=cst_in), writes=[Bcst])
    kb.dma("sp", lambda e: e.dma_start(out=mk[:], in_=mk_in), writes=[Bmk])
    for (m_, Bm_, c_, Bc_) in [(mu, Bmu, c0, Bc0), (mu2, Bmu2, c02, Bc02)]:
        kb.op("dve", lambda e, m_=m_, c_=c_: e.tensor_tensor(out=c_[:], in0=m_[:, :, 0:1], in1=m_[:, :, 1:2], op=ALU.add), reads=[Bm_], writes=[Bc_])
        kb.op("dve", lambda e, c_=c_: e.tensor_scalar(out=c_[:], in0=c_[:], scalar1=-1.0, scalar2=1.0, op0=ALU.mult, op1=ALU.add),
              reads=[Bc_], writes=[Bc_])
    kb.op("dve", lambda e: e.tensor_scalar(out=omk[:], in0=par[:, 3, :], scalar1=-1.0, scalar2=1.0, op0=ALU.mult, op1=ALU.add),
          reads=[Bpar], writes=[Bomk])

    NB = RW_BLK
    NH = 8
    U, BU = T([64, 24, NB + 2]); U2, BU2 = T([128, 4, NB + 2])
    P, BP = T([64, 24, NB])
    P2, BP2 = T([128, 4, NB]); TMPP2, BTMPP2 = T([128, 4, NB])
    TWD, BTWD = T([128, NB]); SG, BSG = T([128, 2, NB])
    LW, BLW = T([64, NH, NB]); A, BA = T([64, NH, NB])
    KK, BKK = T([64, NH, NB]); SQ, BSQ = T([64, NH, NB]); RN, BRN = T([64, NH, NB])
    KD, BKD = T([64, NH, NB]); BS, BBS = T([64, NH, NB]); TA, BTA = RN, BRN
    LREL, BLREL = T([64, NH, NB]); EPOS, BEPOS = T([64, NH, NB]); ENEG, BENEG = T([64, NH, NB])
    EPREV, BEPREV = T([64, NH, NB]); EBAR, BEBAR = T([64, NH, NB]); PC, BPC = T([64, NH, 2])
    GATE, BGATE = EPOS, BEPOS; BON, BBON = ENEG, BENEG
    R32 = mybir.dt.float32r
    AR, BAR = T([64, NH, 2, 128], R32); BH, BBH = T([64, NH, 2, 64], R32); KH, BKH = T([64, NH, 2, 64], R32)
    BB, BBB = T([64, NH, 2, 64]); KBr, BKBr = T([64, NH, 2, 64])
    TM, BTM = T([64, 16, 5, 64], R32); GS, BGS = T([64, 16, 256], R32)
    MM = [T([64, 16, 64], R32) for _ in range(2)]
    WT = [T([64, 16, 128], R32) for _ in range(2)]
    XF, BXF = T([64, 16, 64], R32); UA, BUA = T([64, 16, 128]); APT, BAPT = T([64, 16, 64], R32)
    UU, BUU = T([64, 8, 64], R32); YO, BYO = T([64, NH, NB])
    TMPP = UA[:].rearrange("p a b -> p (a b)")[:, 0:12 * NB].rearrange("p (a b) -> p a b", a=12); BTMPP = BUA
    ST = [T([64, NH, 64], R32) for _ in range(2)]
    kb.op("dve", lambda e: e.tensor_scalar(out=ST[0][0][:], in0=identU[:, 0:8, :], scalar1=0.0, scalar2=None, op0=ALU.mult), reads=[Bmk], writes=[ST[0][1]])
    kb.op("dve", lambda e: e.tensor_scalar(out=ST[1][0][:], in0=identU[:, 0:8, :], scalar1=0.0, scalar2=None, op0=ALU.mult), reads=[Bmk], writes=[ST[1][1]])
    psl = [(C.ps([128, 512]), Buf(excl=True)) for _ in range(8)]
    pcnt = [0]

    def nps():
        r = psl[pcnt[0] % 8]
        pcnt[0] += 1
        return r

    def bc(ap, shape):
        return ap.broadcast_to(shape)

    def v3(ps, a, n=None):
        n = n or 512
        return ps[0:64, 0:n].rearrange("p (a b) -> p a b", a=a)

    yv = y_out.rearrange("(h p) t -> p h t", p=64)
    bv = bon_out.rearrange("(h p) t -> p h t", p=64)
    gv = gate_out.rearrange("(h p) t -> p h t", p=64)
    uv = u_in.rearrange("(i p) t -> p i t", p=64)
    u2v = u2_in.rearrange("(i p) t -> p i t", p=128)

    blocks = [(1 + i * NB, i * NB) for i in range(CTX // NB)] + [(259 + i * NB, CTX + i * NB) for i in range(SEQ // NB)]
    gc = 0
    for (pc0, t0) in blocks:
        if io.get("hook") is not None:
            io["hook"]()
        for i in range(24):
            q = "sp" if i % 2 == 0 else "act"
            kb.dma(q, lambda e, i=i, pc0=pc0: e.dma_start(out=U[:, i, :], in_=uv[:, i, pc0 - 1:pc0 + NB + 1]), writes=[BU])
        for i in range(4):
            kb.dma("sp", lambda e, i=i, pc0=pc0: e.dma_start(out=U2[:, i, :], in_=u2v[:, i, pc0 - 1:pc0 + NB + 1]), writes=[BU2])
        for (U_, BU_, P_, BP_, TP_, BTP_, m_, Bm_, c_, Bc_, S3) in [
                (U[:, 0:12, :], BU, P[:, 0:12, :], BP, TMPP, BTMPP, mu[:, 0:12, :], Bmu, c0[:, 0:12, :], Bc0, [64, 12, NB]),
                (U[:, 12:24, :], BU, P[:, 12:24, :], BP, TMPP, BTMPP, mu[:, 12:24, :], Bmu, c0[:, 12:24, :], Bc0, [64, 12, NB]),
                (U2, BU2, P2, BP2, TMPP2, BTMPP2, mu2, Bmu2, c02, Bc02, [128, 4, NB])]:
            kb.op("dve", lambda e, U_=U_, P_=P_, c_=c_, S3=S3: e.tensor_tensor(out=P_[:], in0=U_[:, :, 1:NB + 1], in1=bc(c_[:], S3), op=ALU.mult),
                  reads=[BU_, Bc_], writes=[BP_])
            kb.op("pool", lambda e, U_=U_, TP_=TP_, m_=m_, S3=S3: e.tensor_tensor(out=TP_[:], in0=U_[:, :, 0:NB], in1=bc(m_[:, :, 0:1], S3), op=ALU.mult),
                  reads=[BU_, Bm_], writes=[BTP_])
            kb.op("dve", lambda e, P_=P_, TP_=TP_: e.tensor_tensor(out=P_[:], in0=P_[:], in1=TP_[:], op=ALU.add), reads=[BP_, BTP_], writes=[BP_])
            kb.op("pool", lambda e, U_=U_, TP_=TP_, m_=m_, S3=S3: e.tensor_tensor(out=TP_[:], in0=U_[:, :, 2:NB + 2], in1=bc(m_[:, :, 1:2], S3), op=ALU.mult),
                  reads=[BU_, Bm_], writes=[BTP_])
            kb.op("dve", lambda e, P_=P_, TP_=TP_: e.tensor_tensor(out=P_[:], in0=P_[:], in1=TP_[:], op=ALU.add), reads=[BP_, BTP_], writes=[BP_])
        r_ = P[:, 0:8, :]; k_ = P[:, 8:16, :]; v_ = P[:, 16:24, :]
        S4 = [64, NH, NB]
        for (src_i, upw, Bup, pidx, dst, Bdst, func) in [(0, wup, Bwup, 0, LW, BLW, AF.Tanh), (1, aup, Baup, 1, A, BA, AF.Identity)]:
            kb.op("act", lambda e, src_i=src_i, func=func: e.activation(out=TWD[:], in_=P2[:, src_i, :], func=func),
                  reads=[BP2], writes=[BTWD])
            for hh in range(2):
                ps, Bps = nps()
                for hi in range(4):
                    h = hh * 4 + hi
                    kb.op("pe", lambda e, ps=ps, upw=upw, h=h, hi=hi: e.matmul(ps[0:64, hi * NB:(hi + 1) * NB], lhsT=upw[0:96, h * 64:(h + 1) * 64],
                                                                         rhs=TWD[0:96, :], start=True, stop=True),
                          reads=[Bup, BTWD], writes=[Bps])
                for hi in range(4):
                    h = hh * 4 + hi
                    kb.op("act", lambda e, ps=ps, h=h, hi=hi, dst=dst, pidx=pidx: e.activation(
                        out=dst[:, h, :], in_=ps[0:64, hi * NB:(hi + 1) * NB], func=AF.Sigmoid, bias=par[:, pidx, h:h + 1], scale=1.0),
                        reads=[Bps, Bpar], writes=[Bdst])
        kb.op("dve", lambda e: e.tensor_scalar(out=LW[:], in0=LW[:], scalar1=-RW_SCALE, scalar2=None, op0=ALU.mult),
              reads=[BLW], writes=[BLW])
        AUX = io.get("aux", True)
        if AUX:
            kb.op("act", lambda e: e.activation(out=SG[:], in_=P2[:, 2:4, :], func=AF.Sigmoid), reads=[BP2], writes=[BSG])
            for hh in range(2):
                ps, Bps = nps()
                for hi in range(4):
                    h = hh * 4 + hi
                    for kc in range(2):
                        kb.op("pe", lambda e, ps=ps, h=h, hi=hi, kc=kc: e.matmul(ps[0:64, hi * NB:(hi + 1) * NB], lhsT=gup[:, kc, h * 64:(h + 1) * 64],
                                                                           rhs=SG[:, kc, :], start=(kc == 0), stop=(kc == 1)),
                              reads=[Bgup, BSG], writes=[Bps])
                kb.op("act", lambda e, ps=ps, hh=hh: e.copy(out=GATE[:, hh * 4:hh * 4 + 4, :], in_=v3(ps, 4)), reads=[Bps], writes=[BGATE])
            kb.dma("sp", lambda e, t0=t0: e.dma_start(out=gv[:, :, t0:t0 + NB], in_=GATE[:]), reads=[BGATE], writes=[Buf()], is_out=True)

        def headsum(src, Bsrc, fn_evac):
            for hh in range(2):
                ps, Bps = nps()
                for hi in range(4):
                    h = hh * 4 + hi
                    kb.op("pe", lambda e, ps=ps, h=h, hi=hi: e.matmul(ps[0:64, hi * NB:(hi + 1) * NB], lhsT=ONE64, rhs=src[:, h, :], start=True, stop=True),
                          reads=[Bcst, Bsrc], writes=[Bps])
                fn_evac(ps, Bps, hh)
        kb.op("dve", lambda e: e.tensor_tensor(out=KK[:], in0=k_, in1=bc(par[:, 2, :].unsqueeze(2), S4), op=ALU.mult),
              reads=[BP, Bpar], writes=[BKK])
        kb.op("pool", lambda e: e.tensor_tensor(out=SQ[:], in0=KK[:], in1=KK[:], op=ALU.mult), reads=[BKK], writes=[BSQ])
        headsum(SQ, BSQ, lambda ps, Bps, hh: kb.op("act", lambda e: e.activation(out=RN[:, hh * 4:hh * 4 + 4, :], in_=v3(ps, 4), func=AF.Sqrt,
                                                                                    bias=eps_ap(C, 1e-12)[0:64, :], scale=1.0), reads=[Bps], writes=[BRN]))
        kb.op("dve", lambda e: e.reciprocal(out=RN[:], in_=RN[:]), reads=[BRN], writes=[BRN])
        kb.op("dve", lambda e: e.tensor_tensor(out=KK[:], in0=KK[:], in1=RN[:], op=ALU.mult), reads=[BKK, BRN], writes=[BKK])
        if AUX:
            kb.op("pool", lambda e: e.tensor_tensor(out=SQ[:], in0=r_, in1=k_, op=ALU.mult), reads=[BP], writes=[BSQ])
            kb.op("pool", lambda e: e.tensor_tensor(out=SQ[:], in0=SQ[:], in1=bc(par[:, 4, :].unsqueeze(2), S4), op=ALU.mult),
                  reads=[BSQ, Bpar], writes=[BSQ])
            headsum(SQ, BSQ, lambda ps, Bps, hh: kb.op("dve", lambda e: e.tensor_tensor(out=BON[:, hh * 4:hh * 4 + 4, :], in0=v3(ps, 4),
                                                                                         in1=v_[:, hh * 4:hh * 4 + 4, :], op=ALU.mult),
                                                       reads=[Bps, BP], writes=[BBON]))
            kb.dma("sp", lambda e, t0=t0: e.dma_start(out=bv[:, :, t0:t0 + NB], in_=BON[:]), reads=[BBON], writes=[Buf()], is_out=True)
        kb.op("dve", lambda e: e.tensor_tensor(out=TA[:], in0=A[:], in1=bc(par[:, 3, :].unsqueeze(2), S4), op=ALU.mult),
              reads=[BA, Bpar], writes=[BTA])
        kb.op("dve", lambda e: e.tensor_tensor(out=TA[:], in0=TA[:], in1=bc(omk[:].unsqueeze(2), S4), op=ALU.add),
              reads=[BTA, Bomk], writes=[BTA])
        kb.op("dve", lambda e: e.tensor_tensor(out=KD[:], in0=TA[:], in1=k_, op=ALU.mult), reads=[BTA, BP], writes=[BKD])
        kb.op("pool", lambda e: e.tensor_tensor(out=BS[:], in0=KK[:], in1=A[:], op=ALU.mult), reads=[BKK, BA], writes=[BBS])
        for h in range(NH):
            for ch in range(2):
                kb.op("dve", lambda e, h=h, ch=ch: e.tensor_tensor_scan(
                    out=LREL[:, h, ch * 64:(ch + 1) * 64], data0=ONES,
                    data1=LW[:, h, ch * 64:(ch + 1) * 64], initial=0.0, op0=ALU.mult, op1=ALU.add),
                    reads=[BLW, BONES], writes=[BLREL])
        kb.op("act", lambda e: e.activation(out=EPOS[:], in_=LREL[:], func=AF.Exp), reads=[BLREL], writes=[BEPOS])
        kb.op("act", lambda e: e.activation(out=ENEG[:], in_=LREL[:], func=AF.Exp, scale=-1.0), reads=[BLREL], writes=[BENEG])
        kb.op("dve", lambda e: e.tensor_tensor(out=EPREV[:], in0=LREL[:], in1=LW[:], op=ALU.subtract), reads=[BLREL, BLW], writes=[BEPREV])
        kb.op("act", lambda e: e.activation(out=EPREV[:], in_=EPREV[:], func=AF.Exp), reads=[BEPREV], writes=[BEPREV])
        L5 = LREL[:].rearrange("p c (h t) -> p c h t", h=2)
        kb.op("dve", lambda e, L5=L5: e.tensor_tensor(out=EBAR[:].rearrange("p c (h t) -> p c h t", h=2),
                                                       in0=bc(L5[:, :, :, 63:64], [64, NH, 2, 64]), in1=L5, op=ALU.subtract),
              reads=[BLREL], writes=[BEBAR])
        kb.op("act", lambda e: e.activation(out=EBAR[:], in_=EBAR[:], func=AF.Exp), reads=[BEBAR], writes=[BEBAR])
        kb.op("act", lambda e, L5=L5: e.activation(out=PC[:].unsqueeze(3), in_=L5[:, :, :, 63:64], func=AF.Exp), reads=[BLREL], writes=[BPC])

        def v5(t):
            return t[:].rearrange("p c (h t) -> p c h t", h=2)
        kb.op("dve", lambda e: e.scalar_tensor_tensor(out=AR[:].rearrange("p c h t -> p (c h) t")[:, :, 0:64], in0=KK[:].rearrange("p c (h t) -> p (c h) t", h=2),
                                                      scalar=-1.0, in1=EPREV[:].rearrange("p c (h t) -> p (c h) t", h=2), op0=ALU.mult, op1=ALU.mult),
              reads=[BKK, BEPREV], writes=[BAR])
        kb.op("dve", lambda e: e.tensor_tensor(out=AR[:, :, :, 64:128], in0=r_.rearrange("p c (h t) -> p c h t", h=2), in1=v5(EPOS), op=ALU.mult),
              reads=[BP, BEPOS], writes=[BAR])
        kb.op("dve", lambda e: e.tensor_tensor(out=BH[:], in0=v5(BS), in1=v5(ENEG), op=ALU.mult), reads=[BBS, BENEG], writes=[BBH])
        kb.op("dve", lambda e: e.tensor_tensor(out=KH[:], in0=v5(KD), in1=v5(ENEG), op=ALU.mult), reads=[BKD, BENEG], writes=[BKH])
        kb.op("dve", lambda e: e.tensor_tensor(out=BB[:], in0=v5(BS), in1=v5(EBAR), op=ALU.mult), reads=[BBS, BEBAR], writes=[BBB])
        kb.op("pool", lambda e: e.tensor_tensor(out=KBr[:], in0=v5(KD), in1=v5(EBAR), op=ALU.mult), reads=[BKD, BEBAR], writes=[BKBr])

        for ch in range(2):
            for hp in range(4):
                u0 = ch * 8 + hp * 2
                ps, Bps = nps()
                for e_ in range(2):
                    h = hp * 2 + e_
                    srcs = [(AR[:, h, ch, 0:64].bitcast(F32), BAR), (P[:, 16 + h, ch * 64:(ch + 1) * 64], BP), (BB[:, h, ch, :], BBB), (KBr[:, h, ch, :], BKBr)]
                    for ai, (s_ap, s_b) in enumerate(srcs):
                        col = (e_ * 4 + ai) * 64
                        kb.op("pe", lambda e, ps=ps, col=col, s_ap=s_ap: e.transpose(out=ps[0:64, col:col + 64], in_=s_ap, identity=ident),
                              reads=[s_b, Bcst], writes=[Bps])
                kb.op("act", lambda e, ps=ps, u0=u0: e.copy(out=TM[:, u0:u0 + 2, 1:5, :],
                                                             in_=ps[0:64, :].rearrange("p (e a k) -> p e a k", a=4, e=2)),
                      reads=[Bps], writes=[BTM])
        for ch in range(2):
            for hp in range(4):
                u0 = ch * 8 + hp * 2
                ps, Bps = nps()
                for e_ in range(2):
                    h = hp * 2 + e_
                    for hi, (lt, Blt) in enumerate([(BH, BBH), (KH, BKH)]):
                        kb.op("pe", lambda e, ps=ps, lt=lt, e_=e_, hi=hi, h=h, ch=ch: e.matmul(
                            ps[0:64, e_ * 256 + hi * 128:e_ * 256 + hi * 128 + 128], lhsT=lt[:, h, ch, :], rhs=AR[:, h, ch, :],
                            start=True, stop=True), reads=[Blt, BAR], writes=[Bps])
                kb.op("dve", lambda e, ps=ps, u0=u0: e.tensor_tensor(out=GS[:, u0:u0 + 2, :], in0=v3(ps, 2), in1=maskA, op=ALU.mult),
                      reads=[Bps, Bmk], writes=[BGS])
        for ch in range(2):
            ps, Bps = nps()
            for h in range(8):
                kb.op("pe", lambda e, ps=ps, h=h, ch=ch: e.matmul(ps[0:64, h * 64:(h + 1) * 64], lhsT=AR[:, h, ch, 0:64],
                                                                rhs=BH[:, h, ch, :], start=True, stop=True),
                      reads=[BAR, BBH], writes=[Bps])
            kb.op("dve", lambda e, ps=ps, ch=ch: e.tensor_tensor(out=MM[0][0][:, ch * 8:(ch + 1) * 8, :], in0=v3(ps, 8),
                                                                  in1=maskL, op=ALU.mult), reads=[Bps, Bmk], writes=[MM[0][1]])
        kb.op("dve", lambda e: e.tensor_tensor(out=WT[0][0][:, :, 64:128], in0=GS[:, :, 0:64], in1=identU, op=ALU.add),
              reads=[BGS, Bmk], writes=[WT[0][1]])
        for g in range(4):
            us = range(g * 4, g * 4 + 4)
            ps1, Bps1 = nps(); ps2, Bps2 = nps()
            for ui, u in enumerate(us):
                kb.op("pe", lambda e, ps1=ps1, ui=ui, u=u: e.matmul(ps1[0:64, ui * 64:(ui + 1) * 64], lhsT=MM[0][0][:, u, :], rhs=GS[:, u, 0:64],
                                                                   start=True, stop=True), reads=[MM[0][1], BGS], writes=[Bps1])
                kb.op("pe", lambda e, ps2=ps2, ui=ui, u=u: e.matmul(ps2[0:64, ui * 64:(ui + 1) * 64], lhsT=GS[:, u, 0:64], rhs=MM[0][0][:, u, :],
                                                                   start=True, stop=True), reads=[MM[0][1], BGS], writes=[Bps2])
            kb.op("act", lambda e, ps1=ps1, g=g: e.copy(out=WT[0][0][:, g * 4:g * 4 + 4, 0:64], in_=v3(ps1, 4, 256)),
                  reads=[Bps1], writes=[WT[0][1]])
            kb.op("act", lambda e, ps2=ps2, g=g: e.copy(out=MM[1][0][:, g * 4:g * 4 + 4, :], in_=v3(ps2, 4, 256)),
                  reads=[Bps2], writes=[MM[1][1]])
        cur, mcur = 0, 1
        for lvl in range(4):
            for g in range(4):
                us = range(g * 4, g * 4 + 4)
                ps1, Bps1 = nps(); ps2, Bps2 = nps()
                for ui, u in enumerate(us):
                    kb.op("pe", lambda e, ps1=ps1, ui=ui, u=u, cur=cur, mcur=mcur: e.matmul(
                        ps1[0:64, ui * 128:(ui + 1) * 128], lhsT=MM[mcur][0][:, u, :], rhs=WT[cur][0][:, u, :], start=True, stop=True),
                        reads=[MM[mcur][1], WT[cur][1]], writes=[Bps1])
                    kb.op("pe", lambda e, ps2=ps2, ui=ui, u=u, cur=cur, mcur=mcur: e.matmul(
                        ps2[0:64, ui * 64:(ui + 1) * 64], lhsT=WT[cur][0][:, u, 0:64], rhs=MM[mcur][0][:, u, :], start=True, stop=True),
                        reads=[MM[mcur][1], WT[cur][1]], writes=[Bps2])
                v1 = v3(ps1, 4)
                kb.op("act", lambda e, v1=v1, g=g, cur=cur: e.copy(out=WT[1 - cur][0][:, g * 4:g * 4 + 4, 0:64], in_=v1[:, :, 0:64]),
                      reads=[Bps1], writes=[WT[1 - cur][1]])
                kb.op("dve", lambda e, v1=v1, g=g, cur=cur: e.tensor_tensor(out=WT[1 - cur][0][:, g * 4:g * 4 + 4, 64:128], in0=v1[:, :, 64:128],
                                                                            in1=WT[cur][0][:, g * 4:g * 4 + 4, 64:128], op=ALU.add),
                      reads=[Bps1, WT[cur][1]], writes=[WT[1 - cur][1]])
                kb.op("act", lambda e, ps2=ps2, g=g, mcur=mcur: e.copy(out=MM[1 - mcur][0][:, g * 4:g * 4 + 4, :], in_=v3(ps2, 4, 256)),
                      reads=[Bps2], writes=[MM[1 - mcur][1]])
            cur, mcur = 1 - cur, 1 - mcur
        for g in range(2):
            ps1, Bps1 = nps()
            for ui in range(8):
                u = g * 8 + ui
                kb.op("pe", lambda e, ps1=ps1, ui=ui, u=u, cur=cur, mcur=mcur: e.matmul(
                    ps1[0:64, ui * 64:(ui + 1) * 64], lhsT=MM[mcur][0][:, u, :], rhs=WT[cur][0][:, u, 64:128], start=True, stop=True),
                    reads=[MM[mcur][1], WT[cur][1]], writes=[Bps1])
            kb.op("dve", lambda e, ps1=ps1, g=g, cur=cur: e.tensor_tensor(out=XF[:, g * 8:g * 8 + 8, :], in0=v3(ps1, 8),
                                                                         in1=WT[cur][0][:, g * 8:g * 8 + 8, 64:128], op=ALU.add),
                  reads=[Bps1, WT[cur][1]], writes=[BXF])
        for g in range(2):
            ps1, Bps1 = nps()
            for ui in range(8):
                u = g * 8 + ui
                kb.op("pe", lambda e, ps1=ps1, ui=ui, u=u: e.matmul(ps1[0:64, ui * 64:(ui + 1) * 64], lhsT=GS[:, u, 128:192], rhs=TM[:, u, 2, :],
                                                                   start=True, stop=True), reads=[BGS, BTM], writes=[Bps1])
            kb.op("act", lambda e, ps1=ps1, g=g: e.copy(out=TM[:, g * 8:g * 8 + 8, 0, :], in_=v3(ps1, 8)), reads=[Bps1], writes=[BTM])
        for g in range(4):
            ps1, Bps1 = nps()
            for ui in range(4):
                u = g * 4 + ui
                kb.op("pe", lambda e, ps1=ps1, ui=ui, u=u: e.matmul(ps1[0:64, ui * 128:(ui + 1) * 128], lhsT=XF[:, u, :],
                                                                   rhs=TM[:, u, 0:2, :].rearrange("p a k -> p (a k)"),
                                                                   start=True, stop=True), reads=[BXF, BTM], writes=[Bps1])
            kb.op("act", lambda e, ps1=ps1, g=g: e.copy(out=UA[:, g * 4:g * 4 + 4, :], in_=v3(ps1, 4)), reads=[Bps1], writes=[BUA])
        for g in range(2):
            ps1, Bps1 = nps()
            for ui in range(8):
                u = g * 8 + ui
                kb.op("pe", lambda e, ps1=ps1, ui=ui, u=u: e.matmul(ps1[0:64, ui * 64:(ui + 1) * 64], lhsT=TM[:, u, 1, :], rhs=XF[:, u, :],
                                                                   start=True, stop=True), reads=[BTM, BXF], writes=[Bps1])
            kb.op("act", lambda e, ps1=ps1, g=g: e.copy(out=APT[:, g * 8:g * 8 + 8, :], in_=v3(ps1, 8)), reads=[Bps1], writes=[BAPT])
        for ch in range(2):
            scur, snxt = ST[gc % 2], ST[(gc + 1) % 2]
            gc += 1
            psu, Bpsu = nps()
            for h in range(8):
                u = ch * 8 + h
                kb.op("pe", lambda e, psu=psu, h=h, u=u, scur=scur: e.matmul(
                    psu[0:64, h * 64:(h + 1) * 64], lhsT=APT[:, u, :], rhs=scur[0][:, h, :], start=True, stop=True),
                    reads=[BAPT, scur[1]], writes=[Bpsu])
            kb.op("dve", lambda e, psu=psu, ch=ch: e.tensor_tensor(out=UU[:], in0=v3(psu, 8), in1=UA[:, ch * 8:ch * 8 + 8, 0:64], op=ALU.add),
                  reads=[Bpsu, BUA], writes=[BUU])
            psy, Bpsy = nps(); pss_, Bpss = nps()
            for h in range(8):
                u = ch * 8 + h
                oy = psy[0:64, h * 64:(h + 1) * 64]
                kb.op("pe", lambda e, oy=oy, h=h, u=u: e.matmul(oy, lhsT=UU[:, h, :], rhs=GS[:, u, 64:128], start=True, stop=False),
                      reads=[BUU, BGS], writes=[Bpsy])
                kb.op("pe", lambda e, oy=oy, u=u: e.matmul(oy, lhsT=TM[:, u, 2, :], rhs=GS[:, u, 192:256], start=False, stop=False),
                      reads=[BTM, BGS], writes=[Bpsy])
                kb.op("pe", lambda e, oy=oy, h=h, ch=ch, scur=scur: e.matmul(oy, lhsT=scur[0][:, h, :], rhs=AR[:, h, ch, 64:128],
                                                                            start=False, stop=True),
                      reads=[scur[1], BAR], writes=[Bpsy])
                os_ = pss_[0:64, h * 64:(h + 1) * 64]
                kb.op("pe", lambda e, os_=os_, h=h, u=u: e.matmul(os_, lhsT=TM[:, u, 3, :], rhs=UU[:, h, :], start=True, stop=False),
                      reads=[BTM, BUU], writes=[Bpss])
                kb.op("pe", lambda e, os_=os_, u=u: e.matmul(os_, lhsT=TM[:, u, 4, :], rhs=TM[:, u, 2, :], start=False, stop=True),
                      reads=[BTM], writes=[Bpss])
            kb.op("act", lambda e, psy=psy, ch=ch: e.copy(out=YO[:, :, ch * 64:(ch + 1) * 64], in_=v3(psy, 8)), reads=[Bpsy], writes=[BYO])
            kb.op("dve", lambda e, pss_=pss_, ch=ch, scur=scur, snxt=snxt: e.tensor_tensor(
                out=snxt[0][:], in0=scur[0][:], in1=bc(PC[:, :, ch:ch + 1], [64, NH, 64]), op=ALU.mult),
                reads=[scur[1], BPC], writes=[snxt[1]])
            kb.op("dve", lambda e, pss_=pss_, snxt=snxt: e.tensor_tensor(out=snxt[0][:], in0=v3(pss_, 8), in1=snxt[0][:], op=ALU.add),
                  reads=[Bpss, snxt[1]], writes=[snxt[1]])
        kb.dma("sp", lambda e, t0=t0: e.dma_start(out=yv[:, :, t0:t0 + NB], in_=YO[:]), reads=[BYO], writes=[Buf()], is_out=True)
    C.pop()


def emit_p2d(C, io):
    kb = C.kb
    C.push()
    rope_in = io["rope"]
    rd_in = io["rd"]
    gn_in = io["gn"]
    tab_in = io["tab"]
    cst_in = io["cst"]
    y_out = io["yT"]

    def T(shape, dt=F32):
        return C.sb(shape, dt), Buf()
    rope, Brope = T([128, 2, TALL])
    rd, Brd = T([128, 8]); lg, Blg = T([128, 8]); gch, Bgch = T([128, 8])
    gn, Bgn = T([128, 4]); tab, Btab = T([128, 770]); cst, Bcst = T([128, 256])
    ident = cst[:, 0:128]; ONESM = cst[:, 128:256]
    DEC = [T([128, 128]) for _ in range(8)]
    XI = [T([128, 128]) for _ in range(8)]
    ZE = [T([128, 1]) for _ in range(8)]
    kb.dma("sp", lambda e: e.dma_start(out=rope[:], in_=rope_in.rearrange("a p t -> p a t")), writes=[Brope])
    kb.dma("sp", lambda e: e.dma_start(out=rd[:], in_=rd_in), writes=[Brd])
    kb.dma("sp", lambda e: e.dma_start(out=gn[:], in_=gn_in), writes=[Bgn])
    kb.dma("sp", lambda e: e.dma_start(out=tab[:], in_=tab_in), writes=[Btab])
    kb.dma("sp", lambda e: e.dma_start(out=cst[:], in_=cst_in), writes=[Bcst])
    kb.op("act", lambda e: e.activation(out=lg[:], in_=rd[:], func=AF.Exp), reads=[Brd], writes=[Blg])
    kb.op("dve", lambda e: e.tensor_scalar(out=lg[:], in0=lg[:], scalar1=-1.0, scalar2=None, op0=ALU.mult), reads=[Blg], writes=[Blg])
    kb.op("act", lambda e: e.activation(out=gch[:], in_=lg[:], func=AF.Exp, scale=128.0), reads=[Blg], writes=[Bgch])
    for d in range(2):
        for h in range(4):
            i = d * 4 + h
            kb.op("act", lambda e, d=d, i=i: e.activation(out=DEC[i][0][:], in_=tab[:, d * 256:d * 256 + 128], func=AF.Exp, scale=lg[:, i:i + 1]),
                  reads=[Btab, Blg], writes=[DEC[i][1]])
            kb.op("dve", lambda e, d=d, i=i: e.tensor_tensor(out=DEC[i][0][:], in0=DEC[i][0][:], in1=tab[:, d * 256 + 128:d * 256 + 256], op=ALU.mult),
                  reads=[Btab, DEC[i][1]], writes=[DEC[i][1]])
            kb.op("act", lambda e, d=d, i=i: e.activation(out=XI[i][0][:], in_=tab[:, 512 + d * 128:512 + d * 128 + 128], func=AF.Exp, scale=lg[:, i:i + 1]),
                  reads=[Btab, Blg], writes=[XI[i][1]])
            kb.op("act", lambda e, d=d, i=i: e.activation(out=ZE[i][0][:], in_=tab[:, 768 + d:769 + d], func=AF.Exp, scale=lg[:, i:i + 1]),
                  reads=[Btab, Blg], writes=[ZE[i][1]])
    X6 = [T([128, TALL], mybir.dt.float32r if i_ < 2 else F32) for i_ in range(6)]
    QX = [T([128, TALL], mybir.dt.float32r) for _ in range(2)]
    KT, BKT = T([128, NCH, 128]); VT, BVT = T([128, NCH, 128], mybir.dt.float32r)
    KZ = [T([128, 128], mybir.dt.float32r) for _ in range(2)]
    ATT = [T([128, 128], mybir.dt.float32r) for _ in range(2)]
    O, BO = T([128, TALL]); TMP, BTMP = T([128, TALL])
    S = [T([128, 128], mybir.dt.float32r) for _ in range(2)]
    psl = [(C.ps([128, 512]), Buf(excl=True)) for _ in range(8)]
    pcnt = [0]

    def nps():
        r = psl[pcnt[0] % 8]
        pcnt[0] += 1
        return r
    yv = y_out.rearrange("(h p) t -> p h t", p=128)
    SCALE = 128.0 ** -0.5
    for h in range(4):
        for a in range(6):
            for pi_, (pr0, pr1, srcap) in enumerate(io["qkv"](h, a)):
                q_ = "sp" if (a + pi_) % 2 == 0 else "act"
                kb.dma(q_, lambda e, a=a, pr0=pr0, pr1=pr1, srcap=srcap: e.dma_start(out=(X6[a][0][pr0:pr1, :].bitcast(F32) if a < 2 else X6[a][0][pr0:pr1, :]), in_=srcap), writes=[X6[a][1]])
        (q, Bq), (k, Bk), (v, Bv), (g, Bg), (q2, Bq2), (k2, Bk2) = X6
        for (x, Bx, x2, Bx2, eng2) in [(q, Bq, q2, Bq2, "pool"), (k, Bk, k2, Bk2, "pool")]:
            kb.op("dve", lambda e, x=x: e.tensor_tensor(out=TMP[:], in0=x[:].bitcast(F32), in1=rope[:, 0, :], op=ALU.mult), reads=[Bx, Brope], writes=[BTMP])
            kb.op(eng2, lambda e, x2=x2: e.tensor_tensor(out=x2[:], in0=x2[:], in1=rope[:, 1, :], op=ALU.mult), reads=[Bx2, Brope], writes=[Bx2])
            kb.op("dve", lambda e, x=x, x2=x2: e.tensor_tensor(out=x[:], in0=TMP[:], in1=x2[:], op=ALU.add), reads=[BTMP, Bx2], writes=[Bx])
        kb.op("act", lambda e: e.mul(out=k[:], in_=k[:].bitcast(F32), mul=SCALE), reads=[Bk], writes=[Bk])
        for (src, Bsrc, dst, Bdst) in [(k, Bk, KT, BKT), (v, Bv, VT, BVT)]:
            for c4 in range(0, NCH, 4):
                ps, Bps = nps()
                n = min(4, NCH - c4)
                for ci in range(n):
                    c = c4 + ci
                    kb.op("pe", lambda e, ps=ps, ci=ci, c=c, src=src: e.transpose(out=ps[:, ci * 128:(ci + 1) * 128], in_=(src[:, c * 128:(c + 1) * 128].bitcast(F32) if src is k else src[:, c * 128:(c + 1) * 128]), identity=ident),
                          reads=[Bsrc, Bcst], writes=[Bps])
                kb.op("act", lambda e, ps=ps, c4=c4, n=n, dst=dst: e.copy(out=dst[:, c4:c4 + n, :], in_=ps[:, 0:n * 128].rearrange("p (a b) -> p a b", a=n)),
                      reads=[Bps], writes=[Bdst])
        for d in range(2):
            i = d * 4 + h
            kb.op("dve", lambda e, d=d, i=i: e.tensor_tensor(
                out=QX[d][0][:].rearrange("p (c t) -> p c t", c=NCH), in0=q[:].bitcast(F32).rearrange("p (c t) -> p c t", c=NCH),
                in1=XI[i][0][:].unsqueeze(1).broadcast_to([128, NCH, 128]), op=ALU.mult), reads=[Bq, XI[i][1]], writes=[QX[d][1]])
        for d in range(2):
            i = d * 4 + h
            st, Bst = S[d]
            kb.op("dve", lambda e, st=st: e.tensor_scalar(out=st[:], in0=ident, scalar1=0.0, scalar2=None, op0=ALU.mult), reads=[Bcst], writes=[Bst])
            order = [0, 1] + list(range(2, NCH)) if d == 0 else [1, 0] + list(range(NCH - 1, 1, -1))
            for n_, c in enumerate(order):
                cs = slice(c * 128, (c + 1) * 128)
                at, Bat = ATT[n_ % 2]
                kz, Bkz = KZ[n_ % 2]
                ps, Bps = nps()
                kb.op("pe", lambda e, ps=ps, cs=cs: e.matmul(ps[:, 0:128], lhsT=k[:, cs], rhs=q[:, cs], start=True, stop=True),
                      reads=[Bk, Bq], writes=[Bps])
                kb.op("dve", lambda e, ps=ps, at=at, i=i: e.tensor_tensor(out=at[:], in0=ps[:, 0:128], in1=DEC[i][0][:], op=ALU.mult),
                      reads=[Bps, DEC[i][1]], writes=[Bat])
                kb.op("dve", lambda e, kz=kz, c=c, i=i: e.tensor_tensor(out=kz[:], in0=KT[:, c, :], in1=ZE[i][0][:].broadcast_to([128, 128]), op=ALU.mult),
                      reads=[BKT, ZE[i][1]], writes=[Bkz])
                po, Bpo = nps()
                kb.op("pe", lambda e, po=po, c=c, at=at: e.matmul(po[:, 0:128], lhsT=VT[:, c, :], rhs=at[:], start=True, stop=False),
                      reads=[BVT, Bat], writes=[Bpo])
                kb.op("pe", lambda e, po=po, cs=cs, st=st, d=d: e.matmul(po[:, 0:128], lhsT=st[:], rhs=QX[d][0][:, cs], start=False, stop=True),
                      reads=[Bst, QX[d][1]], writes=[Bpo])
                if d == 0:
                    kb.op("act", lambda e, po=po, cs=cs: e.copy(out=O[:, cs], in_=po[:, 0:128]), reads=[Bpo], writes=[BO])
                else:
                    kb.op("dve", lambda e, po=po, cs=cs: e.tensor_tensor(out=O[:, cs], in0=po[:, 0:128], in1=O[:, cs], op=ALU.add),
                          reads=[Bpo, BO], writes=[BO])
                pn, Bpn = nps()
                kb.op("pe", lambda e, pn=pn, kz=kz, c=c: e.matmul(pn[:, 0:128], lhsT=kz[:], rhs=VT[:, c, :], start=True, stop=True),
                      reads=[Bkz, BVT], writes=[Bpn])
                kb.op("dve", lambda e, pn=pn, st=st, i=i: e.scalar_tensor_tensor(out=st[:], in0=st[:].bitcast(F32), scalar=gch[:, i:i + 1], in1=pn[:, 0:128],
                                                                                 op0=ALU.mult, op1=ALU.add), reads=[Bst, Bgch, Bpn], writes=[Bst])
        for (t0, tn) in token_tiles(TALL):
            ts_ = slice(t0, t0 + tn)
            pm, Bpm = nps()
            kb.op("pe", lambda e, pm=pm, ts_=ts_, tn=tn: e.matmul(pm[:, 0:tn], lhsT=ONESM, rhs=O[:, ts_], start=True, stop=True), reads=[Bcst, BO], writes=[Bpm])
            kb.op("dve", lambda e, pm=pm, ts_=ts_, tn=tn: e.tensor_tensor(out=O[:, ts_], in0=O[:, ts_], in1=pm[:, 0:tn], op=ALU.subtract),
                  reads=[Bpm, BO], writes=[BO])
            kb.op("act", lambda e, ts_=ts_: e.activation(out=TMP[:, ts_], in_=O[:, ts_], func=AF.Square), reads=[BO], writes=[BTMP])
            pv, Bpv = nps()
            kb.op("pe", lambda e, pv=pv, ts_=ts_, tn=tn: e.matmul(pv[:, 0:tn], lhsT=ONESM, rhs=TMP[:, ts_], start=True, stop=True), reads=[Bcst, BTMP], writes=[Bpv])
            kb.op("act", lambda e, pv=pv, ts_=ts_, tn=tn: e.activation(out=TMP[:, ts_], in_=pv[:, 0:tn], func=AF.Sqrt, bias=eps_ap(C, RET_EPS), scale=1.0),
                  reads=[Bpv], writes=[BTMP])
            kb.op("dve", lambda e, ts_=ts_: e.reciprocal(out=TMP[:, ts_], in_=TMP[:, ts_]), reads=[BTMP], writes=[BTMP])
            kb.op("dve", lambda e, ts_=ts_, h=h: e.scalar_tensor_tensor(out=O[:, ts_], in0=O[:, ts_], scalar=gn[:, h:h + 1], in1=TMP[:, ts_],
                                                                       op0=ALU.mult, op1=ALU.mult), reads=[BO, Bgn, BTMP], writes=[BO])
            kb.op("act", lambda e, ts_=ts_: e.activation(out=TMP[:, ts_], in_=g[:, ts_], func=AF.Silu), reads=[Bg], writes=[BTMP])
            kb.op("dve", lambda e, ts_=ts_: e.tensor_tensor(out=O[:, ts_], in0=O[:, ts_], in1=TMP[:, ts_], op=ALU.mult), reads=[BO, BTMP], writes=[BO])
        kb.dma("sp", lambda e, h=h: e.dma_start(out=yv[:, h, :], in_=O[:]), reads=[BO], writes=[Buf()], is_out=True)
    C.pop()


def emit_p2b(C, io):
    kb = C.kb
    C.push()
    nrm_in = io["nrm"]
    wq_in = io["wq"]
    wkv_in = io["wkv"]
    rope_in = io["rope"]
    cst_in = io["cst"]
    y_out = io["y"]

    def T(shape, dt=F32):
        return C.sb(shape, dt), Buf()
    cxr, Bcx = T([128, 5, TALL], mybir.dt.float32r); cx = cxr[:].bitcast(F32); kr, Bkr = T([64, 2, TALL]); nrm, Bnrm = T([128, 5])
    wq, Bwq = T([128, 3, 4, 256], mybir.dt.float32r); wkv, Bwkv = T([128, 2, 4, 256], mybir.dt.float32r)
    C.push()
    wq32, Bwq32 = T([128, 3, 4, 256]); wkv32, Bwkv32 = T([128, 2, 4, 256])
    kb.dma("sp", lambda e: e.dma_start(out=wq32[:], in_=wq_in), writes=[Bwq32])
    kb.dma("sp", lambda e: e.dma_start(out=wkv32[:], in_=wkv_in), writes=[Bwkv32])
    kb.op("act", lambda e: e.copy(out=wq[:], in_=wq32[:]), reads=[Bwq32], writes=[Bwq])
    kb.op("act", lambda e: e.copy(out=wkv[:], in_=wkv32[:]), reads=[Bwkv32], writes=[Bwkv])
    C.pop()
    rope, Brope = T([64, 2, TALL])
    cst, Bcst = T([128, 384])
    ident = cst[:, 0:128]
    rstd, Brstd = T([128, TALL])
    R32 = mybir.dt.float32r
    krr, Bkrr = T([64, TALL], R32); qn, Bqn = T([128, TALL], R32); qrr, Bqrr = T([64, TALL], R32); kn, Bkn = T([128, TALL], R32)
    t1, Bt1 = T([64, TALL]); VT, BVT = T([128, NCH, 128], R32)
    Pm, BPm = T([128, TALL]); PT, BPT = T([128, NCH, 128], R32)
    krr32, Bkrr32 = T([64, TALL]); qrr32, Bqrr32 = T([64, TALL])
    sq = [T([128, 512]) for _ in range(2)]
    mx, Bmx = T([128, 1]); nb, Bnb = T([128, 1]); rs, Brs = T([128, 1]); ri, Bri = T([128, 1])
    OT = [T([128, 128]) for _ in range(2)]
    OT2 = [T([128, 128]) for _ in range(2)]
    big = C.ps([128, 2560]); Bbig = Buf(excl=True)
    psl = [(C.ps([128, 512]), Buf(excl=True)) for _ in range(3)]
    pcnt = [0]

    def nps():
        r = psl[pcnt[0] % 3]
        pcnt[0] += 1
        return r
    for a in range(5):
        kb.dma("sp" if a % 2 == 0 else "act", lambda e, a=a: e.dma_start(out=cx[:, a, :], in_=io["cx"][a]), writes=[Bcx])
    for (pr0, pr1, ai, srcap) in io["kr"]:
        kb.dma("sp", lambda e, pr0=pr0, pr1=pr1, ai=ai, srcap=srcap: e.dma_start(out=kr[pr0:pr1, ai, :], in_=srcap), writes=[Bkr])
    kb.dma("sp", lambda e: e.dma_start(out=rope[:], in_=rope_in.rearrange("a p t -> p a t")), writes=[Brope])
    kb.dma("sp", lambda e: e.dma_start(out=nrm[:], in_=nrm_in), writes=[Bnrm])
    kb.dma("sp", lambda e: e.dma_start(out=cst[:], in_=cst_in), writes=[Bcst])
    k = 0
    for (tiles, ones_ap) in [([0, 1, 2], cst[:, 128:256]), ([3, 4], cst[:, 256:384])]:
        for (t0, tn) in token_tiles(TALL):
            ps, Bps = nps()
            for ii, a in enumerate(tiles):
                s_, Bs_ = sq[k % 2]; k += 1
                kb.op("act", lambda e, s_=s_, a=a, t0=t0, tn=tn: e.activation(out=s_[:, 0:tn], in_=cx[:, a, t0:t0 + tn], func=AF.Square),
                      reads=[Bcx], writes=[Bs_])
                kb.op("pe", lambda e, ps=ps, s_=s_, tn=tn, ii=ii, ones_ap=ones_ap, n=len(tiles): e.matmul(
                    ps[:, 0:tn], lhsT=ones_ap, rhs=s_[:, 0:tn], start=(ii == 0), stop=(ii == n - 1)), reads=[Bcst, Bs_], writes=[Bps])
            kb.op("act", lambda e, ps=ps, t0=t0, tn=tn: e.activation(out=rstd[:, t0:t0 + tn], in_=ps[:, 0:tn], func=AF.Sqrt,
                                                                    bias=eps_ap(C, NORM_EPS), scale=1.0), reads=[Bps], writes=[Brstd])
        kb.op("dve", lambda e: e.reciprocal(out=rstd[:], in_=rstd[:]), reads=[Brstd], writes=[Brstd])
        for a in tiles:
            kb.op("dve", lambda e, a=a: e.scalar_tensor_tensor(out=cxr[:, a, :], in0=cx[:, a, :], scalar=nrm[:, a:a + 1], in1=rstd[:],
                                                               op0=ALU.mult, op1=ALU.mult), reads=[Bcx, Bnrm, Brstd], writes=[Bcx])
    kb.op("dve", lambda e: e.tensor_tensor(out=krr32[:], in0=kr[:, 0, :], in1=rope[:, 0, :], op=ALU.mult), reads=[Bkr, Brope], writes=[Bkrr32])
    kb.op("pool", lambda e: e.tensor_tensor(out=t1[:], in0=kr[:, 1, :], in1=rope[:, 1, :], op=ALU.mult), reads=[Bkr, Brope], writes=[Bt1])
    kb.op("dve", lambda e: e.tensor_tensor(out=krr[:], in0=krr32[:], in1=t1[:], op=ALU.add), reads=[Bkrr32, Bt1], writes=[Bkrr])
    oi = 0
    for hh in range(4):
        for (t0, tn) in token_tiles(TALL):
            ts_ = slice(t0, t0 + tn)
            ps, Bps = nps()
            for kc in range(3):
                kb.op("pe", lambda e, ps=ps, kc=kc, ts_=ts_, tn=tn, hh=hh: e.matmul(ps[:, 0:tn], lhsT=wq[:, kc, hh, 0:128], rhs=cxr[:, kc, ts_],
                                                                                   start=(kc == 0), stop=(kc == 2)), reads=[Bwq, Bcx], writes=[Bps])
            kb.op("act", lambda e, ps=ps, ts_=ts_, tn=tn: e.copy(out=qn[:, ts_], in_=ps[:, 0:tn]), reads=[Bps], writes=[Bqn])
            ps, Bps = nps()
            for kc in range(2):
                kb.op("pe", lambda e, ps=ps, kc=kc, ts_=ts_, tn=tn, hh=hh: e.matmul(ps[:, 0:tn], lhsT=wkv[:, kc, hh, 0:128], rhs=cxr[:, 3 + kc, ts_],
                                                                                   start=(kc == 0), stop=(kc == 1)), reads=[Bwkv, Bcx], writes=[Bps])
            kb.op("act", lambda e, ps=ps, ts_=ts_, tn=tn: e.copy(out=kn[:, ts_], in_=ps[:, 0:tn]), reads=[Bps], writes=[Bkn])
            for ri_, (c0, dst, Bdst) in enumerate([(128, qrr32, Bqrr32), (192, t1, Bt1)]):
                ps, Bps = nps()
                for kc in range(3):
                    kb.op("pe", lambda e, ps=ps, kc=kc, ts_=ts_, tn=tn, hh=hh, c0=c0: e.matmul(ps[0:64, 0:tn], lhsT=wq[:, kc, hh, c0:c0 + 64], rhs=cxr[:, kc, ts_],
                                                                                              start=(kc == 0), stop=(kc == 2)), reads=[Bwq, Bcx], writes=[Bps])
                kb.op("dve", lambda e, ps=ps, ts_=ts_, tn=tn, dst=dst, ri_=ri_: e.tensor_tensor(out=dst[:, ts_], in0=ps[0:64, 0:tn], in1=rope[:, ri_, ts_], op=ALU.mult),
                      reads=[Bps, Brope], writes=[Bdst])
        kb.op("dve", lambda e: e.tensor_tensor(out=qrr[:], in0=qrr32[:], in1=t1[:], op=ALU.add), reads=[Bqrr32, Bt1], writes=[Bqrr])
        for c4 in range(0, NCH, 4):
            ps, Bps = nps()
            n = min(4, NCH - c4)
            for ci in range(n):
                c = c4 + ci
                for kc in range(2):
                    kb.op("pe", lambda e, ps=ps, ci=ci, c=c, kc=kc, hh=hh: e.matmul(ps[:, ci * 128:(ci + 1) * 128], lhsT=cxr[:, 3 + kc, c * 128:(c + 1) * 128],
                                                                                   rhs=wkv[:, kc, hh, 128:256], start=(kc == 0), stop=(kc == 1)),
                          reads=[Bcx, Bwkv], writes=[Bps])
            kb.op("act", lambda e, ps=ps, c4=c4, n=n: e.copy(out=VT[:, c4:c4 + n, :], in_=ps[:, 0:n * 128].rearrange("p (a b) -> p a b", a=n)),
                  reads=[Bps], writes=[BVT])
        for qt in range(NCH):
            qs = slice(qt * 128, (qt + 1) * 128)
            nk = CTX if qt < 2 else TALL
            for (t0, tn) in token_tiles(nk):
                kb.op("pe", lambda e, qs=qs, t0=t0, tn=tn: e.matmul(big[:, t0:t0 + tn], lhsT=qn[:, qs], rhs=kn[:, t0:t0 + tn], start=True, stop=False),
                      reads=[Bqn, Bkn], writes=[Bbig])
                kb.op("pe", lambda e, qs=qs, t0=t0, tn=tn: e.matmul(big[:, t0:t0 + tn], lhsT=qrr[:, qs], rhs=krr[:, t0:t0 + tn], start=False, stop=True),
                      reads=[Bqrr, Bkrr], writes=[Bbig])
            kb.op("dve", lambda e, nk=nk: e.tensor_reduce(out=mx[:], in_=big[:, 0:nk], axis=AX.X, op=ALU.max), reads=[Bbig], writes=[Bmx])
            kb.op("dve", lambda e: e.tensor_scalar(out=nb[:], in0=mx[:], scalar1=-MLA_SCALE, scalar2=None, op0=ALU.mult), reads=[Bmx], writes=[Bnb])
            kb.op("act", lambda e, nk=nk: e.activation(out=Pm[:, 0:nk], in_=big[:, 0:nk], func=AF.Exp, bias=nb[:, 0:1], scale=MLA_SCALE, accum_out=rs[:, 0:1]),
                  reads=[Bbig, Bnb], writes=[BPm, Brs])
            kb.op("dve", lambda e: e.reciprocal(out=ri[:], in_=rs[:]), reads=[Brs], writes=[Bri])
            nkt = nk // 128
            for c4 in range(0, nkt, 4):
                ps, Bps = nps()
                n = min(4, nkt - c4)
                for ci in range(n):
                    c = c4 + ci
                    kb.op("pe", lambda e, ps=ps, ci=ci, c=c: e.transpose(out=ps[:, ci * 128:(ci + 1) * 128], in_=Pm[:, c * 128:(c + 1) * 128], identity=ident),
                          reads=[BPm, Bcst], writes=[Bps])
                ev = "act" if (c4 // 4) % 2 == 0 else "dve"
                if ev == "act":
                    kb.op("act", lambda e, ps=ps, c4=c4, n=n: e.copy(out=PT[:, c4:c4 + n, :], in_=ps[:, 0:n * 128].rearrange("p (a b) -> p a b", a=n)),
                          reads=[Bps], writes=[BPT])
                else:
                    kb.op("dve", lambda e, ps=ps, c4=c4, n=n: e.tensor_copy(out=PT[:, c4:c4 + n, :], in_=ps[:, 0:n * 128].rearrange("p (a b) -> p a b", a=n)),
                          reads=[Bps], writes=[BPT])
            po, Bpo = nps()
            for c in range(nkt):
                kb.op("pe", lambda e, po=po, c=c, nkt=nkt: e.matmul(po[:, 0:128], lhsT=PT[:, c, :], rhs=VT[:, c, :], start=(c == 0), stop=(c == nkt - 1)),
                      reads=[BPT, BVT], writes=[Bpo])
            ot, Bot = OT[oi % 2]; oi += 1
            kb.op("act", lambda e, po=po, ot=ot: e.activation(out=ot[:], in_=po[:, 0:128], func=AF.Copy, scale=ri[:, 0:1]), reads=[Bpo, Bri], writes=[Bot])
            pt2, Bpt2 = nps()
            kb.op("pe", lambda e, pt2=pt2, ot=ot: e.transpose(out=pt2[:, 0:128], in_=ot[:], identity=ident), reads=[Bot, Bcst], writes=[Bpt2])
            ot2, Bot2 = OT2[oi % 2]
            kb.op("dve", lambda e, pt2=pt2, ot2=ot2: e.tensor_copy(out=ot2[:], in_=pt2[:, 0:128]), reads=[Bpt2], writes=[Bot2])
            kb.dma("sp", lambda e, ot2=ot2, qs=qs, hh=hh: e.dma_start(out=y_out[hh * 128:(hh + 1) * 128, qs], in_=ot2[:]), reads=[Bot2], writes=[io["By"]])
    C.pop()


def emit_p2c(C, io):
    kb = C.kb
    C.push()
    cw_in = io["cw"]
    ln_in = io["ln"]
    cst_in = io["cst"]

    def T(shape, dt=F32):
        return C.sb(shape, dt), Buf()
    cw, Bcw = T([128, 4, 3]); ln, Bln = T([128, 4, 2]); cst, Bcst = T([128, 128])
    kb.dma("sp", lambda e: e.dma_start(out=cw[:], in_=cw_in), writes=[Bcw])
    kb.dma("sp", lambda e: e.dma_start(out=ln[:], in_=ln_in), writes=[Bln])
    kb.dma("sp", lambda e: e.dma_start(out=cst[:], in_=cst_in), writes=[Bcst])
    X3 = [T([128, TP]) for _ in range(3)]
    OC, BOC = T([128, TP])
    Y, BY = T([128, TALL]); Y2, BY2 = T([128, TALL]); TMP, BTMP = T([128, TALL])
    BN, BBN = T([128, TALL]); GT, BGT = T([128, TALL])
    psl = [(C.ps([128, 512]), Buf(excl=True)) for _ in range(4)]
    pcnt = [0]

    def nps():
        r = psl[pcnt[0] % 4]
        pcnt[0] += 1
        return r
    n = TP - 2
    for a in range(4):
        for i in range(3):
            srcap = io["cv"](i, a)
            kb.op("pool", lambda e, i=i: e.memset(X3[i][0][:, 0:1], 0.0), writes=[X3[i][1]])
            kb.op("pool", lambda e, i=i: e.memset(X3[i][0][:, 257:259], 0.0), writes=[X3[i][1]])
            kb.op("pool", lambda e, i=i: e.memset(X3[i][0][:, TP - 1:TP], 0.0), writes=[X3[i][1]])
            kb.dma("sp", lambda e, i=i, srcap=srcap: e.dma_start(out=X3[i][0][:, 1:257], in_=srcap[:, 0:CTX]), writes=[X3[i][1]])
            kb.dma("act", lambda e, i=i, srcap=srcap: e.dma_start(out=X3[i][0][:, 259:259 + SEQ], in_=srcap[:, CTX:TALL]), writes=[X3[i][1]])
        (bgt, Bbgt), (cg, Bcg), (u, Bu) = X3
        kb.op("dve", lambda e: e.tensor_tensor(out=cg[:], in0=cg[:], in1=u[:], op=ALU.mult), reads=[Bcg, Bu], writes=[Bcg])
        kb.op("pool", lambda e: e.memset(OC[:], 0.0), writes=[BOC])
        kb.op("act", lambda e, a=a: e.activation(out=OC[:, 1:1 + n], in_=cg[:, 1:1 + n], func=AF.Copy, scale=cw[:, a, 1:2]), reads=[Bcg, Bcw], writes=[BOC])
        kb.op("dve", lambda e, a=a: e.scalar_tensor_tensor(out=OC[:, 1:1 + n], in0=cg[:, 0:n], scalar=cw[:, a, 0:1], in1=OC[:, 1:1 + n], op0=ALU.mult, op1=ALU.add),
              reads=[Bcg, Bcw, BOC], writes=[BOC])
        kb.op("dve", lambda e, a=a: e.scalar_tensor_tensor(out=OC[:, 1:1 + n], in0=cg[:, 2:2 + n], scalar=cw[:, a, 2:3], in1=OC[:, 1:1 + n], op0=ALU.mult, op1=ALU.add),
              reads=[Bcg, Bcw, BOC], writes=[BOC])
        kb.op("dve", lambda e: e.tensor_tensor(out=OC[:], in0=OC[:], in1=bgt[:], op=ALU.mult), reads=[BOC, Bbgt], writes=[BOC])
        kb.dma("sp", lambda e, a=a: e.dma_start(out=io["yc"](a)[:, 0:CTX], in_=OC[:, 1:257]), reads=[BOC], writes=[io["By"]])
        kb.dma("act", lambda e, a=a: e.dma_start(out=io["yc"](a)[:, CTX:TALL], in_=OC[:, 259:259 + SEQ]), reads=[BOC], writes=[io["By"]])
        kb.dma("sp", lambda e, a=a: e.dma_start(out=Y[:], in_=io["yf"](a)), reads=[io["Byf"]], writes=[BY])
        kb.dma("act", lambda e, a=a: e.dma_start(out=Y2[:], in_=io["yb"](a)), reads=[io["Byb"]], writes=[BY2])
        kb.dma("sp", lambda e, a=a: e.dma_start(out=BN[:], in_=io["bon"](a)), reads=[io["Bbg"]], writes=[BBN])
        kb.dma("act", lambda e, a=a: e.dma_start(out=GT[:], in_=io["gate"](a)), reads=[io["Bbg"]], writes=[BGT])
        kb.op("dve", lambda e: e.tensor_tensor(out=Y[:], in0=Y[:], in1=Y2[:], op=ALU.add), reads=[BY, BY2], writes=[BY])
        for (t0, tn) in token_tiles(TALL):
            ts_ = slice(t0, t0 + tn)
            pm, Bpm = nps()
            kb.op("pe", lambda e, pm=pm, ts_=ts_, tn=tn: e.matmul(pm[:, 0:tn], lhsT=cst[:], rhs=Y[:, ts_], start=True, stop=True), reads=[Bcst, BY], writes=[Bpm])
            kb.op("dve", lambda e, pm=pm, ts_=ts_, tn=tn: e.tensor_tensor(out=Y[:, ts_], in0=Y[:, ts_], in1=pm[:, 0:tn], op=ALU.subtract),
                  reads=[Bpm, BY], writes=[BY])
            kb.op("act", lambda e, ts_=ts_: e.activation(out=TMP[:, ts_], in_=Y[:, ts_], func=AF.Square), reads=[BY], writes=[BTMP])
            pv, Bpv = nps()
            kb.op("pe", lambda e, pv=pv, ts_=ts_, tn=tn: e.matmul(pv[:, 0:tn], lhsT=cst[:], rhs=TMP[:, ts_], start=True, stop=True), reads=[Bcst, BTMP], writes=[Bpv])
            kb.op("act", lambda e, pv=pv, ts_=ts_, tn=tn: e.activation(out=TMP[:, ts_], in_=pv[:, 0:tn], func=AF.Sqrt, bias=eps_ap(C, RWKV_GN_EPS), scale=1.0),
                  reads=[Bpv], writes=[BTMP])
            kb.op("dve", lambda e, ts_=ts_: e.reciprocal(out=TMP[:, ts_], in_=TMP[:, ts_]), reads=[BTMP], writes=[BTMP])
            kb.op("dve", lambda e, ts_=ts_: e.tensor_tensor(out=Y[:, ts_], in0=Y[:, ts_], in1=TMP[:, ts_], op=ALU.mult), reads=[BY, BTMP], writes=[BY])
            kb.op("act", lambda e, ts_=ts_, a=a: e.activation(out=Y[:, ts_], in_=Y[:, ts_], func=AF.Identity, bias=ln[:, a, 1:2], scale=ln[:, a, 0:1]),
                  reads=[BY, Bln], writes=[BY])
            kb.op("dve", lambda e, ts_=ts_: e.tensor_tensor(out=Y[:, ts_], in0=Y[:, ts_], in1=BN[:, ts_], op=ALU.add), reads=[BY, BBN], writes=[BY])
            kb.op("dve", lambda e, ts_=ts_: e.tensor_tensor(out=Y[:, ts_], in0=Y[:, ts_], in1=GT[:, ts_], op=ALU.mult), reads=[BY, BGT], writes=[BY])
        kb.dma("sp", lambda e, a=a: e.dma_start(out=io["ya"](a), in_=Y[:]), reads=[BY], writes=[io["By"]])
    C.pop()


def emit_p3(C, io, segs):
    kb = C.kb
    C.push()
    yT = io["yT"]
    xT = io["xT"]
    msk = io["msk"]
    wo = io["wo"]
    wu = io["wu"]
    wc = io["wc"]
    wd = io["wd"]
    xo = io["xo"]
    SM = max(s[1] for s in segs); TM_ = SM + 2

    def T(shape, dt=F32):
        return C.sb(shape, dt), Buf()
    x_sb, Bx = T([128, 16, TM_]); y_sb, By = T([128, 16, TM_], BF16); o_sb, Bo = T([128, 16, TM_])
    a_sb, Ba = T([128, NFF, SM], BF16)
    v_sb, Bv = T([128, 2, 7, 16]); m_sb, Bm = T([128, 2 * len(segs)]); wc_sb, Bwc = T([128, 2 * NFF, 3])
    gm, Bgm = T([128, 2, 4, 16])
    ones, Bones = T([128, 128]); rstd, Brstd = T([128, TM_])
    sq = [T([128, 512]) for _ in range(2)]
    tmp = [T([128, TM_]) for _ in range(2)]
    wt = [T([128, 16, 256], BF16) for _ in range(4)]
    wdt = [T([128, NFF, 128], BF16) for _ in range(2)]
    ug = [T([128, 2, TM_]) for _ in range(2)]
    cvt = [T([128, 2, SM]) for _ in range(2)]
    pss = [(C.ps([128, 512]), Buf(excl=True)) for _ in range(2)]
    psm = [(C.ps([128, 512]), Buf(excl=True)) for _ in range(6)]
    pcnt = [0]

    def nps():
        r = psm[pcnt[0] % 6]
        pcnt[0] += 1
        return r
    for (ci_, vi, vsrc) in io["vec_srcs"]:
        kb.dma("sp", lambda e, ci_=ci_, vi=vi, vsrc=vsrc: e.dma_start(out=v_sb[:, ci_, vi, :], in_=vsrc), writes=[Bv])
    kb.dma("sp", lambda e: e.dma_start(out=m_sb[:], in_=msk), writes=[Bm])
    kb.dma("sp", lambda e: e.dma_start(out=wc_sb[:], in_=wc), writes=[Bwc])
    kb.op("dve", lambda e: e.memset(ones[:], 1.0 / D), writes=[Bones])
    for ci in range(2):
        kb.op("dve", lambda e, ci=ci: e.tensor_tensor(out=gm[:, ci, 0, :], in0=v_sb[:, ci, 0, :], in1=v_sb[:, ci, 3, :], op=ALU.mult), reads=[Bv], writes=[Bgm])
        kb.op("dve", lambda e, ci=ci: e.scalar_tensor_tensor(out=gm[:, ci, 1, :], in0=v_sb[:, ci, 5, :], scalar=1.0, in1=v_sb[:, ci, 1, :],
                                                             op0=ALU.add, op1=ALU.mult), reads=[Bv], writes=[Bgm])
        kb.op("dve", lambda e, ci=ci: e.tensor_tensor(out=gm[:, ci, 2, :], in0=v_sb[:, ci, 2, :], in1=v_sb[:, ci, 6, :], op=ALU.mult), reads=[Bv], writes=[Bgm])
    xv = xT.rearrange("(kc p) t -> p kc t", p=128)
    yv = yT.rearrange("(kc p) t -> p kc t", p=128)
    xov = xo.rearrange("(kc p) t -> p kc t", p=128)
    wi = [0]

    def half_tiles(n):
        h = (n + 1) // 2
        return [(0, h), (h, n - h)] if n > 512 else [(0, n)]
    for si, (lo, S, ci, hl, hr) in enumerate(segs):
        Tn = S + 2
        out0 = lo
        a0 = 0 if hl else 1
        a1 = Tn if hr else Tn - 1
        g0 = lo - 1 + a0
        if not hl:
            kb.op("dve", lambda e: e.memset(x_sb[:, :, 0:1], 0.0), writes=[Bx])
            kb.op("dve", lambda e: e.memset(y_sb[:, :, 0:1], 0.0), writes=[By])
        if not hr:
            kb.op("dve", lambda e, Tn=Tn: e.memset(x_sb[:, :, Tn - 1:Tn], 0.0), writes=[Bx])
            kb.op("dve", lambda e, Tn=Tn: e.memset(y_sb[:, :, Tn - 1:Tn], 0.0), writes=[By])
        for kc in range(16):
            kb.dma("sp" if kc % 2 == 0 else "act", lambda e, kc=kc, a0=a0, a1=a1, g0=g0: e.dma_start(out=x_sb[:, kc, a0:a1], in_=xv[:, kc, g0:g0 + a1 - a0]), reads=[io["Bx"]], writes=[Bx])
        for kc in range(16):
            kb.dma("pool", lambda e, kc=kc, a0=a0, a1=a1, g0=g0: e.dma_start(out=y_sb[:, kc, a0:a1], in_=yv[:, kc, g0:g0 + a1 - a0]), reads=[io["By"]], writes=[By])
        for nb in range(8):
            j = wi[0] % 4; wi[0] += 1
            w_, Bw_ = wt[j]
            kb.dma(io["wdma"](), lambda e, w_=w_, nb=nb: e.dma_start(out=w_[:], in_=wo[:, nb * 256:(nb + 1) * 256].rearrange("(kc p) n -> p kc n", p=128)), writes=[Bw_])
            for nc_ in range(2):
                dch = nb * 2 + nc_
                for (t0, tn) in half_tiles(Tn):
                    p, Bp = nps()
                    for kc in range(16):
                        kb.op("pe", lambda e, p=p, w_=w_, kc=kc, nc_=nc_, t0=t0, tn=tn: e.matmul(p[:, 0:tn], lhsT=w_[:, kc, nc_ * 128:(nc_ + 1) * 128],
                                                                                            rhs=y_sb[:, kc, t0:t0 + tn], start=(kc == 0), stop=(kc == 15)),
                              reads=[Bw_, By], writes=[Bp])
                    kb.op("act", lambda e, p=p, dch=dch, t0=t0, tn=tn: e.copy(out=o_sb[:, dch, t0:t0 + tn], in_=p[:, 0:tn]), reads=[Bp], writes=[Bo])
        emit_rstd(C, o_sb, Bo, 16, Tn, ones, Bones, rstd, Brstd, [s[0] for s in sq], [s[1] for s in sq], [p[0] for p in pss], [p[1] for p in pss], NORM_EPS)
        for kc in range(16):
            t_, Bt_ = tmp[kc % 2]
            kb.op("pool", lambda e, t_=t_, kc=kc, Tn=Tn: e.tensor_tensor(out=t_[:, 0:Tn], in0=o_sb[:, kc, 0:Tn], in1=rstd[:, 0:Tn], op=ALU.mult),
                  reads=[Bo, Brstd], writes=[Bt_])
            kb.op("dve", lambda e, t_=t_, kc=kc, Tn=Tn, ci=ci: e.scalar_tensor_tensor(out=x_sb[:, kc, 0:Tn], in0=t_[:, 0:Tn], scalar=gm[:, ci, 0, kc:kc + 1],
                                                                                 in1=x_sb[:, kc, 0:Tn], op0=ALU.mult, op1=ALU.add),
                  reads=[Bt_, Bgm, Bx], writes=[Bx])
        emit_rstd(C, x_sb, Bx, 16, Tn, ones, Bones, rstd, Brstd, [s[0] for s in sq], [s[1] for s in sq], [p[0] for p in pss], [p[1] for p in pss], NORM_EPS)
        for kc in range(16):
            t_, Bt_ = tmp[kc % 2]
            kb.op("dve", lambda e, t_=t_, kc=kc, Tn=Tn: e.tensor_tensor(out=t_[:, 0:Tn], in0=x_sb[:, kc, 0:Tn], in1=rstd[:, 0:Tn], op=ALU.mult),
                  reads=[Bx, Brstd], writes=[Bt_])
            kb.op("act", lambda e, t_=t_, kc=kc, Tn=Tn, ci=ci: e.activation(out=y_sb[:, kc, 0:Tn], in_=t_[:, 0:Tn], func=AF.Identity,
                                                                        bias=v_sb[:, ci, 4, kc:kc + 1], scale=gm[:, ci, 1, kc:kc + 1]),
                  reads=[Bt_, Bgm, Bv], writes=[By])
        for blk in range(NFF // 2):
            wts = []
            for half in range(2):
                j = wi[0] % 4; wi[0] += 1
                w_, Bw_ = wt[j]
                c0 = half * D_FF + blk * 256
                kb.dma(io["wdma"](), lambda e, w_=w_, c0=c0: e.dma_start(out=w_[:], in_=wu[:, c0:c0 + 256].rearrange("(kc p) n -> p kc n", p=128)), writes=[Bw_])
                wts.append((w_, Bw_))
            for cc in range(2):
                c = blk * 2 + cc
                u_, Bu_ = ug[c % 2]
                cv_, Bcv_ = cvt[c % 2]
                for half in range(2):
                    w_, Bw_ = wts[half]
                    for (t0, tn) in half_tiles(Tn):
                        p, Bp = nps()
                        for kc in range(16):
                            kb.op("pe", lambda e, p=p, w_=w_, kc=kc, cc=cc, t0=t0, tn=tn: e.matmul(p[:, 0:tn], lhsT=w_[:, kc, cc * 128:(cc + 1) * 128],
                                                                                              rhs=y_sb[:, kc, t0:t0 + tn], start=(kc == 0), stop=(kc == 15)),
                                  reads=[Bw_, By], writes=[Bp])
                        kb.op("act", lambda e, p=p, u_=u_, half=half, t0=t0, tn=tn: e.copy(out=u_[:, half, t0:t0 + tn], in_=p[:, 0:tn]), reads=[Bp], writes=[Bu_])
                kb.op("dve", lambda e, u_=u_, si=si: e.tensor_tensor(out=u_[:, :, 0:1], in0=u_[:, :, 0:1], in1=m_sb[:, 2 * si:2 * si + 1].unsqueeze(1).broadcast_to([128, 2, 1]),
                                                                      op=ALU.mult), reads=[Bu_, Bm], writes=[Bu_])
                kb.op("dve", lambda e, u_=u_, si=si, Tn=Tn: e.tensor_tensor(out=u_[:, :, Tn - 1:Tn], in0=u_[:, :, Tn - 1:Tn],
                                                                             in1=m_sb[:, 2 * si + 1:2 * si + 2].unsqueeze(1).broadcast_to([128, 2, 1]), op=ALU.mult),
                      reads=[Bu_, Bm], writes=[Bu_])
                for half in range(2):
                    wrow = half * NFF + c
                    kb.op("act", lambda e, u_=u_, cv_=cv_, half=half, wrow=wrow, S=S: e.activation(out=cv_[:, half, 0:S], in_=u_[:, half, 1:1 + S], func=AF.Copy,
                                                                                             scale=wc_sb[:, wrow, 1:2]), reads=[Bu_, Bwc], writes=[Bcv_])
                    eng = "dve"
                    kb.op(eng, lambda e, u_=u_, cv_=cv_, half=half, wrow=wrow, S=S: e.scalar_tensor_tensor(out=cv_[:, half, 0:S], in0=u_[:, half, 0:S], scalar=wc_sb[:, wrow, 0:1],
                                                                                                    in1=cv_[:, half, 0:S], op0=ALU.mult, op1=ALU.add),
                          reads=[Bu_, Bwc, Bcv_], writes=[Bcv_])
                    kb.op(eng, lambda e, u_=u_, cv_=cv_, half=half, wrow=wrow, S=S: e.scalar_tensor_tensor(out=cv_[:, half, 0:S], in0=u_[:, half, 2:2 + S], scalar=wc_sb[:, wrow, 2:3],
                                                                                                    in1=cv_[:, half, 0:S], op0=ALU.mult, op1=ALU.add),
                          reads=[Bu_, Bwc, Bcv_], writes=[Bcv_])
                kb.op("act", lambda e, cv_=cv_, S=S: e.activation(out=cv_[:, 0, 0:S], in_=cv_[:, 0, 0:S], func=AF.Silu), reads=[Bcv_], writes=[Bcv_])
                kb.op("pool", lambda e, cv_=cv_, c=c, S=S: e.tensor_tensor(out=a_sb[:, c, 0:S], in0=cv_[:, 0, 0:S], in1=cv_[:, 1, 0:S], op=ALU.mult),
                      reads=[Bcv_], writes=[Ba])
        for nb in range(D // 128):
            w_, Bw_ = wdt[nb % 2]
            kb.dma(io["wdma"](), lambda e, w_=w_, nb=nb: e.dma_start(out=w_[:], in_=wd[nb]), writes=[Bw_])
            for nc_ in range(1):
                dch = nb
                p, Bp = nps()
                for c in range(NFF):
                    kb.op("pe", lambda e, p=p, w_=w_, c=c, nc_=nc_, S=S: e.matmul(p[:, 0:S], lhsT=w_[:, c, nc_ * 128:(nc_ + 1) * 128], rhs=a_sb[:, c, 0:S],
                                                                             start=(c == 0), stop=(c == NFF - 1)), reads=[Bw_, Ba], writes=[Bp])
                kb.op("act", lambda e, p=p, dch=dch, S=S: e.copy(out=o_sb[:, dch, 0:S], in_=p[:, 0:S]), reads=[Bp], writes=[Bo])
        emit_rstd(C, o_sb, Bo, 16, S, ones, Bones, rstd, Brstd, [s[0] for s in sq], [s[1] for s in sq], [p[0] for p in pss], [p[1] for p in pss], NORM_EPS)
        for kc in range(16):
            t_, Bt_ = tmp[kc % 2]
            kb.op("pool", lambda e, t_=t_, kc=kc, S=S: e.tensor_tensor(out=t_[:, 0:S], in0=o_sb[:, kc, 0:S], in1=rstd[:, 0:S], op=ALU.mult),
                  reads=[Bo, Brstd], writes=[Bt_])
            kb.op("dve", lambda e, t_=t_, kc=kc, S=S, ci=ci: e.scalar_tensor_tensor(out=t_[:, 0:S], in0=t_[:, 0:S], scalar=gm[:, ci, 2, kc:kc + 1],
                                                                               in1=x_sb[:, kc, 1:1 + S], op0=ALU.mult, op1=ALU.add),
                  reads=[Bt_, Bgm, Bx], writes=[Bt_])
            kb.dma("sp", lambda e, t_=t_, kc=kc, S=S, out0=out0: e.dma_start(out=xov[:, kc, out0:out0 + S], in_=t_[:, 0:S]), reads=[Bt_], writes=[io["Bxo"]])
    C.pop()


def _ctx_push(self):
    self._stk.append(self.st)
    self.st = ExitStack()


def _ctx_pop(self):
    self.kb.barrier()
    self.st.close()
    self.st = self._stk.pop()


def _ctx_scratch(self, name, shape, dt=F32):
    return self.nc.dram_tensor(name, list(shape), dt, kind="Internal").ap()


Ctx.push = _ctx_push
Ctx.pop = _ctx_pop
Ctx.scratch = _ctx_scratch


def emit_p0f(C, io, nl):
    kb = C.kb
    C.push()

    def T(shape, dt=F32):
        return C.sb(shape, dt), Buf()
    c_sb, Bc = T([128, 16, 2]); s_sb, Bs = T([128, 16, 2]); mb_sb, Bmb = T([128, nl, 96]); mv, Bmv = T([128, 2, nl, 96])
    wt = [T([128, 16, 512]) for _ in range(4)]
    pst = [(C.ps([128, 512]), Buf(excl=True)) for _ in range(4)]
    kb.dma("sp", lambda e: e.dma_start(out=c_sb[:], in_=io["cT"]), writes=[Bc])
    kb.dma("sp", lambda e: e.dma_start(out=mb_sb[:], in_=io["mb"]), writes=[Bmb])
    kb.op("act", lambda e: e.activation(out=s_sb[:], in_=c_sb[:], func=AF.Silu), reads=[Bc], writes=[Bs])
    it = 0
    for l in range(nl):
        for nt in range(24):
            w, Bw = wt[it % 4]; p, Bp = pst[it % 4]
            src = io["mw"][l, :, nt * 512:(nt + 1) * 512].rearrange("(kc p) n -> p kc n", p=128)
            kb.dma("sp" if it % 2 == 0 else "act", lambda e, w=w, src=src: e.dma_start(out=w[:], in_=src), writes=[Bw])
            for cc in range(4):
                for kc in range(16):
                    kb.op("pe", lambda e, p=p, w=w, cc=cc, kc=kc: e.matmul(p[:, cc * 2:cc * 2 + 2], lhsT=w[:, kc, cc * 128:(cc + 1) * 128], rhs=s_sb[:, kc, :],
                                                                         start=(kc == 0), stop=(kc == 15)), reads=[Bw, Bs], writes=[Bp])
            g0 = nt * 4
            kb.op("dve", lambda e, p=p, l=l, g0=g0: e.tensor_tensor(out=mv[:, :, l, g0:g0 + 4], in0=p[:, 0:8].rearrange("p (c r) -> p r c", r=2),
                                                                   in1=mb_sb[:, l, g0:g0 + 4].unsqueeze(1).broadcast_to([128, 2, 4]), op=ALU.add),
                  reads=[Bp, Bmb], writes=[Bmv])
            it += 1
    kb.dma("sp", lambda e: e.dma_start(out=io["mscr"], in_=mv[:]), reads=[Bmv], writes=[Buf()])
    C.pop()


def emit_reverse(C, io, jobs):
    kb = C.kb
    C.push()

    def T(shape, dt=F32):
        return C.sb(shape, dt), Buf()
    cst, Bcst = T([128, 256])
    kb.dma("sp", lambda e: e.dma_start(out=cst[:, 0:128], in_=io["ident"]), writes=[Bcst])
    kb.dma("sp", lambda e: e.dma_start(out=cst[:, 128:256], in_=io["jmat"]), writes=[Bcst])
    ident = cst[:, 0:128]; J = cst[:, 128:256]
    X = [T([128, TALL]) for _ in range(2)]
    XR = [T([128, TALL]) for _ in range(2)]
    TK = [T([128, 512]) for _ in range(2)]
    psl = [(C.ps([128, 512]), Buf(excl=True)) for _ in range(4)]
    pc = [0]

    def nps():
        r = psl[pc[0] % 4]; pc[0] += 1
        return r
    for ji, (src, dst, n, off_c, off_l) in enumerate(jobs):
        x, Bx = X[ji % 2]; xr, Bxr = XR[ji % 2]
        kb.dma("sp" if ji % 2 == 0 else "act", lambda e, x=x, src=src, n=n: e.dma_start(out=x[0:n, :], in_=src), writes=[Bx])
        for b4 in range(0, NCH, 4):
            nb = min(4, NCH - b4)
            tk, Btk = TK[(b4 // 4) % 2]
            ps, Bps = nps()
            for bi in range(nb):
                b = b4 + bi
                kb.op("pe", lambda e, ps=ps, bi=bi, b=b, x=x, n=n: e.transpose(out=ps[:, bi * 128:bi * 128 + n], in_=x[0:n, b * 128:(b + 1) * 128], identity=ident[0:n, 0:n]),
                      reads=[Bx, Bcst], writes=[Bps])
            kb.op("act", lambda e, ps=ps, tk=tk, nb=nb, n=n: e.copy(out=tk[:, 0:nb * 128].rearrange("p (a b) -> p a b", a=nb)[:, :, 0:n],
                                                                   in_=ps[:, 0:nb * 128].rearrange("p (a b) -> p a b", a=nb)[:, :, 0:n]), reads=[Bps], writes=[Btk])
            ps2, Bps2 = nps()
            for bi in range(nb):
                kb.op("pe", lambda e, ps2=ps2, bi=bi, tk=tk, n=n: e.matmul(ps2[0:n, bi * 128:(bi + 1) * 128], lhsT=tk[:, bi * 128:bi * 128 + n], rhs=J, start=True, stop=True),
                      reads=[Btk, Bcst], writes=[Bps2])
            for bi in range(nb):
                b = b4 + bi
                mb_ = (1 - b) if b < 2 else (2 + (15 - (b - 2)))
                kb.op("dve", lambda e, ps2=ps2, bi=bi, mb_=mb_, xr=xr, n=n: e.tensor_copy(out=xr[0:n, mb_ * 128:(mb_ + 1) * 128], in_=ps2[0:n, bi * 128:(bi + 1) * 128]),
                      reads=[Bps2], writes=[Bxr])
        kb.dma("sp", lambda e, xr=xr, dst=dst, n=n, off_c=off_c: e.dma_start(out=dst[:, off_c:off_c + CTX], in_=xr[0:n, 0:CTX]), reads=[Bxr], writes=[Buf()])
        kb.dma("act", lambda e, xr=xr, dst=dst, n=n, off_l=off_l: e.dma_start(out=dst[:, off_l:off_l + SEQ], in_=xr[0:n, CTX:TALL]), reads=[Bxr], writes=[Buf()])
    C.pop()


def cast_jobs(jobs):
    out = []
    for (dst, src, rows, cols, rblk, cb) in jobs:
        for r0 in range(0, rows, rblk):
            n = min(rblk, rows - r0)
            out.append((dst, src, r0, n, cb))
    return out


def emit_cast_one(C, job):
    if job[0] == "dn":
        _, dst, src, nb = job
        C.kb.dma("pool", lambda e: e.dma_start(out=dst[nb], in_=src[:, nb * 128:(nb + 1) * 128].rearrange("(c p) n -> p c n", p=128)), writes=[Buf()], bg=True)
        return
    dst, src, r0, n, cb = job
    C.kb.dma("pool", lambda e: e.dma_start(out=dst[r0:r0 + n, :].rearrange("r (a b) -> r a b", b=cb),
                                           in_=src[r0:r0 + n, :].rearrange("r (a b) -> r a b", b=cb)), writes=[Buf()], bg=True)


def emit_cast(C, jobs):
    for j in cast_jobs(jobs):
        emit_cast_one(C, j)


_WQ = [0]


def _wdma():
    _WQ[0] += 1
    return "sp" if _WQ[0] % 2 == 0 else "act"


P3F_SEGS = [(0, 256, 0, False, False), (256, 410, 1, False, True), (666, 410, 1, True, True), (1076, 410, 1, True, True), (1486, 410, 1, True, True), (1896, 408, 1, True, False)]


def build_fused(nl=DEPTH, dbg=False):
    C = Ctx()
    C._stk = []
    kb = C.kb
    for eps in (NORM_EPS, 1e-12, RET_EPS, RWKV_GN_EPS):
        make_eps(C, eps)
    I = C.dram_in
    x0T = I("x0T", [D, TALL]); cT = I("cT", [128, 16, 2]); mw = I("mw", [nl, D, NMOD * D]); mb = I("mb", [128, nl, 96]); ng = I("ng", [128, nl, 4, 16])
    w_in = I("w_in", [nl, D, IN_W]); w_out = I("w_out", [nl, D, D]); w_up = I("w_up", [nl, D, 2 * D_FF]); w_cv = I("w_cv", [nl, 128, 2 * NFF, 3]); w_dn = I("w_dn", [nl, D_FF, D])
    r_mu = I("r_mu", [nl, 2, 64, 24, 2]); r_mu2 = I("r_mu2", [nl, 2, 128, 4, 2]); r_par = I("r_par", [nl, 2, 64, 5, 8])
    r_wup = I("r_wup", [nl, 2, 96, 512]); r_aup = I("r_aup", [nl, 2, 96, 512]); r_gup = I("r_gup", [nl, 256, 512])
    r_cst = I("r_cst", [128, 256]); r_mk = I("r_mk", [64, 2112]); r_ln = I("r_ln", [nl, 128, 4, 2]); c_w = I("c_w", [nl, 128, 4, 3]); c_bd = I("c_bd", [128, 128])
    a_nrm = I("a_nrm", [nl, 128, 5]); a_wq = I("a_wq", [nl, 128, 3, 4, 256]); a_wkv = I("a_wkv", [nl, 128, 2, 4, 256]); a_rope = I("a_rope", [2, 64, TALL]); a_cst = I("a_cst", [128, 384])
    d_rope = I("d_rope", [2, 128, TALL]); d_rd = I("d_rd", [nl, 128, 8]); d_gn = I("d_gn", [nl, 128, 4]); d_tab = I("d_tab", [128, 770]); d_cst = I("d_cst", [128, 256])
    p3_msk = I("p3_msk", [128, 12]); jmat = I("jmat", [128, 128]); identm = I("identm", [128, 128])
    xo = C.dram_out("xo", [D, SEQ])
    S = C.scratch
    xs = [S("xsA", [D, TALL]), S("xsB", [D, TALL])]
    pT = S("pT", [IN_W, TALL]); yT = S("yT", [D, TALL]); mscr = S("mscr", [128, 2, nl, 96])
    u_f = S("u_f", [1536, TP]); u2_f = S("u2_f", [512, TP]); u_r = S("u_r", [1536, TP]); u2_r = S("u2_r", [512, TP])
    y_f = S("y_f", [512, TALL]); y_r = S("y_r", [512, TALL]); y_b = S("y_b", [512, TALL])
    bon = S("bon", [512, TALL]); gate = S("gate", [512, TALL]); bon2 = S("bon2", [512, TALL]); gate2 = S("gate2", [512, TALL])
    wb_in = S("wb_in", [D, IN_W], BF16); wb_out = S("wb_out", [D, D], BF16); wb_up = S("wb_up", [D, 2 * D_FF], BF16); wb_dn = S("wb_dn", [16, 128, NFF, 128], BF16)
    emit_cast(C, [(wb_in, w_in[0], D, IN_W, 512, 896)])
    dbg_outs = {}
    if dbg:
        dbg_outs = {"d_pT": C.dram_out("d_pT", [IN_W, TALL]), "d_yT": C.dram_out("d_yT", [D, TALL]), "d_x1": C.dram_out("d_x1", [D, TALL]),
                    "d_m": C.dram_out("d_m", [128, 2, nl, 96])}
    C.push()
    Z = C.sb([128, TP]); Bz = Buf()
    kb.op("dve", lambda e: e.memset(Z[:], 0.0), writes=[Bz])
    qi = 0
    for (t_, nr) in [(u_f, 1536), (u2_f, 512), (u_r, 1536), (u2_r, 512)]:
        for r0 in range(0, nr, 128):
            kb.dma("sp" if qi % 2 == 0 else "act", lambda e, t_=t_, r0=r0: e.dma_start(out=t_[r0:r0 + 128, :], in_=Z[:]), reads=[Bz], writes=[Buf()])
            qi += 1
    C.pop()
    emit_p0f(C, {"cT": cT, "mw": mw, "mb": mb, "mscr": mscr}, nl)
    if dbg:
        kb.dma("sp", lambda e: e.dma_start(out=dbg_outs["d_m"], in_=mscr), writes=[Buf()], is_out=True)
    sw64 = [(0, 16, 16), (16, 32, 0), (32, 48, 48), (48, 64, 32)]
    sw128 = [(0, 32, 32), (32, 64, 0), (64, 96, 96), (96, 128, 64)]
    x_cur = x0T
    for l in range(nl):
        x_next = xs[l % 2]
        for (c0, classes) in [(0, [(0, 0, CTX), (1, CTX, T1)]), (T1, [(1, 0, T1)])]:
            emit_p1(C, {"xT": x_cur[:, c0:c0 + T1], "w": wb_in, "wdma": _wdma, "pT": pT[:, c0:c0 + T1],
                        "vec_srcs": [ng[:, l, 0, :], mscr[:, 1, l, 0:16], mscr[:, 1, l, 16:32], mscr[:, 0, l, 0:16], mscr[:, 0, l, 16:32]]}, classes)
        if dbg and l == 0:
            kb.dma("sp", lambda e: e.dma_start(out=dbg_outs["d_pT"], in_=pT), writes=[Buf()], is_out=True)
        for (r0, n, dst, d0) in [(0, 1536, u_f, 0), (1536, 96, u2_f, 0), (1632, 96, u2_f, 128), (1728, 256, u2_f, 256)]:
            kb.dma("sp", lambda e, r0=r0, n=n, dst=dst, d0=d0: e.dma_start(out=dst[d0:d0 + n, 1:1 + CTX], in_=pT[r0:r0 + n, 0:CTX]), writes=[Buf()])
            kb.dma("act", lambda e, r0=r0, n=n, dst=dst, d0=d0: e.dma_start(out=dst[d0:d0 + n, 259:259 + SEQ], in_=pT[r0:r0 + n, CTX:TALL]), writes=[Buf()])
        jobs = [(pT[128 * i:128 * i + 128, :], u_r[128 * i:128 * i + 128, :], 128, 1, 259) for i in range(12)]
        jobs += [(pT[1536:1632, :], u2_r[0:96, :], 96, 1, 259), (pT[1632:1728, :], u2_r[128:224, :], 96, 1, 259)]
        emit_reverse(C, {"ident": identm, "jmat": jmat}, jobs)
        cj = [(wb_out, w_out[l], D, D, 1024, 1024), (wb_up, w_up[l], D, 2 * D_FF, 256, 1024), ]
        cjd = [("dn", wb_dn, w_dn[l], nb) for nb in range(16)]
        if l + 1 < nl:
            cj.append((wb_in, w_in[l + 1], D, IN_W, 512, 896))
        pending = cast_jobs(cj) + cjd

        def hook(pending=pending):
            if pending:
                emit_cast_one(C, pending.pop(0))
        for d, (u_, u2_, yo_, bo_, go_) in enumerate([(u_f, u2_f, y_f, bon, gate), (u_r, u2_r, y_r, bon2, gate2)]):
            emit_p2r(C, {"u": u_, "u2": u2_, "mu": r_mu[l, d], "mu2": r_mu2[l, d], "par": r_par[l, d], "wup": r_wup[l, d], "aup": r_aup[l, d],
                         "gup": r_gup[l], "cst": r_cst, "mk": r_mk, "yT": yo_, "bonT": bo_, "gateT": go_, "hook": hook, "aux": (d == 0)})
        while pending:
            hook()
        emit_reverse(C, {"ident": identm, "jmat": jmat}, [(y_r[128 * i:128 * i + 128, :], y_b[128 * i:128 * i + 128, :], 128, 0, CTX) for i in range(4)])
        dB = Buf()
        emit_p2c(C, {"cw": c_w[l], "ln": r_ln[l], "cst": c_bd, "cv": (lambda i, a: pT[2688 + 512 * i + 128 * a:2688 + 512 * i + 128 * a + 128, :]),
                     "yf": (lambda a: y_f[128 * a:128 * a + 128, :]), "yb": (lambda a: y_b[128 * a:128 * a + 128, :]),
                     "bon": (lambda a: bon[128 * a:128 * a + 128, :]), "gate": (lambda a: gate[128 * a:128 * a + 128, :]),
                     "yc": (lambda a: yT[1024 + 128 * a:1024 + 128 * a + 128, :]), "ya": (lambda a: yT[128 * a:128 * a + 128, :]),
                     "By": dB, "Byf": dB, "Byb": dB, "Bbg": dB})
        emit_p2b(C, {"nrm": a_nrm[l], "wq": a_wq[l], "wkv": a_wkv[l], "rope": a_rope, "cst": a_cst,
                     "cx": [pT[1984 + 128 * a:1984 + 128 * a + 128, :] for a in range(5)],
                     "kr": [(0, 64, 0, pT[2624:2688, :])] + [(a0, a1, 1, pT[2624 + s0:2624 + s0 + 16, :]) for (a0, a1, s0) in sw64],
                     "y": yT[512:1024, :], "By": dB})

        def qkv_src(h, a):
            base = 4224 + 128 * h
            if a < 4:
                return [(0, 128, pT[base + 512 * a:base + 512 * a + 128, :])]
            b2 = base + 512 * (a - 4)
            return [(a0, a1, pT[b2 + s0:b2 + s0 + 32, :]) for (a0, a1, s0) in sw128]
        emit_p2d(C, {"qkv": qkv_src, "rope": d_rope, "rd": d_rd[l], "gn": d_gn[l], "tab": d_tab, "cst": d_cst, "yT": yT[1536:2048, :]})
        if dbg and l == 0:
            kb.dma("sp", lambda e: e.dma_start(out=dbg_outs["d_yT"], in_=yT), writes=[Buf()], is_out=True)
        last = (l == DEPTH - 1) and (nl == DEPTH)
        segs = P3F_SEGS[1:] if last else P3F_SEGS
        msk_ap = p3_msk[:, 2:12] if last else p3_msk
        vs = []
        for ci_, r in ((0, 1), (1, 0)):
            for k in range(3):
                vs.append((ci_, k, ng[:, l, k + 1, :]))
            for k in range(4):
                vs.append((ci_, 3 + k, mscr[:, r, l, (2 + k) * 16:(3 + k) * 16]))
        emit_p3(C, {"yT": yT, "xT": x_cur, "vec_srcs": vs, "msk": msk_ap, "wo": wb_out, "wu": wb_up, "wc": w_cv[l], "wd": wb_dn, "wdma": _wdma, "xo": x_next,
                    "Bx": dB, "By": dB, "Bxo": dB}, segs)
        if dbg and l == 0:
            kb.dma("sp", lambda e, x_next=x_next: e.dma_start(out=dbg_outs["d_x1"], in_=x_next), writes=[Buf()], is_out=True)
        x_cur = x_next
    kb.barrier()
    kb.dma("sp", lambda e, x_cur=x_cur: e.dma_start(out=xo, in_=x_cur[:, CTX:TALL]), writes=[Buf()], is_out=True)
    return C.done()


_FUSED = {}


def _prep_inputs(b, nl, x, c, ctx, c_ctx, mod_w, mod_b, norm_g, w_in, rwkv_shift, rwkv_w0, rwkv_w_up, rwkv_a0, rwkv_a_up,
                 rwkv_g_up, rwkv_vecs, mla_q_norm, mla_kv_norm, mla_w_uq, mla_w_ukv, conv_w, ret_decay, ret_gn_g,
                 w_out, mlp_w_up, mlp_conv, mlp_w_down, shared):
    ca = np.ascontiguousarray
    im = dict(shared)
    im["x0T"] = ca(np.concatenate([ctx[b], x[b]], axis=0).T)
    cc = np.stack([c[b], c_ctx], axis=1)
    im["cT"] = ca(cc.reshape(16, 128, 2).transpose(1, 0, 2))
    return im


def _prep_shared(nl, mod_w, mod_b, norm_g, w_in, rwkv_shift, rwkv_w0, rwkv_w_up, rwkv_a0, rwkv_a_up,
                 rwkv_g_up, rwkv_vecs, mla_q_norm, mla_kv_norm, mla_w_uq, mla_w_ukv, conv_w, ret_decay, ret_gn_g,
                 w_out, mlp_w_up, mlp_conv, mlp_w_down):
    ca = np.ascontiguousarray
    sh = {}
    sh["mw"] = ca(mod_w[:nl]); sh["mb"] = ca(mod_b[:nl].reshape(nl, 96, 128).transpose(2, 0, 1))
    sh["ng"] = ca(norm_g[:nl].reshape(nl, 4, 16, 128).transpose(3, 0, 1, 2))
    sh["w_in"] = ca(w_in[:nl]); sh["w_out"] = ca(w_out[:nl]); sh["w_up"] = ca(mlp_w_up[:nl]); sh["w_dn"] = ca(mlp_w_down[:nl])
    sh["w_cv"] = ca(mlp_conv[:nl].transpose(0, 2, 1).reshape(nl, 2 * NFF, 128, 3).transpose(0, 2, 1, 3))
    r_mu = np.zeros((nl, 2, 64, 24, 2), np.float32); r_mu2 = np.zeros((nl, 2, 128, 4, 2), np.float32); r_par = np.zeros((nl, 2, 64, 5, 8), np.float32)
    for l in range(nl):
        for d in range(2):
            sh_ = rwkv_shift[l] if d == 0 else rwkv_shift[l][::-1]
            r_mu[l, d] = sh_[:, 0:1536].T.reshape(24, 64, 2).transpose(1, 0, 2)
            m2 = np.zeros((512, 2), np.float32)
            m2[0:96] = sh_[:, 1536:1632].T; m2[128:224] = sh_[:, 1632:1728].T; m2[256:512] = sh_[:, 1728:1984].T
            r_mu2[l, d] = m2.reshape(4, 128, 2).transpose(1, 0, 2)
            r_par[l, d] = np.stack([_tile8(rwkv_w0[l, d]), _tile8(rwkv_a0[l, d]), _tile8(rwkv_vecs[l, 0]), _tile8(rwkv_vecs[l, 1]), _tile8(rwkv_vecs[l, 2])], axis=1)
    sh["r_mu"] = r_mu; sh["r_mu2"] = r_mu2; sh["r_par"] = r_par
    sh["r_wup"] = ca(rwkv_w_up[:nl]); sh["r_aup"] = ca(rwkv_a_up[:nl]); sh["r_gup"] = ca(rwkv_g_up[:nl])
    cst, mk = _rw_consts()
    sh["r_cst"] = cst; sh["r_mk"] = mk
    sh["r_ln"] = ca(np.stack([rwkv_vecs[:nl, 3].reshape(nl, 4, 128), rwkv_vecs[:nl, 4].reshape(nl, 4, 128)], axis=-1).transpose(0, 2, 1, 3))
    sh["c_w"] = ca(conv_w[:nl].reshape(nl, 3, 4, 128).transpose(0, 3, 2, 1))
    sh["c_bd"] = (np.kron(np.eye(2, dtype=np.float32), np.ones((64, 64), np.float32)) / 64.0).astype(np.float32)
    sh["a_nrm"] = ca(np.concatenate([mla_q_norm[:nl].reshape(nl, 3, 128), mla_kv_norm[:nl].reshape(nl, 2, 128)], axis=1).transpose(0, 2, 1))
    sw = _rope_swap_idx(64)
    wq = np.zeros((nl, 384, 4, 256), np.float32); wkv = np.zeros((nl, 256, 4, 256), np.float32)
    for l in range(nl):
        for h in range(4):
            wh = mla_w_uq[l][:, 192 * h:192 * h + 192]
            wq[l, :, h, 0:192] = wh; wq[l, :, h, 192:256] = wh[:, 128:192][:, sw]
            wkv[l, :, h, :] = mla_w_ukv[l][:, 256 * h:256 * h + 256]
    sh["a_wq"] = ca(wq.reshape(nl, 3, 128, 4, 256).transpose(0, 2, 1, 3, 4)); sh["a_wkv"] = ca(wkv.reshape(nl, 2, 128, 4, 256).transpose(0, 2, 1, 3, 4))
    cos, sin = _rope_tables(64); sh["a_rope"] = ca(np.stack([cos, sin]))
    sh["a_cst"] = np.concatenate([np.eye(128, dtype=np.float32), np.full((128, 128), 1.0 / 384, np.float32), np.full((128, 128), 1.0 / 256, np.float32)], axis=1)
    cos, sin = _rope_tables(128); sh["d_rope"] = ca(np.stack([cos, sin]))
    sh["d_rd"] = ca(np.tile(ret_decay[:nl].reshape(nl, 1, 8), (1, 128, 1)))
    sh["d_gn"] = ca(ret_gn_g[:nl].reshape(nl, 4, 128).transpose(0, 2, 1))
    j = np.arange(128)[:, None]; i = np.arange(128)[None, :]
    tab = np.zeros((128, 770), np.float32)
    tab[:, 0:128] = np.maximum(i - j, 0); tab[:, 128:256] = (i >= j)
    tab[:, 256:384] = np.maximum(j - i, 0); tab[:, 384:512] = (j > i)
    tab[:, 512:640] = (i + 1); tab[:, 640:768] = (128 - i)
    tab[:, 768] = 127 - np.arange(128); tab[:, 769] = np.arange(128)
    sh["d_tab"] = tab
    sh["d_cst"] = np.concatenate([np.eye(128, dtype=np.float32), np.full((128, 128), 1.0 / 128, np.float32)], axis=1)
    mk_ = []
    for (_, _, _, hl, hr) in P3F_SEGS:
        mk_ += [float(hl), float(hr)]
    sh["p3_msk"] = ca(np.tile(np.array(mk_, np.float32)[None], (128, 1)))
    sh["jmat"] = ca(np.eye(128, dtype=np.float32)[::-1]); sh["identm"] = np.eye(128, dtype=np.float32)
    return sh


def run_fused(inputs, nl=DEPTH, dbg=False):
    key = (nl, dbg)
    if key not in _FUSED:
        _FUSED[key] = build_fused(nl, dbg)
    f = lambda a: np.ascontiguousarray(np.asarray(a, dtype=np.float32))
    inp = {k: f(v) for k, v in inputs.items()}
    names = ["mod_w", "mod_b", "norm_g", "w_in", "rwkv_shift", "rwkv_w0", "rwkv_w_up", "rwkv_a0", "rwkv_a_up", "rwkv_g_up", "rwkv_vecs", "mla_q_norm",
             "mla_kv_norm", "mla_w_uq", "mla_w_ukv", "conv_w", "ret_decay", "ret_gn_g", "w_out", "mlp_w_up", "mlp_conv", "mlp_w_down"]
    shared = _prep_shared(nl, *[inp[k] for k in names])
    in_maps = []
    for core in range(NCORES):
        b = core % BATCH
        in_maps.append(_prep_inputs(b, nl, inp["x"], inp["c"], inp["ctx"], inp["c_ctx"], *[None] * 22, shared))
    res = run_bass_kernel_spmd(_FUSED[key], in_maps, core_ids=list(range(NCORES)))
    return res


def kernel(**inputs):
    res = run_fused(inputs)
    out = np.stack([res.results[b]["xo"].T for b in range(BATCH)], axis=0)
    return np.ascontiguousarray(out).astype(np.float32)
```

```python
import math
from contextlib import ExitStack
import numpy as np
import concourse.bass as bass
import concourse.mybir as mybir
from concourse.bass_utils import run_bass_kernel_spmd

F32 = mybir.dt.float32
BF16 = mybir.dt.bfloat16
AF = mybir.ActivationFunctionType
ALU = mybir.AluOpType
AX = mybir.AxisListType

N_DMA_SEMS = 12
N_BG_SEMS = 12
NCORES = 8

D = 2048
DEPTH = 4
BATCH = 4
SEQ = 2048
CTX = 256
TALL = SEQ + CTX
IN_W = 6272
D_FF = 5632
NMOD = 6
NORM_EPS = 1e-6


class Buf:
    __slots__ = ("name", "w", "r", "excl")

    def __init__(self, name="", excl=False):
        self.name = name
        self.w = {}
        self.r = {}
        self.excl = excl


class KB:
    def __init__(self, nc, stack):
        self.nc = nc
        self.engs = ["pe", "act", "dve", "pool", "sp"]
        self.ops = {e: [] for e in self.engs}
        self.cnt = {e: 0 for e in self.engs}
        self.waited = {e: {} for e in self.engs}
        self.sems = {}
        for e in self.engs:
            self.sems[e] = stack.enter_context(nc.semaphore("s_" + e))
        for i in range(N_DMA_SEMS):
            self.sems["d%d" % i] = stack.enter_context(nc.semaphore("s_d%d" % i))
        for i in range(N_BG_SEMS):
            self.sems["c%d" % i] = stack.enter_context(nc.semaphore("s_c%d" % i))
        self.dcnt = [0] * N_DMA_SEMS
        self.dnext = 0
        self.ccnt = [0] * N_BG_SEMS
        self.cnext = 0
        self.out_toks = []

    def _deps(self, reads, writes):
        deps = {}

        def add(dd):
            for sk, v in dd.items():
                if deps.get(sk, 0) < v:
                    deps[sk] = v
        for b in reads:
            add(b.w)
            if b.excl:
                add(b.r)
        for b in writes:
            add(b.w)
            add(b.r)
        return deps

    def _emit_waits(self, eng, deps, skip_self=False):
        for sk, v in deps.items():
            if skip_self and sk == eng:
                continue
            if self.waited[eng].get(sk, 0) >= v:
                continue
            self.waited[eng][sk] = v
            self.ops[eng].append(("w", self.sems[sk], v))

    @staticmethod
    def _mark(tok, reads, writes):
        sk, v = tok
        for b in reads:
            if b.r.get(sk, 0) < v:
                b.r[sk] = v
        for b in writes:
            if b.w.get(sk, 0) < v:
                b.w[sk] = v

    def op(self, eng, fn, reads=(), writes=()):
        deps = self._deps(reads, writes)
        self._emit_waits(eng, deps, skip_self=(eng == "pe"))
        self.cnt[eng] += 1
        tok = (eng, self.cnt[eng])
        self.ops[eng].append(("i", fn, self.sems[eng], 1))
        self._mark(tok, reads, writes)
        return tok

    def dma(self, eng, fn, reads=(), writes=(), is_out=False, bg=False):
        deps = self._deps(reads, writes)
        if bg:
            i = self.cnext
            self.cnext = (self.cnext + 1) % N_BG_SEMS
            sk = "c%d" % i
            cnts = self.ccnt
        else:
            i = self.dnext
            self.dnext = (self.dnext + 1) % N_DMA_SEMS
            sk = "d%d" % i
            cnts = self.dcnt
        if cnts[i] > 0:
            v = 16 * cnts[i]
            if deps.get(sk, 0) < v:
                deps[sk] = v
        self._emit_waits(eng, deps)
        cnts[i] += 1
        tok = (sk, 16 * cnts[i])
        self.ops[eng].append(("i", fn, self.sems[sk], 16))
        self._mark(tok, reads, writes)
        if is_out:
            self.out_toks.append(tok)
        return tok

    def barrier(self):
        deps = {e: self.cnt[e] for e in self.engs if self.cnt[e] > 0}
        for i in range(N_DMA_SEMS):
            if self.dcnt[i] > 0:
                deps["d%d" % i] = 16 * self.dcnt[i]
        for i in range(N_BG_SEMS):
            if self.ccnt[i] > 0:
                deps["c%d" % i] = 16 * self.ccnt[i]
        for e in self.engs:
            self._emit_waits(e, dict(deps))

    def finish(self, block):
        deps = {}
        for sk, v in self.out_toks:
            if deps.get(sk, 0) < v:
                deps[sk] = v
        self._emit_waits("sp", deps)
        m = {"pe": block.tensor, "act": block.scalar, "dve": block.vector,
             "pool": block.gpsimd, "sp": block.sync}

        def mk(e):
            lst = self.ops[e]

            def body(engine):
                for it in lst:
                    if it[0] == "w":
                        engine.wait_ge(it[1], it[2])
                    else:
                        it[1](engine).then_inc(it[2], it[3])
            return body

        for e in self.engs:
            if self.ops[e]:
                m[e](mk(e))


class Ctx:
    def __init__(self, name="k"):
        self.nc = bass.Bass("TRN2", target_bir_lowering=False)
        self.st = ExitStack()
        self.kb = KB(self.nc, self.st)
        self.n = 0

    def dram_in(self, name, shape, dt=F32):
        return self.nc.dram_tensor(name, list(shape), dt, kind="ExternalInput").ap()

    def dram_out(self, name, shape, dt=F32):
        return self.nc.dram_tensor(name, list(shape), dt, kind="ExternalOutput").ap()

    def sb(self, shape, dt=F32, name=None):
        self.n += 1
        return self.st.enter_context(self.nc.sbuf_tensor(name or ("t%d" % self.n), list(shape), dt))

    def ps(self, shape, dt=F32, name=None):
        self.n += 1
        return self.st.enter_context(self.nc.psum_tensor(name or ("p%d" % self.n), list(shape), dt))

    def done(self):
        block = self.st.enter_context(self.nc.Block())
        self.kb.finish(block)
        self.st.close()
        return self.nc


def token_tiles(T, mx=512):
    out = []
    s = 0
    while s < T:
        n = min(mx, T - s)
        out.append((s, n))
        s += n
    return out


T1 = 1152
TP = TALL + 4
RW_BLK = 128
RW_SCALE = math.exp(-0.5)
NCH = TALL // 128
RET_EPS = 1e-5
MLA_SCALE = 192.0 ** -0.5
RWKV_GN_EPS = 64e-5
NFF = D_FF // 128

def emit_rstd(C, x_sb, Bx, nk, T, ones_sb, Bones, rstd_sb, Brstd, sq_tiles, Bsq, ps_tiles, Bps, eps, ranges=None):
    kb = C.kb
    k = 0
    for (t0, tn) in token_tiles(T):
        pj = (t0 // 512) % len(ps_tiles)
        p = ps_tiles[pj]
        for kc in range(nk):
            j = k % len(sq_tiles); k += 1
            sq = sq_tiles[j]
            kb.op("act", lambda e, sq=sq, kc=kc, t0=t0, tn=tn: e.activation(out=sq[:, 0:tn], in_=x_sb[:, kc, t0:t0 + tn],
                                                                              func=AF.Square),
                  reads=[Bx], writes=[Bsq[j]])
            kb.op("pe", lambda e, p=p, sq=sq, kc=kc, tn=tn: e.matmul(p[:, 0:tn], lhsT=ones_sb[:], rhs=sq[:, 0:tn],
                                                                      start=(kc == 0), stop=(kc == nk - 1)),
                  reads=[Bones, Bsq[j]], writes=[Bps[pj]])
        kb.op("act", lambda e, p=p, t0=t0, tn=tn: e.activation(out=rstd_sb[:, t0:t0 + tn], in_=p[:, 0:tn],
                                                                func=AF.Sqrt, bias=eps_ap(C, eps), scale=1.0),
              reads=[Bps[pj]], writes=[Brstd])
        kb.op("dve", lambda e, t0=t0, tn=tn: e.reciprocal(out=rstd_sb[:, t0:t0 + tn], in_=rstd_sb[:, t0:t0 + tn]),
              reads=[Brstd], writes=[Brstd])


_EPS = {}


def eps_ap(C, eps):
    return _EPS[(id(C), eps)][:, 0:1]


def make_eps(C, eps):
    t = C.sb([128, 1])
    b = Buf()
    C.kb.op("dve", lambda e: e.memset(t[:], eps), writes=[b])
    C.kb.barrier()
    _EPS[(id(C), eps)] = t
    return t


def vec_layout(v):
    n = v.shape[0]
    return np.ascontiguousarray(v.reshape(n, 16, 128).transpose(2, 0, 1))


def _rw_consts():
    ident = np.eye(128, dtype=np.float32)
    cst = np.concatenate([ident, np.ones((128, 128), np.float32)], axis=1)
    s = np.arange(64)[:, None]; t = np.arange(64)[None, :]
    us = (s < t).astype(np.float32); ui = (s <= t).astype(np.float32)
    unit = np.concatenate([us, ui, us, ui], axis=1)
    maskA = np.concatenate([unit, unit], axis=1)
    ls = (t < s).astype(np.float32)
    maskL = np.tile(ls, (1, 8))
    idu = np.tile(np.eye(64, dtype=np.float32), (1, 16))
    mk = np.concatenate([maskA, maskL, idu, np.ones((64, 64), np.float32)], axis=1)
    return cst, np.ascontiguousarray(mk)


def _tile8(v):
    return np.ascontiguousarray(v.reshape(8, 64).T)


def _rope_swap_idx(d):
    q = d // 4
    idx = np.arange(d)
    out = np.empty(d, np.int64)
    for base in (0, d // 2):
        out[base:base + q] = idx[base + q:base + 2 * q]
        out[base + q:base + 2 * q] = idx[base:base + q]
    return out


def _rope_tables(d):
    half = d // 2
    inv = 10000.0 ** (-np.arange(0, half, 2, dtype=np.float32) / half)
    rows = SEQ // 64
    row = np.repeat(np.arange(rows, dtype=np.float32), 64)
    col = np.tile(np.arange(64, dtype=np.float32), rows)
    ar = (row[:, None] * inv[None, :]).astype(np.float32)
    ac = (col[:, None] * inv[None, :]).astype(np.float32)
    cos = np.ones((d, TALL), np.float32); sin = np.zeros((d, TALL), np.float32)
    q = d // 4
    for base, ang in ((0, ar), (half, ac)):
        c = np.cos(ang).T.astype(np.float32); s = np.sin(ang).T.astype(np.float32)
        cos[base:base + q, CTX:] = c; cos[base + q:base + 2 * q, CTX:] = c
        sin[base:base + q, CTX:] = -s; sin[base + q:base + 2 * q, CTX:] = s
    return cos, sin


def emit_p1(C, io, classes):
    kb = C.kb
    C.push()
    xT = io["xT"]
    w = io["w"]
    pT = io["pT"]
    x_sb = C.sb([128, 16, T1]); Bx = Buf()
    h_sb = C.sb([128, 16, T1], BF16); Bh = Buf()
    v_sb = C.sb([128, 5, 16]); Bv = Buf()
    gp = C.sb([128, 2, 16]); Bgp = Buf()
    ones = C.sb([128, 128]); Bones = Buf()
    rstd = C.sb([128, T1]); Brstd = Buf()
    sq = [C.sb([128, 512]) for _ in range(2)]; Bsq = [Buf(), Buf()]
    tmp = [C.sb([128, T1]) for _ in range(2)]; Btmp = [Buf(), Buf()]
    wt = [C.sb([128, 16, 512], BF16) for _ in range(2)]; Bw = [Buf(), Buf()]
    ot = [C.sb([128, T1]) for _ in range(2)]; Bo = [Buf(), Buf()]
    pss = [C.ps([128, 512]) for _ in range(2)]; Bpss = [Buf(), Buf()]
    psm = [C.ps([128, 512]) for _ in range(4)]; Bpsm = [Buf() for _ in range(4)]

    xv = xT.rearrange("(kc p) t -> p kc t", p=128)
    for kc in range(16):
        q = "sp" if kc % 2 == 0 else "act"
        kb.dma(q, lambda e, kc=kc: e.dma_start(out=x_sb[:, kc, :], in_=xv[:, kc, :]), writes=[Bx])
    for vi, vsrc in enumerate(io["vec_srcs"]):
        kb.dma("sp", lambda e, vi=vi, vsrc=vsrc: e.dma_start(out=v_sb[:, vi, :], in_=vsrc), writes=[Bv])
    kb.op("dve", lambda e: e.memset(ones[:], 1.0 / D), writes=[Bones])
    for ci in range(2):
        kb.op("dve", lambda e, ci=ci: e.scalar_tensor_tensor(out=gp[:, ci, :], in0=v_sb[:, 2 + 2 * ci, :], scalar=1.0,
                                                             in1=v_sb[:, 0, :], op0=ALU.add, op1=ALU.mult),
              reads=[Bv], writes=[Bgp])
    emit_rstd(C, x_sb, Bx, 16, T1, ones, Bones, rstd, Brstd, sq, Bsq, pss, Bpss, NORM_EPS)
    for kc in range(16):
        j = kc % 2
        t = tmp[j]
        kb.op("dve", lambda e, t=t, kc=kc: e.tensor_tensor(out=t[:], in0=x_sb[:, kc, :], in1=rstd[:], op=ALU.mult),
              reads=[Bx, Brstd], writes=[Btmp[j]])
        for (ci, a, b) in classes:
            kb.op("act", lambda e, t=t, kc=kc, ci=ci, a=a, b=b: e.activation(
                out=h_sb[:, kc, a:b], in_=t[:, a:b], func=AF.Identity,
                bias=v_sb[:, 1 + 2 * ci, kc:kc + 1], scale=gp[:, ci, kc:kc + 1]),
                reads=[Btmp[j], Bgp, Bv], writes=[Bh])
    nblk = (IN_W + 511) // 512
    oi = 0
    pi = 0
    for nb in range(nblk):
        n0 = nb * 512
        nw = min(512, IN_W - n0)
        j = nb % 2
        wtile = wt[j]
        src = w[:, n0:n0 + nw].rearrange("(kc p) n -> p kc n", p=128)
        kb.dma(io["wdma"](), lambda e, wtile=wtile, src=src, nw=nw: e.dma_start(out=wtile[:, :, 0:nw], in_=src),
               writes=[Bw[j]])
        for nc_ in range(nw // 128):
            oj = oi % 2; oi += 1
            o = ot[oj]
            for (t0, tn) in token_tiles(T1):
                pj = pi % 4; pi += 1
                p = psm[pj]
                for kc in range(16):
                    kb.op("pe", lambda e, p=p, wtile=wtile, kc=kc, nc_=nc_, t0=t0, tn=tn: e.matmul(
                        p[:, 0:tn], lhsT=wtile[:, kc, nc_ * 128:(nc_ + 1) * 128], rhs=h_sb[:, kc, t0:t0 + tn],
                        start=(kc == 0), stop=(kc == 15)),
                        reads=[Bw[j], Bh], writes=[Bpsm[pj]])
                ev = "act" if pi % 2 == 0 else "dve"
                if ev == "act":
                    kb.op("act", lambda e, p=p, o=o, t0=t0, tn=tn: e.copy(out=o[:, t0:t0 + tn], in_=p[:, 0:tn]),
                          reads=[Bpsm[pj]], writes=[Bo[oj]])
                else:
                    kb.op("dve", lambda e, p=p, o=o, t0=t0, tn=tn: e.tensor_copy(out=o[:, t0:t0 + tn], in_=p[:, 0:tn]),
                          reads=[Bpsm[pj]], writes=[Bo[oj]])
            r0 = n0 + nc_ * 128
            kb.dma("sp", lambda e, o=o, r0=r0: e.dma_start(out=pT[r0:r0 + 128, :], in_=o[:]),
                   reads=[Bo[oj]], writes=[Buf()], is_out=True)
    C.pop()


def emit_p2r(C, io):
    kb = C.kb
    C.push()
    u_in = io["u"]
    u2_in = io["u2"]
    mu_in = io["mu"]
    mu2_in = io["mu2"]
    par_in = io["par"]
    wup_in = io["wup"]
    aup_in = io["aup"]
    gup_in = io["gup"]
    cst_in = io["cst"]
    mk_in = io["mk"]
    y_out = io["yT"]
    bon_out = io["bonT"]
    gate_out = io["gateT"]

    def T(shape, dt=F32):
        return C.sb(shape, dt), Buf()

    mu, Bmu = T([64, 24, 2]); c0, Bc0 = T([64, 24, 1])
    mu2, Bmu2 = T([128, 4, 2]); c02, Bc02 = T([128, 4, 1])
    par, Bpar = T([64, 5, 8]); omk, Bomk = T([64, 8])
    wup, Bwup = T([96, 512]); aup, Baup = T([96, 512]); gup, Bgup = T([128, 2, 512])
    cst, Bcst = T([128, 256]); mk, Bmk = T([64, 2 * 256 + 8 * 64 + 16 * 64 + 64])
    ident = cst[0:64, 0:64]; ONE64 = cst[0:64, 128:192]
    maskA = mk[:, 0:512].rearrange("p (a b) -> p a b", a=2)
    maskL = mk[:, 512:1024].rearrange("p (a b) -> p a b", a=8)
    identU = mk[:, 1024:2048].rearrange("p (a b) -> p a b", a=16)
    ONES = mk[:, 2048:2112]; BONES = Bmk
    kb.dma("sp", lambda e: e.dma_start(out=mu[:], in_=mu_in), writes=[Bmu])
    kb.dma("sp", lambda e: e.dma_start(out=mu2[:], in_=mu2_in), writes=[Bmu2])
    kb.dma("sp", lambda e: e.dma_start(out=par[:], in_=par_in), writes=[Bpar])
    kb.dma("sp", lambda e: e.dma_start(out=wup[:], in_=wup_in), writes=[Bwup])
    kb.dma("sp", lambda e: e.dma_start(out=aup[:], in_=aup_in), writes=[Baup])
    kb.dma("sp", lambda e: e.dma_start(out=gup[:], in_=gup_in.rearrange("(kc p) n -> p kc n", p=128)), writes=[Bgup])
    kb.dma("sp", lambda e: e.dma_start(out=cst[:], in_=cst_in), writes=[Bcst])
    kb.dma("sp", lambda e: e.dma_start(out=mk[:], in_=mk_in), writes=[Bmk])
    for (m_, Bm_, c_, Bc_) in [(mu, Bmu, c0, Bc0), (mu2, Bmu2, c02, Bc02)]:
        kb.op("dve", lambda e, m_=m_, c_=c_: e.tensor_tensor(out=c_[:], in0=m_[:, :, 0:1], in1=m_[:, :, 1:2], op=ALU.add), reads=[Bm_], writes=[Bc_])
        kb.op("dve", lambda e, c_=c_: e.tensor_scalar(out=c_[:], in0=c_[:], scalar1=-1.0, scalar2=1.0, op0=ALU.mult, op1=ALU.add),
              reads=[Bc_], writes=[Bc_])
    kb.op("dve", lambda e: e.tensor_scalar(out=omk[:], in0=par[:, 3, :], scalar1=-1.0, scalar2=1.0, op0=ALU.mult, op1=ALU.add),
          reads=[Bpar], writes=[Bomk])

    NB = RW_BLK
    NH = 8
    U, BU = T([64, 24, NB + 2]); U2, BU2 = T([128, 4, NB + 2])
    P, BP = T([64, 24, NB])
    P2, BP2 = T([128, 4, NB]); TMPP2, BTMPP2 = T([128, 4, NB])
    TWD, BTWD = T([128, NB]); SG, BSG = T([128, 2, NB])
    LW, BLW = T([64, NH, NB]); A, BA = T([64, NH, NB])
    KK, BKK = T([64, NH, NB]); SQ, BSQ = T([64, NH, NB]); RN, BRN = T([64, NH, NB])
    KD, BKD = T([64, NH, NB]); BS, BBS = T([64, NH, NB]); TA, BTA = RN, BRN
    LREL, BLREL = T([64, NH, NB]); EPOS, BEPOS = T([64, NH, NB]); ENEG, BENEG = T([64, NH, NB])
    EPREV, BEPREV = T([64, NH, NB]); EBAR, BEBAR = T([64, NH, NB]); PC, BPC = T([64, NH, 2])
    GATE, BGATE = EPOS, BEPOS; BON, BBON = ENEG, BENEG
    R32 = mybir.dt.float32r
    AR, BAR = T([64, NH, 2, 128], R32); BH, BBH = T([64, NH, 2, 64], R32); KH, BKH = T([64, NH, 2, 64], R32)
    BB, BBB = T([64, NH, 2, 64]); KBr, BKBr = T([64, NH, 2, 64])
    TM, BTM = T([64, 16, 5, 64], R32); GS, BGS = T([64, 16, 256], R32)
    MM = [T([64, 16, 64], R32) for _ in range(2)]
    WT = [T([64, 16, 128], R32) for _ in range(2)]
    XF, BXF = T([64, 16, 64], R32); UA, BUA = T([64, 16, 128]); APT, BAPT = T([64, 16, 64], R32)
    UU, BUU = T([64, 8, 64], R32); YO, BYO = T([64, NH, NB])
    TMPP = UA[:].rearrange("p a b -> p (a b)")[:, 0:12 * NB].rearrange("p (a b) -> p a b", a=12); BTMPP = BUA
    ST = [T([64, NH, 64], R32) for _ in range(2)]
    kb.op("dve", lambda e: e.tensor_scalar(out=ST[0][0][:], in0=identU[:, 0:8, :], scalar1=0.0, scalar2=None, op0=ALU.mult), reads=[Bmk], writes=[ST[0][1]])
    kb.op("dve", lambda e: e.tensor_scalar(out=ST[1][0][:], in0=identU[:, 0:8, :], scalar1=0.0, scalar2=None, op0=ALU.mult), reads=[Bmk], writes=[ST[1][1]])
    psl = [(C.ps([128, 512]), Buf(excl=True)) for _ in range(8)]
    pcnt = [0]

    def nps():
        r = psl[pcnt[0] % 8]
        pcnt[0] += 1
        return r

    def bc(ap, shape):
        return ap.broadcast_to(shape)

    def v3(ps, a, n=None):
        n = n or 512
        return ps[0:64, 0:n].rearrange("p (a b) -> p a b", a=a)

    yv = y_out.rearrange("(h p) t -> p h t", p=64)
    bv = bon_out.rearrange("(h p) t -> p h t", p=64)
    gv = gate_out.rearrange("(h p) t -> p h t", p=64)
    uv = u_in.rearrange("(i p) t -> p i t", p=64)
    u2v = u2_in.rearrange("(i p) t -> p i t", p=128)

    blocks = [(1 + i * NB, i * NB) for i in range(CTX // NB)] + [(259 + i * NB, CTX + i * NB) for i in range(SEQ // NB)]
    gc = 0
    for (pc0, t0) in blocks:
        if io.get("hook") is not None:
            io["hook"]()
        for i in range(24):
            q = "sp" if i % 2 == 0 else "act"
            kb.dma(q, lambda e, i=i, pc0=pc0: e.dma_start(out=U[:, i, :], in_=uv[:, i, pc0 - 1:pc0 + NB + 1]), writes=[BU])
        for i in range(4):
            kb.dma("sp", lambda e, i=i, pc0=pc0: e.dma_start(out=U2[:, i, :], in_=u2v[:, i, pc0 - 1:pc0 + NB + 1]), writes=[BU2])
        for (U_, BU_, P_, BP_, TP_, BTP_, m_, Bm_, c_, Bc_, S3) in [
                (U[:, 0:12, :], BU, P[:, 0:12, :], BP, TMPP, BTMPP, mu[:, 0:12, :], Bmu, c0[:, 0:12, :], Bc0, [64, 12, NB]),
                (U[:, 12:24, :], BU, P[:, 12:24, :], BP, TMPP, BTMPP, mu[:, 12:24, :], Bmu, c0[:, 12:24, :], Bc0, [64, 12, NB]),
                (U2, BU2, P2, BP2, TMPP2, BTMPP2, mu2, Bmu2, c02, Bc02, [128, 4, NB])]:
            kb.op("dve", lambda e, U_=U_, P_=P_, c_=c_, S3=S3: e.tensor_tensor(out=P_[:], in0=U_[:, :, 1:NB + 1], in1=bc(c_[:], S3), op=ALU.mult),
                  reads=[BU_, Bc_], writes=[BP_])
            kb.op("pool", lambda e, U_=U_, TP_=TP_, m_=m_, S3=S3: e.tensor_tensor(out=TP_[:], in0=U_[:, :, 0:NB], in1=bc(m_[:, :, 0:1], S3), op=ALU.mult),
                  reads=[BU_, Bm_], writes=[BTP_])
            kb.op("dve", lambda e, P_=P_, TP_=TP_: e.tensor_tensor(out=P_[:], in0=P_[:], in1=TP_[:], op=ALU.add), reads=[BP_, BTP_], writes=[BP_])
            kb.op("pool", lambda e, U_=U_, TP_=TP_, m_=m_, S3=S3: e.tensor_tensor(out=TP_[:], in0=U_[:, :, 2:NB + 2], in1=bc(m_[:, :, 1:2], S3), op=ALU.mult),
                  reads=[BU_, Bm_], writes=[BTP_])
            kb.op("dve", lambda e, P_=P_, TP_=TP_: e.tensor_tensor(out=P_[:], in0=P_[:], in1=TP_[:], op=ALU.add), reads=[BP_, BTP_], writes=[BP_])
        r_ = P[:, 0:8, :]; k_ = P[:, 8:16, :]; v_ = P[:, 16:24, :]
        S4 = [64, NH, NB]
        for (src_i, upw, Bup, pidx, dst, Bdst, func) in [(0, wup, Bwup, 0, LW, BLW, AF.Tanh), (1, aup, Baup, 1, A, BA, AF.Identity)]:
            kb.op("act", lambda e, src_i=src_i, func=func: e.activation(out=TWD[:], in_=P2[:, src_i, :], func=func),
                  reads=[BP2], writes=[BTWD])
            for hh in range(2):
                ps, Bps = nps()
                for hi in range(4):
                    h = hh * 4 + hi
                    kb.op("pe", lambda e, ps=ps, upw=upw, h=h, hi=hi: e.matmul(ps[0:64, hi * NB:(hi + 1) * NB], lhsT=upw[0:96, h * 64:(h + 1) * 64],
                                                                         rhs=TWD[0:96, :], start=True, stop=True),
                          reads=[Bup, BTWD], writes=[Bps])
                for hi in range(4):
                    h = hh * 4 + hi
                    kb.op("act", lambda e, ps=ps, h=h, hi=hi, dst=dst, pidx=pidx: e.activation(
                        out=dst[:, h, :], in_=ps[0:64, hi * NB:(hi + 1) * NB], func=AF.Sigmoid, bias=par[:, pidx, h:h + 1], scale=1.0),
                        reads=[Bps, Bpar], writes=[Bdst])
        kb.op("dve", lambda e: e.tensor_scalar(out=LW[:], in0=LW[:], scalar1=-RW_SCALE, scalar2=None, op0=ALU.mult),
              reads=[BLW], writes=[BLW])
        AUX = io.get("aux", True)
        if AUX:
            kb.op("act", lambda e: e.activation(out=SG[:], in_=P2[:, 2:4, :], func=AF.Sigmoid), reads=[BP2], writes=[BSG])
            for hh in range(2):
                ps, Bps = nps()
                for hi in range(4):
                    h = hh * 4 + hi
                    for kc in range(2):
                        kb.op("pe", lambda e, ps=ps, h=h, hi=hi, kc=kc: e.matmul(ps[0:64, hi * NB:(hi + 1) * NB], lhsT=gup[:, kc, h * 64:(h + 1) * 64],
                                                                           rhs=SG[:, kc, :], start=(kc == 0), stop=(kc == 1)),
                              reads=[Bgup, BSG], writes=[Bps])
                kb.op("act", lambda e, ps=ps, hh=hh: e.copy(out=GATE[:, hh * 4:hh * 4 + 4, :], in_=v3(ps, 4)), reads=[Bps], writes=[BGATE])
            kb.dma("sp", lambda e, t0=t0: e.dma_start(out=gv[:, :, t0:t0 + NB], in_=GATE[:]), reads=[BGATE], writes=[Buf()], is_out=True)

        def headsum(src, Bsrc, fn_evac):
            for hh in range(2):
                ps, Bps = nps()
                for hi in range(4):
                    h = hh * 4 + hi
                    kb.op("pe", lambda e, ps=ps, h=h, hi=hi: e.matmul(ps[0:64, hi * NB:(hi + 1) * NB], lhsT=ONE64, rhs=src[:, h, :], start=True, stop=True),
                          reads=[Bcst, Bsrc], writes=[Bps])
                fn_evac(ps, Bps, hh)
        kb.op("dve", lambda e: e.tensor_tensor(out=KK[:], in0=k_, in1=bc(par[:, 2, :].unsqueeze(2), S4), op=ALU.mult),
              reads=[BP, Bpar], writes=[BKK])
        kb.op("pool", lambda e: e.tensor_tensor(out=SQ[:], in0=KK[:], in1=KK[:], op=ALU.mult), reads=[BKK], writes=[BSQ])
        headsum(SQ, BSQ, lambda ps, Bps, hh: kb.op("act", lambda e: e.activation(out=RN[:, hh * 4:hh * 4 + 4, :], in_=v3(ps, 4), func=AF.Sqrt,
                                                                                    bias=eps_ap(C, 1e-12)[0:64, :], scale=1.0), reads=[Bps], writes=[BRN]))
        kb.op("dve", lambda e: e.reciprocal(out=RN[:], in_=RN[:]), reads=[BRN], writes=[BRN])
        kb.op("dve", lambda e: e.tensor_tensor(out=KK[:], in0=KK[:], in1=RN[:], op=ALU.mult), reads=[BKK, BRN], writes=[BKK])
        if AUX:
            kb.op("pool", lambda e: e.tensor_tensor(out=SQ[:], in0=r_, in1=k_, op=ALU.mult), reads=[BP], writes=[BSQ])
            kb.op("pool", lambda e: e.tensor_tensor(out=SQ[:], in0=SQ[:], in1=bc(par[:, 4, :].unsqueeze(2), S4), op=ALU.mult),
                  reads=[BSQ, Bpar], writes=[BSQ])
            headsum(SQ, BSQ, lambda ps, Bps, hh: kb.op("dve", lambda e: e.tensor_tensor(out=BON[:, hh * 4:hh * 4 + 4, :], in0=v3(ps, 4),
                                                                                         in1=v_[:, hh * 4:hh * 4 + 4, :], op=ALU.mult),
                                                       reads=[Bps, BP], writes=[BBON]))
            kb.dma("sp", lambda e, t0=t0: e.dma_start(out=bv[:, :, t0:t0 + NB], in_=BON[:]), reads=[BBON], writes=[Buf()], is_out=True)
        kb.op("dve", lambda e: e.tensor_tensor(out=TA[:], in0=A[:], in1=bc(par[:, 3, :].unsqueeze(2), S4), op=ALU.mult),
              reads=[BA, Bpar], writes=[BTA])
        kb.op("dve", lambda e: e.tensor_tensor(out=TA[:], in0=TA[:], in1=bc(omk[:].unsqueeze(2), S4), op=ALU.add),
              reads=[BTA, Bomk], writes=[BTA])
        kb.op("dve", lambda e: e.tensor_tensor(out=KD[:], in0=TA[:], in1=k_, op=ALU.mult), reads=[BTA, BP], writes=[BKD])
        kb.op("pool", lambda e: e.tensor_tensor(out=BS[:], in0=KK[:], in1=A[:], op=ALU.mult), reads=[BKK, BA], writes=[BBS])
        for h in range(NH):
            for ch in range(2):
                kb.op("dve", lambda e, h=h, ch=ch: e.tensor_tensor_scan(
                    out=LREL[:, h, ch * 64:(ch + 1) * 64], data0=ONES,
                    data1=LW[:, h, ch * 64:(ch + 1) * 64], initial=0.0, op0=ALU.mult, op1=ALU.add),
                    reads=[BLW, BONES], writes=[BLREL])
        kb.op("act", lambda e: e.activation(out=EPOS[:], in_=LREL[:], func=AF.Exp), reads=[BLREL], writes=[BEPOS])
        kb.op("act", lambda e: e.activation(out=ENEG[:], in_=LREL[:], func=AF.Exp, scale=-1.0), reads=[BLREL], writes=[BENEG])
        kb.op("dve", lambda e: e.tensor_tensor(out=EPREV[:], in0=LREL[:], in1=LW[:], op=ALU.subtract), reads=[BLREL, BLW], writes=[BEPREV])
        kb.op("act", lambda e: e.activation(out=EPREV[:], in_=EPREV[:], func=AF.Exp), reads=[BEPREV], writes=[BEPREV])
        L5 = LREL[:].rearrange("p c (h t) -> p c h t", h=2)
        kb.op("dve", lambda e, L5=L5: e.tensor_tensor(out=EBAR[:].rearrange("p c (h t) -> p c h t", h=2),
                                                       in0=bc(L5[:, :, :, 63:64], [64, NH, 2, 64]), in1=L5, op=ALU.subtract),
              reads=[BLREL], writes=[BEBAR])
        kb.op("act", lambda e: e.activation(out=EBAR[:], in_=EBAR[:], func=AF.Exp), reads=[BEBAR], writes=[BEBAR])
        kb.op("act", lambda e, L5=L5: e.activation(out=PC[:].unsqueeze(3), in_=L5[:, :, :, 63:64], func=AF.Exp), reads=[BLREL], writes=[BPC])

        def v5(t):
            return t[:].rearrange("p c (h t) -> p c h t", h=2)
        kb.op("dve", lambda e: e.scalar_tensor_tensor(out=AR[:].rearrange("p c h t -> p (c h) t")[:, :, 0:64], in0=KK[:].rearrange("p c (h t) -> p (c h) t", h=2),
                                                      scalar=-1.0, in1=EPREV[:].rearrange("p c (h t) -> p (c h) t", h=2), op0=ALU.mult, op1=ALU.mult),
              reads=[BKK, BEPREV], writes=[BAR])
        kb.op("dve", lambda e: e.tensor_tensor(out=AR[:, :, :, 64:128], in0=r_.rearrange("p c (h t) -> p c h t", h=2), in1=v5(EPOS), op=ALU.mult),
              reads=[BP, BEPOS], writes=[BAR])
        kb.op("dve", lambda e: e.tensor_tensor(out=BH[:], in0=v5(BS), in1=v5(ENEG), op=ALU.mult), reads=[BBS, BENEG], writes=[BBH])
        kb.op("dve", lambda e: e.tensor_tensor(out=KH[:], in0=v5(KD), in1=v5(ENEG), op=ALU.mult), reads=[BKD, BENEG], writes=[BKH])
        kb.op("dve", lambda e: e.tensor_tensor(out=BB[:], in0=v5(BS), in1=v5(EBAR), op=ALU.mult), reads=[BBS, BEBAR], writes=[BBB])
        kb.op("pool", lambda e: e.tensor_tensor(out=KBr[:], in0=v5(KD), in1=v5(EBAR), op=ALU.mult), reads=[BKD, BEBAR], writes=[BKBr])

        for ch in range(2):
            for hp in range(4):
                u0 = ch * 8 + hp * 2
                ps, Bps = nps()
                for e_ in range(2):
                    h = hp * 2 + e_
                    srcs = [(AR[:, h, ch, 0:64].bitcast(F32), BAR), (P[:, 16 + h, ch * 64:(ch + 1) * 64], BP), (BB[:, h, ch, :], BBB), (KBr[:, h, ch, :], BKBr)]
                    for ai, (s_ap, s_b) in enumerate(srcs):
                        col = (e_ * 4 + ai) * 64
                        kb.op("pe", lambda e, ps=ps, col=col, s_ap=s_ap: e.transpose(out=ps[0:64, col:col + 64], in_=s_ap, identity=ident),
                              reads=[s_b, Bcst], writes=[Bps])
                kb.op("act", lambda e, ps=ps, u0=u0: e.copy(out=TM[:, u0:u0 + 2, 1:5, :],
                                                             in_=ps[0:64, :].rearrange("p (e a k) -> p e a k", a=4, e=2)),
                      reads=[Bps], writes=[BTM])
        for ch in range(2):
            for hp in range(4):
                u0 = ch * 8 + hp * 2
                ps, Bps = nps()
                for e_ in range(2):
                    h = hp * 2 + e_
                    for hi, (lt, Blt) in enumerate([(BH, BBH), (KH, BKH)]):
                        kb.op("pe", lambda e, ps=ps, lt=lt, e_=e_, hi=hi, h=h, ch=ch: e.matmul(
                            ps[0:64, e_ * 256 + hi * 128:e_ * 256 + hi * 128 + 128], lhsT=lt[:, h, ch, :], rhs=AR[:, h, ch, :],
                            start=True, stop=True), reads=[Blt, BAR], writes=[Bps])
                kb.op("dve", lambda e, ps=ps, u0=u0: e.tensor_tensor(out=GS[:, u0:u0 + 2, :], in0=v3(ps, 2), in1=maskA, op=ALU.mult),
                      reads=[Bps, Bmk], writes=[BGS])
        for ch in range(2):
            ps, Bps = nps()
            for h in range(8):
                kb.op("pe", lambda e, ps=ps, h=h, ch=ch: e.matmul(ps[0:64, h * 64:(h + 1) * 64], lhsT=AR[:, h, ch, 0:64],
                                                                rhs=BH[:, h, ch, :], start=True, stop=True),
                      reads=[BAR, BBH], writes=[Bps])
            kb.op("dve", lambda e, ps=ps, ch=ch: e.tensor_tensor(out=MM[0][0][:, ch * 8:(ch + 1) * 8, :], in0=v3(ps, 8),
                                                                  in1=maskL, op=ALU.mult), reads=[Bps, Bmk], writes=[MM[0][1]])
        kb.op("dve", lambda e: e.tensor_tensor(out=WT[0][0][:, :, 64:128], in0=GS[:, :, 0:64], in1=identU, op=ALU.add),
              reads=[BGS, Bmk], writes=[WT[0][1]])
        for g in range(4):
            us = range(g * 4, g * 4 + 4)
            ps1, Bps1 = nps(); ps2, Bps2 = nps()
            for ui, u in enumerate(us):
                kb.op("pe", lambda e, ps1=ps1, ui=ui, u=u: e.matmul(ps1[0:64, ui * 64:(ui + 1) * 64], lhsT=MM[0][0][:, u, :], rhs=GS[:, u, 0:64],
                                                                   start=True, stop=True), reads=[MM[0][1], BGS], writes=[Bps1])
                kb.op("pe", lambda e, ps2=ps2, ui=ui, u=u: e.matmul(ps2[0:64, ui * 64:(ui + 1) * 64], lhsT=GS[:, u, 0:64], rhs=MM[0][0][:, u, :],
                                                                   start=True, stop=True), reads=[MM[0][1], BGS], writes=[Bps2])
            kb.op("act", lambda e, ps1=ps1, g=g: e.copy(out=WT[0][0][:, g * 4:g * 4 + 4, 0:64], in_=v3(ps1, 4, 256)),
                  reads=[Bps1], writes=[WT[0][1]])
            kb.op("act", lambda e, ps2=ps2, g=g: e.copy(out=MM[1][0][:, g * 4:g * 4 + 4, :], in_=v3(ps2, 4, 256)),
                  reads=[Bps2], writes=[MM[1][1]])
        cur, mcur = 0, 1
        for lvl in range(4):
            for g in range(4):
                us = range(g * 4, g * 4 + 4)
                ps1, Bps1 = nps(); ps2, Bps2 = nps()
                for ui, u in enumerate(us):
                    kb.op("pe", lambda e, ps1=ps1, ui=ui, u=u, cur=cur, mcur=mcur: e.matmul(
                        ps1[0:64, ui * 128:(ui + 1) * 128], lhsT=MM[mcur][0][:, u, :], rhs=WT[cur][0][:, u, :], start=True, stop=True),
                        reads=[MM[mcur][1], WT[cur][1]], writes=[Bps1])
                    kb.op("pe", lambda e, ps2=ps2, ui=ui, u=u, cur=cur, mcur=mcur: e.matmul(
                        ps2[0:64, ui * 64:(ui + 1) * 64], lhsT=WT[cur][0][:, u, 0:64], rhs=MM[mcur][0][:, u, :], start=True, stop=True),
                        reads=[MM[mcur][1], WT[cur][1]], writes=[Bps2])
                v1 = v3(ps1, 4)
                kb.op("act", lambda e, v1=v1, g=g, cur=cur: e.copy(out=WT[1 - cur][0][:, g * 4:g * 4 + 4, 0:64], in_=v1[:, :, 0:64]),
                      reads=[Bps1], writes=[WT[1 - cur][1]])
                kb.op("dve", lambda e, v1=v1, g=g, cur=cur: e.tensor_tensor(out=WT[1 - cur][0][:, g * 4:g * 4 + 4, 64:128], in0=v1[:, :, 64:128],
                                                                            in1=WT[cur][0][:, g * 4:g * 4 + 4, 64:128], op=ALU.add),
                      reads=[Bps1, WT[cur][1]], writes=[WT[1 - cur][1]])
                kb.op("act", lambda e, ps2=ps2, g=g, mcur=mcur: e.copy(out=MM[1 - mcur][0][:, g * 4:g * 4 + 4, :], in_=v3(ps2, 4, 256)),
                      reads=[Bps2], writes=[MM[1 - mcur][1]])
            cur, mcur = 1 - cur, 1 - mcur
        for g in range(2):
            ps1, Bps1 = nps()
            for ui in range(8):
                u = g * 8 + ui
                kb.op("pe", lambda e, ps1=ps1, ui=ui, u=u, cur=cur, mcur=mcur: e.matmul(
                    ps1[0:64, ui * 64:(ui + 1) * 64], lhsT=MM[mcur][0][:, u, :], rhs=WT[cur][0][:, u, 64:128], start=True, stop=True),
                    reads=[MM[mcur][1], WT[cur][1]], writes=[Bps1])
            kb.op("dve", lambda e, ps1=ps1, g=g, cur=cur: e.tensor_tensor(out=XF[:, g * 8:g * 8 + 8, :], in0=v3(ps1, 8),
                                                                         in1=WT[cur][0][:, g * 8:g * 8 + 8, 64:128], op=ALU.add),
                  reads=[Bps1, WT[cur][1]], writes=[BXF])
        for g in range(2):
            ps1, Bps1 = nps()
            for ui in range(8):
                u = g * 8 + ui
                kb.op("pe", lambda e, ps1=ps1, ui=ui, u=u: e.matmul(ps1[0:64, ui * 64:(ui + 1) * 64], lhsT=GS[:, u, 128:192], rhs=TM[:, u, 2, :],
                                                                   start=True, stop=True), reads=[BGS, BTM], writes=[Bps1])
            kb.op("act", lambda e, ps1=ps1, g=g: e.copy(out=TM[:, g * 8:g * 8 + 8, 0, :], in_=v3(ps1, 8)), reads=[Bps1], writes=[BTM])
        for g in range(4):
            ps1, Bps1 = nps()
            for ui in range(4):
                u = g * 4 + ui
                kb.op("pe", lambda e, ps1=ps1, ui=ui, u=u: e.matmul(ps1[0:64, ui * 128:(ui + 1) * 128], lhsT=XF[:, u, :],
                                                                   rhs=TM[:, u, 0:2, :].rearrange("p a k -> p (a k)"),
                                                                   start=True, stop=True), reads=[BXF, BTM], writes=[Bps1])
            kb.op("act", lambda e, ps1=ps1, g=g: e.copy(out=UA[:, g * 4:g * 4 + 4, :], in_=v3(ps1, 4)), reads=[Bps1], writes=[BUA])
        for g in range(2):
            ps1, Bps1 = nps()
            for ui in range(8):
                u = g * 8 + ui
                kb.op("pe", lambda e, ps1=ps1, ui=ui, u=u: e.matmul(ps1[0:64, ui * 64:(ui + 1) * 64], lhsT=TM[:, u, 1, :], rhs=XF[:, u, :],
                                                                   start=True, stop=True), reads=[BTM, BXF], writes=[Bps1])
            kb.op("act", lambda e, ps1=ps1, g=g: e.copy(out=APT[:, g * 8:g * 8 + 8, :], in_=v3(ps1, 8)), reads=[Bps1], writes=[BAPT])
        for ch in range(2):
            scur, snxt = ST[gc % 2], ST[(gc + 1) % 2]
            gc += 1
            psu, Bpsu = nps()
            for h in range(8):
                u = ch * 8 + h
                kb.op("pe", lambda e, psu=psu, h=h, u=u, scur=scur: e.matmul(
                    psu[0:64, h * 64:(h + 1) * 64], lhsT=APT[:, u, :], rhs=scur[0][:, h, :], start=True, stop=True),
                    reads=[BAPT, scur[1]], writes=[Bpsu])
            kb.op("dve", lambda e, psu=psu, ch=ch: e.tensor_tensor(out=UU[:], in0=v3(psu, 8), in1=UA[:, ch * 8:ch * 8 + 8, 0:64], op=ALU.add),
                  reads=[Bpsu, BUA], writes=[BUU])
            psy, Bpsy = nps(); pss_, Bpss = nps()
            for h in range(8):
                u = ch * 8 + h
                oy = psy[0:64, h * 64:(h + 1) * 64]
                kb.op("pe", lambda e, oy=oy, h=h, u=u: e.matmul(oy, lhsT=UU[:, h, :], rhs=GS[:, u, 64:128], start=True, stop=False),
                      reads=[BUU, BGS], writes=[Bpsy])
                kb.op("pe", lambda e, oy=oy, u=u: e.matmul(oy, lhsT=TM[:, u, 2, :], rhs=GS[:, u, 192:256], start=False, stop=False),
                      reads=[BTM, BGS], writes=[Bpsy])
                kb.op("pe", lambda e, oy=oy, h=h, ch=ch, scur=scur: e.matmul(oy, lhsT=scur[0][:, h, :], rhs=AR[:, h, ch, 64:128],
                                                                            start=False, stop=True),
                      reads=[scur[1], BAR], writes=[Bpsy])
                os_ = pss_[0:64, h * 64:(h + 1) * 64]
                kb.op("pe", lambda e, os_=os_, h=h, u=u: e.matmul(os_, lhsT=TM[:, u, 3, :], rhs=UU[:, h, :], start=True, stop=False),
                      reads=[BTM, BUU], writes=[Bpss])
                kb.op("pe", lambda e, os_=os_, u=u: e.matmul(os_, lhsT=TM[:, u, 4, :], rhs=TM[:, u, 2, :], start=False, stop=True),
                      reads=[BTM], writes=[Bpss])
            kb.op("act", lambda e, psy=psy, ch=ch: e.copy(out=YO[:, :, ch * 64:(ch + 1) * 64], in_=v3(psy, 8)), reads=[Bpsy], writes=[BYO])
            kb.op("dve", lambda e, pss_=pss_, ch=ch, scur=scur, snxt=snxt: e.tensor_tensor(
                out=snxt[0][:], in0=scur[0][:], in1=bc(PC[:, :, ch:ch + 1], [64, NH, 64]), op=ALU.mult),
                reads=[scur[1], BPC], writes=[snxt[1]])
            kb.op("dve", lambda e, pss_=pss_, snxt=snxt: e.tensor_tensor(out=snxt[0][:], in0=v3(pss_, 8), in1=snxt[0][:], op=ALU.add),
                  reads=[Bpss, snxt[1]], writes=[snxt[1]])
        kb.dma("sp", lambda e, t0=t0: e.dma_start(out=yv[:, :, t0:t0 + NB], in_=YO[:]), reads=[BYO], writes=[Buf()], is_out=True)
    C.pop()


def emit_p2d(C, io):
    kb = C.kb
    C.push()
    rope_in = io["rope"]
    rd_in = io["rd"]
    gn_in = io["gn"]
    tab_in = io["tab"]
    cst_in = io["cst"]
    y_out = io["yT"]

    def T(shape, dt=F32):
        return C.sb(shape, dt), Buf()
    rope, Brope = T([128, 2, TALL])
    rd, Brd = T([128, 8]); lg, Blg = T([128, 8]); gch, Bgch = T([128, 8])
    gn, Bgn = T([128, 4]); tab, Btab = T([128, 770]); cst, Bcst = T([128, 256])
    ident = cst[:, 0:128]; ONESM = cst[:, 128:256]
    DEC = [T([128, 128]) for _ in range(8)]
    XI = [T([128, 128]) for _ in range(8)]
    ZE = [T([128, 1]) for _ in range(8)]
    kb.dma("sp", lambda e: e.dma_start(out=rope[:], in_=rope_in.rearrange("a p t -> p a t")), writes=[Brope])
    kb.dma("sp", lambda e: e.dma_start(out=rd[:], in_=rd_in), writes=[Brd])
    kb.dma("sp", lambda e: e.dma_start(out=gn[:], in_=gn_in), writes=[Bgn])
    kb.dma("sp", lambda e: e.dma_start(out=tab[:], in_=tab_in), writes=[Btab])
    kb.dma("sp", lambda e: e.dma_start(out=cst[:], in_=cst_in), writes=[Bcst])
    kb.op("act", lambda e: e.activation(out=lg[:], in_=rd[:], func=AF.Exp), reads=[Brd], writes=[Blg])
    kb.op("dve", lambda e: e.tensor_scalar(out=lg[:], in0=lg[:], scalar1=-1.0, scalar2=None, op0=ALU.mult), reads=[Blg], writes=[Blg])
    kb.op("act", lambda e: e.activation(out=gch[:], in_=lg[:], func=AF.Exp, scale=128.0), reads=[Blg], writes=[Bgch])
    for d in range(2):
        for h in range(4):
            i = d * 4 + h
            kb.op("act", lambda e, d=d, i=i: e.activation(out=DEC[i][0][:], in_=tab[:, d * 256:d * 256 + 128], func=AF.Exp, scale=lg[:, i:i + 1]),
                  reads=[Btab, Blg], writes=[DEC[i][1]])
            kb.op("dve", lambda e, d=d, i=i: e.tensor_tensor(out=DEC[i][0][:], in0=DEC[i][0][:], in1=tab[:, d * 256 + 128:d * 256 + 256], op=ALU.mult),
                  reads=[Btab, DEC[i][1]], writes=[DEC[i][1]])
            kb.op("act", lambda e, d=d, i=i: e.activation(out=XI[i][0][:], in_=tab[:, 512 + d * 128:512 + d * 128 + 128], func=AF.Exp, scale=lg[:, i:i + 1]),
                  reads=[Btab, Blg], writes=[XI[i][1]])
            kb.op("act", lambda e, d=d, i=i: e.activation(out=ZE[i][0][:], in_=tab[:, 768 + d:769 + d], func=AF.Exp, scale=lg[:, i:i + 1]),
                  reads=[Btab, Blg], writes=[ZE[i][1]])
    X6 = [T([128, TALL], mybir.dt.float32r if i_ < 2 else F32) for i_ in range(6)]
    QX = [T([128, TALL], mybir.dt.float32r) for _ in range(2)]
    KT, BKT = T([128, NCH, 128]); VT, BVT = T([128, NCH, 128], mybir.dt.float32r)
    KZ = [T([128, 128], mybir.dt.float32r) for _ in range(2)]
    ATT = [T([128, 128], mybir.dt.float32r) for _ in range(2)]
    O, BO = T([128, TALL]); TMP, BTMP = T([128, TALL])
    S = [T([128, 128], mybir.dt.float32r) for _ in range(2)]
    psl = [(C.ps([128, 512]), Buf(excl=True)) for _ in range(8)]
    pcnt = [0]

    def nps():
        r = psl[pcnt[0] % 8]
        pcnt[0] += 1
        return r
    yv = y_out.rearrange("(h p) t -> p h t", p=128)
    SCALE = 128.0 ** -0.5
    for h in range(4):
        for a in range(6):
            for pi_, (pr0, pr1, srcap) in enumerate(io["qkv"](h, a)):
                q_ = "sp" if (a + pi_) % 2 == 0 else "act"
                kb.dma(q_, lambda e, a=a, pr0=pr0, pr1=pr1, srcap=srcap: e.dma_start(out=(X6[a][0][pr0:pr1, :].bitcast(F32) if a < 2 else X6[a][0][pr0:pr1, :]), in_=srcap), writes=[X6[a][1]])
        (q, Bq), (k, Bk), (v, Bv), (g, Bg), (q2, Bq2), (k2, Bk2) = X6
        for (x, Bx, x2, Bx2, eng2) in [(q, Bq, q2, Bq2, "pool"), (k, Bk, k2, Bk2, "pool")]:
            kb.op("dve", lambda e, x=x: e.tensor_tensor(out=TMP[:], in0=x[:].bitcast(F32), in1=rope[:, 0, :], op=ALU.mult), reads=[Bx, Brope], writes=[BTMP])
            kb.op(eng2, lambda e, x2=x2: e.tensor_tensor(out=x2[:], in0=x2[:], in1=rope[:, 1, :], op=ALU.mult), reads=[Bx2, Brope], writes=[Bx2])
            kb.op("dve", lambda e, x=x, x2=x2: e.tensor_tensor(out=x[:], in0=TMP[:], in1=x2[:], op=ALU.add), reads=[BTMP, Bx2], writes=[Bx])
        kb.op("act", lambda e: e.mul(out=k[:], in_=k[:].bitcast(F32), mul=SCALE), reads=[Bk], writes=[Bk])
        for (src, Bsrc, dst, Bdst) in [(k, Bk, KT, BKT), (v, Bv, VT, BVT)]:
            for c4 in range(0, NCH, 4):
                ps, Bps = nps()
                n = min(4, NCH - c4)
                for ci in range(n):
                    c = c4 + ci
                    kb.op("pe", lambda e, ps=ps, ci=ci, c=c, src=src: e.transpose(out=ps[:, ci * 128:(ci + 1) * 128], in_=(src[:, c * 128:(c + 1) * 128].bitcast(F32) if src is k else src[:, c * 128:(c + 1) * 128]), identity=ident),
                          reads=[Bsrc, Bcst], writes=[Bps])
                kb.op("act", lambda e, ps=ps, c4=c4, n=n, dst=dst: e.copy(out=dst[:, c4:c4 + n, :], in_=ps[:, 0:n * 128].rearrange("p (a b) -> p a b", a=n)),
                      reads=[Bps], writes=[Bdst])
        for d in range(2):
            i = d * 4 + h
            kb.op("dve", lambda e, d=d, i=i: e.tensor_tensor(
                out=QX[d][0][:].rearrange("p (c t) -> p c t", c=NCH), in0=q[:].bitcast(F32).rearrange("p (c t) -> p c t", c=NCH),
                in1=XI[i][0][:].unsqueeze(1).broadcast_to([128, NCH, 128]), op=ALU.mult), reads=[Bq, XI[i][1]], writes=[QX[d][1]])
        for d in range(2):
            i = d * 4 + h
            st, Bst = S[d]
            kb.op("dve", lambda e, st=st: e.tensor_scalar(out=st[:], in0=ident, scalar1=0.0, scalar2=None, op0=ALU.mult), reads=[Bcst], writes=[Bst])
            order = [0, 1] + list(range(2, NCH)) if d == 0 else [1, 0] + list(range(NCH - 1, 1, -1))
            for n_, c in enumerate(order):
                cs = slice(c * 128, (c + 1) * 128)
                at, Bat = ATT[n_ % 2]
                kz, Bkz = KZ[n_ % 2]
                ps, Bps = nps()
                kb.op("pe", lambda e, ps=ps, cs=cs: e.matmul(ps[:, 0:128], lhsT=k[:, cs], rhs=q[:, cs], start=True, stop=True),
                      reads=[Bk, Bq], writes=[Bps])
                kb.op("dve", lambda e, ps=ps, at=at, i=i: e.tensor_tensor(out=at[:], in0=ps[:, 0:128], in1=DEC[i][0][:], op=ALU.mult),
                      reads=[Bps, DEC[i][1]], writes=[Bat])
                kb.op("dve", lambda e, kz=kz, c=c, i=i: e.tensor_tensor(out=kz[:], in0=KT[:, c, :], in1=ZE[i][0][:].broadcast_to([128, 128]), op=ALU.mult),
                      reads=[BKT, ZE[i][1]], writes=[Bkz])
                po, Bpo = nps()
                kb.op("pe", lambda e, po=po, c=c, at=at: e.matmul(po[:, 0:128], lhsT=VT[:, c, :], rhs=at[:], start=True, stop=False),
                      reads=[BVT, Bat], writes=[Bpo])
                kb.op("pe", lambda e, po=po, cs=cs, st=st, d=d: e.matmul(po[:, 0:128], lhsT=st[:], rhs=QX[d][0][:, cs], start=False, stop=True),
                      reads=[Bst, QX[d][1]], writes=[Bpo])
                if d == 0:
                    kb.op("act", lambda e, po=po, cs=cs: e.copy(out=O[:, cs], in_=po[:, 0:128]), reads=[Bpo], writes=[BO])
                else:
                    kb.op("dve", lambda e, po=po, cs=cs: e.tensor_tensor(out=O[:, cs], in0=po[:, 0:128], in1=O[:, cs], op=ALU.add),
                          reads=[Bpo, BO], writes=[BO])
                pn, Bpn = nps()
                kb.op("pe", lambda e, pn=pn, kz=kz, c=c: e.matmul(pn[:, 0:128], lhsT=kz[:], rhs=VT[:, c, :], start=True, stop=True),
                      reads=[Bkz, BVT], writes=[Bpn])
                kb.op("dve", lambda e, pn=pn, st=st, i=i: e.scalar_tensor_tensor(out=st[:], in0=st[:].bitcast(F32), scalar=gch[:, i:i + 1], in1=pn[:, 0:128],
                                                                                 op0=ALU.mult, op1=ALU.add), reads=[Bst, Bgch, Bpn], writes=[Bst])
        for (t0, tn) in token_tiles(TALL):
            ts_ = slice(t0, t0 + tn)
            pm, Bpm = nps()
            kb.op("pe", lambda e, pm=pm, ts_=ts_, tn=tn: e.matmul(pm[:, 0:tn], lhsT=ONESM, rhs=O[:, ts_], start=True, stop=True), reads=[Bcst, BO], writes=[Bpm])
            kb.op("dve", lambda e, pm=pm, ts_=ts_, tn=tn: e.tensor_tensor(out=O[:, ts_], in0=O[:, ts_], in1=pm[:, 0:tn], op=ALU.subtract),
                  reads=[Bpm, BO], writes=[BO])
            kb.op("act", lambda e, ts_=ts_: e.activation(out=TMP[:, ts_], in_=O[:, ts_], func=AF.Square), reads=[BO], writes=[BTMP])
            pv, Bpv = nps()
            kb.op("pe", lambda e, pv=pv, ts_=ts_, tn=tn: e.matmul(pv[:, 0:tn], lhsT=ONESM, rhs=TMP[:, ts_], start=True, stop=True), reads=[Bcst, BTMP], writes=[Bpv])
            kb.op("act", lambda e, pv=pv, ts_=ts_, tn=tn: e.activation(out=TMP[:, ts_], in_=pv[:, 0:tn], func=AF.Sqrt, bias=eps_ap(C, RET_EPS), scale=1.0),
                  reads=[Bpv], writes=[BTMP])
            kb.op("dve", lambda e, ts_=ts_: e.reciprocal(out=TMP[:, ts_], in_=TMP[:, ts_]), reads=[BTMP], writes=[BTMP])
            kb.op("dve", lambda e, ts_=ts_, h=h: e.scalar_tensor_tensor(out=O[:, ts_], in0=O[:, ts_], scalar=gn[:, h:h + 1], in1=TMP[:, ts_],
                                                                       op0=ALU.mult, op1=ALU.mult), reads=[BO, Bgn, BTMP], writes=[BO])
            kb.op("act", lambda e, ts_=ts_: e.activation(out=TMP[:, ts_], in_=g[:, ts_], func=AF.Silu), reads=[Bg], writes=[BTMP])
            kb.op("dve", lambda e, ts_=ts_: e.tensor_tensor(out=O[:, ts_], in0=O[:, ts_], in1=TMP[:, ts_], op=ALU.mult), reads=[BO, BTMP], writes=[BO])
        kb.dma("sp", lambda e, h=h: e.dma_start(out=yv[:, h, :], in_=O[:]), reads=[BO], writes=[Buf()], is_out=True)
    C.pop()


def emit_p2b(C, io):
    kb = C.kb
    C.push()
    nrm_in = io["nrm"]
    wq_in = io["wq"]
    wkv_in = io["wkv"]
    rope_in = io["rope"]
    cst_in = io["cst"]
    y_out = io["y"]

    def T(shape, dt=F32):
        return C.sb(shape, dt), Buf()
    cxr, Bcx = T([128, 5, TALL], mybir.dt.float32r); cx = cxr[:].bitcast(F32); kr, Bkr = T([64, 2, TALL]); nrm, Bnrm = T([128, 5])
    wq, Bwq = T([128, 3, 4, 256], mybir.dt.float32r); wkv, Bwkv = T([128, 2, 4, 256], mybir.dt.float32r)
    C.push()
    wq32, Bwq32 = T([128, 3, 4, 256]); wkv32, Bwkv32 = T([128, 2, 4, 256])
    kb.dma("sp", lambda e: e.dma_start(out=wq32[:], in_=wq_in), writes=[Bwq32])
    kb.dma("sp", lambda e: e.dma_start(out=wkv32[:], in_=wkv_in), writes=[Bwkv32])
    kb.op("act", lambda e: e.copy(out=wq[:], in_=wq32[:]), reads=[Bwq32], writes=[Bwq])
    kb.op("act", lambda e: e.copy(out=wkv[:], in_=wkv32[:]), reads=[Bwkv32], writes=[Bwkv])
    C.pop()
    rope, Brope = T([64, 2, TALL])
    cst, Bcst = T([128, 384])
    ident = cst[:, 0:128]
    rstd, Brstd = T([128, TALL])
    R32 = mybir.dt.float32r
    krr, Bkrr = T([64, TALL], R32); qn, Bqn = T([128, TALL], R32); qrr, Bqrr = T([64, TALL], R32); kn, Bkn = T([128, TALL], R32)
    t1, Bt1 = T([64, TALL]); VT, BVT = T([128, NCH, 128], R32)
    Pm, BPm = T([128, TALL]); PT, BPT = T([128, NCH, 128], R32)
    krr32, Bkrr32 = T([64, TALL]); qrr32, Bqrr32 = T([64, TALL])
    sq = [T([128, 512]) for _ in range(2)]
    mx, Bmx = T([128, 1]); nb, Bnb = T([128, 1]); rs, Brs = T([128, 1]); ri, Bri = T([128, 1])
    OT = [T([128, 128]) for _ in range(2)]
    OT2 = [T([128, 128]) for _ in range(2)]
    big = C.ps([128, 2560]); Bbig = Buf(excl=True)
    psl = [(C.ps([128, 512]), Buf(excl=True)) for _ in range(3)]
    pcnt = [0]

    def nps():
        r = psl[pcnt[0] % 3]
        pcnt[0] += 1
        return r
    for a in range(5):
        kb.dma("sp" if a % 2 == 0 else "act", lambda e, a=a: e.dma_start(out=cx[:, a, :], in_=io["cx"][a]), writes=[Bcx])
    for (pr0, pr1, ai, srcap) in io["kr"]:
        kb.dma("sp", lambda e, pr0=pr0, pr1=pr1, ai=ai, srcap=srcap: e.dma_start(out=kr[pr0:pr1, ai, :], in_=srcap), writes=[Bkr])
    kb.dma("sp", lambda e: e.dma_start(out=rope[:], in_=rope_in.rearrange("a p t -> p a t")), writes=[Brope])
    kb.dma("sp", lambda e: e.dma_start(out=nrm[:], in_=nrm_in), writes=[Bnrm])
    kb.dma("sp", lambda e: e.dma_start(out=cst[:], in_=cst_in), writes=[Bcst])
    k = 0
    for (tiles, ones_ap) in [([0, 1, 2], cst[:, 128:256]), ([3, 4], cst[:, 256:384])]:
        for (t0, tn) in token_tiles(TALL):
            ps, Bps = nps()
            for ii, a in enumerate(tiles):
                s_, Bs_ = sq[k % 2]; k += 1
                kb.op("act", lambda e, s_=s_, a=a, t0=t0, tn=tn: e.activation(out=s_[:, 0:tn], in_=cx[:, a, t0:t0 + tn], func=AF.Square),
                      reads=[Bcx], writes=[Bs_])
                kb.op("pe", lambda e, ps=ps, s_=s_, tn=tn, ii=ii, ones_ap=ones_ap, n=len(tiles): e.matmul(
                    ps[:, 0:tn], lhsT=ones_ap, rhs=s_[:, 0:tn], start=(ii == 0), stop=(ii == n - 1)), reads=[Bcst, Bs_], writes=[Bps])
            kb.op("act", lambda e, ps=ps, t0=t0, tn=tn: e.activation(out=rstd[:, t0:t0 + tn], in_=ps[:, 0:tn], func=AF.Sqrt,
                                                                    bias=eps_ap(C, NORM_EPS), scale=1.0), reads=[Bps], writes=[Brstd])
        kb.op("dve", lambda e: e.reciprocal(out=rstd[:], in_=rstd[:]), reads=[Brstd], writes=[Brstd])
        for a in tiles:
            kb.op("dve", lambda e, a=a: e.scalar_tensor_tensor(out=cxr[:, a, :], in0=cx[:, a, :], scalar=nrm[:, a:a + 1], in1=rstd[:],
                                                               op0=ALU.mult, op1=ALU.mult), reads=[Bcx, Bnrm, Brstd], writes=[Bcx])
    kb.op("dve", lambda e: e.tensor_tensor(out=krr32[:], in0=kr[:, 0, :], in1=rope[:, 0, :], op=ALU.mult), reads=[Bkr, Brope], writes=[Bkrr32])
    kb.op("pool", lambda e: e.tensor_tensor(out=t1[:], in0=kr[:, 1, :], in1=rope[:, 1, :], op=ALU.mult), reads=[Bkr, Brope], writes=[Bt1])
    kb.op("dve", lambda e: e.tensor_tensor(out=krr[:], in0=krr32[:], in1=t1[:], op=ALU.add), reads=[Bkrr32, Bt1], writes=[Bkrr])
    oi = 0
    for hh in range(4):
        for (t0, tn) in token_tiles(TALL):
            ts_ = slice(t0, t0 + tn)
            ps, Bps = nps()
            for kc in range(3):
                kb.op("pe", lambda e, ps=ps, kc=kc, ts_=ts_, tn=tn, hh=hh: e.matmul(ps[:, 0:tn], lhsT=wq[:, kc, hh, 0:128], rhs=cxr[:, kc, ts_],
                                                                                   start=(kc == 0), stop=(kc == 2)), reads=[Bwq, Bcx], writes=[Bps])
            kb.op("act", lambda e, ps=ps, ts_=ts_, tn=tn: e.copy(out=qn[:, ts_], in_=ps[:, 0:tn]), reads=[Bps], writes=[Bqn])
            ps, Bps = nps()
            for kc in range(2):
                kb.op("pe", lambda e, ps=ps, kc=kc, ts_=ts_, tn=tn, hh=hh: e.matmul(ps[:, 0:tn], lhsT=wkv[:, kc, hh, 0:128], rhs=cxr[:, 3 + kc, ts_],
                                                                                   start=(kc == 0), stop=(kc == 1)), reads=[Bwkv, Bcx], writes=[Bps])
            kb.op("act", lambda e, ps=ps, ts_=ts_, tn=tn: e.copy(out=kn[:, ts_], in_=ps[:, 0:tn]), reads=[Bps], writes=[Bkn])
            for ri_, (c0, dst, Bdst) in enumerate([(128, qrr32, Bqrr32), (192, t1, Bt1)]):
                ps, Bps = nps()
                for kc in range(3):
                    kb.op("pe", lambda e, ps=ps, kc=kc, ts_=ts_, tn=tn, hh=hh, c0=c0: e.matmul(ps[0:64, 0:tn], lhsT=wq[:, kc, hh, c0:c0 + 64], rhs=cxr[:, kc, ts_],
                                                                                              start=(kc == 0), stop=(kc == 2)), reads=[Bwq, Bcx], writes=[Bps])
                kb.op("dve", lambda e, ps=ps, ts_=ts_, tn=tn, dst=dst, ri_=ri_: e.tensor_tensor(out=dst[:, ts_], in0=ps[0:64, 0:tn], in1=rope[:, ri_, ts_], op=ALU.mult),
                      reads=[Bps, Brope], writes=[Bdst])
        kb.op("dve", lambda e: e.tensor_tensor(out=qrr[:], in0=qrr32[:], in1=t1[:], op=ALU.add), reads=[Bqrr32, Bt1], writes=[Bqrr])
        for c4 in range(0, NCH, 4):
            ps, Bps = nps()
            n = min(4, NCH - c4)
            for ci in range(n):
                c = c4 + ci
                for kc in range(2):
                    kb.op("pe", lambda e, ps=ps, ci=ci, c=c, kc=kc, hh=hh: e.matmul(ps[:, ci * 128:(ci + 1) * 128], lhsT=cxr[:, 3 + kc, c * 128:(c + 1) * 128],
                                                                                   rhs=wkv[:, kc, hh, 128:256], start=(kc == 0), stop=(kc == 1)),
                          reads=[Bcx, Bwkv], writes=[Bps])
            kb.op("act", lambda e, ps=ps, c4=c4, n=n: e.copy(out=VT[:, c4:c4 + n, :], in_=ps[:, 0:n * 128].rearrange("p (a b) -> p a b", a=n)),
                  reads=[Bps], writes=[BVT])
        for qt in range(NCH):
            qs = slice(qt * 128, (qt + 1) * 128)
            nk = CTX if qt < 2 else TALL
            for (t0, tn) in token_tiles(nk):
                kb.op("pe", lambda e, qs=qs, t0=t0, tn=tn: e.matmul(big[:, t0:t0 + tn], lhsT=qn[:, qs], rhs=kn[:, t0:t0 + tn], start=True, stop=False),
                      reads=[Bqn, Bkn], writes=[Bbig])
                kb.op("pe", lambda e, qs=qs, t0=t0, tn=tn: e.matmul(big[:, t0:t0 + tn], lhsT=qrr[:, qs], rhs=krr[:, t0:t0 + tn], start=False, stop=True),
                      reads=[Bqrr, Bkrr], writes=[Bbig])
            kb.op("dve", lambda e, nk=nk: e.tensor_reduce(out=mx[:], in_=big[:, 0:nk], axis=AX.X, op=ALU.max), reads=[Bbig], writes=[Bmx])
            kb.op("dve", lambda e: e.tensor_scalar(out=nb[:], in0=mx[:], scalar1=-MLA_SCALE, scalar2=None, op0=ALU.mult), reads=[Bmx], writes=[Bnb])
            kb.op("act", lambda e, nk=nk: e.activation(out=Pm[:, 0:nk], in_=big[:, 0:nk], func=AF.Exp, bias=nb[:, 0:1], scale=MLA_SCALE, accum_out=rs[:, 0:1]),
                  reads=[Bbig, Bnb], writes=[BPm, Brs])
            kb.op("dve", lambda e: e.reciprocal(out=ri[:], in_=rs[:]), reads=[Brs], writes=[Bri])
            nkt = nk // 128
            for c4 in range(0, nkt, 4):
                ps, Bps = nps()
                n = min(4, nkt - c4)
                for ci in range(n):
                    c = c4 + ci
                    kb.op("pe", lambda e, ps=ps, ci=ci, c=c: e.transpose(out=ps[:, ci * 128:(ci + 1) * 128], in_=Pm[:, c * 128:(c + 1) * 128], identity=ident),
                          reads=[BPm, Bcst], writes=[Bps])
                ev = "act" if (c4 // 4) % 2 == 0 else "dve"
                if ev == "act":
                    kb.op("act", lambda e, ps=ps, c4=c4, n=n: e.copy(out=PT[:, c4:c4 + n, :], in_=ps[:, 0:n * 128].rearrange("p (a b) -> p a b", a=n)),
                          reads=[Bps], writes=[BPT])
                else:
                    kb.op("dve", lambda e, ps=ps, c4=c4, n=n: e.tensor_copy(out=PT[:, c4:c4 + n, :], in_=ps[:, 0:n * 128].rearrange("p (a b) -> p a b", a=n)),
                          reads=[Bps], writes=[BPT])
            po, Bpo = nps()
            for c in range(nkt):
                kb.op("pe", lambda e, po=po, c=c, nkt=nkt: e.matmul(po[:, 0:128], lhsT=PT[:, c, :], rhs=VT[:, c, :], start=(c == 0), stop=(c == nkt - 1)),
                      reads=[BPT, BVT], writes=[Bpo])
            ot, Bot = OT[oi % 2]; oi += 1
            kb.op("act", lambda e, po=po, ot=ot: e.activation(out=ot[:], in_=po[:, 0:128], func=AF.Copy, scale=ri[:, 0:1]), reads=[Bpo, Bri], writes=[Bot])
            pt2, Bpt2 = nps()
            kb.op("pe", lambda e, pt2=pt2, ot=ot: e.transpose(out=pt2[:, 0:128], in_=ot[:], identity=ident), reads=[Bot, Bcst], writes=[Bpt2])
            ot2, Bot2 = OT2[oi % 2]
            kb.op("dve", lambda e, pt2=pt2, ot2=ot2: e.tensor_copy(out=ot2[:], in_=pt2[:, 0:128]), reads=[Bpt2], writes=[Bot2])
            kb.dma("sp", lambda e, ot2=ot2, qs=qs, hh=hh: e.dma_start(out=y_out[hh * 128:(hh + 1) * 128, qs], in_=ot2[:]), reads=[Bot2], writes=[io["By"]])
    C.pop()


def emit_p2c(C, io):
    kb = C.kb
    C.push()
    cw_in = io["cw"]
    ln_in = io["ln"]
    cst_in = io["cst"]

    def T(shape, dt=F32):
        return C.sb(shape, dt), Buf()
    cw, Bcw = T([128, 4, 3]); ln, Bln = T([128, 4, 2]); cst, Bcst = T([128, 128])
    kb.dma("sp", lambda e: e.dma_start(out=cw[:], in_=cw_in), writes=[Bcw])
    kb.dma("sp", lambda e: e.dma_start(out=ln[:], in_=ln_in), writes=[Bln])
    kb.dma("sp", lambda e: e.dma_start(out=cst[:], in_=cst_in), writes=[Bcst])
    X3 = [T([128, TP]) for _ in range(3)]
    OC, BOC = T([128, TP])
    Y, BY = T([128, TALL]); Y2, BY2 = T([128, TALL]); TMP, BTMP = T([128, TALL])
    BN, BBN = T([128, TALL]); GT, BGT = T([128, TALL])
    psl = [(C.ps([128, 512]), Buf(excl=True)) for _ in range(4)]
    pcnt = [0]

    def nps():
        r = psl[pcnt[0] % 4]
        pcnt[0] += 1
        return r
    n = TP - 2
    for a in range(4):
        for i in range(3):
            srcap = io["cv"](i, a)
            kb.op("pool", lambda e, i=i: e.memset(X3[i][0][:, 0:1], 0.0), writes=[X3[i][1]])
            kb.op("pool", lambda e, i=i: e.memset(X3[i][0][:, 257:259], 0.0), writes=[X3[i][1]])
            kb.op("pool", lambda e, i=i: e.memset(X3[i][0][:, TP - 1:TP], 0.0), writes=[X3[i][1]])
            kb.dma("sp", lambda e, i=i, srcap=srcap: e.dma_start(out=X3[i][0][:, 1:257], in_=srcap[:, 0:CTX]), writes=[X3[i][1]])
            kb.dma("act", lambda e, i=i, srcap=srcap: e.dma_start(out=X3[i][0][:, 259:259 + SEQ], in_=srcap[:, CTX:TALL]), writes=[X3[i][1]])
        (bgt, Bbgt), (cg, Bcg), (u, Bu) = X3
        kb.op("dve", lambda e: e.tensor_tensor(out=cg[:], in0=cg[:], in1=u[:], op=ALU.mult), reads=[Bcg, Bu], writes=[Bcg])
        kb.op("pool", lambda e: e.memset(OC[:], 0.0), writes=[BOC])
        kb.op("act", lambda e, a=a: e.activation(out=OC[:, 1:1 + n], in_=cg[:, 1:1 + n], func=AF.Copy, scale=cw[:, a, 1:2]), reads=[Bcg, Bcw], writes=[BOC])
        kb.op("dve", lambda e, a=a: e.scalar_tensor_tensor(out=OC[:, 1:1 + n], in0=cg[:, 0:n], scalar=cw[:, a, 0:1], in1=OC[:, 1:1 + n], op0=ALU.mult, op1=ALU.add),
              reads=[Bcg, Bcw, BOC], writes=[BOC])
        kb.op("dve", lambda e, a=a: e.scalar_tensor_tensor(out=OC[:, 1:1 + n], in0=cg[:, 2:2 + n], scalar=cw[:, a, 2:3], in1=OC[:, 1:1 + n], op0=ALU.mult, op1=ALU.add),
              reads=[Bcg, Bcw, BOC], writes=[BOC])
        kb.op("dve", lambda e: e.tensor_tensor(out=OC[:], in0=OC[:], in1=bgt[:], op=ALU.mult), reads=[BOC, Bbgt], writes=[BOC])
        kb.dma("sp", lambda e, a=a: e.dma_start(out=io["yc"](a)[:, 0:CTX], in_=OC[:, 1:257]), reads=[BOC], writes=[io["By"]])
        kb.dma("act", lambda e, a=a: e.dma_start(out=io["yc"](a)[:, CTX:TALL], in_=OC[:, 259:259 + SEQ]), reads=[BOC], writes=[io["By"]])
        kb.dma("sp", lambda e, a=a: e.dma_start(out=Y[:], in_=io["yf"](a)), reads=[io["Byf"]], writes=[BY])
        kb.dma("act", lambda e, a=a: e.dma_start(out=Y2[:], in_=io["yb"](a)), reads=[io["Byb"]], writes=[BY2])
        kb.dma("sp", lambda e, a=a: e.dma_start(out=BN[:], in_=io["bon"](a)), reads=[io["Bbg"]], writes=[BBN])
        kb.dma("act", lambda e, a=a: e.dma_start(out=GT[:], in_=io["gate"](a)), reads=[io["Bbg"]], writes=[BGT])
        kb.op("dve", lambda e: e.tensor_tensor(out=Y[:], in0=Y[:], in1=Y2[:], op=ALU.add), reads=[BY, BY2], writes=[BY])
        for (t0, tn) in token_tiles(TALL):
            ts_ = slice(t0, t0 + tn)
            pm, Bpm = nps()
            kb.op("pe", lambda e, pm=pm, ts_=ts_, tn=tn: e.matmul(pm[:, 0:tn], lhsT=cst[:], rhs=Y[:, ts_], start=True, stop=True), reads=[Bcst, BY], writes=[Bpm])
            kb.op("dve", lambda e, pm=pm, ts_=ts_, tn=tn: e.tensor_tensor(out=Y[:, ts_], in0=Y[:, ts_], in1=pm[:, 0:tn], op=ALU.subtract),
                  reads=[Bpm, BY], writes=[BY])
            kb.op("act", lambda e, ts_=ts_: e.activation(out=TMP[:, ts_], in_=Y[:, ts_], func=AF.Square), reads=[BY], writes=[BTMP])
            pv, Bpv = nps()
            kb.op("pe", lambda e, pv=pv, ts_=ts_, tn=tn: e.matmul(pv[:, 0:tn], lhsT=cst[:], rhs=TMP[:, ts_], start=True, stop=True), reads=[Bcst, BTMP], writes=[Bpv])
            kb.op("act", lambda e, pv=pv, ts_=ts_, tn=tn: e.activation(out=TMP[:, ts_], in_=pv[:, 0:tn], func=AF.Sqrt, bias=eps_ap(C, RWKV_GN_EPS), scale=1.0),
                  reads=[Bpv], writes=[BTMP])
            kb.op("dve", lambda e, ts_=ts_: e.reciprocal(out=TMP[:, ts_], in_=TMP[:, ts_]), reads=[BTMP], writes=[BTMP])
            kb.op("dve", lambda e, ts_=ts_: e.tensor_tensor(out=Y[:, ts_], in0=Y[:, ts_], in1=TMP[:, ts_], op=ALU.mult), reads=[BY, BTMP], writes=[BY])
            kb.op("act", lambda e, ts_=ts_, a=a: e.activation(out=Y[:, ts_], in_=Y[:, ts_], func=AF.Identity, bias=ln[:, a, 1:2], scale=ln[:, a, 0:1]),
                  reads=[BY, Bln], writes=[BY])
            kb.op("dve", lambda e, ts_=ts_: e.tensor_tensor(out=Y[:, ts_], in0=Y[:, ts_], in1=BN[:, ts_], op=ALU.add), reads=[BY, BBN], writes=[BY])
            kb.op("dve", lambda e, ts_=ts_: e.tensor_tensor(out=Y[:, ts_], in0=Y[:, ts_], in1=GT[:, ts_], op=ALU.mult), reads=[BY, BGT], writes=[BY])
        kb.dma("sp", lambda e, a=a: e.dma_start(out=io["ya"](a), in_=Y[:]), reads=[BY], writes=[io["By"]])
    C.pop()


def emit_p3(C, io, segs):
    kb = C.kb
    C.push()
    yT = io["yT"]
    xT = io["xT"]
    msk = io["msk"]
    wo = io["wo"]
    wu = io["wu"]
    wc = io["wc"]
    wd = io["wd"]
    xo = io["xo"]
    SM = max(s[1] for s in segs); TM_ = SM + 2

    def T(shape, dt=F32):
        return C.sb(shape, dt), Buf()
    x_sb, Bx = T([128, 16, TM_]); y_sb, By = T([128, 16, TM_], BF16); o_sb, Bo = T([128, 16, TM_])
    a_sb, Ba = T([128, NFF, SM], BF16)
    v_sb, Bv = T([128, 2, 7, 16]); m_sb, Bm = T([128, 2 * len(segs)]); wc_sb, Bwc = T([128, 2 * NFF, 3])
    gm, Bgm = T([128, 2, 4, 16])
    ones, Bones = T([128, 128]); rstd, Brstd = T([128, TM_])
    sq = [T([128, 512]) for _ in range(2)]
    tmp = [T([128, TM_]) for _ in range(2)]
    wt = [T([128, 16, 256], BF16) for _ in range(4)]
    wdt = [T([128, NFF, 128], BF16) for _ in range(2)]
    ug = [T([128, 2, TM_]) for _ in range(2)]
    cvt = [T([128, 2, SM]) for _ in range(2)]
    pss = [(C.ps([128, 512]), Buf(excl=True)) for _ in range(2)]
    psm = [(C.ps([128, 512]), Buf(excl=True)) for _ in range(6)]
    pcnt = [0]

    def nps():
        r = psm[pcnt[0] % 6]
        pcnt[0] += 1
        return r
    for (ci_, vi, vsrc) in io["vec_srcs"]:
        kb.dma("sp", lambda e, ci_=ci_, vi=vi, vsrc=vsrc: e.dma_start(out=v_sb[:, ci_, vi, :], in_=vsrc), writes=[Bv])
    kb.dma("sp", lambda e: e.dma_start(out=m_sb[:], in_=msk), writes=[Bm])
    kb.dma("sp", lambda e: e.dma_start(out=wc_sb[:], in_=wc), writes=[Bwc])
    kb.op("dve", lambda e: e.memset(ones[:], 1.0 / D), writes=[Bones])
    for ci in range(2):
        kb.op("dve", lambda e, ci=ci: e.tensor_tensor(out=gm[:, ci, 0, :], in0=v_sb[:, ci, 0, :], in1=v_sb[:, ci, 3, :], op=ALU.mult), reads=[Bv], writes=[Bgm])
        kb.op("dve", lambda e, ci=ci: e.scalar_tensor_tensor(out=gm[:, ci, 1, :], in0=v_sb[:, ci, 5, :], scalar=1.0, in1=v_sb[:, ci, 1, :],
                                                             op0=ALU.add, op1=ALU.mult), reads=[Bv], writes=[Bgm])
        kb.op("dve", lambda e, ci=ci: e.tensor_tensor(out=gm[:, ci, 2, :], in0=v_sb[:, ci, 2, :], in1=v_sb[:, ci, 6, :], op=ALU.mult), reads=[Bv], writes=[Bgm])
    xv = xT.rearrange("(kc p) t -> p kc t", p=128)
    yv = yT.rearrange("(kc p) t -> p kc t", p=128)
    xov = xo.rearrange("(kc p) t -> p kc t", p=128)
    wi = [0]

    def half_tiles(n):
        h = (n + 1) // 2
        return [(0, h), (h, n - h)] if n > 512 else [(0, n)]
    for si, (lo, S, ci, hl, hr) in enumerate(segs):
        Tn = S + 2
        out0 = lo
        a0 = 0 if hl else 1
        a1 = Tn if hr else Tn - 1
        g0 = lo - 1 + a0
        if not hl:
            kb.op("dve", lambda e: e.memset(x_sb[:, :, 0:1], 0.0), writes=[Bx])
            kb.op("dve", lambda e: e.memset(y_sb[:, :, 0:1], 0.0), writes=[By])
        if not hr:
            kb.op("dve", lambda e, Tn=Tn: e.memset(x_sb[:, :, Tn - 1:Tn], 0.0), writes=[Bx])
            kb.op("dve", lambda e, Tn=Tn: e.memset(y_sb[:, :, Tn - 1:Tn], 0.0), writes=[By])
        for kc in range(16):
            kb.dma("sp" if kc % 2 == 0 else "act", lambda e, kc=kc, a0=a0, a1=a1, g0=g0: e.dma_start(out=x_sb[:, kc, a0:a1], in_=xv[:, kc, g0:g0 + a1 - a0]), reads=[io["Bx"]], writes=[Bx])
        for kc in range(16):
            kb.dma("pool", lambda e, kc=kc, a0=a0, a1=a1, g0=g0: e.dma_start(out=y_sb[:, kc, a0:a1], in_=yv[:, kc, g0:g0 + a1 - a0]), reads=[io["By"]], writes=[By])
        for nb in range(8):
            j = wi[0] % 4; wi[0] += 1
            w_, Bw_ = wt[j]
            kb.dma(io["wdma"](), lambda e, w_=w_, nb=nb: e.dma_start(out=w_[:], in_=wo[:, nb * 256:(nb + 1) * 256].rearrange("(kc p) n -> p kc n", p=128)), writes=[Bw_])
            for nc_ in range(2):
                dch = nb * 2 + nc_
                for (t0, tn) in half_tiles(Tn):
                    p, Bp = nps()
                    for kc in range(16):
                        kb.op("pe", lambda e, p=p, w_=w_, kc=kc, nc_=nc_, t0=t0, tn=tn: e.matmul(p[:, 0:tn], lhsT=w_[:, kc, nc_ * 128:(nc_ + 1) * 128],
                                                                                            rhs=y_sb[:, kc, t0:t0 + tn], start=(kc == 0), stop=(kc == 15)),
                              reads=[Bw_, By], writes=[Bp])
                    kb.op("act", lambda e, p=p, dch=dch, t0=t0, tn=tn: e.copy(out=o_sb[:, dch, t0:t0 + tn], in_=p[:, 0:tn]), reads=[Bp], writes=[Bo])
        emit_rstd(C, o_sb, Bo, 16, Tn, ones, Bones, rstd, Brstd, [s[0] for s in sq], [s[1] for s in sq], [p[0] for p in pss], [p[1] for p in pss], NORM_EPS)
        for kc in range(16):
            t_, Bt_ = tmp[kc % 2]
            kb.op("pool", lambda e, t_=t_, kc=kc, Tn=Tn: e.tensor_tensor(out=t_[:, 0:Tn], in0=o_sb[:, kc, 0:Tn], in1=rstd[:, 0:Tn], op=ALU.mult),
                  reads=[Bo, Brstd], writes=[Bt_])
            kb.op("dve", lambda e, t_=t_, kc=kc, Tn=Tn, ci=ci: e.scalar_tensor_tensor(out=x_sb[:, kc, 0:Tn], in0=t_[:, 0:Tn], scalar=gm[:, ci, 0, kc:kc + 1],
                                                                                 in1=x_sb[:, kc, 0:Tn], op0=ALU.mult, op1=ALU.add),
                  reads=[Bt_, Bgm, Bx], writes=[Bx])
        emit_rstd(C, x_sb, Bx, 16, Tn, ones, Bones, rstd, Brstd, [s[0] for s in sq], [s[1] for s in sq], [p[0] for p in pss], [p[1] for p in pss], NORM_EPS)
        for kc in range(16):
            t_, Bt_ = tmp[kc % 2]
            kb.op("dve", lambda e, t_=t_, kc=kc, Tn=Tn: e.tensor_tensor(out=t_[:, 0:Tn], in0=x_sb[:, kc, 0:Tn], in1=rstd[:, 0:Tn], op=ALU.mult),
                  reads=[Bx, Brstd], writes=[Bt_])
            kb.op("act", lambda e, t_=t_, kc=kc, Tn=Tn, ci=ci: e.activation(out=y_sb[:, kc, 0:Tn], in_=t_[:, 0:Tn], func=AF.Identity,
                                                                        bias=v_sb[:, ci, 4, kc:kc + 1], scale=gm[:, ci, 1, kc:kc + 1]),
                  reads=[Bt_, Bgm, Bv], writes=[By])
        for blk in range(NFF // 2):
            wts = []
            for half in range(2):
                j = wi[0] % 4; wi[0] += 1
                w_, Bw_ = wt[j]
                c0 = half * D_FF + blk * 256
                kb.dma(io["wdma"](), lambda e, w_=w_, c0=c0: e.dma_start(out=w_[:], in_=wu[:, c0:c0 + 256].rearrange("(kc p) n -> p kc n", p=128)), writes=[Bw_])
                wts.append((w_, Bw_))
            for cc in range(2):
                c = blk * 2 + cc
                u_, Bu_ = ug[c % 2]
                cv_, Bcv_ = cvt[c % 2]
                for half in range(2):
                    w_, Bw_ = wts[half]
                    for (t0, tn) in half_tiles(Tn):
                        p, Bp = nps()
                        for kc in range(16):
                            kb.op("pe", lambda e, p=p, w_=w_, kc=kc, cc=cc, t0=t0, tn=tn: e.matmul(p[:, 0:tn], lhsT=w_[:, kc, cc * 128:(cc + 1) * 128],
                                                                                              rhs=y_sb[:, kc, t0:t0 + tn], start=(kc == 0), stop=(kc == 15)),
                                  reads=[Bw_, By], writes=[Bp])
                        kb.op("act", lambda e, p=p, u_=u_, half=half, t0=t0, tn=tn: e.copy(out=u_[:, half, t0:t0 + tn], in_=p[:, 0:tn]), reads=[Bp], writes=[Bu_])
                kb.op("dve", lambda e, u_=u_, si=si: e.tensor_tensor(out=u_[:, :, 0:1], in0=u_[:, :, 0:1], in1=m_sb[:, 2 * si:2 * si + 1].unsqueeze(1).broadcast_to([128, 2, 1]),
                                                                      op=ALU.mult), reads=[Bu_, Bm], writes=[Bu_])
                kb.op("dve", lambda e, u_=u_, si=si, Tn=Tn: e.tensor_tensor(out=u_[:, :, Tn - 1:Tn], in0=u_[:, :, Tn - 1:Tn],
                                                                             in1=m_sb[:, 2 * si + 1:2 * si + 2].unsqueeze(1).broadcast_to([128, 2, 1]), op=ALU.mult),
                      reads=[Bu_, Bm], writes=[Bu_])
                for half in range(2):
                    wrow = half * NFF + c
                    kb.op("act", lambda e, u_=u_, cv_=cv_, half=half, wrow=wrow, S=S: e.activation(out=cv_[:, half, 0:S], in_=u_[:, half, 1:1 + S], func=AF.Copy,
                                                                                             scale=wc_sb[:, wrow, 1:2]), reads=[Bu_, Bwc], writes=[Bcv_])
                    eng = "dve"
                    kb.op(eng, lambda e, u_=u_, cv_=cv_, half=half, wrow=wrow, S=S: e.scalar_tensor_tensor(out=cv_[:, half, 0:S], in0=u_[:, half, 0:S], scalar=wc_sb[:, wrow, 0:1],
                                                                                                    in1=cv_[:, half, 0:S], op0=ALU.mult, op1=ALU.add),
                          reads=[Bu_, Bwc, Bcv_], writes=[Bcv_])
                    kb.op(eng, lambda e, u_=u_, cv_=cv_, half=half, wrow=wrow, S=S: e.scalar_tensor_tensor(out=cv_[:, half, 0:S], in0=u_[:, half, 2:2 + S], scalar=wc_sb[:, wrow, 2:3],
                                                                                                    in1=cv_[:, half, 0:S], op0=ALU.mult, op1=ALU.add),
                          reads=[Bu_, Bwc, Bcv_], writes=[Bcv_])
                kb.op("act", lambda e, cv_=cv_, S=S: e.activation(out=cv_[:, 0, 0:S], in_=cv_[:, 0, 0:S], func=AF.Silu), reads=[Bcv_], writes=[Bcv_])
                kb.op("pool", lambda e, cv_=cv_, c=c, S=S: e.tensor_tensor(out=a_sb[:, c, 0:S], in0=cv_[:, 0, 0:S], in1=cv_[:, 1, 0:S], op=ALU.mult),
                      reads=[Bcv_], writes=[Ba])
        for nb in range(D // 128):
            w_, Bw_ = wdt[nb % 2]
            kb.dma(io["wdma"](), lambda e, w_=w_, nb=nb: e.dma_start(out=w_[:], in_=wd[nb]), writes=[Bw_])
            for nc_ in range(1):
                dch = nb
                p, Bp = nps()
                for c in range(NFF):
                    kb.op("pe", lambda e, p=p, w_=w_, c=c, nc_=nc_, S=S: e.matmul(p[:, 0:S], lhsT=w_[:, c, nc_ * 128:(nc_ + 1) * 128], rhs=a_sb[:, c, 0:S],
                                                                             start=(c == 0), stop=(c == NFF - 1)), reads=[Bw_, Ba], writes=[Bp])
                kb.op("act", lambda e, p=p, dch=dch, S=S: e.copy(out=o_sb[:, dch, 0:S], in_=p[:, 0:S]), reads=[Bp], writes=[Bo])
        emit_rstd(C, o_sb, Bo, 16, S, ones, Bones, rstd, Brstd, [s[0] for s in sq], [s[1] for s in sq], [p[0] for p in pss], [p[1] for p in pss], NORM_EPS)
        for kc in range(16):
            t_, Bt_ = tmp[kc % 2]
            kb.op("pool", lambda e, t_=t_, kc=kc, S=S: e.tensor_tensor(out=t_[:, 0:S], in0=o_sb[:, kc, 0:S], in1=rstd[:, 0:S], op=ALU.mult),
                  reads=[Bo, Brstd], writes=[Bt_])
            kb.op("dve", lambda e, t_=t_, kc=kc, S=S, ci=ci: e.scalar_tensor_tensor(out=t_[:, 0:S], in0=t_[:, 0:S], scalar=gm[:, ci, 2, kc:kc + 1],
                                                                               in1=x_sb[:, kc, 1:1 + S], op0=ALU.mult, op1=ALU.add),
                  reads=[Bt_, Bgm, Bx], writes=[Bt_])
            kb.dma("sp", lambda e, t_=t_, kc=kc, S=S, out0=out0: e.dma_start(out=xov[:, kc, out0:out0 + S], in_=t_[:, 0:S]), reads=[Bt_], writes=[io["Bxo"]])
    C.pop()


def _ctx_push(self):
    self._stk.append(self.st)
    self.st = ExitStack()


def _ctx_pop(self):
    self.kb.barrier()
    self.st.close()
    self.st = self._stk.pop()


def _ctx_scratch(self, name, shape, dt=F32):
    return self.nc.dram_tensor(name, list(shape), dt, kind="Internal").ap()


Ctx.push = _ctx_push
Ctx.pop = _ctx_pop
Ctx.scratch = _ctx_scratch


def emit_p0f(C, io, nl):
    kb = C.kb
    C.push()

    def T(shape, dt=F32):
        return C.sb(shape, dt), Buf()
    c_sb, Bc = T([128, 16, 2]); s_sb, Bs = T([128, 16, 2]); mb_sb, Bmb = T([128, nl, 96]); mv, Bmv = T([128, 2, nl, 96])
    wt = [T([128, 16, 512]) for _ in range(4)]
    pst = [(C.ps([128, 512]), Buf(excl=True)) for _ in range(4)]
    kb.dma("sp", lambda e: e.dma_start(out=c_sb[:], in_=io["cT"]), writes=[Bc])
    kb.dma("sp", lambda e: e.dma_start(out=mb_sb[:], in_=io["mb"]), writes=[Bmb])
    kb.op("act", lambda e: e.activation(out=s_sb[:], in_=c_sb[:], func=AF.Silu), reads=[Bc], writes=[Bs])
    it = 0
    for l in range(nl):
        for nt in range(24):
            w, Bw = wt[it % 4]; p, Bp = pst[it % 4]
            src = io["mw"][l, :, nt * 512:(nt + 1) * 512].rearrange("(kc p) n -> p kc n", p=128)
            kb.dma("sp" if it % 2 == 0 else "act", lambda e, w=w, src=src: e.dma_start(out=w[:], in_=src), writes=[Bw])
            for cc in range(4):
                for kc in range(16):
                    kb.op("pe", lambda e, p=p, w=w, cc=cc, kc=kc: e.matmul(p[:, cc * 2:cc * 2 + 2], lhsT=w[:, kc, cc * 128:(cc + 1) * 128], rhs=s_sb[:, kc, :],
                                                                         start=(kc == 0), stop=(kc == 15)), reads=[Bw, Bs], writes=[Bp])
            g0 = nt * 4
            kb.op("dve", lambda e, p=p, l=l, g0=g0: e.tensor_tensor(out=mv[:, :, l, g0:g0 + 4], in0=p[:, 0:8].rearrange("p (c r) -> p r c", r=2),
                                                                   in1=mb_sb[:, l, g0:g0 + 4].unsqueeze(1).broadcast_to([128, 2, 4]), op=ALU.add),
                  reads=[Bp, Bmb], writes=[Bmv])
            it += 1
    kb.dma("sp", lambda e: e.dma_start(out=io["mscr"], in_=mv[:]), reads=[Bmv], writes=[Buf()])
    C.pop()


def emit_reverse(C, io, jobs):
    kb = C.kb
    C.push()

    def T(shape, dt=F32):
        return C.sb(shape, dt), Buf()
    cst, Bcst = T([128, 256])
    kb.dma("sp", lambda e: e.dma_start(out=cst[:, 0:128], in_=io["ident"]), writes=[Bcst])
    kb.dma("sp", lambda e: e.dma_start(out=cst[:, 128:256], in_=io["jmat"]), writes=[Bcst])
    ident = cst[:, 0:128]; J = cst[:, 128:256]
    X = [T([128, TALL]) for _ in range(3)]
    XR = [T([128, TALL]) for _ in range(3)]
    TK = [T([128, 512]) for _ in range(2)]
    psl = [(C.ps([128, 512]), Buf(excl=True)) for _ in range(4)]
    pc = [0]

    def nps():
        r = psl[pc[0] % 4]; pc[0] += 1
        return r
    for ji, (src, dst, n, off_c, off_l) in enumerate(jobs):
        x, Bx = X[ji % 3]; xr, Bxr = XR[ji % 3]
        kb.dma("sp" if ji % 2 == 0 else "act", lambda e, x=x, src=src, n=n: e.dma_start(out=x[0:n, :], in_=src), writes=[Bx])
        for b4 in range(0, NCH, 4):
            nb = min(4, NCH - b4)
            tk, Btk = TK[(b4 // 4) % 2]
            ps, Bps = nps()
            for bi in range(nb):
                b = b4 + bi
                kb.op("pe", lambda e, ps=ps, bi=bi, b=b, x=x, n=n: e.transpose(out=ps[:, bi * 128:bi * 128 + n], in_=x[0:n, b * 128:(b + 1) * 128], identity=ident[0:n, 0:n]),
                      reads=[Bx, Bcst], writes=[Bps])
            kb.op("act", lambda e, ps=ps, tk=tk, nb=nb, n=n: e.copy(out=tk[:, 0:nb * 128].rearrange("p (a b) -> p a b", a=nb)[:, :, 0:n],
                                                                   in_=ps[:, 0:nb * 128].rearrange("p (a b) -> p a b", a=nb)[:, :, 0:n]), reads=[Bps], writes=[Btk])
            ps2, Bps2 = nps()
            for bi in range(nb):
                kb.op("pe", lambda e, ps2=ps2, bi=bi, tk=tk, n=n: e.matmul(ps2[0:n, bi * 128:(bi + 1) * 128], lhsT=tk[:, bi * 128:bi * 128 + n], rhs=J, start=True, stop=True),
                      reads=[Btk, Bcst], writes=[Bps2])
            for bi in range(nb):
                b = b4 + bi
                mb_ = (1 - b) if b < 2 else (2 + (15 - (b - 2)))
                kb.op("dve", lambda e, ps2=ps2, bi=bi, mb_=mb_, xr=xr, n=n: e.tensor_copy(out=xr[0:n, mb_ * 128:(mb_ + 1) * 128], in_=ps2[0:n, bi * 128:(bi + 1) * 128]),
                      reads=[Bps2], writes=[Bxr])
        kb.dma("sp", lambda e, xr=xr, dst=dst, n=n, off_c=off_c: e.dma_start(out=dst[:, off_c:off_c + CTX], in_=xr[0:n, 0:CTX]), reads=[Bxr], writes=[Buf()])
        kb.dma("act", lambda e, xr=xr, dst=dst, n=n, off_l=off_l: e.dma_start(out=dst[:, off_l:off_l + SEQ], in_=xr[0:n, CTX:TALL]), reads=[Bxr], writes=[Buf()])
    C.pop()


def cast_jobs(jobs):
    out = []
    for (dst, src, rows, cols, rblk, cb) in jobs:
        for r0 in range(0, rows, rblk):
            n = min(rblk, rows - r0)
            out.append((dst, src, r0, n, cb))
    return out


def emit_cast_one(C, job):
    if job[0] == "dn":
        _, dst, src, nb = job
        C.kb.dma("pool", lambda e: e.dma_start(out=dst[nb], in_=src[:, nb * 128:(nb + 1) * 128].rearrange("(c p) n -> p c n", p=128)), writes=[Buf()], bg=True)
        return
    dst, src, r0, n, cb = job
    C.kb.dma("pool", lambda e: e.dma_start(out=dst[r0:r0 + n, :].rearrange("r (a b) -> r a b", b=cb),
                                           in_=src[r0:r0 + n, :].rearrange("r (a b) -> r a b", b=cb)), writes=[Buf()], bg=True)


def emit_cast(C, jobs):
    for j in cast_jobs(jobs):
        emit_cast_one(C, j)


_WQ = [0]


def _wdma():
    _WQ[0] += 1
    return "sp" if _WQ[0] % 2 == 0 else "act"


P3F_SEGS = [(0, 256, 0, False, False), (256, 410, 1, False, True), (666, 410, 1, True, True), (1076, 410, 1, True, True), (1486, 410, 1, True, True), (1896, 408, 1, True, False)]


def build_fused(nl=DEPTH, dbg=False):
    C = Ctx()
    C._stk = []
    kb = C.kb
    for eps in (NORM_EPS, 1e-12, RET_EPS, RWKV_GN_EPS):
        make_eps(C, eps)
    I = C.dram_in
    x0T = I("x0T", [D, TALL]); cT = I("cT", [128, 16, 2]); mw = I("mw", [nl, D, NMOD * D]); mb = I("mb", [128, nl, 96]); ng = I("ng", [128, nl, 4, 16])
    w_in = I("w_in", [nl, D, IN_W]); w_out = I("w_out", [nl, D, D]); w_up = I("w_up", [nl, D, 2 * D_FF]); w_cv = I("w_cv", [nl, 128, 2 * NFF, 3]); w_dn = I("w_dn", [nl, D_FF, D])
    r_mu = I("r_mu", [nl, 2, 64, 24, 2]); r_mu2 = I("r_mu2", [nl, 2, 128, 4, 2]); r_par = I("r_par", [nl, 2, 64, 5, 8])
    r_wup = I("r_wup", [nl, 2, 96, 512]); r_aup = I("r_aup", [nl, 2, 96, 512]); r_gup = I("r_gup", [nl, 256, 512])
    r_cst = I("r_cst", [128, 256]); r_mk = I("r_mk", [64, 2112]); r_ln = I("r_ln", [nl, 128, 4, 2]); c_w = I("c_w", [nl, 128, 4, 3]); c_bd = I("c_bd", [128, 128])
    a_nrm = I("a_nrm", [nl, 128, 5]); a_wq = I("a_wq", [nl, 128, 3, 4, 256]); a_wkv = I("a_wkv", [nl, 128, 2, 4, 256]); a_rope = I("a_rope", [2, 64, TALL]); a_cst = I("a_cst", [128, 384])
    d_rope = I("d_rope", [2, 128, TALL]); d_rd = I("d_rd", [nl, 128, 8]); d_gn = I("d_gn", [nl, 128, 4]); d_tab = I("d_tab", [128, 770]); d_cst = I("d_cst", [128, 256])
    p3_msk = I("p3_msk", [128, 12]); jmat = I("jmat", [128, 128]); identm = I("identm", [128, 128])
    xo = C.dram_out("xo", [D, SEQ])
    S = C.scratch
    xs = [S("xsA", [D, TALL]), S("xsB", [D, TALL])]
    pT = S("pT", [IN_W, TALL]); yT = S("yT", [D, TALL]); mscr = S("mscr", [128, 2, nl, 96])
    u_f = S("u_f", [1536, TP]); u2_f = S("u2_f", [512, TP]); u_r = S("u_r", [1536, TP]); u2_r = S("u2_r", [512, TP])
    y_f = S("y_f", [512, TALL]); y_r = S("y_r", [512, TALL]); y_b = S("y_b", [512, TALL])
    bon = S("bon", [512, TALL]); gate = S("gate", [512, TALL]); bon2 = S("bon2", [512, TALL]); gate2 = S("gate2", [512, TALL])
    wb_in = S("wb_in", [D, IN_W], BF16); wb_out = S("wb_out", [D, D], BF16); wb_up = S("wb_up", [D, 2 * D_FF], BF16); wb_dn = S("wb_dn", [16, 128, NFF, 128], BF16)
    emit_cast(C, [(wb_in, w_in[0], D, IN_W, 512, 896)])
    dbg_outs = {}
    if dbg:
        dbg_outs = {"d_pT": C.dram_out("d_pT", [IN_W, TALL]), "d_yT": C.dram_out("d_yT", [D, TALL]), "d_x1": C.dram_out("d_x1", [D, TALL]),
                    "d_m": C.dram_out("d_m", [128, 2, nl, 96])}
    C.push()
    Z = C.sb([128, TP]); Bz = Buf()
    kb.op("dve", lambda e: e.memset(Z[:], 0.0), writes=[Bz])
    qi = 0
    for (t_, nr) in [(u_f, 1536), (u2_f, 512), (u_r, 1536), (u2_r, 512)]:
        for r0 in range(0, nr, 128):
            kb.dma("sp" if qi % 2 == 0 else "act", lambda e, t_=t_, r0=r0: e.dma_start(out=t_[r0:r0 + 128, :], in_=Z[:]), reads=[Bz], writes=[Buf()])
            qi += 1
    C.pop()
    emit_p0f(C, {"cT": cT, "mw": mw, "mb": mb, "mscr": mscr}, nl)
    if dbg:
        kb.dma("sp", lambda e: e.dma_start(out=dbg_outs["d_m"], in_=mscr), writes=[Buf()], is_out=True)
    sw64 = [(0, 16, 16), (16, 32, 0), (32, 48, 48), (48, 64, 32)]
    sw128 = [(0, 32, 32), (32, 64, 0), (64, 96, 96), (96, 128, 64)]
    x_cur = x0T
    for l in range(nl):
        x_next = xs[l % 2]
        for (c0, classes) in [(0, [(0, 0, CTX), (1, CTX, T1)]), (T1, [(1, 0, T1)])]:
            emit_p1(C, {"xT": x_cur[:, c0:c0 + T1], "w": wb_in, "wdma": _wdma, "pT": pT[:, c0:c0 + T1],
                        "vec_srcs": [ng[:, l, 0, :], mscr[:, 1, l, 0:16], mscr[:, 1, l, 16:32], mscr[:, 0, l, 0:16], mscr[:, 0, l, 16:32]]}, classes)
        if dbg and l == 0:
            kb.dma("sp", lambda e: e.dma_start(out=dbg_outs["d_pT"], in_=pT), writes=[Buf()], is_out=True)
        for (r0, n, dst, d0) in [(0, 1536, u_f, 0), (1536, 96, u2_f, 0), (1632, 96, u2_f, 128), (1728, 256, u2_f, 256)]:
            kb.dma("sp", lambda e, r0=r0, n=n, dst=dst, d0=d0: e.dma_start(out=dst[d0:d0 + n, 1:1 + CTX], in_=pT[r0:r0 + n, 0:CTX]), writes=[Buf()])
            kb.dma("act", lambda e, r0=r0, n=n, dst=dst, d0=d0: e.dma_start(out=dst[d0:d0 + n, 259:259 + SEQ], in_=pT[r0:r0 + n, CTX:TALL]), writes=[Buf()])
        jobs = [(pT[128 * i:128 * i + 128, :], u_r[128 * i:128 * i + 128, :], 128, 1, 259) for i in range(12)]
        jobs += [(pT[1536:1632, :], u2_r[0:96, :], 96, 1, 259), (pT[1632:1728, :], u2_r[128:224, :], 96, 1, 259)]
        emit_reverse(C, {"ident": identm, "jmat": jmat}, jobs)
        cj = [(wb_out, w_out[l], D, D, 1024, 1024), (wb_up, w_up[l], D, 2 * D_FF, 256, 1024), ]
        cjd = [("dn", wb_dn, w_dn[l], nb) for nb in range(16)]
        if l + 1 < nl:
            cj.append((wb_in, w_in[l + 1], D, IN_W, 512, 896))
        pending = cast_jobs(cj) + cjd

        def hook(pending=pending):
            if pending:
                emit_cast_one(C, pending.pop(0))
        for d, (u_, u2_, yo_, bo_, go_) in enumerate([(u_f, u2_f, y_f, bon, gate), (u_r, u2_r, y_r, bon2, gate2)]):
            emit_p2r(C, {"u": u_, "u2": u2_, "mu": r_mu[l, d], "mu2": r_mu2[l, d], "par": r_par[l, d], "wup": r_wup[l, d], "aup": r_aup[l, d],
                         "gup": r_gup[l], "cst": r_cst, "mk": r_mk, "yT": yo_, "bonT": bo_, "gateT": go_, "hook": hook, "aux": (d == 0)})
        while pending:
            hook()
        emit_reverse(C, {"ident": identm, "jmat": jmat}, [(y_r[128 * i:128 * i + 128, :], y_b[128 * i:128 * i + 128, :], 128, 0, CTX) for i in range(4)])
        dB = Buf()
        emit_p2c(C, {"cw": c_w[l], "ln": r_ln[l], "cst": c_bd, "cv": (lambda i, a: pT[2688 + 512 * i + 128 * a:2688 + 512 * i + 128 * a + 128, :]),
                     "yf": (lambda a: y_f[128 * a:128 * a + 128, :]), "yb": (lambda a: y_b[128 * a:128 * a + 128, :]),
                     "bon": (lambda a: bon[128 * a:128 * a + 128, :]), "gate": (lambda a: gate[128 * a:128 * a + 128, :]),
                     "yc": (lambda a: yT[1024 + 128 * a:1024 + 128 * a + 128, :]), "ya": (lambda a: yT[128 * a:128 * a + 128, :]),
                     "By": dB, "Byf": dB, "Byb": dB, "Bbg": dB})
        emit_p2b(C, {"nrm": a_nrm[l], "wq": a_wq[l], "wkv": a_wkv[l], "rope": a_rope, "cst": a_cst,
                     "cx": [pT[1984 + 128 * a:1984 + 128 * a + 128, :] for a in range(5)],
                     "kr": [(0, 64, 0, pT[2624:2688, :])] + [(a0, a1, 1, pT[2624 + s0:2624 + s0 + 16, :]) for (a0, a1, s0) in sw64],
                     "y": yT[512:1024, :], "By": dB})

        def qkv_src(h, a):
            base = 4224 + 128 * h
            if a < 4:
                return [(0, 128, pT[base + 512 * a:base + 512 * a + 128, :])]
            b2 = base + 512 * (a - 4)
            return [(a0, a1, pT[b2 + s0:b2 + s0 + 32, :]) for (a0, a1, s0) in sw128]
        emit_p2d(C, {"qkv": qkv_src, "rope": d_rope, "rd": d_rd[l], "gn": d_gn[l], "tab": d_tab, "cst": d_cst, "yT": yT[1536:2048, :]})
        if dbg and l == 0:
            kb.dma("sp", lambda e: e.dma_start(out=dbg_outs["d_yT"], in_=yT), writes=[Buf()], is_out=True)
        last = (l == DEPTH - 1) and (nl == DEPTH)
        segs = P3F_SEGS[1:] if last else P3F_SEGS
        msk_ap = p3_msk[:, 2:12] if last else p3_msk
        vs = []
        for ci_, r in ((0, 1), (1, 0)):
            for k in range(3):
                vs.append((ci_, k, ng[:, l, k + 1, :]))
            for k in range(4):
                vs.append((ci_, 3 + k, mscr[:, r, l, (2 + k) * 16:(3 + k) * 16]))
        emit_p3(C, {"yT": yT, "xT": x_cur, "vec_srcs": vs, "msk": msk_ap, "wo": wb_out, "wu": wb_up, "wc": w_cv[l], "wd": wb_dn, "wdma": _wdma, "xo": x_next,
                    "Bx": dB, "By": dB, "Bxo": dB}, segs)
        if dbg and l == 0:
            kb.dma("sp", lambda e, x_next=x_next: e.dma_start(out=dbg_outs["d_x1"], in_=x_next), writes=[Buf()], is_out=True)
        x_cur = x_next
    kb.barrier()
    kb.dma("sp", lambda e, x_cur=x_cur: e.dma_start(out=xo, in_=x_cur[:, CTX:TALL]), writes=[Buf()], is_out=True)
    return C.done()


_FUSED = {}


def _prep_inputs(b, nl, x, c, ctx, c_ctx, mod_w, mod_b, norm_g, w_in, rwkv_shift, rwkv_w0, rwkv_w_up, rwkv_a0, rwkv_a_up,
                 rwkv_g_up, rwkv_vecs, mla_q_norm, mla_kv_norm, mla_w_uq, mla_w_ukv, conv_w, ret_decay, ret_gn_g,
                 w_out, mlp_w_up, mlp_conv, mlp_w_down, shared):
    ca = np.ascontiguousarray
    im = dict(shared)
    im["x0T"] = ca(np.concatenate([ctx[b], x[b]], axis=0).T)
    cc = np.stack([c[b], c_ctx], axis=1)
    im["cT"] = ca(cc.reshape(16, 128, 2).transpose(1, 0, 2))
    return im


def _prep_shared(nl, mod_w, mod_b, norm_g, w_in, rwkv_shift, rwkv_w0, rwkv_w_up, rwkv_a0, rwkv_a_up,
                 rwkv_g_up, rwkv_vecs, mla_q_norm, mla_kv_norm, mla_w_uq, mla_w_ukv, conv_w, ret_decay, ret_gn_g,
                 w_out, mlp_w_up, mlp_conv, mlp_w_down):
    ca = np.ascontiguousarray
    sh = {}
    sh["mw"] = ca(mod_w[:nl]); sh["mb"] = ca(mod_b[:nl].reshape(nl, 96, 128).transpose(2, 0, 1))
    sh["ng"] = ca(norm_g[:nl].reshape(nl, 4, 16, 128).transpose(3, 0, 1, 2))
    sh["w_in"] = ca(w_in[:nl]); sh["w_out"] = ca(w_out[:nl]); sh["w_up"] = ca(mlp_w_up[:nl]); sh["w_dn"] = ca(mlp_w_down[:nl])
    sh["w_cv"] = ca(mlp_conv[:nl].transpose(0, 2, 1).reshape(nl, 2 * NFF, 128, 3).transpose(0, 2, 1, 3))
    r_mu = np.zeros((nl, 2, 64, 24, 2), np.float32); r_mu2 = np.zeros((nl, 2, 128, 4, 2), np.float32); r_par = np.zeros((nl, 2, 64, 5, 8), np.float32)
    for l in range(nl):
        for d in range(2):
            sh_ = rwkv_shift[l] if d == 0 else rwkv_shift[l][::-1]
            r_mu[l, d] = sh_[:, 0:1536].T.reshape(24, 64, 2).transpose(1, 0, 2)
            m2 = np.zeros((512, 2), np.float32)
            m2[0:96] = sh_[:, 1536:1632].T; m2[128:224] = sh_[:, 1632:1728].T; m2[256:512] = sh_[:, 1728:1984].T
            r_mu2[l, d] = m2.reshape(4, 128, 2).transpose(1, 0, 2)
            r_par[l, d] = np.stack([_tile8(rwkv_w0[l, d]), _tile8(rwkv_a0[l, d]), _tile8(rwkv_vecs[l, 0]), _tile8(rwkv_vecs[l, 1]), _tile8(rwkv_vecs[l, 2])], axis=1)
    sh["r_mu"] = r_mu; sh["r_mu2"] = r_mu2; sh["r_par"] = r_par
    sh["r_wup"] = ca(rwkv_w_up[:nl]); sh["r_aup"] = ca(rwkv_a_up[:nl]); sh["r_gup"] = ca(rwkv_g_up[:nl])
    cst, mk = _rw_consts()
    sh["r_cst"] = cst; sh["r_mk"] = mk
    sh["r_ln"] = ca(np.stack([rwkv_vecs[:nl, 3].reshape(nl, 4, 128), rwkv_vecs[:nl, 4].reshape(nl, 4, 128)], axis=-1).transpose(0, 2, 1, 3))
    sh["c_w"] = ca(conv_w[:nl].reshape(nl, 3, 4, 128).transpose(0, 3, 2, 1))
    sh["c_bd"] = (np.kron(np.eye(2, dtype=np.float32), np.ones((64, 64), np.float32)) / 64.0).astype(np.float32)
    sh["a_nrm"] = ca(np.concatenate([mla_q_norm[:nl].reshape(nl, 3, 128), mla_kv_norm[:nl].reshape(nl, 2, 128)], axis=1).transpose(0, 2, 1))
    sw = _rope_swap_idx(64)
    wq = np.zeros((nl, 384, 4, 256), np.float32); wkv = np.zeros((nl, 256, 4, 256), np.float32)
    for l in range(nl):
        for h in range(4):
            wh = mla_w_uq[l][:, 192 * h:192 * h + 192]
            wq[l, :, h, 0:192] = wh; wq[l, :, h, 192:256] = wh[:, 128:192][:, sw]
            wkv[l, :, h, :] = mla_w_ukv[l][:, 256 * h:256 * h + 256]
    sh["a_wq"] = ca(wq.reshape(nl, 3, 128, 4, 256).transpose(0, 2, 1, 3, 4)); sh["a_wkv"] = ca(wkv.reshape(nl, 2, 128, 4, 256).transpose(0, 2, 1, 3, 4))
    cos, sin = _rope_tables(64); sh["a_rope"] = ca(np.stack([cos, sin]))
    sh["a_cst"] = np.concatenate([np.eye(128, dtype=np.float32), np.full((128, 128), 1.0 / 384, np.float32), np.full((128, 128), 1.0 / 256, np.float32)], axis=1)
    cos, sin = _rope_tables(128); sh["d_rope"] = ca(np.stack([cos, sin]))
    sh["d_rd"] = ca(np.tile(ret_decay[:nl].reshape(nl, 1, 8), (1, 128, 1)))
    sh["d_gn"] = ca(ret_gn_g[:nl].reshape(nl, 4, 128).transpose(0, 2, 1))
    j = np.arange(128)[:, None]; i = np.arange(128)[None, :]
    tab = np.zeros((128, 770), np.float32)
    tab[:, 0:128] = np.maximum(i - j, 0); tab[:, 128:256] = (i >= j)
    tab[:, 256:384] = np.maximum(j - i, 0); tab[:, 384:512] = (j > i)
    tab[:, 512:640] = (i + 1); tab[:, 640:768] = (128 - i)
    tab[:, 768] = 127 - np.arange(128); tab[:, 769] = np.arange(128)
    sh["d_tab"] = tab
    sh["d_cst"] = np.concatenate([np.eye(128, dtype=np.float32), np.full((128, 128), 1.0 / 128, np.float32)], axis=1)
    mk_ = []
    for (_, _, _, hl, hr) in P3F_SEGS:
        mk_ += [float(hl), float(hr)]
    sh["p3_msk"] = ca(np.tile(np.array(mk_, np.float32)[None], (128, 1)))
    sh["jmat"] = ca(np.eye(128, dtype=np.float32)[::-1]); sh["identm"] = np.eye(128, dtype=np.float32)
    return sh


def run_fused(inputs, nl=DEPTH, dbg=False):
    key = (nl, dbg)
    if key not in _FUSED:
        _FUSED[key] = build_fused(nl, dbg)
    f = lambda a: np.ascontiguousarray(np.asarray(a, dtype=np.float32))
    inp = {k: f(v) for k, v in inputs.items()}
    names = ["mod_w", "mod_b", "norm_g", "w_in", "rwkv_shift", "rwkv_w0", "rwkv_w_up", "rwkv_a0", "rwkv_a_up", "rwkv_g_up", "rwkv_vecs", "mla_q_norm",
             "mla_kv_norm", "mla_w_uq", "mla_w_ukv", "conv_w", "ret_decay", "ret_gn_g", "w_out", "mlp_w_up", "mlp_conv", "mlp_w_down"]
    shared = _prep_shared(nl, *[inp[k] for k in names])
    in_maps = []
    for core in range(NCORES):
        b = core % BATCH
        in_maps.append(_prep_inputs(b, nl, inp["x"], inp["c"], inp["ctx"], inp["c_ctx"], *[None] * 22, shared))
    res = run_bass_kernel_spmd(_FUSED[key], in_maps, core_ids=list(range(NCORES)))
    return res


def kernel(**inputs):
    res = run_fused(inputs)
    out = np.stack([res.results[b]["xo"].T for b in range(BATCH)], axis=0)
    return np.ascontiguousarray(out).astype(np.float32)
```

```python
import math
from contextlib import ExitStack
import numpy as np
import concourse.bass as bass
import concourse.mybir as mybir
from concourse.bass_utils import run_bass_kernel_spmd

F32 = mybir.dt.float32
BF16 = mybir.dt.bfloat16
AF = mybir.ActivationFunctionType
ALU = mybir.AluOpType
AX = mybir.AxisListType

N_DMA_SEMS = 32
N_BG_SEMS = 12
NCORES = 8

D = 2048
DEPTH = 4
BATCH = 4
SEQ = 2048
CTX = 256
TALL = SEQ + CTX
IN_W = 6272
D_FF = 5632
NMOD = 6
NORM_EPS = 1e-6


class Buf:
    __slots__ = ("name", "w", "r", "excl")

    def __init__(self, name="", excl=False):
        self.name = name
        self.w = {}
        self.r = {}
        self.excl = excl


class KB:
    def __init__(self, nc, stack):
        self.nc = nc
        self.engs = ["pe", "act", "dve", "pool", "sp"]
        self.ops = {e: [] for e in self.engs}
        self.cnt = {e: 0 for e in self.engs}
        self.waited = {e: {} for e in self.engs}
        self.sems = {}
        for e in self.engs:
            self.sems[e] = stack.enter_context(nc.semaphore("s_" + e))
        for i in range(N_DMA_SEMS):
            self.sems["d%d" % i] = stack.enter_context(nc.semaphore("s_d%d" % i))
        for i in range(N_BG_SEMS):
            self.sems["c%d" % i] = stack.enter_context(nc.semaphore("s_c%d" % i))
        self.dcnt = [0] * N_DMA_SEMS
        self.dnext = 0
        self.ccnt = [0] * N_BG_SEMS
        self.cnext = 0
        self.out_toks = []

    def _deps(self, reads, writes):
        deps = {}

        def add(dd):
            for sk, v in dd.items():
                if deps.get(sk, 0) < v:
                    deps[sk] = v
        for b in reads:
            add(b.w)
            if b.excl:
                add(b.r)
        for b in writes:
            add(b.w)
            add(b.r)
        return deps

    def _emit_waits(self, eng, deps, skip_self=False):
        for sk, v in deps.items():
            if skip_self and sk == eng:
                continue
            if self.waited[eng].get(sk, 0) >= v:
                continue
            self.waited[eng][sk] = v
            self.ops[eng].append(("w", self.sems[sk], v))

    @staticmethod
    def _mark(tok, reads, writes):
        sk, v = tok
        for b in reads:
            if b.r.get(sk, 0) < v:
                b.r[sk] = v
        for b in writes:
            if b.w.get(sk, 0) < v:
                b.w[sk] = v

    def op(self, eng, fn, reads=(), writes=()):
        deps = self._deps(reads, writes)
        self._emit_waits(eng, deps, skip_self=(eng == "pe"))
        self.cnt[eng] += 1
        tok = (eng, self.cnt[eng])
        self.ops[eng].append(("i", fn, self.sems[eng], 1))
        self._mark(tok, reads, writes)
        return tok

    def dma(self, eng, fn, reads=(), writes=(), is_out=False, bg=False):
        deps = self._deps(reads, writes)
        if bg:
            i = self.cnext
            self.cnext = (self.cnext + 1) % N_BG_SEMS
            sk = "c%d" % i
            cnts = self.ccnt
        else:
            i = self.dnext
            self.dnext = (self.dnext + 1) % N_DMA_SEMS
            sk = "d%d" % i
            cnts = self.dcnt
        if cnts[i] > 0:
            v = 16 * cnts[i]
            if deps.get(sk, 0) < v:
                deps[sk] = v
        self._emit_waits(eng, deps)
        cnts[i] += 1
        tok = (sk, 16 * cnts[i])
        self.ops[eng].append(("i", fn, self.sems[sk], 16))
        self._mark(tok, reads, writes)
        if is_out:
            self.out_toks.append(tok)
        return tok

    def barrier(self):
        deps = {e: self.cnt[e] for e in self.engs if self.cnt[e] > 0}
        for i in range(N_DMA_SEMS):
            if self.dcnt[i] > 0:
                deps["d%d" % i] = 16 * self.dcnt[i]
        for i in range(N_BG_SEMS):
            if self.ccnt[i] > 0:
                deps["c%d" % i] = 16 * self.ccnt[i]
        for e in self.engs:
            self._emit_waits(e, dict(deps))

    def finish(self, block):
        deps = {}
        for sk, v in self.out_toks:
            if deps.get(sk, 0) < v:
                deps[sk] = v
        self._emit_waits("sp", deps)
        m = {"pe": block.tensor, "act": block.scalar, "dve": block.vector,
             "pool": block.gpsimd, "sp": block.sync}

        def mk(e):
            lst = self.ops[e]

            def body(engine):
                for it in lst:
                    if it[0] == "w":
                        engine.wait_ge(it[1], it[2])
                    else:
                        it[1](engine).then_inc(it[2], it[3])
            return body

        for e in self.engs:
            if self.ops[e]:
                m[e](mk(e))


class Ctx:
    def __init__(self, name="k"):
        self.nc = bass.Bass("TRN2", target_bir_lowering=False)
        self.st = ExitStack()
        self.kb = KB(self.nc, self.st)
        self.n = 0

    def dram_in(self, name, shape, dt=F32):
        return self.nc.dram_tensor(name, list(shape), dt, kind="ExternalInput").ap()

    def dram_out(self, name, shape, dt=F32):
        return self.nc.dram_tensor(name, list(shape), dt, kind="ExternalOutput").ap()

    def sb(self, shape, dt=F32, name=None):
        self.n += 1
        return self.st.enter_context(self.nc.sbuf_tensor(name or ("t%d" % self.n), list(shape), dt))

    def ps(self, shape, dt=F32, name=None):
        self.n += 1
        return self.st.enter_context(self.nc.psum_tensor(name or ("p%d" % self.n), list(shape), dt))

    def done(self):
        block = self.st.enter_context(self.nc.Block())
        self.kb.finish(block)
        self.st.close()
        return self.nc


def token_tiles(T, mx=512):
    out = []
    s = 0
    while s < T:
        n = min(mx, T - s)
        out.append((s, n))
        s += n
    return out


T1 = 1152
TP = TALL + 4
RW_BLK = 128
RW_SCALE = math.exp(-0.5)
NCH = TALL // 128
RET_EPS = 1e-5
MLA_SCALE = 192.0 ** -0.5
RWKV_GN_EPS = 64e-5
NFF = D_FF // 128

def emit_rstd(C, x_sb, Bx, nk, T, ones_sb, Bones, rstd_sb, Brstd, sq_tiles, Bsq, ps_tiles, Bps, eps, ranges=None):
    kb = C.kb
    k = 0
    for (t0, tn) in token_tiles(T):
        pj = (t0 // 512) % len(ps_tiles)
        p = ps_tiles[pj]
        for kc in range(nk):
            j = k % len(sq_tiles); k += 1
            sq = sq_tiles[j]
            kb.op("act", lambda e, sq=sq, kc=kc, t0=t0, tn=tn: e.activation(out=sq[:, 0:tn], in_=x_sb[:, kc, t0:t0 + tn],
                                                                              func=AF.Square),
                  reads=[Bx], writes=[Bsq[j]])
            kb.op("pe", lambda e, p=p, sq=sq, kc=kc, tn=tn: e.matmul(p[:, 0:tn], lhsT=ones_sb[:], rhs=sq[:, 0:tn],
                                                                      start=(kc == 0), stop=(kc == nk - 1)),
                  reads=[Bones, Bsq[j]], writes=[Bps[pj]])
        kb.op("act", lambda e, p=p, t0=t0, tn=tn: e.activation(out=rstd_sb[:, t0:t0 + tn], in_=p[:, 0:tn],
                                                                func=AF.Sqrt, bias=eps_ap(C, eps), scale=1.0),
              reads=[Bps[pj]], writes=[Brstd])
        kb.op("dve", lambda e, t0=t0, tn=tn: e.reciprocal(out=rstd_sb[:, t0:t0 + tn], in_=rstd_sb[:, t0:t0 + tn]),
              reads=[Brstd], writes=[Brstd])


_EPS = {}


def eps_ap(C, eps):
    return _EPS[(id(C), eps)][:, 0:1]


def make_eps(C, eps):
    t = C.sb([128, 1])
    b = Buf()
    C.kb.op("dve", lambda e: e.memset(t[:], eps), writes=[b])
    C.kb.barrier()
    _EPS[(id(C), eps)] = t
    return t


def vec_layout(v):
    n = v.shape[0]
    return np.ascontiguousarray(v.reshape(n, 16, 128).transpose(2, 0, 1))


def _rw_consts():
    ident = np.eye(128, dtype=np.float32)
    cst = np.concatenate([ident, np.ones((128, 128), np.float32)], axis=1)
    s = np.arange(64)[:, None]; t = np.arange(64)[None, :]
    us = (s < t).astype(np.float32); ui = (s <= t).astype(np.float32)
    unit = np.concatenate([us, ui, us, ui], axis=1)
    maskA = np.concatenate([unit, unit], axis=1)
    ls = (t < s).astype(np.float32)
    maskL = np.tile(ls, (1, 8))
    idu = np.tile(np.eye(64, dtype=np.float32), (1, 16))
    mk = np.concatenate([maskA, maskL, idu, np.ones((64, 64), np.float32)], axis=1)
    return cst, np.ascontiguousarray(mk)


def _tile8(v):
    return np.ascontiguousarray(v.reshape(8, 64).T)


def _rope_swap_idx(d):
    q = d // 4
    idx = np.arange(d)
    out = np.empty(d, np.int64)
    for base in (0, d // 2):
        out[base:base + q] = idx[base + q:base + 2 * q]
        out[base + q:base + 2 * q] = idx[base:base + q]
    return out


def _rope_tables(d):
    half = d // 2
    inv = 10000.0 ** (-np.arange(0, half, 2, dtype=np.float32) / half)
    rows = SEQ // 64
    row = np.repeat(np.arange(rows, dtype=np.float32), 64)
    col = np.tile(np.arange(64, dtype=np.float32), rows)
    ar = (row[:, None] * inv[None, :]).astype(np.float32)
    ac = (col[:, None] * inv[None, :]).astype(np.float32)
    cos = np.ones((d, TALL), np.float32); sin = np.zeros((d, TALL), np.float32)
    q = d // 4
    for base, ang in ((0, ar), (half, ac)):
        c = np.cos(ang).T.astype(np.float32); s = np.sin(ang).T.astype(np.float32)
        cos[base:base + q, CTX:] = c; cos[base + q:base + 2 * q, CTX:] = c
        sin[base:base + q, CTX:] = -s; sin[base + q:base + 2 * q, CTX:] = s
    return cos, sin


def emit_p1(C, io, classes):
    kb = C.kb
    C.push()
    xT = io["xT"]
    w = io["w"]
    pT = io["pT"]
    x_sb = C.sb([128, 16, T1]); Bx = Buf()
    h_sb = C.sb([128, 16, T1], BF16); Bh = Buf()
    v_sb = C.sb([128, 5, 16]); Bv = Buf()
    gp = C.sb([128, 2, 16]); Bgp = Buf()
    ones = C.sb([128, 128]); Bones = Buf()
    rstd = C.sb([128, T1]); Brstd = Buf()
    sq = [C.sb([128, 512]) for _ in range(2)]; Bsq = [Buf(), Buf()]
    tmp = [C.sb([128, T1]) for _ in range(2)]; Btmp = [Buf(), Buf()]
    wt = [C.sb([128, 16, 512], BF16) for _ in range(2)]; Bw = [Buf(), Buf()]
    ot = [C.sb([128, T1]) for _ in range(2)]; Bo = [Buf(), Buf()]
    pss = [C.ps([128, 512]) for _ in range(2)]; Bpss = [Buf(), Buf()]
    psm = [C.ps([128, 512]) for _ in range(4)]; Bpsm = [Buf() for _ in range(4)]

    xv = xT.rearrange("(kc p) t -> p kc t", p=128)
    for kc in range(16):
        q = "sp" if kc % 2 == 0 else "act"
        kb.dma(q, lambda e, kc=kc: e.dma_start(out=x_sb[:, kc, :], in_=xv[:, kc, :]), writes=[Bx])
    for vi, vsrc in enumerate(io["vec_srcs"]):
        kb.dma("sp", lambda e, vi=vi, vsrc=vsrc: e.dma_start(out=v_sb[:, vi, :], in_=vsrc), writes=[Bv])
    kb.op("dve", lambda e: e.memset(ones[:], 1.0 / D), writes=[Bones])
    for ci in range(2):
        kb.op("dve", lambda e, ci=ci: e.scalar_tensor_tensor(out=gp[:, ci, :], in0=v_sb[:, 2 + 2 * ci, :], scalar=1.0,
                                                             in1=v_sb[:, 0, :], op0=ALU.add, op1=ALU.mult),
              reads=[Bv], writes=[Bgp])
    emit_rstd(C, x_sb, Bx, 16, T1, ones, Bones, rstd, Brstd, sq, Bsq, pss, Bpss, NORM_EPS)
    for kc in range(16):
        j = kc % 2
        t = tmp[j]
        kb.op("dve", lambda e, t=t, kc=kc: e.tensor_tensor(out=t[:], in0=x_sb[:, kc, :], in1=rstd[:], op=ALU.mult),
              reads=[Bx, Brstd], writes=[Btmp[j]])
        for (ci, a, b) in classes:
            kb.op("act", lambda e, t=t, kc=kc, ci=ci, a=a, b=b: e.activation(
                out=h_sb[:, kc, a:b], in_=t[:, a:b], func=AF.Identity,
                bias=v_sb[:, 1 + 2 * ci, kc:kc + 1], scale=gp[:, ci, kc:kc + 1]),
                reads=[Btmp[j], Bgp, Bv], writes=[Bh])
    nblk = (IN_W + 511) // 512
    oi = 0
    pi = 0
    for nb in range(nblk):
        n0 = nb * 512
        nw = min(512, IN_W - n0)
        j = nb % 2
        wtile = wt[j]
        src = w[:, n0:n0 + nw].rearrange("(kc p) n -> p kc n", p=128)
        kb.dma(io["wdma"](), lambda e, wtile=wtile, src=src, nw=nw: e.dma_start(out=wtile[:, :, 0:nw], in_=src),
               writes=[Bw[j]])
        for nc_ in range(nw // 128):
            oj = oi % 2; oi += 1
            o = ot[oj]
            for (t0, tn) in token_tiles(T1):
                pj = pi % 4; pi += 1
                p = psm[pj]
                for kc in range(16):
                    kb.op("pe", lambda e, p=p, wtile=wtile, kc=kc, nc_=nc_, t0=t0, tn=tn: e.matmul(
                        p[:, 0:tn], lhsT=wtile[:, kc, nc_ * 128:(nc_ + 1) * 128], rhs=h_sb[:, kc, t0:t0 + tn],
                        start=(kc == 0), stop=(kc == 15)),
                        reads=[Bw[j], Bh], writes=[Bpsm[pj]])
                ev = "act" if pi % 2 == 0 else "dve"
                if ev == "act":
                    kb.op("act", lambda e, p=p, o=o, t0=t0, tn=tn: e.copy(out=o[:, t0:t0 + tn], in_=p[:, 0:tn]),
                          reads=[Bpsm[pj]], writes=[Bo[oj]])
                else:
                    kb.op("dve", lambda e, p=p, o=o, t0=t0, tn=tn: e.tensor_copy(out=o[:, t0:t0 + tn], in_=p[:, 0:tn]),
                          reads=[Bpsm[pj]], writes=[Bo[oj]])
            r0 = n0 + nc_ * 128
            kb.dma("sp", lambda e, o=o, r0=r0: e.dma_start(out=pT[r0:r0 + 128, :], in_=o[:]),
                   reads=[Bo[oj]], writes=[Buf()], is_out=True)
    C.pop()


def emit_p2r(C, io):
    kb = C.kb
    C.push()
    u_in = io["u"]
    u2_in = io["u2"]
    mu_in = io["mu"]
    mu2_in = io["mu2"]
    par_in = io["par"]
    wup_in = io["wup"]
    aup_in = io["aup"]
    gup_in = io["gup"]
    cst_in = io["cst"]
    mk_in = io["mk"]
    y_out = io["yT"]
    bon_out = io["bonT"]
    gate_out = io["gateT"]

    def T(shape, dt=F32):
        return C.sb(shape, dt), Buf()

    mu, Bmu = T([64, 24, 2]); c0, Bc0 = T([64, 24, 1])
    mu2, Bmu2 = T([128, 4, 2]); c02, Bc02 = T([128, 4, 1])
    par, Bpar = T([64, 5, 8]); omk, Bomk = T([64, 8])
    wup, Bwup = T([96, 512]); aup, Baup = T([96, 512]); gup, Bgup = T([128, 2, 512])
    cst, Bcst = T([128, 256]); mk, Bmk = T([64, 2 * 256 + 8 * 64 + 16 * 64 + 64])
    ident = cst[0:64, 0:64]; ONE64 = cst[0:64, 128:192]
    maskA = mk[:, 0:512].rearrange("p (a b) -> p a b", a=2)
    maskL = mk[:, 512:1024].rearrange("p (a b) -> p a b", a=8)
    identU = mk[:, 1024:2048].rearrange("p (a b) -> p a b", a=16)
    ONES = mk[:, 2048:2112]; BONES = Bmk
    kb.dma("sp", lambda e: e.dma_start(out=mu[:], in_=mu_in), writes=[Bmu])
    kb.dma("sp", lambda e: e.dma_start(out=mu2[:], in_=mu2_in), writes=[Bmu2])
    kb.dma("sp", lambda e: e.dma_start(out=par[:], in_=par_in), writes=[Bpar])
    kb.dma("sp", lambda e: e.dma_start(out=wup[:], in_=wup_in), writes=[Bwup])
    kb.dma("sp", lambda e: e.dma_start(out=aup[:], in_=aup_in), writes=[Baup])
    kb.dma("sp", lambda e: e.dma_start(out=gup[:], in_=gup_in.rearrange("(kc p) n -> p kc n", p=128)), writes=[Bgup])
    kb.dma("sp", lambda e: e.dma_start(out=cst[:], in_=cst_in), writes=[Bcst])
    kb.dma("sp", lambda e: e.dma_start(out=mk[:], in_=mk_in), writes=[Bmk])
    for (m_, Bm_, c_, Bc_) in [(mu, Bmu, c0, Bc0), (mu2, Bmu2, c02, Bc02)]:
        kb.op("dve", lambda e, m_=m_, c_=c_: e.tensor_tensor(out=c_[:], in0=m_[:, :, 0:1], in1=m_[:, :, 1:2], op=ALU.add), reads=[Bm_], writes=[Bc_])
        kb.op("dve", lambda e, c_=c_: e.tensor_scalar(out=c_[:], in0=c_[:], scalar1=-1.0, scalar2=1.0, op0=ALU.mult, op1=ALU.add),
              reads=[Bc_], writes=[Bc_])
    kb.op("dve", lambda e: e.tensor_scalar(out=omk[:], in0=par[:, 3, :], scalar1=-1.0, scalar2=1.0, op0=ALU.mult, op1=ALU.add),
          reads=[Bpar], writes=[Bomk])

    NB = RW_BLK
    NH = 8
    U, BU = T([64, 24, NB + 2]); U2, BU2 = T([128, 4, NB + 2])
    P, BP = T([64, 24, NB])
    P2, BP2 = T([128, 4, NB]); TMPP2, BTMPP2 = T([128, 4, NB])
    TWD, BTWD = T([128, NB]); SG, BSG = T([128, 2, NB])
    LW, BLW = T([64, NH, NB]); A, BA = T([64, NH, NB])
    KK, BKK = T([64, NH, NB]); SQ, BSQ = T([64, NH, NB]); RN, BRN = T([64, NH, NB])
    KD, BKD = T([64, NH, NB]); BS, BBS = T([64, NH, NB]); TA, BTA = RN, BRN
    LREL, BLREL = T([64, NH, NB]); EPOS, BEPOS = T([64, NH, NB]); ENEG, BENEG = T([64, NH, NB])
    EPREV, BEPREV = T([64, NH, NB]); EBAR, BEBAR = T([64, NH, NB]); PC, BPC = T([64, NH, 2])
    GATE, BGATE = EPOS, BEPOS; BON, BBON = ENEG, BENEG
    R32 = mybir.dt.float32r
    AR, BAR = T([64, NH, 2, 128], R32); BH, BBH = T([64, NH, 2, 64], R32); KH, BKH = T([64, NH, 2, 64], R32)
    BB, BBB = T([64, NH, 2, 64]); KBr, BKBr = T([64, NH, 2, 64])
    TM, BTM = T([64, 16, 5, 64], R32); GS, BGS = T([64, 16, 256], R32)
    MM = [T([64, 16, 64], R32) for _ in range(2)]
    WT = [T([64, 16, 128], R32) for _ in range(2)]
    XF, BXF = T([64, 16, 64], R32); UA, BUA = T([64, 16, 128]); APT, BAPT = T([64, 16, 64], R32)
    UU, BUU = T([64, 8, 64], R32); YO, BYO = T([64, NH, NB])
    TMPP = UA[:].rearrange("p a b -> p (a b)")[:, 0:12 * NB].rearrange("p (a b) -> p a b", a=12); BTMPP = BUA
    ST = [T([64, NH, 64], R32) for _ in range(2)]
    kb.op("dve", lambda e: e.tensor_scalar(out=ST[0][0][:], in0=identU[:, 0:8, :], scalar1=0.0, scalar2=None, op0=ALU.mult), reads=[Bmk], writes=[ST[0][1]])
    kb.op("dve", lambda e: e.tensor_scalar(out=ST[1][0][:], in0=identU[:, 0:8, :], scalar1=0.0, scalar2=None, op0=ALU.mult), reads=[Bmk], writes=[ST[1][1]])
    psl = [(C.ps([128, 512]), Buf(excl=True)) for _ in range(8)]
    pcnt = [0]

    def nps():
        r = psl[pcnt[0] % 8]
        pcnt[0] += 1
        return r

    def bc(ap, shape):
        return ap.broadcast_to(shape)

    def v3(ps, a, n=None):
        n = n or 512
        return ps[0:64, 0:n].rearrange("p (a b) -> p a b", a=a)

    yv = y_out.rearrange("(h p) t -> p h t", p=64)
    bv = bon_out.rearrange("(h p) t -> p h t", p=64)
    gv = gate_out.rearrange("(h p) t -> p h t", p=64)
    uv = u_in.rearrange("(i p) t -> p i t", p=64)
    u2v = u2_in.rearrange("(i p) t -> p i t", p=128)

    blocks = [(1 + i * NB, i * NB) for i in range(CTX // NB)] + [(259 + i * NB, CTX + i * NB) for i in range(SEQ // NB)]
    gc = 0
    for (pc0, t0) in blocks:
        if io.get("hook") is not None:
            io["hook"]()
        for i in range(24):
            q = "sp" if i % 2 == 0 else "act"
            kb.dma(q, lambda e, i=i, pc0=pc0: e.dma_start(out=U[:, i, :], in_=uv[:, i, pc0 - 1:pc0 + NB + 1]), writes=[BU])
        for i in range(4):
            kb.dma("sp", lambda e, i=i, pc0=pc0: e.dma_start(out=U2[:, i, :], in_=u2v[:, i, pc0 - 1:pc0 + NB + 1]), writes=[BU2])
        for (U_, BU_, P_, BP_, TP_, BTP_, m_, Bm_, c_, Bc_, S3) in [
                (U[:, 0:12, :], BU, P[:, 0:12, :], BP, TMPP, BTMPP, mu[:, 0:12, :], Bmu, c0[:, 0:12, :], Bc0, [64, 12, NB]),
                (U[:, 12:24, :], BU, P[:, 12:24, :], BP, TMPP, BTMPP, mu[:, 12:24, :], Bmu, c0[:, 12:24, :], Bc0, [64, 12, NB]),
                (U2, BU2, P2, BP2, TMPP2, BTMPP2, mu2, Bmu2, c02, Bc02, [128, 4, NB])]:
            kb.op("dve", lambda e, U_=U_, P_=P_, c_=c_, S3=S3: e.tensor_tensor(out=P_[:], in0=U_[:, :, 1:NB + 1], in1=bc(c_[:], S3), op=ALU.mult),
                  reads=[BU_, Bc_], writes=[BP_])
            kb.op("pool", lambda e, U_=U_, TP_=TP_, m_=m_, S3=S3: e.tensor_tensor(out=TP_[:], in0=U_[:, :, 0:NB], in1=bc(m_[:, :, 0:1], S3), op=ALU.mult),
                  reads=[BU_, Bm_], writes=[BTP_])
            kb.op("dve", lambda e, P_=P_, TP_=TP_: e.tensor_tensor(out=P_[:], in0=P_[:], in1=TP_[:], op=ALU.add), reads=[BP_, BTP_], writes=[BP_])
            kb.op("pool", lambda e, U_=U_, TP_=TP_, m_=m_, S3=S3: e.tensor_tensor(out=TP_[:], in0=U_[:, :, 2:NB + 2], in1=bc(m_[:, :, 1:2], S3), op=ALU.mult),
                  reads=[BU_, Bm_], writes=[BTP_])
            kb.op("dve", lambda e, P_=P_, TP_=TP_: e.tensor_tensor(out=P_[:], in0=P_[:], in1=TP_[:], op=ALU.add), reads=[BP_, BTP_], writes=[BP_])
        r_ = P[:, 0:8, :]; k_ = P[:, 8:16, :]; v_ = P[:, 16:24, :]
        S4 = [64, NH, NB]
        for (src_i, upw, Bup, pidx, dst, Bdst, func) in [(0, wup, Bwup, 0, LW, BLW, AF.Tanh), (1, aup, Baup, 1, A, BA, AF.Identity)]:
            kb.op("act", lambda e, src_i=src_i, func=func: e.activation(out=TWD[:], in_=P2[:, src_i, :], func=func),
                  reads=[BP2], writes=[BTWD])
            for hh in range(2):
                ps, Bps = nps()
                for hi in range(4):
                    h = hh * 4 + hi
                    kb.op("pe", lambda e, ps=ps, upw=upw, h=h, hi=hi: e.matmul(ps[0:64, hi * NB:(hi + 1) * NB], lhsT=upw[0:96, h * 64:(h + 1) * 64],
                                                                         rhs=TWD[0:96, :], start=True, stop=True),
                          reads=[Bup, BTWD], writes=[Bps])
                for hi in range(4):
                    h = hh * 4 + hi
                    kb.op("act", lambda e, ps=ps, h=h, hi=hi, dst=dst, pidx=pidx: e.activation(
                        out=dst[:, h, :], in_=ps[0:64, hi * NB:(hi + 1) * NB], func=AF.Sigmoid, bias=par[:, pidx, h:h + 1], scale=1.0),
                        reads=[Bps, Bpar], writes=[Bdst])
        kb.op("dve", lambda e: e.tensor_scalar(out=LW[:], in0=LW[:], scalar1=-RW_SCALE, scalar2=None, op0=ALU.mult),
              reads=[BLW], writes=[BLW])
        AUX = io.get("aux", True)
        if AUX:
            kb.op("act", lambda e: e.activation(out=SG[:], in_=P2[:, 2:4, :], func=AF.Sigmoid), reads=[BP2], writes=[BSG])
            for hh in range(2):
                ps, Bps = nps()
                for hi in range(4):
                    h = hh * 4 + hi
                    for kc in range(2):
                        kb.op("pe", lambda e, ps=ps, h=h, hi=hi, kc=kc: e.matmul(ps[0:64, hi * NB:(hi + 1) * NB], lhsT=gup[:, kc, h * 64:(h + 1) * 64],
                                                                           rhs=SG[:, kc, :], start=(kc == 0), stop=(kc == 1)),
                              reads=[Bgup, BSG], writes=[Bps])
                kb.op("act", lambda e, ps=ps, hh=hh: e.copy(out=GATE[:, hh * 4:hh * 4 + 4, :], in_=v3(ps, 4)), reads=[Bps], writes=[BGATE])
            kb.dma("sp", lambda e, t0=t0: e.dma_start(out=gv[:, :, t0:t0 + NB], in_=GATE[:]), reads=[BGATE], writes=[Buf()], is_out=True)

        def headsum(src, Bsrc, fn_evac):
            for hh in range(2):
                ps, Bps = nps()
                for hi in range(4):
                    h = hh * 4 + hi
                    kb.op("pe", lambda e, ps=ps, h=h, hi=hi: e.matmul(ps[0:64, hi * NB:(hi + 1) * NB], lhsT=ONE64, rhs=src[:, h, :], start=True, stop=True),
                          reads=[Bcst, Bsrc], writes=[Bps])
                fn_evac(ps, Bps, hh)
        kb.op("dve", lambda e: e.tensor_tensor(out=KK[:], in0=k_, in1=bc(par[:, 2, :].unsqueeze(2), S4), op=ALU.mult),
              reads=[BP, Bpar], writes=[BKK])
        kb.op("pool", lambda e: e.tensor_tensor(out=SQ[:], in0=KK[:], in1=KK[:], op=ALU.mult), reads=[BKK], writes=[BSQ])
        headsum(SQ, BSQ, lambda ps, Bps, hh: kb.op("act", lambda e: e.activation(out=RN[:, hh * 4:hh * 4 + 4, :], in_=v3(ps, 4), func=AF.Sqrt,
                                                                                    bias=eps_ap(C, 1e-12)[0:64, :], scale=1.0), reads=[Bps], writes=[BRN]))
        kb.op("dve", lambda e: e.reciprocal(out=RN[:], in_=RN[:]), reads=[BRN], writes=[BRN])
        kb.op("dve", lambda e: e.tensor_tensor(out=KK[:], in0=KK[:], in1=RN[:], op=ALU.mult), reads=[BKK, BRN], writes=[BKK])
        if AUX:
            kb.op("pool", lambda e: e.tensor_tensor(out=SQ[:], in0=r_, in1=k_, op=ALU.mult), reads=[BP], writes=[BSQ])
            kb.op("pool", lambda e: e.tensor_tensor(out=SQ[:], in0=SQ[:], in1=bc(par[:, 4, :].unsqueeze(2), S4), op=ALU.mult),
                  reads=[BSQ, Bpar], writes=[BSQ])
            headsum(SQ, BSQ, lambda ps, Bps, hh: kb.op("dve", lambda e: e.tensor_tensor(out=BON[:, hh * 4:hh * 4 + 4, :], in0=v3(ps, 4),
                                                                                         in1=v_[:, hh * 4:hh * 4 + 4, :], op=ALU.mult),
                                                       reads=[Bps, BP], writes=[BBON]))
            kb.dma("sp", lambda e, t0=t0: e.dma_start(out=bv[:, :, t0:t0 + NB], in_=BON[:]), reads=[BBON], writes=[Buf()], is_out=True)
        kb.op("dve", lambda e: e.tensor_tensor(out=TA[:], in0=A[:], in1=bc(par[:, 3, :].unsqueeze(2), S4), op=ALU.mult),
              reads=[BA, Bpar], writes=[BTA])
        kb.op("dve", lambda e: e.tensor_tensor(out=TA[:], in0=TA[:], in1=bc(omk[:].unsqueeze(2), S4), op=ALU.add),
              reads=[BTA, Bomk], writes=[BTA])
        kb.op("dve", lambda e: e.tensor_tensor(out=KD[:], in0=TA[:], in1=k_, op=ALU.mult), reads=[BTA, BP], writes=[BKD])
        kb.op("pool", lambda e: e.tensor_tensor(out=BS[:], in0=KK[:], in1=A[:], op=ALU.mult), reads=[BKK, BA], writes=[BBS])
        for h in range(NH):
            for ch in range(2):
                kb.op("dve", lambda e, h=h, ch=ch: e.tensor_tensor_scan(
                    out=LREL[:, h, ch * 64:(ch + 1) * 64], data0=ONES,
                    data1=LW[:, h, ch * 64:(ch + 1) * 64], initial=0.0, op0=ALU.mult, op1=ALU.add),
                    reads=[BLW, BONES], writes=[BLREL])
        kb.op("act", lambda e: e.activation(out=EPOS[:], in_=LREL[:], func=AF.Exp), reads=[BLREL], writes=[BEPOS])
        kb.op("act", lambda e: e.activation(out=ENEG[:], in_=LREL[:], func=AF.Exp, scale=-1.0), reads=[BLREL], writes=[BENEG])
        kb.op("dve", lambda e: e.tensor_tensor(out=EPREV[:], in0=LREL[:], in1=LW[:], op=ALU.subtract), reads=[BLREL, BLW], writes=[BEPREV])
        kb.op("act", lambda e: e.activation(out=EPREV[:], in_=EPREV[:], func=AF.Exp), reads=[BEPREV], writes=[BEPREV])
        L5 = LREL[:].rearrange("p c (h t) -> p c h t", h=2)
        kb.op("dve", lambda e, L5=L5: e.tensor_tensor(out=EBAR[:].rearrange("p c (h t) -> p c h t", h=2),
                                                       in0=bc(L5[:, :, :, 63:64], [64, NH, 2, 64]), in1=L5, op=ALU.subtract),
              reads=[BLREL], writes=[BEBAR])
        kb.op("act", lambda e: e.activation(out=EBAR[:], in_=EBAR[:], func=AF.Exp), reads=[BEBAR], writes=[BEBAR])
        kb.op("act", lambda e, L5=L5: e.activation(out=PC[:].unsqueeze(3), in_=L5[:, :, :, 63:64], func=AF.Exp), reads=[BLREL], writes=[BPC])

        def v5(t):
            return t[:].rearrange("p c (h t) -> p c h t", h=2)
        kb.op("dve", lambda e: e.scalar_tensor_tensor(out=AR[:].rearrange("p c h t -> p (c h) t")[:, :, 0:64], in0=KK[:].rearrange("p c (h t) -> p (c h) t", h=2),
                                                      scalar=-1.0, in1=EPREV[:].rearrange("p c (h t) -> p (c h) t", h=2), op0=ALU.mult, op1=ALU.mult),
              reads=[BKK, BEPREV], writes=[BAR])
        kb.op("dve", lambda e: e.tensor_tensor(out=AR[:, :, :, 64:128], in0=r_.rearrange("p c (h t) -> p c h t", h=2), in1=v5(EPOS), op=ALU.mult),
              reads=[BP, BEPOS], writes=[BAR])
        kb.op("dve", lambda e: e.tensor_tensor(out=BH[:], in0=v5(BS), in1=v5(ENEG), op=ALU.mult), reads=[BBS, BENEG], writes=[BBH])
        kb.op("dve", lambda e: e.tensor_tensor(out=KH[:], in0=v5(KD), in1=v5(ENEG), op=ALU.mult), reads=[BKD, BENEG], writes=[BKH])
        kb.op("dve", lambda e: e.tensor_tensor(out=BB[:], in0=v5(BS), in1=v5(EBAR), op=ALU.mult), reads=[BBS, BEBAR], writes=[BBB])
        kb.op("pool", lambda e: e.tensor_tensor(out=KBr[:], in0=v5(KD), in1=v5(EBAR), op=ALU.mult), reads=[BKD, BEBAR], writes=[BKBr])

        for ch in range(2):
            for hp in range(4):
                u0 = ch * 8 + hp * 2
                ps, Bps = nps()
                for e_ in range(2):
                    h = hp * 2 + e_
                    srcs = [(AR[:, h, ch, 0:64].bitcast(F32), BAR), (P[:, 16 + h, ch * 64:(ch + 1) * 64], BP), (BB[:, h, ch, :], BBB), (KBr[:, h, ch, :], BKBr)]
                    for ai, (s_ap, s_b) in enumerate(srcs):
                        col = (e_ * 4 + ai) * 64
                        kb.op("pe", lambda e, ps=ps, col=col, s_ap=s_ap: e.transpose(out=ps[0:64, col:col + 64], in_=s_ap, identity=ident),
                              reads=[s_b, Bcst], writes=[Bps])
                kb.op("act", lambda e, ps=ps, u0=u0: e.copy(out=TM[:, u0:u0 + 2, 1:5, :],
                                                             in_=ps[0:64, :].rearrange("p (e a k) -> p e a k", a=4, e=2)),
                      reads=[Bps], writes=[BTM])
        for ch in range(2):
            for hp in range(4):
                u0 = ch * 8 + hp * 2
                ps, Bps = nps()
                for e_ in range(2):
                    h = hp * 2 + e_
                    for hi, (lt, Blt) in enumerate([(BH, BBH), (KH, BKH)]):
                        kb.op("pe", lambda e, ps=ps, lt=lt, e_=e_, hi=hi, h=h, ch=ch: e.matmul(
                            ps[0:64, e_ * 256 + hi * 128:e_ * 256 + hi * 128 + 128], lhsT=lt[:, h, ch, :], rhs=AR[:, h, ch, :],
                            start=True, stop=True), reads=[Blt, BAR], writes=[Bps])
                kb.op("dve", lambda e, ps=ps, u0=u0: e.tensor_tensor(out=GS[:, u0:u0 + 2, :], in0=v3(ps, 2), in1=maskA, op=ALU.mult),
                      reads=[Bps, Bmk], writes=[BGS])
        for ch in range(2):
            ps, Bps = nps()
            for h in range(8):
                kb.op("pe", lambda e, ps=ps, h=h, ch=ch: e.matmul(ps[0:64, h * 64:(h + 1) * 64], lhsT=AR[:, h, ch, 0:64],
                                                                rhs=BH[:, h, ch, :], start=True, stop=True),
                      reads=[BAR, BBH], writes=[Bps])
            kb.op("dve", lambda e, ps=ps, ch=ch: e.tensor_tensor(out=MM[0][0][:, ch * 8:(ch + 1) * 8, :], in0=v3(ps, 8),
                                                                  in1=maskL, op=ALU.mult), reads=[Bps, Bmk], writes=[MM[0][1]])
        kb.op("dve", lambda e: e.tensor_tensor(out=WT[0][0][:, :, 64:128], in0=GS[:, :, 0:64], in1=identU, op=ALU.add),
              reads=[BGS, Bmk], writes=[WT[0][1]])
        for g in range(4):
            us = range(g * 4, g * 4 + 4)
            ps1, Bps1 = nps(); ps2, Bps2 = nps()
            for ui, u in enumerate(us):
                kb.op("pe", lambda e, ps1=ps1, ui=ui, u=u: e.matmul(ps1[0:64, ui * 64:(ui + 1) * 64], lhsT=MM[0][0][:, u, :], rhs=GS[:, u, 0:64],
                                                                   start=True, stop=True), reads=[MM[0][1], BGS], writes=[Bps1])
                kb.op("pe", lambda e, ps2=ps2, ui=ui, u=u: e.matmul(ps2[0:64, ui * 64:(ui + 1) * 64], lhsT=GS[:, u, 0:64], rhs=MM[0][0][:, u, :],
                                                                   start=True, stop=True), reads=[MM[0][1], BGS], writes=[Bps2])
            kb.op("act", lambda e, ps1=ps1, g=g: e.copy(out=WT[0][0][:, g * 4:g * 4 + 4, 0:64], in_=v3(ps1, 4, 256)),
                  reads=[Bps1], writes=[WT[0][1]])
            kb.op("act", lambda e, ps2=ps2, g=g: e.copy(out=MM[1][0][:, g * 4:g * 4 + 4, :], in_=v3(ps2, 4, 256)),
                  reads=[Bps2], writes=[MM[1][1]])
        cur, mcur = 0, 1
        for lvl in range(4):
            for g in range(4):
                us = range(g * 4, g * 4 + 4)
                ps1, Bps1 = nps(); ps2, Bps2 = nps()
                for ui, u in enumerate(us):
                    kb.op("pe", lambda e, ps1=ps1, ui=ui, u=u, cur=cur, mcur=mcur: e.matmul(
                        ps1[0:64, ui * 128:(ui + 1) * 128], lhsT=MM[mcur][0][:, u, :], rhs=WT[cur][0][:, u, :], start=True, stop=True),
                        reads=[MM[mcur][1], WT[cur][1]], writes=[Bps1])
                    kb.op("pe", lambda e, ps2=ps2, ui=ui, u=u, cur=cur, mcur=mcur: e.matmul(
                        ps2[0:64, ui * 64:(ui + 1) * 64], lhsT=WT[cur][0][:, u, 0:64], rhs=MM[mcur][0][:, u, :], start=True, stop=True),
                        reads=[MM[mcur][1], WT[cur][1]], writes=[Bps2])
                v1 = v3(ps1, 4)
                kb.op("act", lambda e, v1=v1, g=g, cur=cur: e.copy(out=WT[1 - cur][0][:, g * 4:g * 4 + 4, 0:64], in_=v1[:, :, 0:64]),
                      reads=[Bps1], writes=[WT[1 - cur][1]])
                kb.op("dve", lambda e, v1=v1, g=g, cur=cur: e.tensor_tensor(out=WT[1 - cur][0][:, g * 4:g * 4 + 4, 64:128], in0=v1[:, :, 64:128],
                                                                            in1=WT[cur][0][:, g * 4:g * 4 + 4, 64:128], op=ALU.add),
                      reads=[Bps1, WT[cur][1]], writes=[WT[1 - cur][1]])
                kb.op("act", lambda e, ps2=ps2, g=g, mcur=mcur: e.copy(out=MM[1 - mcur][0][:, g * 4:g * 4 + 4, :], in_=v3(ps2, 4, 256)),
                      reads=[Bps2], writes=[MM[1 - mcur][1]])
            cur, mcur = 1 - cur, 1 - mcur
        for g in range(2):
            ps1, Bps1 = nps()
            for ui in range(8):
                u = g * 8 + ui
                kb.op("pe", lambda e, ps1=ps1, ui=ui, u=u, cur=cur, mcur=mcur: e.matmul(
                    ps1[0:64, ui * 64:(ui + 1) * 64], lhsT=MM[mcur][0][:, u, :], rhs=WT[cur][0][:, u, 64:128], start=True, stop=True),
                    reads=[MM[mcur][1], WT[cur][1]], writes=[Bps1])
            kb.op("dve", lambda e, ps1=ps1, g=g, cur=cur: e.tensor_tensor(out=XF[:, g * 8:g * 8 + 8, :], in0=v3(ps1, 8),
                                                                         in1=WT[cur][0][:, g * 8:g * 8 + 8, 64:128], op=ALU.add),
                  reads=[Bps1, WT[cur][1]], writes=[BXF])
        for g in range(2):
            ps1, Bps1 = nps()
            for ui in range(8):
                u = g * 8 + ui
                kb.op("pe", lambda e, ps1=ps1, ui=ui, u=u: e.matmul(ps1[0:64, ui * 64:(ui + 1) * 64], lhsT=GS[:, u, 128:192], rhs=TM[:, u, 2, :],
                                                                   start=True, stop=True), reads=[BGS, BTM], writes=[Bps1])
            kb.op("act", lambda e, ps1=ps1, g=g: e.copy(out=TM[:, g * 8:g * 8 + 8, 0, :], in_=v3(ps1, 8)), reads=[Bps1], writes=[BTM])
        for g in range(4):
            ps1, Bps1 = nps()
            for ui in range(4):
                u = g * 4 + ui
                kb.op("pe", lambda e, ps1=ps1, ui=ui, u=u: e.matmul(ps1[0:64, ui * 128:(ui + 1) * 128], lhsT=XF[:, u, :],
                                                                   rhs=TM[:, u, 0:2, :].rearrange("p a k -> p (a k)"),
                                                                   start=True, stop=True), reads=[BXF, BTM], writes=[Bps1])
            kb.op("act", lambda e, ps1=ps1, g=g: e.copy(out=UA[:, g * 4:g * 4 + 4, :], in_=v3(ps1, 4)), reads=[Bps1], writes=[BUA])
        for g in range(2):
            ps1, Bps1 = nps()
            for ui in range(8):
                u = g * 8 + ui
                kb.op("pe", lambda e, ps1=ps1, ui=ui, u=u: e.matmul(ps1[0:64, ui * 64:(ui + 1) * 64], lhsT=TM[:, u, 1, :], rhs=XF[:, u, :],
                                                                   start=True, stop=True), reads=[BTM, BXF], writes=[Bps1])
            kb.op("act", lambda e, ps1=ps1, g=g: e.copy(out=APT[:, g * 8:g * 8 + 8, :], in_=v3(ps1, 8)), reads=[Bps1], writes=[BAPT])
        for ch in range(2):
            scur, snxt = ST[gc % 2], ST[(gc + 1) % 2]
            gc += 1
            psu, Bpsu = nps()
            for h in range(8):
                u = ch * 8 + h
                kb.op("pe", lambda e, psu=psu, h=h, u=u, scur=scur: e.matmul(
                    psu[0:64, h * 64:(h + 1) * 64], lhsT=APT[:, u, :], rhs=scur[0][:, h, :], start=True, stop=True),
                    reads=[BAPT, scur[1]], writes=[Bpsu])
            kb.op("dve", lambda e, psu=psu, ch=ch: e.tensor_tensor(out=UU[:], in0=v3(psu, 8), in1=UA[:, ch * 8:ch * 8 + 8, 0:64], op=ALU.add),
                  reads=[Bpsu, BUA], writes=[BUU])
            psy, Bpsy = nps(); pss_, Bpss = nps()
            for h in range(8):
                u = ch * 8 + h
                oy = psy[0:64, h * 64:(h + 1) * 64]
                kb.op("pe", lambda e, oy=oy, h=h, u=u: e.matmul(oy, lhsT=UU[:, h, :], rhs=GS[:, u, 64:128], start=True, stop=False),
                      reads=[BUU, BGS], writes=[Bpsy])
                kb.op("pe", lambda e, oy=oy, u=u: e.matmul(oy, lhsT=TM[:, u, 2, :], rhs=GS[:, u, 192:256], start=False, stop=False),
                      reads=[BTM, BGS], writes=[Bpsy])
                kb.op("pe", lambda e, oy=oy, h=h, ch=ch, scur=scur: e.matmul(oy, lhsT=scur[0][:, h, :], rhs=AR[:, h, ch, 64:128],
                                                                            start=False, stop=True),
                      reads=[scur[1], BAR], writes=[Bpsy])
                os_ = pss_[0:64, h * 64:(h + 1) * 64]
                kb.op("pe", lambda e, os_=os_, h=h, u=u: e.matmul(os_, lhsT=TM[:, u, 3, :], rhs=UU[:, h, :], start=True, stop=False),
                      reads=[BTM, BUU], writes=[Bpss])
                kb.op("pe", lambda e, os_=os_, u=u: e.matmul(os_, lhsT=TM[:, u, 4, :], rhs=TM[:, u, 2, :], start=False, stop=True),
                      reads=[BTM], writes=[Bpss])
            kb.op("act", lambda e, psy=psy, ch=ch: e.copy(out=YO[:, :, ch * 64:(ch + 1) * 64], in_=v3(psy, 8)), reads=[Bpsy], writes=[BYO])
            kb.op("dve", lambda e, pss_=pss_, ch=ch, scur=scur, snxt=snxt: e.tensor_tensor(
                out=snxt[0][:], in0=scur[0][:], in1=bc(PC[:, :, ch:ch + 1], [64, NH, 64]), op=ALU.mult),
                reads=[scur[1], BPC], writes=[snxt[1]])
            kb.op("dve", lambda e, pss_=pss_, snxt=snxt: e.tensor_tensor(out=snxt[0][:], in0=v3(pss_, 8), in1=snxt[0][:], op=ALU.add),
                  reads=[Bpss, snxt[1]], writes=[snxt[1]])
        kb.dma("sp", lambda e, t0=t0: e.dma_start(out=yv[:, :, t0:t0 + NB], in_=YO[:]), reads=[BYO], writes=[Buf()], is_out=True)
    C.pop()


def emit_p2d(C, io):
    kb = C.kb
    C.push()
    rope_in = io["rope"]
    rd_in = io["rd"]
    gn_in = io["gn"]
    tab_in = io["tab"]
    cst_in = io["cst"]
    y_out = io["yT"]

    def T(shape, dt=F32):
        return C.sb(shape, dt), Buf()
    rope, Brope = T([128, 2, TALL])
    rd, Brd = T([128, 8]); lg, Blg = T([128, 8]); gch, Bgch = T([128, 8])
    gn, Bgn = T([128, 4]); tab, Btab = T([128, 770]); cst, Bcst = T([128, 256])
    ident = cst[:, 0:128]; ONESM = cst[:, 128:256]
    DEC = [T([128, 128]) for _ in range(8)]
    XI = [T([128, 128]) for _ in range(8)]
    ZE = [T([128, 1]) for _ in range(8)]
    kb.dma("sp", lambda e: e.dma_start(out=rope[:], in_=rope_in.rearrange("a p t -> p a t")), writes=[Brope])
    kb.dma("sp", lambda e: e.dma_start(out=rd[:], in_=rd_in), writes=[Brd])
    kb.dma("sp", lambda e: e.dma_start(out=gn[:], in_=gn_in), writes=[Bgn])
    kb.dma("sp", lambda e: e.dma_start(out=tab[:], in_=tab_in), writes=[Btab])
    kb.dma("sp", lambda e: e.dma_start(out=cst[:], in_=cst_in), writes=[Bcst])
    kb.op("act", lambda e: e.activation(out=lg[:], in_=rd[:], func=AF.Exp), reads=[Brd], writes=[Blg])
    kb.op("dve", lambda e: e.tensor_scalar(out=lg[:], in0=lg[:], scalar1=-1.0, scalar2=None, op0=ALU.mult), reads=[Blg], writes=[Blg])
    kb.op("act", lambda e: e.activation(out=gch[:], in_=lg[:], func=AF.Exp, scale=128.0), reads=[Blg], writes=[Bgch])
    for d in range(2):
        for h in range(4):
            i = d * 4 + h
            kb.op("act", lambda e, d=d, i=i: e.activation(out=DEC[i][0][:], in_=tab[:, d * 256:d * 256 + 128], func=AF.Exp, scale=lg[:, i:i + 1]),
                  reads=[Btab, Blg], writes=[DEC[i][1]])
            kb.op("dve", lambda e, d=d, i=i: e.tensor_tensor(out=DEC[i][0][:], in0=DEC[i][0][:], in1=tab[:, d * 256 + 128:d * 256 + 256], op=ALU.mult),
                  reads=[Btab, DEC[i][1]], writes=[DEC[i][1]])
            kb.op("act", lambda e, d=d, i=i: e.activation(out=XI[i][0][:], in_=tab[:, 512 + d * 128:512 + d * 128 + 128], func=AF.Exp, scale=lg[:, i:i + 1]),
                  reads=[Btab, Blg], writes=[XI[i][1]])
            kb.op("act", lambda e, d=d, i=i: e.activation(out=ZE[i][0][:], in_=tab[:, 768 + d:769 + d], func=AF.Exp, scale=lg[:, i:i + 1]),
                  reads=[Btab, Blg], writes=[ZE[i][1]])
    X6 = [T([128, TALL], mybir.dt.float32r if i_ < 2 else F32) for i_ in range(6)]
    QX = [T([128, TALL], mybir.dt.float32r) for _ in range(2)]
    KT, BKT = T([128, NCH, 128]); VT, BVT = T([128, NCH, 128], mybir.dt.float32r)
    KZ = [T([128, 128], mybir.dt.float32r) for _ in range(2)]
    ATT = [T([128, 128], mybir.dt.float32r) for _ in range(2)]
    O, BO = T([128, TALL]); TMP, BTMP = T([128, TALL])
    S = [T([128, 128], mybir.dt.float32r) for _ in range(2)]
    psl = [(C.ps([128, 512]), Buf(excl=True)) for _ in range(8)]
    pcnt = [0]

    def nps():
        r = psl[pcnt[0] % 8]
        pcnt[0] += 1
        return r
    yv = y_out.rearrange("(h p) t -> p h t", p=128)
    SCALE = 128.0 ** -0.5
    for h in range(4):
        for a in range(6):
            for pi_, (pr0, pr1, srcap) in enumerate(io["qkv"](h, a)):
                q_ = "sp" if (a + pi_) % 2 == 0 else "act"
                kb.dma(q_, lambda e, a=a, pr0=pr0, pr1=pr1, srcap=srcap: e.dma_start(out=(X6[a][0][pr0:pr1, :].bitcast(F32) if a < 2 else X6[a][0][pr0:pr1, :]), in_=srcap), writes=[X6[a][1]])
        (q, Bq), (k, Bk), (v, Bv), (g, Bg), (q2, Bq2), (k2, Bk2) = X6
        for (x, Bx, x2, Bx2, eng2) in [(q, Bq, q2, Bq2, "pool"), (k, Bk, k2, Bk2, "pool")]:
            kb.op("dve", lambda e, x=x: e.tensor_tensor(out=TMP[:], in0=x[:].bitcast(F32), in1=rope[:, 0, :], op=ALU.mult), reads=[Bx, Brope], writes=[BTMP])
            kb.op(eng2, lambda e, x2=x2: e.tensor_tensor(out=x2[:], in0=x2[:], in1=rope[:, 1, :], op=ALU.mult), reads=[Bx2, Brope], writes=[Bx2])
            kb.op("dve", lambda e, x=x, x2=x2: e.tensor_tensor(out=x[:], in0=TMP[:], in1=x2[:], op=ALU.add), reads=[BTMP, Bx2], writes=[Bx])
        kb.op("act", lambda e: e.mul(out=k[:], in_=k[:].bitcast(F32), mul=SCALE), reads=[Bk], writes=[Bk])
        for (src, Bsrc, dst, Bdst) in [(k, Bk, KT, BKT), (v, Bv, VT, BVT)]:
            for c4 in range(0, NCH, 4):
                ps, Bps = nps()
                n = min(4, NCH - c4)
                for ci in range(n):
                    c = c4 + ci
                    kb.op("pe", lambda e, ps=ps, ci=ci, c=c, src=src: e.transpose(out=ps[:, ci * 128:(ci + 1) * 128], in_=(src[:, c * 128:(c + 1) * 128].bitcast(F32) if src is k else src[:, c * 128:(c + 1) * 128]), identity=ident),
                          reads=[Bsrc, Bcst], writes=[Bps])
                kb.op("act", lambda e, ps=ps, c4=c4, n=n, dst=dst: e.copy(out=dst[:, c4:c4 + n, :], in_=ps[:, 0:n * 128].rearrange("p (a b) -> p a b", a=n)),
                      reads=[Bps], writes=[Bdst])
        for d in range(2):
            i = d * 4 + h
            kb.op("dve", lambda e, d=d, i=i: e.tensor_tensor(
                out=QX[d][0][:].rearrange("p (c t) -> p c t", c=NCH), in0=q[:].bitcast(F32).rearrange("p (c t) -> p c t", c=NCH),
                in1=XI[i][0][:].unsqueeze(1).broadcast_to([128, NCH, 128]), op=ALU.mult), reads=[Bq, XI[i][1]], writes=[QX[d][1]])
        for d in range(2):
            i = d * 4 + h
            st, Bst = S[d]
            kb.op("dve", lambda e, st=st: e.tensor_scalar(out=st[:], in0=ident, scalar1=0.0, scalar2=None, op0=ALU.mult), reads=[Bcst], writes=[Bst])
            order = [0, 1] + list(range(2, NCH)) if d == 0 else [1, 0] + list(range(NCH - 1, 1, -1))
            for n_, c in enumerate(order):
                cs = slice(c * 128, (c + 1) * 128)
                at, Bat = ATT[n_ % 2]
                kz, Bkz = KZ[n_ % 2]
                ps, Bps = nps()
                kb.op("pe", lambda e, ps=ps, cs=cs: e.matmul(ps[:, 0:128], lhsT=k[:, cs], rhs=q[:, cs], start=True, stop=True),
                      reads=[Bk, Bq], writes=[Bps])
                kb.op("dve", lambda e, ps=ps, at=at, i=i: e.tensor_tensor(out=at[:], in0=ps[:, 0:128], in1=DEC[i][0][:], op=ALU.mult),
                      reads=[Bps, DEC[i][1]], writes=[Bat])
                kb.op("dve", lambda e, kz=kz, c=c, i=i: e.tensor_tensor(out=kz[:], in0=KT[:, c, :], in1=ZE[i][0][:].broadcast_to([128, 128]), op=ALU.mult),
                      reads=[BKT, ZE[i][1]], writes=[Bkz])
                po, Bpo = nps()
                kb.op("pe", lambda e, po=po, c=c, at=at: e.matmul(po[:, 0:128], lhsT=VT[:, c, :], rhs=at[:], start=True, stop=False),
                      reads=[BVT, Bat], writes=[Bpo])
                kb.op("pe", lambda e, po=po, cs=cs, st=st, d=d: e.matmul(po[:, 0:128], lhsT=st[:], rhs=QX[d][0][:, cs], start=False, stop=True),
                      reads=[Bst, QX[d][1]], writes=[Bpo])
                if d == 0:
                    kb.op("act", lambda e, po=po, cs=cs: e.copy(out=O[:, cs], in_=po[:, 0:128]), reads=[Bpo], writes=[BO])
                else:
                    kb.op("dve", lambda e, po=po, cs=cs: e.tensor_tensor(out=O[:, cs], in0=po[:, 0:128], in1=O[:, cs], op=ALU.add),
                          reads=[Bpo, BO], writes=[BO])
                pn, Bpn = nps()
                kb.op("pe", lambda e, pn=pn, kz=kz, c=c: e.matmul(pn[:, 0:128], lhsT=kz[:], rhs=VT[:, c, :], start=True, stop=True),
                      reads=[Bkz, BVT], writes=[Bpn])
                kb.op("dve", lambda e, pn=pn, st=st, i=i: e.scalar_tensor_tensor(out=st[:], in0=st[:].bitcast(F32), scalar=gch[:, i:i + 1], in1=pn[:, 0:128],
                                                                                 op0=ALU.mult, op1=ALU.add), reads=[Bst, Bgch, Bpn], writes=[Bst])
        for (t0, tn) in token_tiles(TALL):
            ts_ = slice(t0, t0 + tn)
            pm, Bpm = nps()
            kb.op("pe", lambda e, pm=pm, ts_=ts_, tn=tn: e.matmul(pm[:, 0:tn], lhsT=ONESM, rhs=O[:, ts_], start=True, stop=True), reads=[Bcst, BO], writes=[Bpm])
            kb.op("dve", lambda e, pm=pm, ts_=ts_, tn=tn: e.tensor_tensor(out=O[:, ts_], in0=O[:, ts_], in1=pm[:, 0:tn], op=ALU.subtract),
                  reads=[Bpm, BO], writes=[BO])
            kb.op("act", lambda e, ts_=ts_: e.activation(out=TMP[:, ts_], in_=O[:, ts_], func=AF.Square), reads=[BO], writes=[BTMP])
            pv, Bpv = nps()
            kb.op("pe", lambda e, pv=pv, ts_=ts_, tn=tn: e.matmul(pv[:, 0:tn], lhsT=ONESM, rhs=TMP[:, ts_], start=True, stop=True), reads=[Bcst, BTMP], writes=[Bpv])
            kb.op("act", lambda e, pv=pv, ts_=ts_, tn=tn: e.activation(out=TMP[:, ts_], in_=pv[:, 0:tn], func=AF.Sqrt, bias=eps_ap(C, RET_EPS), scale=1.0),
                  reads=[Bpv], writes=[BTMP])
            kb.op("dve", lambda e, ts_=ts_: e.reciprocal(out=TMP[:, ts_], in_=TMP[:, ts_]), reads=[BTMP], writes=[BTMP])
            kb.op("dve", lambda e, ts_=ts_, h=h: e.scalar_tensor_tensor(out=O[:, ts_], in0=O[:, ts_], scalar=gn[:, h:h + 1], in1=TMP[:, ts_],
                                                                       op0=ALU.mult, op1=ALU.mult), reads=[BO, Bgn, BTMP], writes=[BO])
            kb.op("act", lambda e, ts_=ts_: e.activation(out=TMP[:, ts_], in_=g[:, ts_], func=AF.Silu), reads=[Bg], writes=[BTMP])
            kb.op("dve", lambda e, ts_=ts_: e.tensor_tensor(out=O[:, ts_], in0=O[:, ts_], in1=TMP[:, ts_], op=ALU.mult), reads=[BO, BTMP], writes=[BO])
        kb.dma("sp", lambda e, h=h: e.dma_start(out=yv[:, h, :], in_=O[:]), reads=[BO], writes=[Buf()], is_out=True)
    C.pop()


def emit_p2b(C, io):
    kb = C.kb
    C.push()
    nrm_in = io["nrm"]
    wq_in = io["wq"]
    wkv_in = io["wkv"]
    rope_in = io["rope"]
    cst_in = io["cst"]
    y_out = io["y"]

    def T(shape, dt=F32):
        return C.sb(shape, dt), Buf()
    cxr, Bcx = T([128, 5, TALL], mybir.dt.float32r); cx = cxr[:].bitcast(F32); kr, Bkr = T([64, 2, TALL]); nrm, Bnrm = T([128, 5])
    wq, Bwq = T([128, 3, 4, 256], mybir.dt.float32r); wkv, Bwkv = T([128, 2, 4, 256], mybir.dt.float32r)
    C.push()
    wq32, Bwq32 = T([128, 3, 4, 256]); wkv32, Bwkv32 = T([128, 2, 4, 256])
    kb.dma("sp", lambda e: e.dma_start(out=wq32[:], in_=wq_in), writes=[Bwq32])
    kb.dma("sp", lambda e: e.dma_start(out=wkv32[:], in_=wkv_in), writes=[Bwkv32])
    kb.op("act", lambda e: e.copy(out=wq[:], in_=wq32[:]), reads=[Bwq32], writes=[Bwq])
    kb.op("act", lambda e: e.copy(out=wkv[:], in_=wkv32[:]), reads=[Bwkv32], writes=[Bwkv])
    C.pop()
    rope, Brope = T([64, 2, TALL])
    cst, Bcst = T([128, 384])
    ident = cst[:, 0:128]
    rstd, Brstd = T([128, TALL])
    R32 = mybir.dt.float32r
    krr, Bkrr = T([64, TALL], R32); qn, Bqn = T([128, TALL], R32); qrr, Bqrr = T([64, TALL], R32); kn, Bkn = T([128, TALL], R32)
    t1, Bt1 = T([64, TALL]); VT, BVT = T([128, NCH, 128], R32)
    Pm, BPm = T([128, TALL]); PT, BPT = T([128, NCH, 128], R32)
    krr32, Bkrr32 = T([64, TALL]); qrr32, Bqrr32 = T([64, TALL])
    sq = [T([128, 512]) for _ in range(2)]
    mx, Bmx = T([128, 1]); nb, Bnb = T([128, 1]); rs, Brs = T([128, 1]); ri, Bri = T([128, 1])
    OT = [T([128, 128]) for _ in range(2)]
    OT2 = [T([128, 128]) for _ in range(2)]
    big = C.ps([128, 2560]); Bbig = Buf(excl=True)
    psl = [(C.ps([128, 512]), Buf(excl=True)) for _ in range(3)]
    pcnt = [0]

    def nps():
        r = psl[pcnt[0] % 3]
        pcnt[0] += 1
        return r
    for a in range(5):
        kb.dma("sp" if a % 2 == 0 else "act", lambda e, a=a: e.dma_start(out=cx[:, a, :], in_=io["cx"][a]), writes=[Bcx])
    for (pr0, pr1, ai, srcap) in io["kr"]:
        kb.dma("sp", lambda e, pr0=pr0, pr1=pr1, ai=ai, srcap=srcap: e.dma_start(out=kr[pr0:pr1, ai, :], in_=srcap), writes=[Bkr])
    kb.dma("sp", lambda e: e.dma_start(out=rope[:], in_=rope_in.rearrange("a p t -> p a t")), writes=[Brope])
    kb.dma("sp", lambda e: e.dma_start(out=nrm[:], in_=nrm_in), writes=[Bnrm])
    kb.dma("sp", lambda e: e.dma_start(out=cst[:], in_=cst_in), writes=[Bcst])
    k = 0
    for (tiles, ones_ap) in [([0, 1, 2], cst[:, 128:256]), ([3, 4], cst[:, 256:384])]:
        for (t0, tn) in token_tiles(TALL):
            ps, Bps = nps()
            for ii, a in enumerate(tiles):
                s_, Bs_ = sq[k % 2]; k += 1
                kb.op("act", lambda e, s_=s_, a=a, t0=t0, tn=tn: e.activation(out=s_[:, 0:tn], in_=cx[:, a, t0:t0 + tn], func=AF.Square),
                      reads=[Bcx], writes=[Bs_])
                kb.op("pe", lambda e, ps=ps, s_=s_, tn=tn, ii=ii, ones_ap=ones_ap, n=len(tiles): e.matmul(
                    ps[:, 0:tn], lhsT=ones_ap, rhs=s_[:, 0:tn], start=(ii == 0), stop=(ii == n - 1)), reads=[Bcst, Bs_], writes=[Bps])
            kb.op("act", lambda e, ps=ps, t0=t0, tn=tn: e.activation(out=rstd[:, t0:t0 + tn], in_=ps[:, 0:tn], func=AF.Sqrt,
                                                                    bias=eps_ap(C, NORM_EPS), scale=1.0), reads=[Bps], writes=[Brstd])
        kb.op("dve", lambda e: e.reciprocal(out=rstd[:], in_=rstd[:]), reads=[Brstd], writes=[Brstd])
        for a in tiles:
            kb.op("dve", lambda e, a=a: e.scalar_tensor_tensor(out=cxr[:, a, :], in0=cx[:, a, :], scalar=nrm[:, a:a + 1], in1=rstd[:],
                                                               op0=ALU.mult, op1=ALU.mult), reads=[Bcx, Bnrm, Brstd], writes=[Bcx])
    kb.op("dve", lambda e: e.tensor_tensor(out=krr32[:], in0=kr[:, 0, :], in1=rope[:, 0, :], op=ALU.mult), reads=[Bkr, Brope], writes=[Bkrr32])
    kb.op("pool", lambda e: e.tensor_tensor(out=t1[:], in0=kr[:, 1, :], in1=rope[:, 1, :], op=ALU.mult), reads=[Bkr, Brope], writes=[Bt1])
    kb.op("dve", lambda e: e.tensor_tensor(out=krr[:], in0=krr32[:], in1=t1[:], op=ALU.add), reads=[Bkrr32, Bt1], writes=[Bkrr])
    oi = 0
    for hh in range(4):
        for (t0, tn) in token_tiles(TALL):
            ts_ = slice(t0, t0 + tn)
            ps, Bps = nps()
            for kc in range(3):
                kb.op("pe", lambda e, ps=ps, kc=kc, ts_=ts_, tn=tn, hh=hh: e.matmul(ps[:, 0:tn], lhsT=wq[:, kc, hh, 0:128], rhs=cxr[:, kc, ts_],
                                                                                   start=(kc == 0), stop=(kc == 2)), reads=[Bwq, Bcx], writes=[Bps])
            kb.op("act", lambda e, ps=ps, ts_=ts_, tn=tn: e.copy(out=qn[:, ts_], in_=ps[:, 0:tn]), reads=[Bps], writes=[Bqn])
            ps, Bps = nps()
            for kc in range(2):
                kb.op("pe", lambda e, ps=ps, kc=kc, ts_=ts_, tn=tn, hh=hh: e.matmul(ps[:, 0:tn], lhsT=wkv[:, kc, hh, 0:128], rhs=cxr[:, 3 + kc, ts_],
                                                                                   start=(kc == 0), stop=(kc == 1)), reads=[Bwkv, Bcx], writes=[Bps])
            kb.op("act", lambda e, ps=ps, ts_=ts_, tn=tn: e.copy(out=kn[:, ts_], in_=ps[:, 0:tn]), reads=[Bps], writes=[Bkn])
            for ri_, (c0, dst, Bdst) in enumerate([(128, qrr32, Bqrr32), (192, t1, Bt1)]):
                ps, Bps = nps()
                for kc in range(3):
                    kb.op("pe", lambda e, ps=ps, kc=kc, ts_=ts_, tn=tn, hh=hh, c0=c0: e.matmul(ps[0:64, 0:tn], lhsT=wq[:, kc, hh, c0:c0 + 64], rhs=cxr[:, kc, ts_],
                                                                                              start=(kc == 0), stop=(kc == 2)), reads=[Bwq, Bcx], writes=[Bps])
                kb.op("dve", lambda e, ps=ps, ts_=ts_, tn=tn, dst=dst, ri_=ri_: e.tensor_tensor(out=dst[:, ts_], in0=ps[0:64, 0:tn], in1=rope[:, ri_, ts_], op=ALU.mult),
                      reads=[Bps, Brope], writes=[Bdst])
        kb.op("dve", lambda e: e.tensor_tensor(out=qrr[:], in0=qrr32[:], in1=t1[:], op=ALU.add), reads=[Bqrr32, Bt1], writes=[Bqrr])
        for c4 in range(0, NCH, 4):
            ps, Bps = nps()
            n = min(4, NCH - c4)
            for ci in range(n):
                c = c4 + ci
                for kc in range(2):
                    kb.op("pe", lambda e, ps=ps, ci=ci, c=c, kc=kc, hh=hh: e.matmul(ps[:, ci * 128:(ci + 1) * 128], lhsT=cxr[:, 3 + kc, c * 128:(c + 1) * 128],
                                                                                   rhs=wkv[:, kc, hh, 128:256], start=(kc == 0), stop=(kc == 1)),
                          reads=[Bcx, Bwkv], writes=[Bps])
            kb.op("act", lambda e, ps=ps, c4=c4, n=n: e.copy(out=VT[:, c4:c4 + n, :], in_=ps[:, 0:n * 128].rearrange("p (a b) -> p a b", a=n)),
                  reads=[Bps], writes=[BVT])
        for qt in range(NCH):
            qs = slice(qt * 128, (qt + 1) * 128)
            nk = CTX if qt < 2 else TALL
            for (t0, tn) in token_tiles(nk):
                kb.op("pe", lambda e, qs=qs, t0=t0, tn=tn: e.matmul(big[:, t0:t0 + tn], lhsT=qn[:, qs], rhs=kn[:, t0:t0 + tn], start=True, stop=False),
                      reads=[Bqn, Bkn], writes=[Bbig])
                kb.op("pe", lambda e, qs=qs, t0=t0, tn=tn: e.matmul(big[:, t0:t0 + tn], lhsT=qrr[:, qs], rhs=krr[:, t0:t0 + tn], start=False, stop=True),
                      reads=[Bqrr, Bkrr], writes=[Bbig])
            kb.op("dve", lambda e, nk=nk: e.tensor_reduce(out=mx[:], in_=big[:, 0:nk], axis=AX.X, op=ALU.max), reads=[Bbig], writes=[Bmx])
            kb.op("dve", lambda e: e.tensor_scalar(out=nb[:], in0=mx[:], scalar1=-MLA_SCALE, scalar2=None, op0=ALU.mult), reads=[Bmx], writes=[Bnb])
            kb.op("act", lambda e, nk=nk: e.activation(out=Pm[:, 0:nk], in_=big[:, 0:nk], func=AF.Exp, bias=nb[:, 0:1], scale=MLA_SCALE, accum_out=rs[:, 0:1]),
                  reads=[Bbig, Bnb], writes=[BPm, Brs])
            kb.op("dve", lambda e: e.reciprocal(out=ri[:], in_=rs[:]), reads=[Brs], writes=[Bri])
            nkt = nk // 128
            for c4 in range(0, nkt, 4):
                ps, Bps = nps()
                n = min(4, nkt - c4)
                for ci in range(n):
                    c = c4 + ci
                    kb.op("pe", lambda e, ps=ps, ci=ci, c=c: e.transpose(out=ps[:, ci * 128:(ci + 1) * 128], in_=Pm[:, c * 128:(c + 1) * 128], identity=ident),
                          reads=[BPm, Bcst], writes=[Bps])
                ev = "act" if (c4 // 4) % 2 == 0 else "dve"
                if ev == "act":
                    kb.op("act", lambda e, ps=ps, c4=c4, n=n: e.copy(out=PT[:, c4:c4 + n, :], in_=ps[:, 0:n * 128].rearrange("p (a b) -> p a b", a=n)),
                          reads=[Bps], writes=[BPT])
                else:
                    kb.op("dve", lambda e, ps=ps, c4=c4, n=n: e.tensor_copy(out=PT[:, c4:c4 + n, :], in_=ps[:, 0:n * 128].rearrange("p (a b) -> p a b", a=n)),
                          reads=[Bps], writes=[BPT])
            po, Bpo = nps()
            for c in range(nkt):
                kb.op("pe", lambda e, po=po, c=c, nkt=nkt: e.matmul(po[:, 0:128], lhsT=PT[:, c, :], rhs=VT[:, c, :], start=(c == 0), stop=(c == nkt - 1)),
                      reads=[BPT, BVT], writes=[Bpo])
            ot, Bot = OT[oi % 2]; oi += 1
            kb.op("act", lambda e, po=po, ot=ot: e.activation(out=ot[:], in_=po[:, 0:128], func=AF.Copy, scale=ri[:, 0:1]), reads=[Bpo, Bri], writes=[Bot])
            pt2, Bpt2 = nps()
            kb.op("pe", lambda e, pt2=pt2, ot=ot: e.transpose(out=pt2[:, 0:128], in_=ot[:], identity=ident), reads=[Bot, Bcst], writes=[Bpt2])
            ot2, Bot2 = OT2[oi % 2]
            kb.op("dve", lambda e, pt2=pt2, ot2=ot2: e.tensor_copy(out=ot2[:], in_=pt2[:, 0:128]), reads=[Bpt2], writes=[Bot2])
            kb.dma("sp", lambda e, ot2=ot2, qs=qs, hh=hh: e.dma_start(out=y_out[hh * 128:(hh + 1) * 128, qs], in_=ot2[:]), reads=[Bot2], writes=[io["By"]])
    C.pop()


def emit_p2c(C, io):
    kb = C.kb
    C.push()
    cw_in = io["cw"]
    ln_in = io["ln"]
    cst_in = io["cst"]

    def T(shape, dt=F32):
        return C.sb(shape, dt), Buf()
    cw, Bcw = T([128, 4, 3]); ln, Bln = T([128, 4, 2]); cst, Bcst = T([128, 128])
    kb.dma("sp", lambda e: e.dma_start(out=cw[:], in_=cw_in), writes=[Bcw])
    kb.dma("sp", lambda e: e.dma_start(out=ln[:], in_=ln_in), writes=[Bln])
    kb.dma("sp", lambda e: e.dma_start(out=cst[:], in_=cst_in), writes=[Bcst])
    X3 = [T([128, TP]) for _ in range(3)]
    OC, BOC = T([128, TP])
    Y, BY = T([128, TALL]); Y2, BY2 = T([128, TALL]); TMP, BTMP = T([128, TALL])
    BN, BBN = T([128, TALL]); GT, BGT = T([128, TALL])
    psl = [(C.ps([128, 512]), Buf(excl=True)) for _ in range(4)]
    pcnt = [0]

    def nps():
        r = psl[pcnt[0] % 4]
        pcnt[0] += 1
        return r
    n = TP - 2
    for a in range(4):
        for i in range(3):
            srcap = io["cv"](i, a)
            kb.op("pool", lambda e, i=i: e.memset(X3[i][0][:, 0:1], 0.0), writes=[X3[i][1]])
            kb.op("pool", lambda e, i=i: e.memset(X3[i][0][:, 257:259], 0.0), writes=[X3[i][1]])
            kb.op("pool", lambda e, i=i: e.memset(X3[i][0][:, TP - 1:TP], 0.0), writes=[X3[i][1]])
            kb.dma("sp", lambda e, i=i, srcap=srcap: e.dma_start(out=X3[i][0][:, 1:257], in_=srcap[:, 0:CTX]), writes=[X3[i][1]])
            kb.dma("act", lambda e, i=i, srcap=srcap: e.dma_start(out=X3[i][0][:, 259:259 + SEQ], in_=srcap[:, CTX:TALL]), writes=[X3[i][1]])
        (bgt, Bbgt), (cg, Bcg), (u, Bu) = X3
        kb.op("dve", lambda e: e.tensor_tensor(out=cg[:], in0=cg[:], in1=u[:], op=ALU.mult), reads=[Bcg, Bu], writes=[Bcg])
        kb.op("pool", lambda e: e.memset(OC[:], 0.0), writes=[BOC])
        kb.op("act", lambda e, a=a: e.activation(out=OC[:, 1:1 + n], in_=cg[:, 1:1 + n], func=AF.Copy, scale=cw[:, a, 1:2]), reads=[Bcg, Bcw], writes=[BOC])
        kb.op("dve", lambda e, a=a: e.scalar_tensor_tensor(out=OC[:, 1:1 + n], in0=cg[:, 0:n], scalar=cw[:, a, 0:1], in1=OC[:, 1:1 + n], op0=ALU.mult, op1=ALU.add),
              reads=[Bcg, Bcw, BOC], writes=[BOC])
        kb.op("dve", lambda e, a=a: e.scalar_tensor_tensor(out=OC[:, 1:1 + n], in0=cg[:, 2:2 + n], scalar=cw[:, a, 2:3], in1=OC[:, 1:1 + n], op0=ALU.mult, op1=ALU.add),
              reads=[Bcg, Bcw, BOC], writes=[BOC])
        kb.op("dve", lambda e: e.tensor_tensor(out=OC[:], in0=OC[:], in1=bgt[:], op=ALU.mult), reads=[BOC, Bbgt], writes=[BOC])
        kb.dma("sp", lambda e, a=a: e.dma_start(out=io["yc"](a)[:, 0:CTX], in_=OC[:, 1:257]), reads=[BOC], writes=[io["By"]])
        kb.dma("act", lambda e, a=a: e.dma_start(out=io["yc"](a)[:, CTX:TALL], in_=OC[:, 259:259 + SEQ]), reads=[BOC], writes=[io["By"]])
        kb.dma("sp", lambda e, a=a: e.dma_start(out=Y[:], in_=io["yf"](a)), reads=[io["Byf"]], writes=[BY])
        kb.dma("act", lambda e, a=a: e.dma_start(out=Y2[:], in_=io["yb"](a)), reads=[io["Byb"]], writes=[BY2])
        kb.dma("sp", lambda e, a=a: e.dma_start(out=BN[:], in_=io["bon"](a)), reads=[io["Bbg"]], writes=[BBN])
        kb.dma("act", lambda e, a=a: e.dma_start(out=GT[:], in_=io["gate"](a)), reads=[io["Bbg"]], writes=[BGT])
        kb.op("dve", lambda e: e.tensor_tensor(out=Y[:], in0=Y[:], in1=Y2[:], op=ALU.add), reads=[BY, BY2], writes=[BY])
        for (t0, tn) in token_tiles(TALL):
            ts_ = slice(t0, t0 + tn)
            pm, Bpm = nps()
            kb.op("pe", lambda e, pm=pm, ts_=ts_, tn=tn: e.matmul(pm[:, 0:tn], lhsT=cst[:], rhs=Y[:, ts_], start=True, stop=True), reads=[Bcst, BY], writes=[Bpm])
            kb.op("dve", lambda e, pm=pm, ts_=ts_, tn=tn: e.tensor_tensor(out=Y[:, ts_], in0=Y[:, ts_], in1=pm[:, 0:tn], op=ALU.subtract),
                  reads=[Bpm, BY], writes=[BY])
            kb.op("act", lambda e, ts_=ts_: e.activation(out=TMP[:, ts_], in_=Y[:, ts_], func=AF.Square), reads=[BY], writes=[BTMP])
            pv, Bpv = nps()
            kb.op("pe", lambda e, pv=pv, ts_=ts_, tn=tn: e.matmul(pv[:, 0:tn], lhsT=cst[:], rhs=TMP[:, ts_], start=True, stop=True), reads=[Bcst, BTMP], writes=[Bpv])
            kb.op("act", lambda e, pv=pv, ts_=ts_, tn=tn: e.activation(out=TMP[:, ts_], in_=pv[:, 0:tn], func=AF.Sqrt, bias=eps_ap(C, RWKV_GN_EPS), scale=1.0),
                  reads=[Bpv], writes=[BTMP])
            kb.op("dve", lambda e, ts_=ts_: e.reciprocal(out=TMP[:, ts_], in_=TMP[:, ts_]), reads=[BTMP], writes=[BTMP])
            kb.op("dve", lambda e, ts_=ts_: e.tensor_tensor(out=Y[:, ts_], in0=Y[:, ts_], in1=TMP[:, ts_], op=ALU.mult), reads=[BY, BTMP], writes=[BY])
            kb.op("act", lambda e, ts_=ts_, a=a: e.activation(out=Y[:, ts_], in_=Y[:, ts_], func=AF.Identity, bias=ln[:, a, 1:2], scale=ln[:, a, 0:1]),
                  reads=[BY, Bln], writes=[BY])
            kb.op("dve", lambda e, ts_=ts_: e.tensor_tensor(out=Y[:, ts_], in0=Y[:, ts_], in1=BN[:, ts_], op=ALU.add), reads=[BY, BBN], writes=[BY])
            kb.op("dve", lambda e, ts_=ts_: e.tensor_tensor(out=Y[:, ts_], in0=Y[:, ts_], in1=GT[:, ts_], op=ALU.mult), reads=[BY, BGT], writes=[BY])
        kb.dma("sp", lambda e, a=a: e.dma_start(out=io["ya"](a), in_=Y[:]), reads=[BY], writes=[io["By"]])
    C.pop()


def emit_p3(C, io, segs):
    kb = C.kb
    C.push()
    yT = io["yT"]
    xT = io["xT"]
    msk = io["msk"]
    wo = io["wo"]
    wu = io["wu"]
    wc = io["wc"]
    wd = io["wd"]
    xo = io["xo"]
    SM = max(s[1] for s in segs); TM_ = SM + 2

    def T(shape, dt=F32):
        return C.sb(shape, dt), Buf()
    x_sb, Bx = T([128, 16, TM_]); y_sb, By = T([128, 16, TM_], BF16); o_sb, Bo = T([128, 16, TM_])
    a_sb, Ba = T([128, NFF, SM], BF16)
    v_sb, Bv = T([128, 2, 7, 16]); m_sb, Bm = T([128, 2 * len(segs)]); wc_sb, Bwc = T([128, 2 * NFF, 3])
    gm, Bgm = T([128, 2, 4, 16])
    ones, Bones = T([128, 128]); rstd, Brstd = T([128, TM_])
    sq = [T([128, 512]) for _ in range(2)]
    tmp = [T([128, TM_]) for _ in range(2)]
    wt = [T([128, 16, 256], BF16) for _ in range(4)]
    wdt = [T([128, NFF, 128], BF16) for _ in range(2)]
    ug = [T([128, 2, TM_]) for _ in range(2)]
    cvt = [T([128, 2, SM]) for _ in range(2)]
    pss = [(C.ps([128, 512]), Buf(excl=True)) for _ in range(2)]
    psm = [(C.ps([128, 512]), Buf(excl=True)) for _ in range(6)]
    pcnt = [0]

    def nps():
        r = psm[pcnt[0] % 6]
        pcnt[0] += 1
        return r
    for (ci_, vi, vsrc) in io["vec_srcs"]:
        kb.dma("sp", lambda e, ci_=ci_, vi=vi, vsrc=vsrc: e.dma_start(out=v_sb[:, ci_, vi, :], in_=vsrc), writes=[Bv])
    kb.dma("sp", lambda e: e.dma_start(out=m_sb[:], in_=msk), writes=[Bm])
    kb.dma("sp", lambda e: e.dma_start(out=wc_sb[:], in_=wc), writes=[Bwc])
    kb.op("dve", lambda e: e.memset(ones[:], 1.0 / D), writes=[Bones])
    for ci in range(2):
        kb.op("dve", lambda e, ci=ci: e.tensor_tensor(out=gm[:, ci, 0, :], in0=v_sb[:, ci, 0, :], in1=v_sb[:, ci, 3, :], op=ALU.mult), reads=[Bv], writes=[Bgm])
        kb.op("dve", lambda e, ci=ci: e.scalar_tensor_tensor(out=gm[:, ci, 1, :], in0=v_sb[:, ci, 5, :], scalar=1.0, in1=v_sb[:, ci, 1, :],
                                                             op0=ALU.add, op1=ALU.mult), reads=[Bv], writes=[Bgm])
        kb.op("dve", lambda e, ci=ci: e.tensor_tensor(out=gm[:, ci, 2, :], in0=v_sb[:, ci, 2, :], in1=v_sb[:, ci, 6, :], op=ALU.mult), reads=[Bv], writes=[Bgm])
    xv = xT.rearrange("(kc p) t -> p kc t", p=128)
    yv = yT.rearrange("(kc p) t -> p kc t", p=128)
    xov = xo.rearrange("(kc p) t -> p kc t", p=128)
    wi = [0]

    def half_tiles(n):
        h = (n + 1) // 2
        return [(0, h), (h, n - h)] if n > 512 else [(0, n)]
    for si, (lo, S, ci, hl, hr) in enumerate(segs):
        Tn = S + 2
        out0 = lo
        a0 = 0 if hl else 1
        a1 = Tn if hr else Tn - 1
        g0 = lo - 1 + a0
        if not hl:
            kb.op("dve", lambda e: e.memset(x_sb[:, :, 0:1], 0.0), writes=[Bx])
            kb.op("dve", lambda e: e.memset(y_sb[:, :, 0:1], 0.0), writes=[By])
        if not hr:
            kb.op("dve", lambda e, Tn=Tn: e.memset(x_sb[:, :, Tn - 1:Tn], 0.0), writes=[Bx])
            kb.op("dve", lambda e, Tn=Tn: e.memset(y_sb[:, :, Tn - 1:Tn], 0.0), writes=[By])
        for kc in range(16):
            kb.dma("sp" if kc % 2 == 0 else "act", lambda e, kc=kc, a0=a0, a1=a1, g0=g0: e.dma_start(out=x_sb[:, kc, a0:a1], in_=xv[:, kc, g0:g0 + a1 - a0]), reads=[io["Bx"]], writes=[Bx])
        for kc in range(16):
            kb.dma("pool", lambda e, kc=kc, a0=a0, a1=a1, g0=g0: e.dma_start(out=y_sb[:, kc, a0:a1], in_=yv[:, kc, g0:g0 + a1 - a0]), reads=[io["By"]], writes=[By])
        for nb in range(8):
            j = wi[0] % 4; wi[0] += 1
            w_, Bw_ = wt[j]
            kb.dma(io["wdma"](), lambda e, w_=w_, nb=nb: e.dma_start(out=w_[:], in_=wo[:, nb * 256:(nb + 1) * 256].rearrange("(kc p) n -> p kc n", p=128)), writes=[Bw_])
            for nc_ in range(2):
                dch = nb * 2 + nc_
                for (t0, tn) in half_tiles(Tn):
                    p, Bp = nps()
                    for kc in range(16):
                        kb.op("pe", lambda e, p=p, w_=w_, kc=kc, nc_=nc_, t0=t0, tn=tn: e.matmul(p[:, 0:tn], lhsT=w_[:, kc, nc_ * 128:(nc_ + 1) * 128],
                                                                                            rhs=y_sb[:, kc, t0:t0 + tn], start=(kc == 0), stop=(kc == 15)),
                              reads=[Bw_, By], writes=[Bp])
                    kb.op("act", lambda e, p=p, dch=dch, t0=t0, tn=tn: e.copy(out=o_sb[:, dch, t0:t0 + tn], in_=p[:, 0:tn]), reads=[Bp], writes=[Bo])
        emit_rstd(C, o_sb, Bo, 16, Tn, ones, Bones, rstd, Brstd, [s[0] for s in sq], [s[1] for s in sq], [p[0] for p in pss], [p[1] for p in pss], NORM_EPS)
        for kc in range(16):
            t_, Bt_ = tmp[kc % 2]
            kb.op("pool", lambda e, t_=t_, kc=kc, Tn=Tn: e.tensor_tensor(out=t_[:, 0:Tn], in0=o_sb[:, kc, 0:Tn], in1=rstd[:, 0:Tn], op=ALU.mult),
                  reads=[Bo, Brstd], writes=[Bt_])
            kb.op("dve", lambda e, t_=t_, kc=kc, Tn=Tn, ci=ci: e.scalar_tensor_tensor(out=x_sb[:, kc, 0:Tn], in0=t_[:, 0:Tn], scalar=gm[:, ci, 0, kc:kc + 1],
                                                                                 in1=x_sb[:, kc, 0:Tn], op0=ALU.mult, op1=ALU.add),
                  reads=[Bt_, Bgm, Bx], writes=[Bx])
        emit_rstd(C, x_sb, Bx, 16, Tn, ones, Bones, rstd, Brstd, [s[0] for s in sq], [s[1] for s in sq], [p[0] for p in pss], [p[1] for p in pss], NORM_EPS)
        for kc in range(16):
            t_, Bt_ = tmp[kc % 2]
            kb.op("dve", lambda e, t_=t_, kc=kc, Tn=Tn: e.tensor_tensor(out=t_[:, 0:Tn], in0=x_sb[:, kc, 0:Tn], in1=rstd[:, 0:Tn], op=ALU.mult),
                  reads=[Bx, Brstd], writes=[Bt_])
            kb.op("act", lambda e, t_=t_, kc=kc, Tn=Tn, ci=ci: e.activation(out=y_sb[:, kc, 0:Tn], in_=t_[:, 0:Tn], func=AF.Identity,
                                                                        bias=v_sb[:, ci, 4, kc:kc + 1], scale=gm[:, ci, 1, kc:kc + 1]),
                  reads=[Bt_, Bgm, Bv], writes=[By])
        for blk in range(NFF // 2):
            wts = []
            for half in range(2):
                j = wi[0] % 4; wi[0] += 1
                w_, Bw_ = wt[j]
                c0 = half * D_FF + blk * 256
                kb.dma(io["wdma"](), lambda e, w_=w_, c0=c0: e.dma_start(out=w_[:], in_=wu[:, c0:c0 + 256].rearrange("(kc p) n -> p kc n", p=128)), writes=[Bw_])
                wts.append((w_, Bw_))
            for cc in range(2):
                c = blk * 2 + cc
                u_, Bu_ = ug[c % 2]
                cv_, Bcv_ = cvt[c % 2]
                for half in range(2):
                    w_, Bw_ = wts[half]
                    for (t0, tn) in half_tiles(Tn):
                        p, Bp = nps()
                        for kc in range(16):
                            kb.op("pe", lambda e, p=p, w_=w_, kc=kc, cc=cc, t0=t0, tn=tn: e.matmul(p[:, 0:tn], lhsT=w_[:, kc, cc * 128:(cc + 1) * 128],
                                                                                              rhs=y_sb[:, kc, t0:t0 + tn], start=(kc == 0), stop=(kc == 15)),
                                  reads=[Bw_, By], writes=[Bp])
                        kb.op("act", lambda e, p=p, u_=u_, half=half, t0=t0, tn=tn: e.copy(out=u_[:, half, t0:t0 + tn], in_=p[:, 0:tn]), reads=[Bp], writes=[Bu_])
                kb.op("dve", lambda e, u_=u_, si=si: e.tensor_tensor(out=u_[:, :, 0:1], in0=u_[:, :, 0:1], in1=m_sb[:, 2 * si:2 * si + 1].unsqueeze(1).broadcast_to([128, 2, 1]),
                                                                      op=ALU.mult), reads=[Bu_, Bm], writes=[Bu_])
                kb.op("dve", lambda e, u_=u_, si=si, Tn=Tn: e.tensor_tensor(out=u_[:, :, Tn - 1:Tn], in0=u_[:, :, Tn - 1:Tn],
                                                                             in1=m_sb[:, 2 * si + 1:2 * si + 2].unsqueeze(1).broadcast_to([128, 2, 1]), op=ALU.mult),
                      reads=[Bu_, Bm], writes=[Bu_])
                for half in range(2):
                    wrow = half * NFF + c
                    kb.op("act", lambda e, u_=u_, cv_=cv_, half=half, wrow=wrow, S=S: e.activation(out=cv_[:, half, 0:S], in_=u_[:, half, 1:1 + S], func=AF.Copy,
                                                                                             scale=wc_sb[:, wrow, 1:2]), reads=[Bu_, Bwc], writes=[Bcv_])
                    eng = "dve"
                    kb.op(eng, lambda e, u_=u_, cv_=cv_, half=half, wrow=wrow, S=S: e.scalar_tensor_tensor(out=cv_[:, half, 0:S], in0=u_[:, half, 0:S], scalar=wc_sb[:, wrow, 0:1],
                                                                                                    in1=cv_[:, half, 0:S], op0=ALU.mult, op1=ALU.add),
                          reads=[Bu_, Bwc, Bcv_], writes=[Bcv_])
                    kb.op(eng, lambda e, u_=u_, cv_=cv_, half=half, wrow=wrow, S=S: e.scalar_tensor_tensor(out=cv_[:, half, 0:S], in0=u_[:, half, 2:2 + S], scalar=wc_sb[:, wrow, 2:3],
                                                                                                    in1=cv_[:, half, 0:S], op0=ALU.mult, op1=ALU.add),
                          reads=[Bu_, Bwc, Bcv_], writes=[Bcv_])
                kb.op("act", lambda e, cv_=cv_, S=S: e.activation(out=cv_[:, 0, 0:S], in_=cv_[:, 0, 0:S], func=AF.Silu), reads=[Bcv_], writes=[Bcv_])
                kb.op("pool", lambda e, cv_=cv_, c=c, S=S: e.tensor_tensor(out=a_sb[:, c, 0:S], in0=cv_[:, 0, 0:S], in1=cv_[:, 1, 0:S], op=ALU.mult),
                      reads=[Bcv_], writes=[Ba])
        for nb in range(D // 128):
            w_, Bw_ = wdt[nb % 2]
            kb.dma(io["wdma"](), lambda e, w_=w_, nb=nb: e.dma_start(out=w_[:], in_=wd[nb]), writes=[Bw_])
            for nc_ in range(1):
                dch = nb
                p, Bp = nps()
                for c in range(NFF):
                    kb.op("pe", lambda e, p=p, w_=w_, c=c, nc_=nc_, S=S: e.matmul(p[:, 0:S], lhsT=w_[:, c, nc_ * 128:(nc_ + 1) * 128], rhs=a_sb[:, c, 0:S],
                                                                             start=(c == 0), stop=(c == NFF - 1)), reads=[Bw_, Ba], writes=[Bp])
                kb.op("act", lambda e, p=p, dch=dch, S=S: e.copy(out=o_sb[:, dch, 0:S], in_=p[:, 0:S]), reads=[Bp], writes=[Bo])
        emit_rstd(C, o_sb, Bo, 16, S, ones, Bones, rstd, Brstd, [s[0] for s in sq], [s[1] for s in sq], [p[0] for p in pss], [p[1] for p in pss], NORM_EPS)
        for kc in range(16):
            t_, Bt_ = tmp[kc % 2]
            kb.op("pool", lambda e, t_=t_, kc=kc, S=S: e.tensor_tensor(out=t_[:, 0:S], in0=o_sb[:, kc, 0:S], in1=rstd[:, 0:S], op=ALU.mult),
                  reads=[Bo, Brstd], writes=[Bt_])
            kb.op("dve", lambda e, t_=t_, kc=kc, S=S, ci=ci: e.scalar_tensor_tensor(out=t_[:, 0:S], in0=t_[:, 0:S], scalar=gm[:, ci, 2, kc:kc + 1],
                                                                               in1=x_sb[:, kc, 1:1 + S], op0=ALU.mult, op1=ALU.add),
                  reads=[Bt_, Bgm, Bx], writes=[Bt_])
            kb.dma("sp", lambda e, t_=t_, kc=kc, S=S, out0=out0: e.dma_start(out=xov[:, kc, out0:out0 + S], in_=t_[:, 0:S]), reads=[Bt_], writes=[io["Bxo"]])
    C.pop()


def _ctx_push(self):
    self._stk.append(self.st)
    self.st = ExitStack()


def _ctx_pop(self):
    self.kb.barrier()
    self.st.close()
    self.st = self._stk.pop()


def _ctx_scratch(self, name, shape, dt=F32):
    return self.nc.dram_tensor(name, list(shape), dt, kind="Internal").ap()


Ctx.push = _ctx_push
Ctx.pop = _ctx_pop
Ctx.scratch = _ctx_scratch


def emit_p0f(C, io, nl):
    kb = C.kb
    C.push()

    def T(shape, dt=F32):
        return C.sb(shape, dt), Buf()
    c_sb, Bc = T([128, 16, 2]); s_sb, Bs = T([128, 16, 2]); mb_sb, Bmb = T([128, nl, 96]); mv, Bmv = T([128, 2, nl, 96])
    wt = [T([128, 16, 512]) for _ in range(4)]
    pst = [(C.ps([128, 512]), Buf(excl=True)) for _ in range(4)]
    kb.dma("sp", lambda e: e.dma_start(out=c_sb[:], in_=io["cT"]), writes=[Bc])
    kb.dma("sp", lambda e: e.dma_start(out=mb_sb[:], in_=io["mb"]), writes=[Bmb])
    kb.op("act", lambda e: e.activation(out=s_sb[:], in_=c_sb[:], func=AF.Silu), reads=[Bc], writes=[Bs])
    it = 0
    for l in range(nl):
        for nt in range(24):
            w, Bw = wt[it % 4]; p, Bp = pst[it % 4]
            src = io["mw"][l, :, nt * 512:(nt + 1) * 512].rearrange("(kc p) n -> p kc n", p=128)
            kb.dma("sp" if it % 2 == 0 else "act", lambda e, w=w, src=src: e.dma_start(out=w[:], in_=src), writes=[Bw])
            for cc in range(4):
                for kc in range(16):
                    kb.op("pe", lambda e, p=p, w=w, cc=cc, kc=kc: e.matmul(p[:, cc * 2:cc * 2 + 2], lhsT=w[:, kc, cc * 128:(cc + 1) * 128], rhs=s_sb[:, kc, :],
                                                                         start=(kc == 0), stop=(kc == 15)), reads=[Bw, Bs], writes=[Bp])
            g0 = nt * 4
            kb.op("dve", lambda e, p=p, l=l, g0=g0: e.tensor_tensor(out=mv[:, :, l, g0:g0 + 4], in0=p[:, 0:8].rearrange("p (c r) -> p r c", r=2),
                                                                   in1=mb_sb[:, l, g0:g0 + 4].unsqueeze(1).broadcast_to([128, 2, 4]), op=ALU.add),
                  reads=[Bp, Bmb], writes=[Bmv])
            it += 1
    kb.dma("sp", lambda e: e.dma_start(out=io["mscr"], in_=mv[:]), reads=[Bmv], writes=[Buf()])
    C.pop()


def emit_reverse(C, io, jobs):
    kb = C.kb
    C.push()

    def T(shape, dt=F32):
        return C.sb(shape, dt), Buf()
    cst, Bcst = T([128, 256])
    kb.dma("sp", lambda e: e.dma_start(out=cst[:, 0:128], in_=io["ident"]), writes=[Bcst])
    kb.dma("sp", lambda e: e.dma_start(out=cst[:, 128:256], in_=io["jmat"]), writes=[Bcst])
    ident = cst[:, 0:128]; J = cst[:, 128:256]
    X = [T([128, TALL]) for _ in range(3)]
    XR = [T([128, TALL]) for _ in range(3)]
    TK = [T([128, 512]) for _ in range(2)]
    psl = [(C.ps([128, 512]), Buf(excl=True)) for _ in range(4)]
    pc = [0]

    def nps():
        r = psl[pc[0] % 4]; pc[0] += 1
        return r
    for ji, (src, dst, n, off_c, off_l) in enumerate(jobs):
        x, Bx = X[ji % 3]; xr, Bxr = XR[ji % 3]
        kb.dma("sp" if ji % 2 == 0 else "act", lambda e, x=x, src=src, n=n: e.dma_start(out=x[0:n, :], in_=src), writes=[Bx])
        for b4 in range(0, NCH, 4):
            nb = min(4, NCH - b4)
            tk, Btk = TK[(b4 // 4) % 2]
            ps, Bps = nps()
            for bi in range(nb):
                b = b4 + bi
                kb.op("pe", lambda e, ps=ps, bi=bi, b=b, x=x, n=n: e.transpose(out=ps[:, bi * 128:bi * 128 + n], in_=x[0:n, b * 128:(b + 1) * 128], identity=ident[0:n, 0:n]),
                      reads=[Bx, Bcst], writes=[Bps])
            kb.op("act", lambda e, ps=ps, tk=tk, nb=nb, n=n: e.copy(out=tk[:, 0:nb * 128].rearrange("p (a b) -> p a b", a=nb)[:, :, 0:n],
                                                                   in_=ps[:, 0:nb * 128].rearrange("p (a b) -> p a b", a=nb)[:, :, 0:n]), reads=[Bps], writes=[Btk])
            ps2, Bps2 = nps()
            for bi in range(nb):
                kb.op("pe", lambda e, ps2=ps2, bi=bi, tk=tk, n=n: e.matmul(ps2[0:n, bi * 128:(bi + 1) * 128], lhsT=tk[:, bi * 128:bi * 128 + n], rhs=J, start=True, stop=True),
                      reads=[Btk, Bcst], writes=[Bps2])
            for bi in range(nb):
                b = b4 + bi
                mb_ = (1 - b) if b < 2 else (2 + (15 - (b - 2)))
                kb.op("dve", lambda e, ps2=ps2, bi=bi, mb_=mb_, xr=xr, n=n: e.tensor_copy(out=xr[0:n, mb_ * 128:(mb_ + 1) * 128], in_=ps2[0:n, bi * 128:(bi + 1) * 128]),
                      reads=[Bps2], writes=[Bxr])
        kb.dma("sp", lambda e, xr=xr, dst=dst, n=n, off_c=off_c: e.dma_start(out=dst[:, off_c:off_c + CTX], in_=xr[0:n, 0:CTX]), reads=[Bxr], writes=[Buf()])
        kb.dma("act", lambda e, xr=xr, dst=dst, n=n, off_l=off_l: e.dma_start(out=dst[:, off_l:off_l + SEQ], in_=xr[0:n, CTX:TALL]), reads=[Bxr], writes=[Buf()])
    C.pop()


def cast_jobs(jobs):
    out = []
    for (dst, src, rows, cols, rblk, cb) in jobs:
        for r0 in range(0, rows, rblk):
            n = min(rblk, rows - r0)
            out.append((dst, src, r0, n, cb))
    return out


def emit_cast_one(C, job):
    if job[0] == "dn":
        _, dst, src, nb = job
        C.kb.dma("pool", lambda e: e.dma_start(out=dst[nb], in_=src[:, nb * 128:(nb + 1) * 128].rearrange("(c p) n -> p c n", p=128)), writes=[Buf()], bg=True)
        return
    dst, src, r0, n, cb = job
    C.kb.dma("pool", lambda e: e.dma_start(out=dst[r0:r0 + n, :].rearrange("r (a b) -> r a b", b=cb),
                                           in_=src[r0:r0 + n, :].rearrange("r (a b) -> r a b", b=cb)), writes=[Buf()], bg=True)


def emit_cast(C, jobs):
    for j in cast_jobs(jobs):
        emit_cast_one(C, j)


_WQ = [0]


def _wdma():
    _WQ[0] += 1
    return "sp" if _WQ[0] % 2 == 0 else "act"


P3F_SEGS = [(0, 256, 0, False, False), (256, 410, 1, False, True), (666, 410, 1, True, True), (1076, 410, 1, True, True), (1486, 410, 1, True, True), (1896, 408, 1, True, False)]


def build_fused(nl=DEPTH, dbg=False):
    C = Ctx()
    C._stk = []
    kb = C.kb
    for eps in (NORM_EPS, 1e-12, RET_EPS, RWKV_GN_EPS):
        make_eps(C, eps)
    I = C.dram_in
    x0T = I("x0T", [D, TALL]); cT = I("cT", [128, 16, 2]); mw = I("mw", [nl, D, NMOD * D]); mb = I("mb", [128, nl, 96]); ng = I("ng", [128, nl, 4, 16])
    w_in = I("w_in", [nl, D, IN_W]); w_out = I("w_out", [nl, D, D]); w_up = I("w_up", [nl, D, 2 * D_FF]); w_cv = I("w_cv", [nl, 128, 2 * NFF, 3]); w_dn = I("w_dn", [nl, D_FF, D])
    r_mu = I("r_mu", [nl, 2, 64, 24, 2]); r_mu2 = I("r_mu2", [nl, 2, 128, 4, 2]); r_par = I("r_par", [nl, 2, 64, 5, 8])
    r_wup = I("r_wup", [nl, 2, 96, 512]); r_aup = I("r_aup", [nl, 2, 96, 512]); r_gup = I("r_gup", [nl, 256, 512])
    r_cst = I("r_cst", [128, 256]); r_mk = I("r_mk", [64, 2112]); r_ln = I("r_ln", [nl, 128, 4, 2]); c_w = I("c_w", [nl, 128, 4, 3]); c_bd = I("c_bd", [128, 128])
    a_nrm = I("a_nrm", [nl, 128, 5]); a_wq = I("a_wq", [nl, 128, 3, 4, 256]); a_wkv = I("a_wkv", [nl, 128, 2, 4, 256]); a_rope = I("a_rope", [2, 64, TALL]); a_cst = I("a_cst", [128, 384])
    d_rope = I("d_rope", [2, 128, TALL]); d_rd = I("d_rd", [nl, 128, 8]); d_gn = I("d_gn", [nl, 128, 4]); d_tab = I("d_tab", [128, 770]); d_cst = I("d_cst", [128, 256])
    p3_msk = I("p3_msk", [128, 12]); jmat = I("jmat", [128, 128]); identm = I("identm", [128, 128])
    xo = C.dram_out("xo", [D, SEQ])
    S = C.scratch
    xs = [S("xsA", [D, TALL]), S("xsB", [D, TALL])]
    pT = S("pT", [IN_W, TALL]); yT = S("yT", [D, TALL]); mscr = S("mscr", [128, 2, nl, 96])
    u_f = S("u_f", [1536, TP]); u2_f = S("u2_f", [512, TP]); u_r = S("u_r", [1536, TP]); u2_r = S("u2_r", [512, TP])
    y_f = S("y_f", [512, TALL]); y_r = S("y_r", [512, TALL]); y_b = S("y_b", [512, TALL])
    bon = S("bon", [512, TALL]); gate = S("gate", [512, TALL]); bon2 = S("bon2", [512, TALL]); gate2 = S("gate2", [512, TALL])
    wb_in = S("wb_in", [D, IN_W], BF16); wb_out = S("wb_out", [D, D], BF16); wb_up = S("wb_up", [D, 2 * D_FF], BF16); wb_dn = S("wb_dn", [16, 128, NFF, 128], BF16)
    emit_cast(C, [(wb_in, w_in[0], D, IN_W, 512, 896)])
    dbg_outs = {}
    if dbg:
        dbg_outs = {"d_pT": C.dram_out("d_pT", [IN_W, TALL]), "d_yT": C.dram_out("d_yT", [D, TALL]), "d_x1": C.dram_out("d_x1", [D, TALL]),
                    "d_m": C.dram_out("d_m", [128, 2, nl, 96])}
    C.push()
    Z = C.sb([128, TP]); Bz = Buf()
    kb.op("dve", lambda e: e.memset(Z[:], 0.0), writes=[Bz])
    qi = 0
    for (t_, nr) in [(u_f, 1536), (u2_f, 512), (u_r, 1536), (u2_r, 512)]:
        for r0 in range(0, nr, 128):
            kb.dma("sp" if qi % 2 == 0 else "act", lambda e, t_=t_, r0=r0: e.dma_start(out=t_[r0:r0 + 128, :], in_=Z[:]), reads=[Bz], writes=[Buf()])
            qi += 1
    C.pop()
    emit_p0f(C, {"cT": cT, "mw": mw, "mb": mb, "mscr": mscr}, nl)
    if dbg:
        kb.dma("sp", lambda e: e.dma_start(out=dbg_outs["d_m"], in_=mscr), writes=[Buf()], is_out=True)
    sw64 = [(0, 16, 16), (16, 32, 0), (32, 48, 48), (48, 64, 32)]
    sw128 = [(0, 32, 32), (32, 64, 0), (64, 96, 96), (96, 128, 64)]
    x_cur = x0T
    for l in range(nl):
        x_next = xs[l % 2]
        for (c0, classes) in [(0, [(0, 0, CTX), (1, CTX, T1)]), (T1, [(1, 0, T1)])]:
            emit_p1(C, {"xT": x_cur[:, c0:c0 + T1], "w": wb_in, "wdma": _wdma, "pT": pT[:, c0:c0 + T1],
                        "vec_srcs": [ng[:, l, 0, :], mscr[:, 1, l, 0:16], mscr[:, 1, l, 16:32], mscr[:, 0, l, 0:16], mscr[:, 0, l, 16:32]]}, classes)
        if dbg and l == 0:
            kb.dma("sp", lambda e: e.dma_start(out=dbg_outs["d_pT"], in_=pT), writes=[Buf()], is_out=True)
        for (r0, n, dst, d0) in [(0, 1536, u_f, 0), (1536, 96, u2_f, 0), (1632, 96, u2_f, 128), (1728, 256, u2_f, 256)]:
            kb.dma("sp", lambda e, r0=r0, n=n, dst=dst, d0=d0: e.dma_start(out=dst[d0:d0 + n, 1:1 + CTX], in_=pT[r0:r0 + n, 0:CTX]), writes=[Buf()])
            kb.dma("act", lambda e, r0=r0, n=n, dst=dst, d0=d0: e.dma_start(out=dst[d0:d0 + n, 259:259 + SEQ], in_=pT[r0:r0 + n, CTX:TALL]), writes=[Buf()])
        jobs = [(pT[128 * i:128 * i + 128, :], u_r[128 * i:128 * i + 128, :], 128, 1, 259) for i in range(12)]
        jobs += [(pT[1536:1632, :], u2_r[0:96, :], 96, 1, 259), (pT[1632:1728, :], u2_r[128:224, :], 96, 1, 259)]
        emit_reverse(C, {"ident": identm, "jmat": jmat}, jobs)
        cj = [(wb_out, w_out[l], D, D, 1024, 1024), (wb_up, w_up[l], D, 2 * D_FF, 256, 1024), ]
        cjd = [("dn", wb_dn, w_dn[l], nb) for nb in range(16)]
        if l + 1 < nl:
            cj.append((wb_in, w_in[l + 1], D, IN_W, 512, 896))
        pending = cast_jobs(cj) + cjd

        def hook(pending=pending):
            if pending:
                emit_cast_one(C, pending.pop(0))
        for d, (u_, u2_, yo_, bo_, go_) in enumerate([(u_f, u2_f, y_f, bon, gate), (u_r, u2_r, y_r, bon2, gate2)]):
            emit_p2r(C, {"u": u_, "u2": u2_, "mu": r_mu[l, d], "mu2": r_mu2[l, d], "par": r_par[l, d], "wup": r_wup[l, d], "aup": r_aup[l, d],
                         "gup": r_gup[l], "cst": r_cst, "mk": r_mk, "yT": yo_, "bonT": bo_, "gateT": go_, "hook": hook, "aux": (d == 0)})
        while pending:
            hook()
        emit_reverse(C, {"ident": identm, "jmat": jmat}, [(y_r[128 * i:128 * i + 128, :], y_b[128 * i:128 * i + 128, :], 128, 0, CTX) for i in range(4)])
        dB = Buf()
        emit_p2c(C, {"cw": c_w[l], "ln": r_ln[l], "cst": c_bd, "cv": (lambda i, a: pT[2688 + 512 * i + 128 * a:2688 + 512 * i + 128 * a + 128, :]),
                     "yf": (lambda a: y_f[128 * a:128 * a + 128, :]), "yb": (lambda a: y_b[128 * a:128 * a + 128, :]),
                     "bon": (lambda a: bon[128 * a:128 * a + 128, :]), "gate": (lambda a: gate[128 * a:128 * a + 128, :]),
                     "yc": (lambda a: yT[1024 + 128 * a:1024 + 128 * a + 128, :]), "ya": (lambda a: yT[128 * a:128 * a + 128, :]),
                     "By": dB, "Byf": dB, "Byb": dB, "Bbg": dB})
        emit_p2b(C, {"nrm": a_nrm[l], "wq": a_wq[l], "wkv": a_wkv[l], "rope": a_rope, "cst": a_cst,
                     "cx": [pT[1984 + 128 * a:1984 + 128 * a + 128, :] for a in range(5)],
                     "kr": [(0, 64, 0, pT[2624:2688, :])] + [(a0, a1, 1, pT[2624 + s0:2624 + s0 + 16, :]) for (a0, a1, s0) in sw64],
                     "y": yT[512:1024, :], "By": dB})

        def qkv_src(h, a):
            base = 4224 + 128 * h
            if a < 4:
                return [(0, 128, pT[base + 512 * a:base + 512 * a + 128, :])]
            b2 = base + 512 * (a - 4)
            return [(a0, a1, pT[b2 + s0:b2 + s0 + 32, :]) for (a0, a1, s0) in sw128]
        emit_p2d(C, {"qkv": qkv_src, "rope": d_rope, "rd": d_rd[l], "gn": d_gn[l], "tab": d_tab, "cst": d_cst, "yT": yT[1536:2048, :]})
        if dbg and l == 0:
            kb.dma("sp", lambda e: e.dma_start(out=dbg_outs["d_yT"], in_=yT), writes=[Buf()], is_out=True)
        last = (l == DEPTH - 1) and (nl == DEPTH)
        segs = P3F_SEGS[1:] if last else P3F_SEGS
        msk_ap = p3_msk[:, 2:12] if last else p3_msk
        vs = []
        for ci_, r in ((0, 1), (1, 0)):
            for k in range(3):
                vs.append((ci_, k, ng[:, l, k + 1, :]))
            for k in range(4):
                vs.append((ci_, 3 + k, mscr[:, r, l, (2 + k) * 16:(3 + k) * 16]))
        emit_p3(C, {"yT": yT, "xT": x_cur, "vec_srcs": vs, "msk": msk_ap, "wo": wb_out, "wu": wb_up, "wc": w_cv[l], "wd": wb_dn, "wdma": _wdma, "xo": x_next,
                    "Bx": dB, "By": dB, "Bxo": dB}, segs)
        if dbg and l == 0:
            kb.dma("sp", lambda e, x_next=x_next: e.dma_start(out=dbg_outs["d_x1"], in_=x_next), writes=[Buf()], is_out=True)
        x_cur = x_next
    kb.barrier()
    kb.dma("sp", lambda e, x_cur=x_cur: e.dma_start(out=xo, in_=x_cur[:, CTX:TALL]), writes=[Buf()], is_out=True)
    return C.done()


_FUSED = {}


def _prep_inputs(b, nl, x, c, ctx, c_ctx, mod_w, mod_b, norm_g, w_in, rwkv_shift, rwkv_w0, rwkv_w_up, rwkv_a0, rwkv_a_up,
                 rwkv_g_up, rwkv_vecs, mla_q_norm, mla_kv_norm, mla_w_uq, mla_w_ukv, conv_w, ret_decay, ret_gn_g,
                 w_out, mlp_w_up, mlp_conv, mlp_w_down, shared):
    ca = np.ascontiguousarray
    im = dict(shared)
    im["x0T"] = ca(np.concatenate([ctx[b], x[b]], axis=0).T)
    cc = np.stack([c[b], c_ctx], axis=1)
    im["cT"] = ca(cc.reshape(16, 128, 2).transpose(1, 0, 2))
    return im


def _prep_shared(nl, mod_w, mod_b, norm_g, w_in, rwkv_shift, rwkv_w0, rwkv_w_up, rwkv_a0, rwkv_a_up,
                 rwkv_g_up, rwkv_vecs, mla_q_norm, mla_kv_norm, mla_w_uq, mla_w_ukv, conv_w, ret_decay, ret_gn_g,
                 w_out, mlp_w_up, mlp_conv, mlp_w_down):
    ca = np.ascontiguousarray
    sh = {}
    sh["mw"] = ca(mod_w[:nl]); sh["mb"] = ca(mod_b[:nl].reshape(nl, 96, 128).transpose(2, 0, 1))
    sh["ng"] = ca(norm_g[:nl].reshape(nl, 4, 16, 128).transpose(3, 0, 1, 2))
    sh["w_in"] = ca(w_in[:nl]); sh["w_out"] = ca(w_out[:nl]); sh["w_up"] = ca(mlp_w_up[:nl]); sh["w_dn"] = ca(mlp_w_down[:nl])
    sh["w_cv"] = ca(mlp_conv[:nl].transpose(0, 2, 1).reshape(nl, 2 * NFF, 128, 3).transpose(0, 2, 1, 3))
    r_mu = np.zeros((nl, 2, 64, 24, 2), np.float32); r_mu2 = np.zeros((nl, 2, 128, 4, 2), np.float32); r_par = np.zeros((nl, 2, 64, 5, 8), np.float32)
    for l in range(nl):
        for d in range(2):
            sh_ = rwkv_shift[l] if d == 0 else rwkv_shift[l][::-1]
            r_mu[l, d] = sh_[:, 0:1536].T.reshape(24, 64, 2).transpose(1, 0, 2)
            m2 = np.zeros((512, 2), np.float32)
            m2[0:96] = sh_[:, 1536:1632].T; m2[128:224] = sh_[:, 1632:1728].T; m2[256:512] = sh_[:, 1728:1984].T
            r_mu2[l, d] = m2.reshape(4, 128, 2).transpose(1, 0, 2)
            r_par[l, d] = np.stack([_tile8(rwkv_w0[l, d]), _tile8(rwkv_a0[l, d]), _tile8(rwkv_vecs[l, 0]), _tile8(rwkv_vecs[l, 1]), _tile8(rwkv_vecs[l, 2])], axis=1)
    sh["r_mu"] = r_mu; sh["r_mu2"] = r_mu2; sh["r_par"] = r_par
    sh["r_wup"] = ca(rwkv_w_up[:nl]); sh["r_aup"] = ca(rwkv_a_up[:nl]); sh["r_gup"] = ca(rwkv_g_up[:nl])
    cst, mk = _rw_consts()
    sh["r_cst"] = cst; sh["r_mk"] = mk
    sh["r_ln"] = ca(np.stack([rwkv_vecs[:nl, 3].reshape(nl, 4, 128), rwkv_vecs[:nl, 4].reshape(nl, 4, 128)], axis=-1).transpose(0, 2, 1, 3))
    sh["c_w"] = ca(conv_w[:nl].reshape(nl, 3, 4, 128).transpose(0, 3, 2, 1))
    sh["c_bd"] = (np.kron(np.eye(2, dtype=np.float32), np.ones((64, 64), np.float32)) / 64.0).astype(np.float32)
    sh["a_nrm"] = ca(np.concatenate([mla_q_norm[:nl].reshape(nl, 3, 128), mla_kv_norm[:nl].reshape(nl, 2, 128)], axis=1).transpose(0, 2, 1))
    sw = _rope_swap_idx(64)
    wq = np.zeros((nl, 384, 4, 256), np.float32); wkv = np.zeros((nl, 256, 4, 256), np.float32)
    for l in range(nl):
        for h in range(4):
            wh = mla_w_uq[l][:, 192 * h:192 * h + 192]
            wq[l, :, h, 0:192] = wh; wq[l, :, h, 192:256] = wh[:, 128:192][:, sw]
            wkv[l, :, h, :] = mla_w_ukv[l][:, 256 * h:256 * h + 256]
    sh["a_wq"] = ca(wq.reshape(nl, 3, 128, 4, 256).transpose(0, 2, 1, 3, 4)); sh["a_wkv"] = ca(wkv.reshape(nl, 2, 128, 4, 256).transpose(0, 2, 1, 3, 4))
    cos, sin = _rope_tables(64); sh["a_rope"] = ca(np.stack([cos, sin]))
    sh["a_cst"] = np.concatenate([np.eye(128, dtype=np.float32), np.full((128, 128), 1.0 / 384, np.float32), np.full((128, 128), 1.0 / 256, np.float32)], axis=1)
    cos, sin = _rope_tables(128); sh["d_rope"] = ca(np.stack([cos, sin]))
    sh["d_rd"] = ca(np.tile(ret_decay[:nl].reshape(nl, 1, 8), (1, 128, 1)))
    sh["d_gn"] = ca(ret_gn_g[:nl].reshape(nl, 4, 128).transpose(0, 2, 1))
    j = np.arange(128)[:, None]; i = np.arange(128)[None, :]
    tab = np.zeros((128, 770), np.float32)
    tab[:, 0:128] = np.maximum(i - j, 0); tab[:, 128:256] = (i >= j)
    tab[:, 256:384] = np.maximum(j - i, 0); tab[:, 384:512] = (j > i)
    tab[:, 512:640] = (i + 1); tab[:, 640:768] = (128 - i)
    tab[:, 768] = 127 - np.arange(128); tab[:, 769] = np.arange(128)
    sh["d_tab"] = tab
    sh["d_cst"] = np.concatenate([np.eye(128, dtype=np.float32), np.full((128, 128), 1.0 / 128, np.float32)], axis=1)
    mk_ = []
    for (_, _, _, hl, hr) in P3F_SEGS:
        mk_ += [float(hl), float(hr)]
    sh["p3_msk"] = ca(np.tile(np.array(mk_, np.float32)[None], (128, 1)))
    sh["jmat"] = ca(np.eye(128, dtype=np.float32)[::-1]); sh["identm"] = np.eye(128, dtype=np.float32)
    return sh


def run_fused(inputs, nl=DEPTH, dbg=False):
    key = (nl, dbg)
    if key not in _FUSED:
        _FUSED[key] = build_fused(nl, dbg)
    f = lambda a: np.ascontiguousarray(np.asarray(a, dtype=np.float32))
    inp = {k: f(v) for k, v in inputs.items()}
    names = ["mod_w", "mod_b", "norm_g", "w_in", "rwkv_shift", "rwkv_w0", "rwkv_w_up", "rwkv_a0", "rwkv_a_up", "rwkv_g_up", "rwkv_vecs", "mla_q_norm",
             "mla_kv_norm", "mla_w_uq", "mla_w_ukv", "conv_w", "ret_decay", "ret_gn_g", "w_out", "mlp_w_up", "mlp_conv", "mlp_w_down"]
    shared = _prep_shared(nl, *[inp[k] for k in names])
    in_maps = []
    for core in range(NCORES):
        b = core % BATCH
        in_maps.append(_prep_inputs(b, nl, inp["x"], inp["c"], inp["ctx"], inp["c_ctx"], *[None] * 22, shared))
    res = run_bass_kernel_spmd(_FUSED[key], in_maps, core_ids=list(range(NCORES)))
    return res


def kernel(**inputs):
    res = run_fused(inputs)
    out = np.stack([res.results[b]["xo"].T for b in range(BATCH)], axis=0)
    return np.ascontiguousarray(out).astype(np.float32)
```
